# Optimizing a Trainium2 kernel written in Bass

```python
import jax, jax.numpy as jnp
from jax import lax
import numpy as np

D_MODEL = 1024
BATCH = 8
SEQ = 8192
DEPTH = 1

CTX_LEN = 256
GRID_W = 64
CHUNK = 64
EPS = 1e-6

GLA_HEADS = 4
GLA_QK = D_MODEL // 2
GLA_V = D_MODEL
GLA_DK = GLA_QK // GLA_HEADS
GLA_DV = GLA_V // GLA_HEADS
GLA_LOWRANK = 16
GLA_GATE_NORM = 16.0

GDN_DK = 128
GDN_DV = 128
GDN_HEADS = D_MODEL // GDN_DV
GDN_QK = GDN_HEADS * GDN_DK
GDN_V = GDN_HEADS * GDN_DV
CONV_W = 5

_FFN_RAW = -(-8 * D_MODEL // 3)
FFN_HIDDEN = -(-_FFN_RAW // 256) * 256

SPLITS = (GLA_QK, GLA_QK, GLA_V, GLA_V, GLA_LOWRANK, GLA_LOWRANK,
          GDN_QK, GDN_QK, GDN_V, GDN_V, 4 * GDN_HEADS,
          D_MODEL, D_MODEL)
P_IN = sum(SPLITS)

kernel_name = 'hybrid_gla_gdn_prefix_dit_block'


def rmsnorm(t, w):
    tf = t.astype(jnp.float32)
    y = tf * lax.rsqrt(jnp.mean(tf * tf, axis=-1, keepdims=True) + EPS)
    return y.astype(t.dtype) * w


def l2norm(t):
    tf = t.astype(jnp.float32)
    return (tf * lax.rsqrt(jnp.sum(tf * tf, axis=-1, keepdims=True) + EPS)).astype(t.dtype)


def modulate(t, shift, scale):
    return t * (1 + scale) + shift


def split_cols(p):
    points = [int(s) for s in np.cumsum(SPLITS)[:-1]]
    return jnp.split(p, points, axis=-1)


def to_heads(t, n_heads):
    B, L, W = t.shape
    return t.reshape(B, L, n_heads, W // n_heads).transpose(0, 2, 1, 3)


def from_heads(t):
    B, H, L, d = t.shape
    return t.transpose(0, 2, 1, 3).reshape(B, L, H * d)


def to_chunks(t):
    B, H, L = t.shape[:3]
    t = t.reshape(B, H, L // CHUNK, CHUNK, *t.shape[3:])
    return jnp.moveaxis(t, 2, 0)


def from_chunks(t):
    t = jnp.moveaxis(t, 0, 2)
    B, H, n, C = t.shape[:4]
    return t.reshape(B, H, n * C, *t.shape[4:])


def to_column_major(t):
    B, L, C = t.shape
    rows = L // GRID_W
    return t.reshape(B, rows, GRID_W, C).transpose(0, 2, 1, 3)


def from_column_major(t):
    B, L, C = t.shape
    rows = L // GRID_W
    return t.reshape(B, GRID_W, rows, C).transpose(0, 2, 1, 3).reshape(B, L, C)


def centred_dwconv(t, w):
    pad = CONV_W // 2
    L = t.shape[-2]
    tp = jnp.pad(t, [(0, 0)] * (t.ndim - 2) + [(pad, pad), (0, 0)])
    return sum(w[i] * tp[..., i:i + L, :] for i in range(CONV_W))


def gla_scan(q, k, v, g, s0):
    out_dtype = v.dtype
    q, k, v, g = (t.astype(jnp.float32) for t in (q, k, v, g))
    incl = jnp.tril(jnp.ones((CHUNK, CHUNK), dtype=bool))

    def step(S, inp):
        qc, kc, vc, gc = inp
        G = jnp.cumsum(gc, axis=-2)
        diff = G[..., :, None, :] - G[..., None, :, :]
        decay = jnp.exp(jnp.where(incl[:, :, None], diff, -jnp.inf))
        A = jnp.einsum('bhid,bhjd,bhijd->bhij', qc, kc, decay)
        o = (jnp.einsum('bhij,bhjv->bhiv', A, vc)
             + jnp.einsum('bhid,bhdv->bhiv', qc * jnp.exp(G), S))
        G_end = G[..., -1, :]
        S = (jnp.exp(G_end)[..., None] * S
             + jnp.einsum('bhjd,bhjv->bhdv', kc * jnp.exp(G_end[..., None, :] - G), vc))
        return S, o

    S, o = lax.scan(step, s0.astype(jnp.float32), tuple(to_chunks(t) for t in (q, k, v, g)))
    return S, from_chunks(o).astype(out_dtype)


def gdn_scan(q, k, v, g, beta, s0):
    out_dtype = v.dtype
    q, k, v, g, beta = (to_chunks(t.astype(jnp.float32)) for t in (q, k, v, g, beta))
    strict = jnp.tril(jnp.ones((CHUNK, CHUNK), dtype=bool), -1)
    incl = jnp.tril(jnp.ones((CHUNK, CHUNK), dtype=bool))
    G = jnp.cumsum(g, axis=-1)
    diff = G[..., :, None] - G[..., None, :]
    kk = jnp.einsum('nbhid,nbhjd->nbhij', k, k)
    Lmat = beta[..., :, None] * kk * jnp.exp(jnp.where(strict, diff, -jnp.inf))
    T = Lmat + jnp.eye(CHUNK, dtype=jnp.float32)
    u = lax.linalg.triangular_solve(T, beta[..., None] * v, left_side=True, lower=True, unit_diagonal=True)
    w = lax.linalg.triangular_solve(T, (beta * jnp.exp(G))[..., None] * k, left_side=True, lower=True, unit_diagonal=True)
    Aqk = jnp.einsum('nbhid,nbhjd->nbhij', q, k) * jnp.exp(jnp.where(incl, diff, -jnp.inf))
    q_dec = q * jnp.exp(G)[..., None]
    G_end = G[..., -1]
    k_dec = k * jnp.exp(G_end[..., None] - G)[..., None]

    def step(S, inp):
        u_c, w_c, A_c, qd, kd, ge = inp
        v_new = u_c - jnp.einsum('bhid,bhdv->bhiv', w_c, S)
        o = jnp.einsum('bhid,bhdv->bhiv', qd, S) + jnp.einsum('bhij,bhjv->bhiv', A_c, v_new)
        S = jnp.exp(ge)[..., None, None] * S + jnp.einsum('bhjd,bhjv->bhdv', kd, v_new)
        return S, o

    S, o = lax.scan(step, s0.astype(jnp.float32), (u, w, Aqk, q_dec, k_dec, G_end))
    return S, from_chunks(o).astype(out_dtype)


def _identity(t):
    return t


def _reverse(t):
    return jnp.flip(t, axis=2)


def bidirectional(scan_fn, ctx_qkv, ctx_gates, lat_qkv, lat_gates, s0):
    outs_c, outs_l = [], []
    for direction, rev in enumerate((_identity, _reverse)):
        s_ctx, o_c = scan_fn(*(rev(t) for t in ctx_qkv), *(rev(t) for t in ctx_gates[direction]), s0)
        _, o_l = scan_fn(*(rev(t) for t in lat_qkv), *(rev(t) for t in lat_gates[direction]), s_ctx)
        outs_c.append(rev(o_c))
        outs_l.append(rev(o_l))
    return outs_c[0] + outs_c[1], outs_l[0] + outs_l[1]


def gla_prepare(q, k, v, lr_f, lr_b, lr_w, lr_bias):
    q = to_heads(q * GLA_DK ** -0.5, GLA_HEADS)
    k = to_heads(k, GLA_HEADS)
    v = to_heads(v, GLA_HEADS)
    gates = tuple(
        (to_heads(jax.nn.log_sigmoid((lr @ lr_w[d] + lr_bias[d]).astype(jnp.float32)) / GLA_GATE_NORM, GLA_HEADS),)
        for d, lr in enumerate((lr_f, lr_b)))
    return (q, k, v), gates


def gdn_prepare(q, k, v, ab, conv_w, a_log, dt_bias, column_major):
    B, L, _ = q.shape
    qkv = jnp.concatenate([q, k, v], axis=-1)
    if column_major:
        qkv = centred_dwconv(to_column_major(qkv), conv_w).reshape(B, L, -1)
        ab = to_column_major(ab).reshape(B, L, -1)
    else:
        qkv = centred_dwconv(qkv, conv_w)
    qkv = jax.nn.silu(qkv)
    q, k, v = jnp.split(qkv, [GDN_QK, 2 * GDN_QK], axis=-1)
    q = l2norm(to_heads(q, GDN_HEADS)) * GDN_DK ** -0.5
    k = l2norm(to_heads(k, GDN_HEADS))
    v = to_heads(v, GDN_HEADS)
    a_f, a_b, b_f, b_b = jnp.split(ab, 4, axis=-1)
    gates = tuple(
        ((-jnp.exp(a_log[d]) * jax.nn.softplus(a.astype(jnp.float32) + dt_bias[d])).transpose(0, 2, 1),
         jax.nn.sigmoid(b.astype(jnp.float32)).transpose(0, 2, 1))
        for d, (a, b) in enumerate(((a_f, b_f), (a_b, b_b))))
    return (q, k, v), gates


def head_norm_gate(o, gain, z):
    B, L, _ = o.shape
    H, dv = gain.shape
    of = o.astype(jnp.float32).reshape(B, L, H, dv)
    y = (of * lax.rsqrt(jnp.mean(of * of, axis=-1, keepdims=True) + EPS)).astype(z.dtype) * gain
    return y.reshape(B, L, H * dv) * jax.nn.silu(z)


def token_mixers(h_c, h_l, w_in, lr_w, lr_bias, gla_gain, conv_w, a_log, dt_bias, gdn_gain, w_out, need_ctx):
    B = h_l.shape[0]
    pc = split_cols(h_c @ w_in)
    pl = split_cols(h_l @ w_in)
    gc_qkv, gc_g = gla_prepare(pc[0], pc[1], pc[2], pc[4], pc[5], lr_w, lr_bias)
    gl_qkv, gl_g = gla_prepare(pl[0], pl[1], pl[2], pl[4], pl[5], lr_w, lr_bias)
    s0_gla = jnp.zeros((B, GLA_HEADS, GLA_DK, GLA_DV), jnp.float32)
    o_gla_c, o_gla_l = bidirectional(gla_scan, gc_qkv, gc_g, gl_qkv, gl_g, s0_gla)
    dc_qkv, dc_g = gdn_prepare(pc[6], pc[7], pc[8], pc[10], conv_w, a_log, dt_bias, False)
    dl_qkv, dl_g = gdn_prepare(pl[6], pl[7], pl[8], pl[10], conv_w, a_log, dt_bias, True)
    s0_gdn = jnp.zeros((B, GDN_HEADS, GDN_DK, GDN_DV), jnp.float32)
    o_gdn_c, o_gdn_l = bidirectional(gdn_scan, dc_qkv, dc_g, dl_qkv, dl_g, s0_gdn)
    o_gdn_l = from_column_major(from_heads(o_gdn_l))

    def merge(p, o_gla, o_gdn):
        y_gla = head_norm_gate(o_gla, gla_gain, p[3])
        y_gdn = head_norm_gate(o_gdn, gdn_gain, p[9])
        return (jax.nn.sigmoid(p[11]) * y_gla + jax.nn.sigmoid(p[12]) * y_gdn) @ w_out

    y_l = merge(pl, from_heads(o_gla_l), o_gdn_l)
    y_c = merge(pc, from_heads(o_gla_c), from_heads(o_gdn_c)) if need_ctx else None
    return y_c, y_l


def swiglu(t, w_gate, w_up, w_down):
    return (jax.nn.silu(t @ w_gate) * (t @ w_up)) @ w_down


def setup_inputs(seed: int = 0) -> dict:
    key = jax.random.key(seed)
    ks = jax.random.split(key, 24)
    f32 = jnp.float32

    def nrm(k, shape, fan_in):
        return jax.random.normal(k, shape, f32) * fan_in ** -0.5

    def gain(k, shape):
        return 1.0 + 0.02 * jax.random.normal(k, shape, f32)

    dt = jnp.exp(jax.random.uniform(ks[14], (DEPTH, 2, GDN_HEADS), f32, np.log(1e-3), np.log(1e-1)))
    return {
        'x': jax.random.normal(ks[0], (BATCH, SEQ, D_MODEL), f32),
        'c': jax.random.normal(ks[1], (BATCH, D_MODEL), f32),
        'ctx': jax.random.normal(ks[2], (BATCH, CTX_LEN, D_MODEL), f32),
        'c_ctx': jax.random.normal(ks[3], (D_MODEL,), f32),
        'w_mod': nrm(ks[4], (DEPTH, D_MODEL, 6 * D_MODEL), D_MODEL),
        'b_mod': 0.02 * jax.random.normal(ks[5], (DEPTH, 6 * D_MODEL), f32),
        'norm1_w': gain(ks[6], (DEPTH, D_MODEL)),
        'norm2_w': gain(ks[7], (DEPTH, D_MODEL)),
        'w_in': nrm(ks[8], (DEPTH, D_MODEL, P_IN), D_MODEL),
        'gla_lr_w': nrm(ks[9], (DEPTH, 2, GLA_LOWRANK, GLA_QK), GLA_LOWRANK),
        'gla_lr_b': 0.1 * jax.random.normal(ks[10], (DEPTH, 2, GLA_QK), f32),
        'gla_norm_w': gain(ks[11], (DEPTH, GLA_HEADS, GLA_DV)),
        'gdn_conv_w': nrm(ks[12], (DEPTH, CONV_W, 2 * GDN_QK + GDN_V), CONV_W),
        'gdn_a_log': jnp.log(jax.random.uniform(ks[13], (DEPTH, 2, GDN_HEADS), f32, 1.0, 16.0)),
        'gdn_dt_bias': dt + jnp.log(-jnp.expm1(-dt)),
        'gdn_norm_w': gain(ks[15], (DEPTH, GDN_HEADS, GDN_DV)),
        'w_out': nrm(ks[16], (DEPTH, D_MODEL, D_MODEL), D_MODEL),
        'ffn_w_gate': nrm(ks[17], (DEPTH, D_MODEL, FFN_HIDDEN), D_MODEL),
        'ffn_w_up': nrm(ks[18], (DEPTH, D_MODEL, FFN_HIDDEN), D_MODEL),
        'ffn_w_down': nrm(ks[19], (DEPTH, FFN_HIDDEN, D_MODEL), FFN_HIDDEN),
        'final_norm_w': gain(ks[20], (D_MODEL,)),
    }


def reference(x, c, ctx, c_ctx, w_mod, b_mod, norm1_w, norm2_w, w_in, gla_lr_w, gla_lr_b, gla_norm_w,
              gdn_conv_w, gdn_a_log, gdn_dt_bias, gdn_norm_w, w_out, ffn_w_gate, ffn_w_up, ffn_w_down,
              final_norm_w):
    h = ctx
    for layer in range(DEPTH):
        need_ctx = layer < DEPTH - 1
        mod_l = jnp.split((jax.nn.silu(c) @ w_mod[layer] + b_mod[layer])[:, None, :], 6, axis=-1)
        mod_c = jnp.split((jax.nn.silu(c_ctx) @ w_mod[layer] + b_mod[layer])[None, None, :], 6, axis=-1)
        hn_l = modulate(rmsnorm(x, norm1_w[layer]), mod_l[0], mod_l[1])
        hn_c = modulate(rmsnorm(h, norm1_w[layer]), mod_c[0], mod_c[1])
        y_c, y_l = token_mixers(hn_c, hn_l, w_in[layer], gla_lr_w[layer], gla_lr_b[layer], gla_norm_w[layer],
                                gdn_conv_w[layer], gdn_a_log[layer], gdn_dt_bias[layer], gdn_norm_w[layer],
                                w_out[layer], need_ctx)
        x = x + mod_l[2] * y_l
        x = x + mod_l[5] * swiglu(modulate(rmsnorm(x, norm2_w[layer]), mod_l[3], mod_l[4]),
                                  ffn_w_gate[layer], ffn_w_up[layer], ffn_w_down[layer])
        if need_ctx:
            h = h + mod_c[2] * y_c
            h = h + mod_c[5] * swiglu(modulate(rmsnorm(h, norm2_w[layer]), mod_c[3], mod_c[4]),
                                      ffn_w_gate[layer], ffn_w_up[layer], ffn_w_down[layer])
    return rmsnorm(x, final_norm_w)
```

```python
import numpy as np
import ml_dtypes
import concourse.bass as bass
import concourse.mybir as mybir
from concourse.bass_utils import run_bass_kernel_spmd

F32 = mybir.dt.float32
BF16 = mybir.dt.bfloat16
AF = mybir.ActivationFunctionType
ALU = mybir.AluOpType

SAME_ENG_SYNC = True
EPOCH = 20000
EPS = 1e-6
DM = 1024
CTX = 256
TT = 256
NEG = -30000.0


class Buf:
    __slots__ = ("t", "lw", "rd", "sem", "cnt", "name", "dram", "_scope")

    def __init__(self, t, name, dram=False):
        self.t = t
        self.name = name
        self.lw = {}
        self.rd = {}
        self.sem = None
        self.cnt = 0
        self.dram = dram
        self._scope = 0

    def __getitem__(self, idx):
        return self.t[idx]


class EngState:
    def __init__(self, name):
        self.name = name
        self.ops = []
        self.count = 0
        self.waited = {}
        self.needed = set()


class KB:
    def __init__(self, nc):
        self.nc = nc
        self.eng = {n: EngState(n) for n in ("pe", "act", "dve", "pool", "sp")}
        self._ctx = []
        self.sems = {}
        self.sempool = []
        self.sembufs = []
        self._sem_ctx = []

    def mark(self):
        return len(self._ctx)

    def release(self, m):
        for b in self.sembufs:
            if b.sem is not None and (not b.dram) and b._scope >= m:
                self.sempool.append((b.sem, b.cnt))
                b.sem = None
        self.sembufs = [b for b in self.sembufs if b.sem is not None]
        while len(self._ctx) > m:
            cm = self._ctx.pop()
            cm.__exit__(None, None, None)

    def sb(self, name, shape, dt=F32):
        cm = self.nc.sbuf_tensor("sb_" + name, list(shape), dt)
        t = cm.__enter__()
        b = Buf(t, name)
        b._scope = len(self._ctx)
        self._ctx.append(cm)
        return b

    def ps(self, name, shape, dt=F32):
        cm = self.nc.psum_tensor(name, list(shape), dt)
        t = cm.__enter__()
        self._ctx.append(cm)
        return t

    def dram(self, name, shape, dt=F32, kind="Internal"):
        t = self.nc.dram_tensor(name, list(shape), dt, kind=kind)
        return Buf(t, name, dram=True)

    def _get_sem(self, b):
        if b.sem is None:
            if self.sempool:
                b.sem, b.cnt = self.sempool.pop()
            else:
                cm = self.nc.semaphore("s%d" % len(self.sems))
                b.sem = cm.__enter__()
                self._sem_ctx.append(cm)
                b.cnt = 0
                self.sems[id(b.sem)] = b.sem
            self.sembufs.append(b)

    def _waits(self, E, reads, writes, extra=None):
        deps = {}

        def add(d):
            for k, v in d.items():
                if deps.get(k, -1) < v:
                    deps[k] = v

        for b in reads:
            add(b.lw)
        for b in writes:
            add(b.lw)
            add(b.rd)
        if extra:
            add(extra)
        for k, v in deps.items():
            if k[0] == "E" and k[1] == E.name:
                if E.name in ("pe", "sp", "pool") or not SAME_ENG_SYNC:
                    continue
            if E.waited.get(k, -1) >= v:
                continue
            E.waited[k] = v
            if k[0] == "E":
                self.eng[k[1]].needed.add(v)
            E.ops.append(("w", k, v))

    def op(self, eng, fn, reads=(), writes=()):
        E = self.eng[eng]
        self._waits(E, reads, writes)
        idx = E.count
        E.count += 1
        E.ops.append(("o", fn, idx))
        key = ("E", eng)
        for b in writes:
            if b.dram:
                b.lw[key] = idx
            else:
                b.lw = {key: idx}
                b.rd = {}
        for b in reads:
            b.rd[key] = idx

    def dma(self, q, out, in_, reads, writes):
        E = self.eng[q]
        self._waits(E, reads, writes)
        cand = [b for b in list(writes) + list(reads) if not b.dram]
        sb = cand[0]
        self._get_sem(sb)
        sb.cnt += 16
        c = sb.cnt
        key = ("S", id(sb.sem))
        E.ops.append(("d", out, in_, sb.sem))
        for b in writes:
            if b.dram:
                b.lw[key] = c
            else:
                b.lw = {key: c}
                b.rd = {}
        for b in reads:
            b.rd[key] = c

    def barrier(self):
        ev = {}
        for n in ("pe", "act", "dve"):
            if self.eng[n].count > 0:
                ev[("E", n)] = self.eng[n].count - 1
        for b in self.sembufs:
            if b.sem is not None and b.cnt > 0:
                ev[("S", id(b.sem))] = b.cnt
        for n, E in self.eng.items():
            deps = dict(ev)
            for k, v in deps.items():
                if k[0] == "E" and k[1] == n:
                    continue
                if E.waited.get(k, -1) >= v:
                    continue
                E.waited[k] = v
                if k[0] == "E":
                    self.eng[k[1]].needed.add(v)
                E.ops.append(("w", k, v))

    def finalize(self):
        nc = self.nc
        engsems = {}
        rank = {}
        for name, E in self.eng.items():
            nd = sorted(E.needed)
            rank[name] = {idx: r for r, idx in enumerate(nd)}
            nsem = (len(nd) + EPOCH - 1) // EPOCH
            lst = []
            for i in range(nsem):
                cm = nc.semaphore("e_%s_%d" % (name, i))
                lst.append(cm.__enter__())
                self._sem_ctx.append(cm)
            engsems[name] = lst
        engobj = {"pe": nc.tensor, "act": nc.scalar, "dve": nc.vector, "pool": nc.gpsimd, "sp": nc.sync}

        def replay(E, e):
            for o in E.ops:
                if o[0] == "w":
                    k, v = o[1], o[2]
                    if k[0] == "E":
                        r = rank[k[1]][v]
                        e.wait_ge(engsems[k[1]][r // EPOCH], r % EPOCH + 1)
                    else:
                        e.wait_ge(self.sems[k[1]], v)
                elif o[0] == "o":
                    inst = o[1](e)
                    r = rank[E.name].get(o[2])
                    if r is not None:
                        inst.then_inc(engsems[E.name][r // EPOCH], 1)
                else:
                    e.dma_start(out=o[1], in_=o[2]).then_inc(o[3], 16)

        with nc.Block() as block:
            @block.tensor
            def _(e):
                replay(self.eng["pe"], e)

            @block.scalar
            def _(e):
                replay(self.eng["act"], e)

            @block.vector
            def _(e):
                replay(self.eng["dve"], e)

            @block.gpsimd
            def _(e):
                replay(self.eng["pool"], e)

            @block.sync
            def _(e):
                replay(self.eng["sp"], e)

    def close(self):
        while self._ctx:
            self._ctx.pop().__exit__(None, None, None)
        while self._sem_ctx:
            self._sem_ctx.pop().__exit__(None, None, None)


def AP(buf, offset, pairs):
    return bass.AP(buf.t if isinstance(buf, Buf) else buf, offset, [list(p) for p in pairs])


SPL = dict(gq=0, gk=512, gv=1024, gz=2048, lr=3072, dq=3104, dk=4128, dv=5152, dz=6176,
           ab=7200, g11=7232, g12=8256)
PIN = 9280
FFH = 2816


class Prog:
    def __init__(self, W, dbg=()):
        self.W = W
        self.L = 128 * W
        self.NT = CTX + self.L
        self.NCH = self.NT // 128
        self.dbg = set(dbg)
        nc = bass.Bass("TRN2", target_bir_lowering=False)
        self.nc = nc
        self.kb = KB(nc)
        kb = self.kb
        L, NT = self.L, self.NT
        ein = lambda n, s: kb.dram(n, s, F32, kind="ExternalInput")
        self.x = ein("x", [L, DM])
        self.ctx = ein("ctx", [CTX, DM])
        self.cT = ein("cT", [128, 8])
        self.cctxT = ein("cctxT", [128, 8])
        self.w_mod = ein("w_mod", [DM, 6 * DM])
        self.b_mod = ein("b_mod", [1, 6 * DM])
        self.n1bc = ein("n1bc", [128, DM])
        self.n2bc = ein("n2bc", [128, DM])
        self.fnbc = ein("fnbc", [128, DM])
        self.w_in = ein("w_in", [DM, PIN])
        self.lrw = ein("lrw", [2, 33, 512])
        self.glag = ein("glag", [128, DM])
        self.gdng = ein("gdng", [128, DM])
        self.cwT = ein("cwT", [128, 120])
        self.alog = ein("alog", [128, 16])
        self.dtb = ein("dtb", [128, 16])
        self.w_out = ein("w_out", [DM, DM])
        self.wg = ein("wg", [DM, FFH])
        self.wu = ein("wu", [DM, FFH])
        self.wd = ein("wd", [FFH, DM])
        self.identf_d = ein("identf", [128, 128])
        self.cumU = ein("cumU", [6, 128, 128])
        self.gmask = ein("gmask", [2, 128, 128])
        self.dmask = ein("dmask", [4, 128, 128])
        self.sel_d = ein("sel", [96, 32 * 128])
        self.lvlm = ein("lvlm", [14, 128, 128])
        self.out = kb.dram("out", [L, DM], F32, kind="ExternalOutput")

        def scr(n, s, dt):
            return kb.dram(n, s, dt, kind="ExternalOutput" if (n in self.dbg or n.startswith("dbg")) else "Internal")
        self.gq = [scr("gq%d" % d, [4, 128, NT], BF16) for d in range(2)]
        self.gk = [scr("gk%d" % d, [4, 128, NT], BF16) for d in range(2)]
        self.gkh = [scr("gkh%d" % d, [NT, 512], BF16) for d in range(2)]
        self.gv = scr("gv", [NT, DM], BF16)
        self.go = [scr("go%d" % d, [NT, DM], F32) for d in range(2)]
        self.dqT = scr("dqT", [8, 128, NT], BF16)
        self.dkt = scr("dkt", [NT, DM], BF16)
        self.du = [scr("du%d" % d, [NT, DM], F32) for d in range(2)]
        self.dw = [scr("dw%d" % d, [8, 128, NT], BF16) for d in range(2)]
        self.daq = [scr("daq%d" % d, [NT, DM], BF16) for d in range(2)]
        self.do = [scr("do%d" % d, [NT, DM], F32) for d in range(2)]
        self.dx1 = scr("dx1", [L, DM], F32)
        if "dbgX" in self.dbg:
            self.dbgX = scr("dbgX", [128, DM], BF16)
            self.dbgXT = scr("dbgXT", [128, DM], BF16)
            self.dbgMT = scr("dbgMT", [128, DM], BF16)
            self.dbg.update(["dbgXT", "dbgMT"])

        self.PP = [kb.ps("pp%d" % i, [128, 1024], F32) for i in range(4)]
        self.PPb = [p.bitcast(BF16) for p in self.PP]
        self.bank = [Buf(self.PP[i // 2], "bank%d" % i) for i in range(8)]

    def pf(self, b, c0, n, p=128):
        return AP(self.PP[b // 2], (b % 2) * 512 + c0, [[1024, p], [1, n]])

    def pb16(self, b, c0, n, p=128):
        return AP(self.PPb[b // 2], (b % 2) * 1024 + c0, [[2048, p], [1, n]])

    def mm(self, out, lhsT, rhs, start, stop, reads, writes):
        self.kb.op("pe", lambda e: e.matmul(out, lhsT=lhsT, rhs=rhs, start=start, stop=stop), reads, writes)

    def tr(self, out, in_, ident, reads, writes):
        self.kb.op("pe", lambda e: e.transpose(out=out, in_=in_, identity=ident), reads, writes)

    def act(self, out, in_, func, reads, writes, scale=None, bias=None, accum=None):
        kw = {}
        if scale is not None:
            kw["scale"] = scale
        if bias is not None:
            kw["bias"] = bias
        if accum is not None:
            kw["accum_out"] = accum
        self.kb.op("act", lambda e: e.activation(out=out, in_=in_, func=func, **kw), reads, writes)

    def tt(self, out, in0, in1, op, reads, writes):
        self.kb.op("dve", lambda e: e.tensor_tensor(out=out, in0=in0, in1=in1, op=op), reads, writes)

    def stt(self, out, in0, scalar, in1, op0, op1, reads, writes):
        self.kb.op("dve", lambda e: e.scalar_tensor_tensor(out=out, in0=in0, scalar=scalar, in1=in1,
                                                           op0=op0, op1=op1), reads, writes)

    def ts(self, out, in0, s1, s2, op0, op1, reads, writes):
        if s2 is None:
            self.kb.op("dve", lambda e: e.tensor_scalar(out=out, in0=in0, scalar1=s1, scalar2=None, op0=op0),
                       reads, writes)
        else:
            self.kb.op("dve", lambda e: e.tensor_scalar(out=out, in0=in0, scalar1=s1, scalar2=s2, op0=op0, op1=op1),
                       reads, writes)

    def cp(self, out, in_, reads, writes):
        self.kb.op("dve", lambda e: e.tensor_copy(out=out, in_=in_), reads, writes)

    def recip(self, out, in_, reads, writes):
        self.kb.op("dve", lambda e: e.reciprocal(out=out, in_=in_), reads, writes)

    def memset(self, ap, val, writes):
        self.kb.op("dve", lambda e: e.memset(ap, val), [], writes)

    def load(self, out, in_, src, dst, q="sp"):
        self.kb.dma(q, out, in_, [src], [dst])

    def store(self, out, in_, src, dst, q="pool"):
        self.kb.dma(q, out, in_, [src], [dst])

    def wload(self, dst, src, col0, ncols, nk=8, rowlen=None):
        rl = rowlen
        step = 4
        for k0 in range(0, nk, step):
            kn = min(step, nk - k0)
            self.kb.dma("pool", AP(dst, k0 * ncols, [[nk * ncols, 128], [ncols, kn], [1, ncols]]),
                        AP(src, k0 * 128 * rl + col0, [[rl, 128], [128 * rl, kn], [1, ncols]]), [src], [dst])

    def consts(self):
        kb = self.kb
        self.identf = kb.sb("identf", [128, 128], F32)
        self.identb = kb.sb("identb", [128, 128], BF16)
        self.load(self.identf[:, :], self.identf_d[:, :], self.identf_d, self.identf)
        self.kb.dma("pool", self.identb[:, :], self.identf_d[:, :], [self.identf_d], [self.identb])
        self.onesf = kb.sb("onesf", [128, 128], F32)
        self.onesb = kb.sb("onesb", [128, 128], BF16)
        self.memset(self.onesf[:, :], 1.0, [self.onesf])
        self.memset(self.onesb[:, :], 1.0, [self.onesb])
        self.junk = kb.sb("junk", [128, DM], BF16)
        self.ssq = kb.sb("ssq", [128, 16], F32)
        self.rsd = kb.sb("rsd", [128, 16], F32)
        self.ntmp = kb.sb("ntmp", [128, DM], F32)
        self.hnb = kb.sb("hnb", [128, DM], BF16)

    def phase0(self):
        kb = self.kb
        m = kb.mark()
        cT = kb.sb("cTs", [128, 8], F32)
        ccT = kb.sb("ccTs", [128, 8], F32)
        bm = kb.sb("bm", [1, 6 * DM], F32)
        n1 = kb.sb("n1", [128, DM], F32)
        n2 = kb.sb("n2", [128, DM], F32)
        SL = kb.sb("SL", [128, 8, 128], F32)
        SC = kb.sb("SC", [128, 8, 128], F32)
        wm = [kb.sb("wm%d" % i, [128, 8, 512], F32) for i in range(2)]
        self.load(cT[:, :], self.cT[:, :], self.cT, cT)
        self.load(ccT[:, :], self.cctxT[:, :], self.cctxT, ccT)
        self.load(bm[:, :], self.b_mod[:, :], self.b_mod, bm)
        self.load(n1[:, :], self.n1bc[:, :], self.n1bc, n1)
        self.load(n2[:, :], self.n2bc[:, :], self.n2bc, n2)
        for kc in range(8):
            self.act(SL[:, kc, :], self.onesf[:, :], AF.Silu, [self.onesf, cT], [SL], scale=cT[:, kc:kc + 1])
            self.act(SC[:, kc, :], self.onesf[:, :], AF.Silu, [self.onesf, ccT], [SC], scale=ccT[:, kc:kc + 1])
        for g in range(12):
            w = wm[g % 2]
            for k0 in (0, 4):
                self.load(AP(w, k0 * 512, [[8 * 512, 128], [512, 4], [1, 512]]),
                          AP(self.w_mod, k0 * 128 * 6 * DM + g * 512, [[6 * DM, 128], [128 * 6 * DM, 4], [1, 512]]),
                          self.w_mod, w)
            variants = [(SL, self.modL1 if g < 6 else self.modL2, 0, (g % 6) * 512)]
            if g < 4:
                variants.append((SC, self.modC, 1, g * 512))
            for (S_, dst, bi, dc0) in variants:
                bk = (g * 2 + bi) % 8
                for kc in range(8):
                    self.mm(self.pf(bk, 0, 512), S_[:, kc, :], w[:, kc, :], kc == 0, False, [S_, w], [self.bank[bk]])
                self.mm(self.pf(bk, 0, 512), self.onesf[0:1, :], bm[0:1, g * 512:(g + 1) * 512], False, True,
                        [self.onesf, bm], [self.bank[bk]])
                self.act(dst[:, dc0:dc0 + 512], self.pf(bk, 0, 512), AF.Copy, [self.bank[bk]], [dst])
        for (dst, nw, c0) in ((self.modL1, n1, DM), (self.modL2, n2, DM), (self.modC, n1, DM)):
            self.stt(dst[:, c0:c0 + DM], dst[:, c0:c0 + DM], 1.0, nw[:, :], ALU.add, ALU.mult, [dst, nw], [dst])
        kb.barrier()
        kb.release(m)
        ML, ML2, MC = self.modL1, self.modL2, self.modC
        self.modL = ML
        self.B1, self.A1, self.G1 = ML[:, 0:DM], ML[:, DM:2 * DM], ML[:, 2 * DM:3 * DM]
        self.B2, self.A2, self.G2 = ML2[:, 0:DM], ML2[:, DM:2 * DM], ML2[:, 2 * DM:3 * DM]
        self.B1c, self.A1c = MC[:, 0:DM], MC[:, DM:2 * DM]

    def norm_T(self, xt, xbuf, A, B, mbuf, hnT, Tn, sub, bk):
        ss, rs = self.ssq, self.rsd
        self.act(self.junk[:, :], xt, AF.Square, [xbuf], [self.junk, ss], accum=ss[:, 0:1])
        self.act(rs[:, 0:1], ss[:, 0:1], AF.Sqrt, [ss], [rs], scale=1.0 / DM, bias=self.epsb[:, 0:1])
        self.recip(rs[:, 1:2], rs[:, 0:1], [rs], [rs])
        self.stt(self.ntmp[:, :], xt, rs[:, 1:2], A, ALU.mult, ALU.mult, [xbuf, rs, mbuf], [self.ntmp])
        self.tt(self.hnb[:, :], self.ntmp[:, :], B, ALU.add, [self.ntmp, mbuf], [self.hnb])
        for kc in range(8):
            self.tr(self.pb16(bk, kc * 128, 128), self.hnb[:, kc * 128:(kc + 1) * 128], self.identb[:, :],
                    [self.hnb, self.identb], [self.bank[bk]])
        self.act(AP(hnT, sub * 128, [[8 * Tn, 128], [Tn, 8], [1, 128]]),
                 AP(self.PPb[bk // 2], (bk % 2) * 1024, [[2048, 128], [128, 8], [1, 128]]), AF.Copy,
                 [self.bank[bk]], [hnT])

    def tiles(self, colmajor):
        res = [(0, 0, True, None)]
        for i in range(self.L // TT):
            res.append((i + 1, CTX + i * TT, False, i))
        return res

    def x_src(self, is_ctx, li, colmajor):
        if is_ctx:
            return AP(self.ctx, 0, [[DM, 128], [128 * DM, 2], [1, DM]]), self.ctx
        if not colmajor:
            return AP(self.x, li * TT * DM, [[DM, 128], [128 * DM, 2], [1, DM]]), self.x
        c0 = li * 2
        return AP(self.x, c0 * DM, [[self.W * DM, 128], [DM, 2], [1, DM]]), self.x

    def phaseA(self):
        kb = self.kb
        m = kb.mark()
        NT = self.NT
        Wqk = kb.sb("Wqk", [128, 8, 1024], BF16)
        Wv = kb.sb("Wv", [128, 8, 1024], BF16)
        Wlr = kb.sb("Wlr", [128, 8, 32], BF16)
        self.wload(Wqk, self.w_in, SPL["gq"], 1024, rowlen=PIN)
        self.wload(Wv, self.w_in, SPL["gv"], 1024, rowlen=PIN)
        self.wload(Wlr, self.w_in, SPL["lr"], 32, rowlen=PIN)
        LRW = [kb.sb("LRW%d" % d, [33, 512], F32) for d in range(2)]
        U = [kb.sb("U%d" % d, [128, 128], F32) for d in range(2)]
        for d in range(2):
            self.load(LRW[d][:, :], AP(self.lrw, d * 33 * 512, [[512, 33], [1, 512]]), self.lrw, LRW[d])
            self.load(U[d][:, :], AP(self.cumU, d * 128 * 128, [[128, 128], [1, 128]]), self.cumU, U[d])
        xt = [kb.sb("xtA%d" % i, [128, 2, DM], F32) for i in range(2)]
        hnT = [kb.sb("hnTA%d" % i, [128, 8, TT], BF16) for i in range(2)]
        qkT = kb.sb("qkT", [128, 8, TT], F32)
        vtok = [kb.sb("vtok%d" % i, [128, 2, DM], BF16) for i in range(2)]
        lraug = kb.sb("lraug", [33, TT], F32)
        self.memset(lraug[32:33, :], 1.0, [lraug])
        e1 = kb.sb("e1", [128, 512], F32)
        sp = kb.sb("sp", [128, 512], F32)
        eG = kb.sb("eG", [128, 512], F32)
        enG = kb.sb("enG", [128, 512], F32)
        dK = kb.sb("dK", [128, 512], F32)
        gend = kb.sb("gend", [128, 4], F32)
        khT = kb.sb("khT", [128, 512], BF16)
        qs = [kb.sb("qs%d" % d, [128, 4, TT], BF16) for d in range(2)]
        ks = [kb.sb("ks%d" % d, [128, 4, TT], BF16) for d in range(2)]
        khs = [kb.sb("khs%d" % d, [128, 2, 512], BF16) for d in range(2)]
        print("phaseA sbuf remaining", self.nc.sbuf_bytes_remaining)
        tl = self.tiles(False)
        src, sbuf = self.x_src(tl[0][2], tl[0][3], False)
        self.load(xt[0][:, :, :], src, sbuf, xt[0])
        for ti, (idx, tok0, is_ctx, li) in enumerate(tl):
            X = xt[ti % 2]
            H = hnT[ti % 2]
            if ti + 1 < len(tl):
                src, sbuf = self.x_src(tl[ti + 1][2], tl[ti + 1][3], False)
                self.load(xt[(ti + 1) % 2][:, :, :], src, sbuf, xt[(ti + 1) % 2])
            A, B, mb = (self.A1c, self.B1c, self.modC) if is_ctx else (self.A1, self.B1, self.modL)
            for sub in range(2):
                self.norm_T(X[:, sub, :], X, A, B, mb, H, TT, sub, 0)
            for fc in range(8):
                bk = 1 + fc % 2
                for kc in range(8):
                    self.mm(self.pf(bk, 0, TT), Wqk[:, kc, fc * 128:(fc + 1) * 128], H[:, kc, :], kc == 0, kc == 7,
                            [Wqk, H], [self.bank[bk]])
                self.act(qkT[:, fc, :], self.pf(bk, 0, TT), AF.Copy, [self.bank[bk]], [qkT],
                         scale=(128.0 ** -0.5 if fc < 4 else 1.0))
            VT = vtok[ti % 2]
            for sub in range(2):
                for half in range(2):
                    bk = 3 + half
                    for kc in range(8):
                        self.mm(self.pf(bk, 0, 512), H[:, kc, sub * 128:(sub + 1) * 128],
                                Wv[:, kc, half * 512:(half + 1) * 512], kc == 0, kc == 7, [H, Wv], [self.bank[bk]])
                    self.cp(VT[:, sub, half * 512:(half + 1) * 512], self.pf(bk, 0, 512), [self.bank[bk]], [VT])
            self.store(AP(self.gv, tok0 * DM, [[DM, 128], [128 * DM, 2], [1, DM]]), VT[:, :, :], VT, self.gv)
            for kc in range(8):
                self.mm(self.pf(5, 0, TT, 32), Wlr[:, kc, :], H[:, kc, :], kc == 0, kc == 7, [Wlr, H], [self.bank[5]])
            self.act(lraug[0:32, :], self.pf(5, 0, TT, 32), AF.Copy, [self.bank[5]], [lraug])
            for sub in range(2):
                ch = (tok0 // 128) + sub
                for d in range(2):
                    self.mm(self.pf(6, 0, 512), lraug[0:33, sub * 128:(sub + 1) * 128], LRW[d][:, :], True, True,
                            [lraug, LRW[d]], [self.bank[6]])
                    self.act(e1[:, :], self.pf(6, 0, 512), AF.Exp, [self.bank[6]], [e1], scale=-1.0)
                    self.act(sp[:, :], e1[:, :], AF.Ln, [e1], [sp], bias=self.oneb[:, 0:1])
                    for h in range(4):
                        self.mm(self.pf(7, h * 128, 128), sp[:, h * 128:(h + 1) * 128], U[d][:, :], True, True,
                                [sp, U[d]], [self.bank[7]])
                    G = self.pf(7, 0, 512)
                    ecol = 127 if d == 0 else 0
                    self.act(eG[:, :], G, AF.Exp, [self.bank[7]], [eG])
                    self.act(enG[:, :], G, AF.Exp, [self.bank[7]], [enG], scale=-1.0)
                    self.act(gend[:, :], AP(self.PP[3], 512 + ecol, [[1024, 128], [128, 4]]), AF.Copy,
                             [self.bank[7]], [gend])
                    for h in range(4):
                        self.act(dK[:, h * 128:(h + 1) * 128], self.pf(7, h * 128, 128), AF.Exp,
                                 [self.bank[7], gend], [dK], scale=-1.0, bias=gend[:, h:h + 1])
                    self.cp(AP(self.Egla, (d * self.NCH + ch) * 4, [[2 * self.NCH * 4, 128], [1, 4]]),
                            AP(eG, ecol, [[512, 128], [128, 4]]), [eG], [self.Egla])
                    qv = AP(qkT, sub * 128, [[8 * TT, 128], [TT, 4], [1, 128]])
                    kv = AP(qkT, 4 * TT + sub * 128, [[8 * TT, 128], [TT, 4], [1, 128]])
                    g3 = lambda t: AP(t, 0, [[512, 128], [128, 4], [1, 128]])
                    self.tt(AP(qs[d], sub * 128, [[4 * TT, 128], [TT, 4], [1, 128]]), qv, g3(eG), ALU.mult,
                            [qkT, eG], [qs[d]])
                    self.tt(AP(ks[d], sub * 128, [[4 * TT, 128], [TT, 4], [1, 128]]), kv, g3(enG), ALU.mult,
                            [qkT, enG], [ks[d]])
                    self.tt(g3(khT), kv, g3(dK), ALU.mult, [qkT, dK], [khT])
                    for h in range(4):
                        self.tr(self.pb16(5, h * 128, 128), khT[:, h * 128:(h + 1) * 128], self.identb[:, :],
                                [khT, self.identb], [self.bank[5]])
                    self.cp(khs[d][:, sub, :], self.pb16(5, 0, 512), [self.bank[5]], [khs[d]])
            for d in range(2):
                self.store(AP(self.gq[d], tok0, [[NT, 128], [128 * NT, 4], [1, TT]]), qs[d][:, :, :], qs[d], self.gq[d])
                self.store(AP(self.gk[d], tok0, [[NT, 128], [128 * NT, 4], [1, TT]]), ks[d][:, :, :], ks[d], self.gk[d])
                self.store(AP(self.gkh[d], tok0 * 512, [[512, 128], [128 * 512, 2], [1, 512]]), khs[d][:, :, :],
                           khs[d], self.gkh[d])
        kb.barrier()
        kb.release(m)

    def gla_scan(self):
        kb = self.kb
        m = kb.mark()
        NT, NCH = self.NT, self.NCH
        S = [kb.sb("Sg%d" % d, [128, 4, 256], F32) for d in range(2)]
        Sb = [kb.sb("Sgb%d" % d, [128, 4, 256], BF16) for d in range(2)]
        msk = [kb.sb("gm%d" % d, [128, 128], F32) for d in range(2)]
        for d in range(2):
            self.memset(S[d][:, :, :], 0.0, [S[d]])
            self.memset(Sb[d][:, :, :], 0.0, [Sb[d]])
            self.load(msk[d][:, :], AP(self.gmask, d * 128 * 128, [[128, 128], [1, 128]]), self.gmask, msk[d])
        nb = 2
        qt = [[kb.sb("sq%d_%d" % (d, i), [128, 4, 128], BF16) for i in range(nb)] for d in range(2)]
        kt = [[kb.sb("sk%d_%d" % (d, i), [128, 4, 128], BF16) for i in range(nb)] for d in range(2)]
        kh = [[kb.sb("skh%d_%d" % (d, i), [128, 512], BF16) for i in range(nb)] for d in range(2)]
        vv = [[kb.sb("sv%d_%d" % (d, i), [128, DM], BF16) for i in range(nb)] for d in range(2)]
        AT = [kb.sb("AT%d" % d, [128, 4, 128], BF16) for d in range(2)]
        osb = [kb.sb("osb%d" % d, [128, DM], F32) for d in range(2)]
        order = [list(range(NCH)), [1, 0] + list(range(NCH - 1, 1, -1))]

        def issue_loads(s):
            for d in range(2):
                c = order[d][s]
                t0 = c * 128
                i = s % nb
                self.load(qt[d][i][:, :, :], AP(self.gq[d], t0, [[NT, 128], [128 * NT, 4], [1, 128]]), self.gq[d], qt[d][i])
                self.load(kt[d][i][:, :, :], AP(self.gk[d], t0, [[NT, 128], [128 * NT, 4], [1, 128]]), self.gk[d], kt[d][i])
                self.load(kh[d][i][:, :], AP(self.gkh[d], t0 * 512, [[512, 128], [1, 512]]), self.gkh[d], kh[d][i])
                self.load(vv[d][i][:, :], AP(self.gv, t0 * DM, [[DM, 128], [1, DM]]), self.gv, vv[d][i])

        issue_loads(0)
        for s in range(NCH):
            if s + 1 < NCH:
                issue_loads(s + 1)
            for d in range(2):
                c = order[d][s]
                i = s % nb
                Q, K_, KH, V = qt[d][i], kt[d][i], kh[d][i], vv[d][i]
                bA = d
                bO = 2 + 2 * d
                bS = 6
                for h in range(4):
                    self.mm(self.pf(bA, h * 128, 128), K_[:, h, :], Q[:, h, :], True, True, [K_, Q], [self.bank[bA]])
                self.tt(AT[d][:, :, :], AP(self.PP[bA // 2], (bA % 2) * 512, [[1024, 128], [128, 4], [1, 128]]),
                        AP(msk[d], 0, [[128, 128], [0, 4], [1, 128]]), ALU.mult, [self.bank[bA], msk[d]], [AT[d]])
                for h in range(4):
                    bk = bO + h // 2
                    o_ap = self.pf(bk, (h % 2) * 256, 256)
                    self.mm(o_ap, AT[d][:, h, :], V[:, h * 256:(h + 1) * 256], True, False, [AT[d], V], [self.bank[bk]])
                    self.mm(o_ap, Q[:, h, :], Sb[d][:, h, :], False, True, [Q, Sb[d]], [self.bank[bk]])
                self.act(osb[d][:, :], AP(self.PP[bO // 2], 0, [[1024, 128], [1, 1024]]), AF.Copy,
                         [self.bank[bO], self.bank[bO + 1]], [osb[d]])
                self.store(AP(self.go[d], c * 128 * DM, [[DM, 128], [1, DM]]), osb[d][:, :], osb[d], self.go[d])
                for h in range(4):
                    bk = bS + h // 2
                    self.mm(self.pf(bk, (h % 2) * 256, 256), KH[:, h * 128:(h + 1) * 128], V[:, h * 256:(h + 1) * 256],
                            True, True, [KH, V], [self.bank[bk]])
                for h in range(4):
                    bk = bS + h // 2
                    self.stt(S[d][:, h, :], S[d][:, h, :],
                             AP(self.Egla, (d * NCH + c) * 4 + h, [[2 * NCH * 4, 128], [1, 1]]),
                             self.pf(bk, (h % 2) * 256, 256), ALU.mult, ALU.add,
                             [S[d], self.Egla, self.bank[bk]], [S[d]])
                self.act(Sb[d][:, :, :], S[d][:, :, :], AF.Copy, [S[d]], [Sb[d]])
        kb.barrier()
        kb.release(m)

    def phaseB(self):
        kb = self.kb
        m = kb.mark()
        NT, NCH = self.NT, self.NCH
        Wgd = kb.sb("Wgd", [128, 8, 3072], BF16)
        Wab = kb.sb("Wab", [128, 8, 32], BF16)
        self.wload(Wgd, self.w_in, SPL["dq"], 3072, rowlen=PIN)
        self.wload(Wab, self.w_in, SPL["ab"], 32, rowlen=PIN)
        cw = kb.sb("cw", [128, 24, 5], F32)
        self.load(AP(cw, 0, [[120, 128], [1, 120]]), self.cwT[:, :], self.cwT, cw)
        negA = kb.sb("negA", [128, 16], F32)
        dtb = kb.sb("dtbs", [128, 16], F32)
        self.load(negA[:, :], self.alog[:, :], self.alog, negA)
        self.load(dtb[:, :], self.dtb[:, :], self.dtb, dtb)
        self.act(negA[:, :], negA[:, :], AF.Exp, [negA], [negA])
        self.kb.op("act", lambda e: e.mul(negA[:, :], negA[:, :], -1.0), [negA], [negA])
        CU = [kb.sb("CU%d" % i, [128, 128], F32) for i in range(4)]
        for i in range(4):
            self.load(CU[i][:, :], AP(self.cumU, (2 + i) * 128 * 128, [[128, 128], [1, 128]]), self.cumU, CU[i])
        UFi, UBi, UBs, UFs = CU
        MK = [kb.sb("MK%d" % i, [128, 128], BF16) for i in range(4)]
        for i in range(4):
            self.kb.dma("pool", MK[i][:, :], AP(self.dmask, i * 128 * 128, [[128, 128], [1, 128]]), [self.dmask], [MK[i]])
        Minc = [MK[0], MK[1]]
        Mlt, Mgt = MK[2], MK[3]
        sel = kb.sb("sel", [96, 32, 128], BF16)
        self.kb.dma("pool", AP(sel, 0, [[4096, 96], [1, 4096]]), self.sel_d[:, :], [self.sel_d], [sel])
        xt = [kb.sb("xtB%d" % i, [128, 2, DM], F32) for i in range(2)]
        hnT = kb.sb("hnTB", [128, 8, TT], BF16)
        pre = kb.sb("pre", [128, TT], F32)
        acc = kb.sb("acc", [128, TT], F32)
        sil = kb.sb("sil", [128, TT], F32)
        sq = kb.sb("sqb", [128, TT], BF16)
        rn = kb.sb("rn", [128, TT], F32)
        vTb = kb.sb("vTb", [128, TT], BF16)
        dqs = kb.sb("dqs", [128, 8, TT], BF16)
        dks = kb.sb("dks", [128, 8, TT], BF16)
        vtk = kb.sb("vtk", [128, 2, DM], BF16)
        ktk = kb.sb("ktk", [128, 2, DM], BF16)
        g16 = kb.sb("g16", [128, 16], F32)
        t16 = kb.sb("t16", [128, 16], F32)
        beta = kb.sb("beta", [128, 16], F32)
        lnb = kb.sb("lnb", [128, 16], F32)
        ba = kb.sb("ba", [128, 16], F32)
        R = kb.sb("R", [128, 32], F32)
        negG = kb.sb("negG", [128, 16], F32)
        R1 = kb.sb("R1", [128, 32], F32)
        Rs = kb.sb("Rs", [128, 96], BF16)
        rows = kb.sb("rows", [96, 128], BF16)
        nrows = kb.sb("nrows", [96, 128], BF16)
        Dm = kb.sb("Dm", [128, 8, 128], F32)
        E1 = kb.sb("E1m", [128, 8, 128], F32)
        E2 = Dm
        aqs = kb.sb("aqs", [128, 8, 128], BF16)
        X0 = kb.sb("X0n", [128, 8, 128], BF16)
        LT = [kb.sb("LTn%d" % d, [128, 8, 128], BF16) for d in range(2)]
        Dn = [[kb.sb("Dn%d" % d, [128, 8, 128], BF16)] * 2 for d in range(2)]
        DTn = [[kb.sb("DTn%d" % d, [128, 8, 128], BF16)] * 2 for d in range(2)]
        Wn = [kb.sb("Wn%d" % d, [128, 8, 128], BF16) for d in range(2)]
        MT = [kb.sb("MTn%d" % d, [128, 8, 128], BF16) for d in range(2)]
        MaT = [kb.sb("MaTn%d" % d, [128, 8, 128], BF16) for d in range(2)]
        LM = kb.sb("LM", [128, 14, 128], BF16)
        self.kb.dma("pool", LM[:, :, :], AP(self.lvlm, 0, [[128, 128], [128 * 128, 14], [1, 128]]), [self.lvlm], [LM])
        lm = lambda i: AP(LM, i * 128, [[14 * 128, 128], [0, 8], [1, 128]])
        us = kb.sb("us", [128, DM], F32)
        ws = kb.sb("wsn", [128, 8, 128], BF16)
        b3 = lambda t, c0: AP(t, c0, [[16, 128], [1, 8], [0, 128]])
        f3 = lambda t: AP(t, 0, [[1024, 128], [128, 8], [1, 128]])
        p3 = lambda k: AP(self.PP[k], 0, [[1024, 128], [128, 8], [1, 128]])
        pbk = lambda k: [self.bank[2 * k], self.bank[2 * k + 1]]
        print("phaseB sbuf remaining", self.nc.sbuf_bytes_remaining)
        tl = self.tiles(True)
        src, sbuf = self.x_src(tl[0][2], tl[0][3], True)
        self.load(xt[0][:, :, :], src, sbuf, xt[0])
        for ti, (idx, tok0, is_ctx, li) in enumerate(tl):
            Xt = xt[ti % 2]
            if ti + 1 < len(tl):
                src, sbuf = self.x_src(tl[ti + 1][2], tl[ti + 1][3], True)
                self.load(xt[(ti + 1) % 2][:, :, :], src, sbuf, xt[(ti + 1) % 2])
            A, B, mb = (self.A1c, self.B1c, self.modC) if is_ctx else (self.A1, self.B1, self.modL)
            for sub in range(2):
                self.norm_T(Xt[:, sub, :], Xt, A, B, mb, hnT, TT, sub, 0)
            nseg, ls = (1, TT) if is_ctx else (2, 128)
            for fc in range(24):
                bk = 1 + fc % 2
                for kc in range(8):
                    self.mm(self.pf(bk, 0, TT), Wgd[:, kc, fc * 128:(fc + 1) * 128], hnT[:, kc, :], kc == 0, kc == 7,
                            [Wgd, hnT], [self.bank[bk]])
                self.act(pre[:, :], self.pf(bk, 0, TT), AF.Copy, [self.bank[bk]], [pre])
                self.ts(acc[:, :], pre[:, :], cw[:, fc, 2:3], None, ALU.mult, None, [pre, cw], [acc])
                for tap in (0, 1, 3, 4):
                    sh = tap - 2
                    n = ls - abs(sh)
                    o0, i0 = (0, sh) if sh > 0 else (-sh, 0)
                    oap = AP(acc, o0, [[TT, 128], [ls, nseg], [1, n]])
                    iap = AP(pre, i0, [[TT, 128], [ls, nseg], [1, n]])
                    self.stt(oap, iap, cw[:, fc, tap:tap + 1], oap, ALU.mult, ALU.add, [pre, cw, acc], [acc])
                if fc < 16:
                    h = fc % 8
                    self.act(sil[:, :], acc[:, :], AF.Silu, [acc], [sil])
                    self.act(sq[:, :], sil[:, :], AF.Square, [sil], [sq])
                    self.mm(self.pf(3, 0, TT), self.onesb[:, :], sq[:, :], True, True, [self.onesb, sq], [self.bank[3]])
                    self.act(rn[:, :], self.pf(3, 0, TT), AF.Sqrt, [self.bank[3]], [rn], bias=self.epsb[:, 0:1])
                    self.recip(rn[:, :], rn[:, :], [rn], [rn])
                    dst = dqs if fc < 8 else dks
                    self.stt(dst[:, h, :], sil[:, :], (128.0 ** -0.5 if fc < 8 else 1.0), rn[:, :], ALU.mult, ALU.mult,
                             [sil, rn], [dst])
                else:
                    h = fc - 16
                    self.act(vTb[:, :], acc[:, :], AF.Silu, [acc], [vTb])
                    for sub in range(2):
                        self.tr(self.pb16(4 + sub, h * 128, 128), vTb[:, sub * 128:(sub + 1) * 128], self.identb[:, :],
                                [vTb, self.identb], [self.bank[4 + sub]])
            for sub in range(2):
                self.cp(vtk[:, sub, :], self.pb16(4 + sub, 0, 1024), [self.bank[4 + sub]], [vtk])
            for sub in range(2):
                for h in range(8):
                    self.tr(self.pb16(4 + sub, h * 128, 128), dks[:, h, sub * 128:(sub + 1) * 128], self.identb[:, :],
                            [dks, self.identb], [self.bank[4 + sub]])
                self.cp(ktk[:, sub, :], self.pb16(4 + sub, 0, 1024), [self.bank[4 + sub]], [ktk])
            self.store(AP(self.dqT, tok0, [[NT, 128], [128 * NT, 8], [1, TT]]), dqs[:, :, :], dqs, self.dqT)
            self.store(AP(self.dkt, tok0 * DM, [[DM, 128], [128 * DM, 2], [1, DM]]), ktk[:, :, :], ktk, self.dkt)
            for sub in range(2):
                ch = tok0 // 128 + sub
                tsl = slice(sub * 128, (sub + 1) * 128)
                for kc in range(8):
                    self.mm(self.pf(6, 0, 32), hnT[:, kc, tsl], Wab[:, kc, :], kc == 0, kc == 7, [hnT, Wab], [self.bank[6]])
                self.tt(t16[:, :], self.pf(6, 0, 16), dtb[:, :], ALU.add, [self.bank[6], dtb], [t16])
                self.act(t16[:, :], t16[:, :], AF.Exp, [t16], [t16])
                self.act(t16[:, :], t16[:, :], AF.Ln, [t16], [t16], bias=self.oneb[:, 0:1])
                self.tt(g16[:, :], t16[:, :], negA[:, :], ALU.mult, [t16, negA], [g16])
                self.act(beta[:, :], self.pf(6, 16, 16), AF.Sigmoid, [self.bank[6]], [beta])
                self.act(lnb[:, :], beta[:, :], AF.Ln, [beta], [lnb])
                self.mm(self.pf(7, 0, 8), UFi[:, :], g16[:, 0:8], True, True, [UFi, g16], [self.bank[7]])
                self.mm(self.pf(7, 8, 8), UBi[:, :], g16[:, 8:16], True, True, [UBi, g16], [self.bank[7]])
                self.mm(self.pf(7, 16, 8), UBs[:, :], g16[:, 0:8], True, True, [UBs, g16], [self.bank[7]])
                self.mm(self.pf(7, 24, 8), UFs[:, :], g16[:, 8:16], True, True, [UFs, g16], [self.bank[7]])
                self.mm(self.pf(7, 32, 16), self.onesf[:, :], g16[:, :], True, True, [self.onesf, g16], [self.bank[7]])
                self.act(R[:, 0:16], self.pf(7, 0, 16), AF.Copy, [self.bank[7]], [R])
                self.act(AP(self.GDa, ch * 16, [[NCH * 16, 128], [1, 16]]), self.pf(7, 0, 16), AF.Exp,
                         [self.bank[7]], [self.GDa])
                self.act(AP(self.GDd, ch * 16, [[NCH * 16, 128], [1, 16]]), self.pf(7, 16, 16), AF.Exp,
                         [self.bank[7]], [self.GDd])
                self.act(AP(self.GDe, ch * 16, [[NCH * 16, 128], [1, 16]]), self.pf(7, 32, 16), AF.Exp,
                         [self.bank[7]], [self.GDe])
                self.tt(ba[:, :], beta[:, :], AP(self.GDa, ch * 16, [[NCH * 16, 128], [1, 16]]), ALU.mult,
                        [beta, self.GDa], [ba])
                self.tt(R[:, 16:32], R[:, 0:16], lnb[:, :], ALU.add, [R, lnb], [R])
                self.kb.op("act", lambda e: e.mul(negG[:, :], R[:, 0:16], -1.0), [R], [negG])
                self.cp(Rs[:, 0:32], R[:, :], [R], [Rs])
                self.tt(R1[:, :], R[:, :], Rs[:, 0:32], ALU.subtract, [R, Rs], [R1])
                self.cp(Rs[:, 32:64], R1[:, :], [R1], [Rs])
                self.tt(R1[:, :], R1[:, :], Rs[:, 32:64], ALU.subtract, [R1, Rs], [R1])
                self.cp(Rs[:, 64:96], R1[:, :], [R1], [Rs])
                self.tr(self.pb16(6, 0, 128, 96), Rs[:, :], self.identb[:, :], [Rs, self.identb], [self.bank[6]])
                self.act(rows[:, :], self.pb16(6, 0, 128, 96), AF.Copy, [self.bank[6]], [rows])
                self.kb.op("act", lambda e: e.mul(nrows[:, :], rows[:, :], -1.0), [rows], [nrows])
                for h in range(8):
                    bk = h // 4
                    self.mm(self.pf(bk, (h % 4) * 128, 128), dks[:, h, tsl], dks[:, h, tsl], True, True,
                            [dks], [self.bank[bk]])
                for h in range(8):
                    bk = 2 + h // 4
                    self.mm(self.pf(bk, (h % 4) * 128, 128), dks[:, h, tsl], dqs[:, h, tsl], True, True,
                            [dks, dqs], [self.bank[bk]])
                for d in range(2):
                    mstrT = Mlt if d == 0 else Mgt
                    mstr = Mgt if d == 0 else Mlt
                    specs = [(Dm, 0, rows, Minc[d], negG, 0), (E1, 16, rows, mstrT, negG, 0), (E2, 0, nrows, mstr, R, 16)]
                    for si, (dst, so, rr, mk, bt, bo) in enumerate(specs):
                        pk = 2 + (si % 2)
                        for h in range(8):
                            c = d * 8 + h
                            bk = 2 * pk + h // 4
                            o_ap = self.pf(bk, (h % 4) * 128, 128)
                            self.mm(o_ap, sel[:, so + c, :], rr[:, :], True, False, [sel, rr], [self.bank[bk]])
                            self.mm(o_ap, self.identb[:, :], mk[:, :], False, True, [self.identb, mk], [self.bank[bk]])
                        for h in range(8):
                            c = d * 8 + h
                            bk = 2 * pk + h // 4
                            self.act(dst[:, h, :], self.pf(bk, (h % 4) * 128, 128), AF.Exp, [self.bank[bk], bt], [dst],
                                     bias=bt[:, bo + c:bo + c + 1])
                        if si == 0:
                            self.tt(f3(aqs), p3(1), f3(Dm), ALU.mult, pbk(1) + [Dm], [aqs])
                            self.store(AP(self.daq[d], ch * 128 * DM, [[DM, 128], [1, DM]]), aqs[:, :, :], aqs, self.daq[d])
                    self.tt(f3(LT[d]), p3(0), f3(E1), ALU.mult, pbk(0) + [E1], [LT[d]])
                    self.tt(f3(X0), p3(0), f3(E2), ALU.mult, pbk(0) + [E2], [X0])
                    if "dbgX" in self.dbg and ch == 2 and d == 0:
                        self.store(AP(self.dbgX, 0, [[DM, 128], [1, DM]]), X0[:, :, :], X0, self.dbgX)
                        self.store(AP(self.dbgXT, 0, [[DM, 128], [1, DM]]), LT[d][:, :, :], LT[d], self.dbgXT)
                    idb = AP(self.identb, 0, [[128, 128], [0, 8], [1, 128]])
                    self.tt(f3(Wn[d]), f3(X0), lm(d * 7), ALU.mult, [X0, LM], [Wn[d]])
                    self.tt(f3(Dn[d][0]), f3(Wn[d]), idb, ALU.add, [Wn[d], self.identb], [Dn[d][0]])
                    self.tt(f3(Wn[d]), f3(LT[d]), lm((1 - d) * 7), ALU.mult, [LT[d], LM], [Wn[d]])
                    self.tt(f3(DTn[d][0]), f3(Wn[d]), idb, ALU.add, [Wn[d], self.identb], [DTn[d][0]])
                cur = 0
                for lvl in range(1, 7):
                    for d in range(2):
                        pa, pb_ = 2 * d, 2 * d + 1
                        Dc, DTc = Dn[d][cur], DTn[d][cur]
                        for h in range(8):
                            bk = 2 * pa + h // 4
                            self.mm(self.pf(bk, (h % 4) * 128, 128), LT[d][:, h, :], Dc[:, h, :], True, True,
                                    [LT[d], Dc], [self.bank[bk]])
                        self.tt(f3(Wn[d]), p3(pa), lm(d * 7 + lvl), ALU.mult, pbk(pa) + [LM], [Wn[d]])
                        if lvl < 6:
                            for h in range(8):
                                bk = 2 * pb_ + h // 4
                                o_ap = self.pf(bk, (h % 4) * 128, 128)
                                self.mm(o_ap, self.identb[:, :], Dc[:, h, :], True, False, [self.identb, Dc], [self.bank[bk]])
                                self.mm(o_ap, DTc[:, h, :], Wn[d][:, h, :], False, True, [DTc, Wn[d]], [self.bank[bk]])
                            self.act(f3(Dn[d][1 - cur]), p3(pb_), AF.Copy, pbk(pb_), [Dn[d][1 - cur]])
                        for h in range(8):
                            bk = 2 * pa + h // 4
                            o_ap = self.pf(bk, (h % 4) * 128, 128)
                            self.mm(o_ap, self.identb[:, :], DTc[:, h, :], True, False, [self.identb, DTc], [self.bank[bk]])
                            self.mm(o_ap, Wn[d][:, h, :], DTc[:, h, :], False, True, [Wn[d], DTc], [self.bank[bk]])
                        if lvl < 6:
                            self.act(f3(DTn[d][1 - cur]), p3(pa), AF.Copy, pbk(pa), [DTn[d][1 - cur]])
                        else:
                            self.tt(f3(MT[d]), p3(pa), b3(beta, d * 8), ALU.mult, pbk(pa) + [beta], [MT[d]])
                            self.tt(f3(MaT[d]), p3(pa), b3(ba, d * 8), ALU.mult, pbk(pa) + [ba], [MaT[d]])
                    cur = 1 - cur
                for d in range(2):
                    pa, pb_ = 2 * d, 2 * d + 1
                    if "dbgX" in self.dbg and ch == 2 and d == 0:
                        self.store(AP(self.dbgMT, 0, [[DM, 128], [1, DM]]), MT[d][:, :, :], MT[d], self.dbgMT)
                    for h in range(8):
                        bk = 2 * pb_ + h // 4
                        self.mm(self.pf(bk, (h % 4) * 128, 128), MT[d][:, h, :], vtk[:, sub, h * 128:(h + 1) * 128], True, True,
                                [MT[d], vtk], [self.bank[bk]])
                    self.act(us[:, :], AP(self.PP[pb_], 0, [[1024, 128], [1, 1024]]), AF.Copy, pbk(pb_), [us])
                    self.store(AP(self.du[d], ch * 128 * DM, [[DM, 128], [1, DM]]), us[:, :], us, self.du[d])
                    for h in range(8):
                        bk = 2 * pa + h // 4
                        self.mm(self.pf(bk, (h % 4) * 128, 128), ktk[:, sub, h * 128:(h + 1) * 128], MaT[d][:, h, :], True, True,
                                [ktk, MaT[d]], [self.bank[bk]])
                    self.cp(f3(ws), p3(pa), pbk(pa), [ws])
                    self.store(AP(self.dw[d], ch * 128, [[NT, 128], [128 * NT, 8], [1, 128]]), ws[:, :, :], ws, self.dw[d])
        kb.barrier()
        kb.release(m)

    def gdn_scan(self):
        kb = self.kb
        m = kb.mark()
        NT, NCH = self.NT, self.NCH
        S = [kb.sb("Sd%d" % d, [128, 8, 128], F32) for d in range(2)]
        Sb = [kb.sb("Sdb%d" % d, [128, 8, 128], BF16) for d in range(2)]
        for d in range(2):
            self.memset(S[d][:, :, :], 0.0, [S[d]])
            self.memset(Sb[d][:, :, :], 0.0, [Sb[d]])
        nb = 2
        mk = lambda n, sh, dt: [[kb.sb("%s%d_%d" % (n, d, i), sh, dt) for i in range(nb)] for d in range(2)]
        wT = mk("lw", [128, 8, 128], BF16)
        qT = mk("lq", [128, 8, 128], BF16)
        uu = mk("lu", [128, DM], F32)
        aq = mk("la", [128, DM], BF16)
        kt = mk("lk", [128, DM], BF16)
        vn32 = kb.sb("vn32", [128, DM], F32)
        vnb = kb.sb("vnb", [128, DM], BF16)
        vnd = kb.sb("vnd", [128, DM], BF16)
        tq = kb.sb("tq", [128, DM], F32)
        osb = [kb.sb("odb%d" % d, [128, DM], F32) for d in range(2)]
        order = [list(range(NCH)), [1, 0] + list(range(NCH - 1, 1, -1))]
        b3 = lambda t, c0: AP(t, c0, [[NCH * 16, 128], [1, 8], [0, 128]])
        f3 = lambda t: AP(t, 0, [[1024, 128], [128, 8], [1, 128]])
        p3 = lambda k: AP(self.PP[k], 0, [[1024, 128], [128, 8], [1, 128]])
        pbk = lambda k: [self.bank[2 * k], self.bank[2 * k + 1]]

        def issue_loads(s):
            for d in range(2):
                c = order[d][s]
                t0 = c * 128
                i = s % nb
                self.load(wT[d][i][:, :, :], AP(self.dw[d], t0, [[NT, 128], [128 * NT, 8], [1, 128]]), self.dw[d], wT[d][i])
                self.load(qT[d][i][:, :, :], AP(self.dqT, t0, [[NT, 128], [128 * NT, 8], [1, 128]]), self.dqT, qT[d][i])
                self.load(uu[d][i][:, :], AP(self.du[d], t0 * DM, [[DM, 128], [1, DM]]), self.du[d], uu[d][i])
                self.load(aq[d][i][:, :], AP(self.daq[d], t0 * DM, [[DM, 128], [1, DM]]), self.daq[d], aq[d][i])
                self.load(kt[d][i][:, :], AP(self.dkt, t0 * DM, [[DM, 128], [1, DM]]), self.dkt, kt[d][i])

        issue_loads(0)
        for s in range(NCH):
            if s + 1 < NCH:
                issue_loads(s + 1)
            for d in range(2):
                c = order[d][s]
                i = s % nb
                Wt, Qt, Uu, Aq, Kt = wT[d][i], qT[d][i], uu[d][i], aq[d][i], kt[d][i]
                for h in range(8):
                    bk = h // 4
                    self.mm(self.pf(bk, (h % 4) * 128, 128), Wt[:, h, :], Sb[d][:, h, :], True, True, [Wt, Sb[d]], [self.bank[bk]])
                for h in range(8):
                    bk = 2 + h // 4
                    self.mm(self.pf(bk, (h % 4) * 128, 128), Qt[:, h, :], Sb[d][:, h, :], True, True, [Qt, Sb[d]], [self.bank[bk]])
                self.tt(vn32[:, :], Uu[:, :], AP(self.PP[0], 0, [[1024, 128], [1, 1024]]), ALU.subtract, [Uu] + pbk(0), [vn32])
                self.act(vnb[:, :], vn32[:, :], AF.Copy, [vn32], [vnb])
                self.tt(f3(vnd), f3(vn32), b3(self.GDd, c * 16 + d * 8), ALU.mult, [vn32, self.GDd], [vnd])
                for h in range(8):
                    bk = 4 + h // 4
                    hs = slice(h * 128, (h + 1) * 128)
                    self.mm(self.pf(bk, (h % 4) * 128, 128), Aq[:, hs], vnb[:, hs], True, True, [Aq, vnb], [self.bank[bk]])
                for h in range(8):
                    bk = 6 + h // 4
                    hs = slice(h * 128, (h + 1) * 128)
                    self.mm(self.pf(bk, (h % 4) * 128, 128), Kt[:, hs], vnd[:, hs], True, True, [Kt, vnd], [self.bank[bk]])
                self.tt(f3(tq), p3(1), b3(self.GDa, c * 16 + d * 8), ALU.mult, pbk(1) + [self.GDa], [tq])
                self.tt(osb[d][:, :], tq[:, :], AP(self.PP[2], 0, [[1024, 128], [1, 1024]]), ALU.add, [tq] + pbk(2), [osb[d]])
                self.store(AP(self.do[d], c * 128 * DM, [[DM, 128], [1, DM]]), osb[d][:, :], osb[d], self.do[d])
                self.tt(f3(S[d]), f3(S[d]), b3(self.GDe, c * 16 + d * 8), ALU.mult, [S[d], self.GDe], [S[d]])
                self.tt(f3(S[d]), f3(S[d]), p3(3), ALU.add, [S[d]] + pbk(3), [S[d]])
                self.act(Sb[d][:, :, :], S[d][:, :, :], AF.Copy, [S[d]], [Sb[d]])
        kb.barrier()
        kb.release(m)

    def phase3a(self):
        kb = self.kb
        m = kb.mark()
        W, NT = self.W, self.NT
        Wz = kb.sb("Wz", [128, 8, 4096], BF16)
        for gi, key in enumerate(("gz", "dz", "g11", "g12")):
            for k0 in (0, 4):
                self.kb.dma("pool", AP(Wz, k0 * 4096 + gi * 1024, [[8 * 4096, 128], [4096, 4], [1, 1024]]),
                            AP(self.w_in, k0 * 128 * PIN + SPL[key], [[PIN, 128], [128 * PIN, 4], [1, 1024]]),
                            [self.w_in], [Wz])
        Wo = kb.sb("Wo", [128, 8, DM], BF16)
        self.wload(Wo, self.w_out, 0, DM, rowlen=DM)
        gg = kb.sb("ggl", [128, DM], F32)
        gd = kb.sb("ggd", [128, DM], F32)
        self.load(gg[:, :], self.glag[:, :], self.glag, gg)
        self.load(gd[:, :], self.gdng[:, :], self.gdng, gd)
        xt = [kb.sb("xt3%d" % i, [128, DM], F32) for i in range(2)]
        ol = [[kb.sb("ol%d_%d" % (j, i), [128, DM], F32) for i in range(2)] for j in range(4)]
        hnT = kb.sb("hnT3", [128, 8, 128], BF16)
        sz = kb.sb("sz", [128, 2048], F32)
        sg = kb.sb("sg", [128, 2048], F32)
        ss = kb.sb("ss3", [128, 12], F32)
        rs = kb.sb("rs3", [128, 12], F32)
        mb16 = kb.sb("mb16", [128, DM], BF16)
        mT = kb.sb("mT", [128, 8, 128], BF16)
        x1 = [kb.sb("x1s%d" % i, [128, DM], F32) for i in range(2)]
        print("phase3a sbuf remaining", self.nc.sbuf_bytes_remaining)
        ntile = self.L // 128

        def issue_loads(t):
            i = t % 2
            self.load(xt[i][:, :], AP(self.x, t * 128 * DM, [[DM, 128], [1, DM]]), self.x, xt[i])
            g0 = (CTX + t * 128) * DM
            self.load(ol[0][i][:, :], AP(self.go[0], g0, [[DM, 128], [1, DM]]), self.go[0], ol[0][i])
            self.load(ol[1][i][:, :], AP(self.go[1], g0, [[DM, 128], [1, DM]]), self.go[1], ol[1][i])
            nr = 128 // W if W <= 128 else 1
            for j in (0, 1):
                for rr in range(128 // W):
                    r = t * (128 // W) + rr
                    self.load(AP(ol[2 + j][i], rr * W * DM, [[DM, W], [1, DM]]),
                              AP(self.do[j], (CTX + r) * DM, [[128 * DM, W], [1, DM]]), self.do[j], ol[2 + j][i])

        issue_loads(0)
        for t in range(ntile):
            i = t % 2
            if t + 1 < ntile:
                issue_loads(t + 1)
            self.norm_T(xt[i][:, :], xt[i], self.A1, self.B1, self.modL, hnT, 128, 0, 0)
            for g in range(8):
                bk = 1 + g % 4
                for kc in range(8):
                    self.mm(self.pf(bk, 0, 512), hnT[:, kc, :], Wz[:, kc, g * 512:(g + 1) * 512], kc == 0, kc == 7,
                            [hnT, Wz], [self.bank[bk]])
                if g < 4:
                    self.act(sz[:, g * 512:(g + 1) * 512], self.pf(bk, 0, 512), AF.Silu, [self.bank[bk]], [sz])
                else:
                    self.act(sg[:, (g - 4) * 512:(g - 3) * 512], self.pf(bk, 0, 512), AF.Sigmoid, [self.bank[bk]], [sg])
            og, od = ol[0][i], ol[2][i]
            self.tt(og[:, :], ol[0][i][:, :], ol[1][i][:, :], ALU.add, [ol[0][i], ol[1][i]], [og])
            self.tt(od[:, :], ol[2][i][:, :], ol[3][i][:, :], ALU.add, [ol[2][i], ol[3][i]], [od])
            for h in range(4):
                self.act(self.junk[:, 0:256], og[:, h * 256:(h + 1) * 256], AF.Square, [og], [self.junk, ss],
                         accum=ss[:, h:h + 1])
            for h in range(8):
                self.act(self.junk[:, 0:128], od[:, h * 128:(h + 1) * 128], AF.Square, [od], [self.junk, ss],
                         accum=ss[:, 4 + h:5 + h])
            self.act(rs[:, 0:4], ss[:, 0:4], AF.Sqrt, [ss], [rs], scale=1.0 / 256, bias=self.epsb[:, 0:1])
            self.act(rs[:, 4:12], ss[:, 4:12], AF.Sqrt, [ss], [rs], scale=1.0 / 128, bias=self.epsb[:, 0:1])
            self.recip(rs[:, :], rs[:, :], [rs], [rs])
            self.tt(AP(og, 0, [[DM, 128], [256, 4], [1, 256]]), AP(og, 0, [[DM, 128], [256, 4], [1, 256]]),
                    AP(rs, 0, [[12, 128], [1, 4], [0, 256]]), ALU.mult, [og, rs], [og])
            self.tt(AP(od, 0, [[DM, 128], [128, 8], [1, 128]]), AP(od, 0, [[DM, 128], [128, 8], [1, 128]]),
                    AP(rs, 4, [[12, 128], [1, 8], [0, 128]]), ALU.mult, [od, rs], [od])
            self.tt(og[:, :], og[:, :], gg[:, :], ALU.mult, [og, gg], [og])
            self.tt(od[:, :], od[:, :], gd[:, :], ALU.mult, [od, gd], [od])
            self.tt(og[:, :], og[:, :], sz[:, 0:1024], ALU.mult, [og, sz], [og])
            self.tt(od[:, :], od[:, :], sz[:, 1024:2048], ALU.mult, [od, sz], [od])
            self.tt(og[:, :], og[:, :], sg[:, 0:1024], ALU.mult, [og, sg], [og])
            self.tt(od[:, :], od[:, :], sg[:, 1024:2048], ALU.mult, [od, sg], [od])
            self.tt(mb16[:, :], og[:, :], od[:, :], ALU.add, [og, od], [mb16])
            for kc in range(8):
                self.tr(self.pb16(5, kc * 128, 128), mb16[:, kc * 128:(kc + 1) * 128], self.identb[:, :],
                        [mb16, self.identb], [self.bank[5]])
            self.act(AP(mT, 0, [[1024, 128], [1, 1024]]), self.pb16(5, 0, 1024), AF.Copy, [self.bank[5]], [mT])
            for half in range(2):
                bk = 6 + half
                for kc in range(8):
                    self.mm(self.pf(bk, 0, 512), mT[:, kc, :], Wo[:, kc, half * 512:(half + 1) * 512], kc == 0, kc == 7,
                            [mT, Wo], [self.bank[bk]])
            self.tt(x1[i][:, :], AP(self.PP[3], 0, [[1024, 128], [1, 1024]]), self.G1, ALU.mult,
                    [self.bank[6], self.bank[7], self.modL], [x1[i]])
            self.tt(x1[i][:, :], x1[i][:, :], xt[i][:, :], ALU.add, [x1[i], xt[i]], [x1[i]])
            self.store(AP(self.dx1, t * 128 * DM, [[DM, 128], [1, DM]]), x1[i][:, :], x1[i], self.dx1)
        kb.barrier()
        kb.release(m)

    def phase3b(self):
        kb = self.kb
        m = kb.mark()
        Wg = kb.sb("Wg", [128, 8, FFH], BF16)
        Wu = kb.sb("Wu", [128, 8, FFH], BF16)
        Wd = kb.sb("Wd", [128, 22, DM], BF16)
        for (dst, src) in ((Wg, self.wg), (Wu, self.wu)):
            for k0 in range(0, 8, 2):
                self.kb.dma("pool", AP(dst, k0 * FFH, [[8 * FFH, 128], [FFH, 2], [1, FFH]]),
                            AP(src, k0 * 128 * FFH, [[FFH, 128], [128 * FFH, 2], [1, FFH]]), [src], [dst])
        for k0 in range(0, 22, 2):
            self.kb.dma("pool", AP(Wd, k0 * DM, [[22 * DM, 128], [DM, 2], [1, DM]]),
                        AP(self.wd, k0 * 128 * DM, [[DM, 128], [128 * DM, 2], [1, DM]]), [self.wd], [Wd])
        fn = kb.sb("fnw", [128, DM], F32)
        self.load(fn[:, :], self.fnbc[:, :], self.fnbc, fn)
        xt = [kb.sb("x1t%d" % i, [128, 2, DM], F32) for i in range(1)]
        h2T = kb.sb("h2T", [128, 8, TT], BF16)
        sgl = kb.sb("sgl", [128, TT], F32)
        actT = kb.sb("actT", [128, 22, TT], BF16)
        ty = kb.sb("ty2", [128, 512], F32)
        x2s = [kb.sb("x2_%d" % i, [128, DM], F32) for i in range(2)]
        print("phase3b sbuf remaining", self.nc.sbuf_bytes_remaining)
        ss, rs = self.ssq, self.rsd
        ntile = self.L // TT

        def issue_load(t):
            self.load(xt[0][:, :, :], AP(self.dx1, t * TT * DM, [[DM, 128], [128 * DM, 2], [1, DM]]), self.dx1, xt[0])

        issue_load(0)
        oc = 0
        for t in range(ntile):
            X = xt[0]
            if t > 0:
                issue_load(t)
            for sub in range(2):
                self.norm_T(X[:, sub, :], X, self.A2, self.B2, self.modL2, h2T, TT, sub, 0)
            for hc in range(22):
                bg = 1 + (hc % 2) * 2
                bu = bg + 1
                for kc in range(8):
                    self.mm(self.pf(bg, 0, TT), Wg[:, kc, hc * 128:(hc + 1) * 128], h2T[:, kc, :], kc == 0, kc == 7,
                            [Wg, h2T], [self.bank[bg]])
                for kc in range(8):
                    self.mm(self.pf(bu, 0, TT), Wu[:, kc, hc * 128:(hc + 1) * 128], h2T[:, kc, :], kc == 0, kc == 7,
                            [Wu, h2T], [self.bank[bu]])
                self.act(sgl[:, :], self.pf(bg, 0, TT), AF.Silu, [self.bank[bg]], [sgl])
                self.tt(actT[:, hc, :], sgl[:, :], self.pf(bu, 0, TT), ALU.mult, [sgl, self.bank[bu]], [actT])
            for sub in range(2):
                x2 = x2s[oc % 2]
                for half in range(2):
                    bk = 5 + half
                    for hc in range(22):
                        self.mm(self.pf(bk, 0, 512), actT[:, hc, sub * 128:(sub + 1) * 128],
                                Wd[:, hc, half * 512:(half + 1) * 512], hc == 0, hc == 21, [actT, Wd], [self.bank[bk]])
                    hs = slice(half * 512, (half + 1) * 512)
                    self.tt(ty[:, :], self.pf(bk, 0, 512), AP(self.modL2, 2 * DM + half * 512, [[3 * DM, 128], [1, 512]]),
                            ALU.mult, [self.bank[bk], self.modL2], [ty])
                    self.tt(x2[:, hs], ty[:, :], X[:, sub, hs], ALU.add, [ty, X], [x2])
                self.act(self.junk[:, :], x2[:, :], AF.Square, [x2], [self.junk, ss], accum=ss[:, 2:3])
                self.act(rs[:, 2:3], ss[:, 2:3], AF.Sqrt, [ss], [rs], scale=1.0 / DM, bias=self.epsb[:, 0:1])
                self.recip(rs[:, 3:4], rs[:, 2:3], [rs], [rs])
                O = x2
                oc += 1
                self.stt(O[:, :], x2[:, :], rs[:, 3:4], fn[:, :], ALU.mult, ALU.mult, [x2, rs, fn], [O])
                self.store(AP(self.out, (t * TT + sub * 128) * DM, [[DM, 128], [1, DM]]), O[:, :], O, self.out)
        kb.barrier()
        kb.release(m)

    def build(self, phases="0ASBG3F"):
        kb = self.kb
        self.consts()
        self.epsb = kb.sb("epsb", [128, 1], F32)
        self.oneb = kb.sb("oneb", [128, 1], F32)
        self.memset(self.epsb[:, :], EPS, [self.epsb])
        self.memset(self.oneb[:, :], 1.0, [self.oneb])
        NCH = self.NCH
        self.modL2 = kb.sb("modL2", [128, 3 * DM], F32)
        ma = kb.mark()
        self.modL1 = kb.sb("modL1", [128, 3 * DM], F32)
        mb_ = kb.mark()
        self.modC = kb.sb("modC", [128, 2 * DM], F32)
        self.Egla = kb.sb("Egla", [128, 2 * NCH * 4], F32)
        self.GDa = kb.sb("GDa", [128, NCH * 16], F32)
        self.GDd = kb.sb("GDd", [128, NCH * 16], F32)
        self.GDe = kb.sb("GDe", [128, NCH * 16], F32)
        self.phase0()
        if "A" in phases:
            self.phaseA()
        if "S" in phases:
            self.gla_scan()
        if "B" in phases:
            self.phaseB()
        if "G" in phases:
            self.gdn_scan()
        kb.release(mb_)
        if "3" in phases:
            self.phase3a()
        kb.release(ma)
        if "F" in phases:
            self.phase3b()
        kb.barrier()
        kb.finalize()
        kb.close()
        return self.nc


def host_consts():
    p = np.arange(128)[:, None]
    f = np.arange(128)[None, :]
    le = (p <= f).astype(np.float32)
    ge = (p >= f).astype(np.float32)
    lt = (p < f).astype(np.float32)
    gt = (p > f).astype(np.float32)
    cumU = np.stack([le * (-1.0 / 16.0), ge * (-1.0 / 16.0), le, ge, gt, lt]).astype(np.float32)
    gmask = np.stack([le, ge]).astype(np.float32)
    dmask = np.stack([(1 - le) * NEG, (1 - ge) * NEG, (1 - lt) * NEG, (1 - gt) * NEG]).astype(np.float32)
    sel = np.zeros((96, 32, 128), np.float32)
    for c in range(32):
        for part in range(3):
            sel[part * 32 + c, c, :] = 1.0
    lv = np.zeros((14, 128, 128), np.float32)
    pi = np.arange(128)[:, None]
    fj = np.arange(128)[None, :]
    for s_ in range(7):
        b = 1 << s_
        mlow = ((pi // (2 * b)) == (fj // (2 * b))) & ((pi % (2 * b)) >= b) & ((fj % (2 * b)) < b)
        lv[s_] = -mlow.astype(np.float32)
        lv[7 + s_] = -mlow.T.astype(np.float32)
    return dict(identf=np.eye(128, dtype=np.float32), cumU=cumU, gmask=gmask, dmask=dmask,
                sel=sel.reshape(96, 32 * 128), lvlm=lv)


def host_inputs(b, x, c, ctx, c_ctx, w_mod, b_mod, norm1_w, norm2_w, w_in, gla_lr_w, gla_lr_b, gla_norm_w,
                gdn_conv_w, gdn_a_log, gdn_dt_bias, gdn_norm_w, w_out, ffn_w_gate, ffn_w_up, ffn_w_down,
                final_norm_w, shared):
    f = lambda a: np.ascontiguousarray(a, dtype=np.float32)
    d = dict(shared)
    d["x"] = f(x[b])
    d["ctx"] = f(ctx[b])
    d["cT"] = f(np.asarray(c[b]).reshape(8, 128).T)
    return d


def shared_inputs(c_ctx, w_mod, b_mod, norm1_w, norm2_w, w_in, gla_lr_w, gla_lr_b, gla_norm_w,
                  gdn_conv_w, gdn_a_log, gdn_dt_bias, gdn_norm_w, w_out, ffn_w_gate, ffn_w_up, ffn_w_down,
                  final_norm_w):
    f = lambda a: np.ascontiguousarray(a, dtype=np.float32)
    bc = lambda v: f(np.broadcast_to(np.asarray(v).reshape(1, -1), (128, np.asarray(v).size)))
    d = host_consts()
    d["cctxT"] = f(np.asarray(c_ctx).reshape(8, 128).T)
    d["w_mod"] = f(w_mod[0])
    d["b_mod"] = f(np.asarray(b_mod[0]).reshape(1, -1))
    d["n1bc"] = bc(norm1_w[0])
    d["n2bc"] = bc(norm2_w[0])
    d["fnbc"] = bc(final_norm_w)
    d["w_in"] = f(w_in[0])
    lrw = np.zeros((2, 33, 512), np.float32)
    lrw[0, 0:16] = np.asarray(gla_lr_w[0, 0])
    lrw[1, 16:32] = np.asarray(gla_lr_w[0, 1])
    lrw[0, 32] = np.asarray(gla_lr_b[0, 0])
    lrw[1, 32] = np.asarray(gla_lr_b[0, 1])
    d["lrw"] = lrw
    d["glag"] = bc(np.asarray(gla_norm_w[0]).reshape(-1))
    d["gdng"] = bc(np.asarray(gdn_norm_w[0]).reshape(-1))
    cw = np.asarray(gdn_conv_w[0])
    d["cwT"] = f(cw.reshape(5, 24, 128).transpose(2, 1, 0).reshape(128, 120))
    d["alog"] = bc(np.asarray(gdn_a_log[0]).reshape(-1))
    d["dtb"] = bc(np.asarray(gdn_dt_bias[0]).reshape(-1))
    d["w_out"] = f(w_out[0])
    d["wg"] = f(ffn_w_gate[0])
    d["wu"] = f(ffn_w_up[0])
    d["wd"] = f(ffn_w_down[0])
    return d


_CACHE = {}


def kernel(x, c, ctx, c_ctx, w_mod, b_mod, norm1_w, norm2_w, w_in, gla_lr_w, gla_lr_b, gla_norm_w,
           gdn_conv_w, gdn_a_log, gdn_dt_bias, gdn_norm_w, w_out, ffn_w_gate, ffn_w_up, ffn_w_down,
           final_norm_w):
    x = np.asarray(x)
    B, L, _ = x.shape
    W = L // 128
    nc = Prog(W).build()
    shared = shared_inputs(c_ctx, w_mod, b_mod, norm1_w, norm2_w, w_in, gla_lr_w, gla_lr_b, gla_norm_w,
                           gdn_conv_w, gdn_a_log, gdn_dt_bias, gdn_norm_w, w_out, ffn_w_gate, ffn_w_up,
                           ffn_w_down, final_norm_w)
    f = lambda a: np.ascontiguousarray(a, dtype=np.float32)
    in_maps = []
    for b in range(B):
        d = dict(shared)
        d["x"] = f(x[b])
        d["ctx"] = f(np.asarray(ctx)[b])
        d["cT"] = f(np.asarray(c)[b].reshape(8, 128).T)
        in_maps.append(d)
    res = run_bass_kernel_spmd(nc, in_maps, core_ids=list(range(B)))
    return np.stack([np.asarray(r["out"], dtype=np.float32) for r in res.results], axis=0)
```

```python
import numpy as np
import ml_dtypes
import concourse.bass as bass
import concourse.mybir as mybir
from concourse.bass_utils import run_bass_kernel_spmd

F32 = mybir.dt.float32
BF16 = mybir.dt.bfloat16
AF = mybir.ActivationFunctionType
ALU = mybir.AluOpType

SAME_ENG_SYNC = True
EPOCH = 20000
EPS = 1e-6
DM = 1024
CTX = 256
TT = 256
NEG = -30000.0


class Buf:
    __slots__ = ("t", "lw", "rd", "sem", "cnt", "name", "dram", "_scope", "semq")

    def __init__(self, t, name, dram=False):
        self.t = t
        self.name = name
        self.lw = {}
        self.rd = {}
        self.sem = None
        self.cnt = 0
        self.dram = dram
        self._scope = 0
        self.semq = None

    def __getitem__(self, idx):
        return self.t[idx]


class EngState:
    def __init__(self, name):
        self.name = name
        self.ops = []
        self.count = 0
        self.waited = {}
        self.needed = set()


class KB:
    def __init__(self, nc):
        self.nc = nc
        self.eng = {n: EngState(n) for n in ("pe", "act", "dve", "pool", "sp")}
        self._ctx = []
        self.sems = {}
        self.sempool = []
        self.sembufs = []
        self._sem_ctx = []

    def mark(self):
        return len(self._ctx)

    def release(self, m):
        for b in self.sembufs:
            if b.sem is not None and (not b.dram) and b._scope >= m:
                self.sempool.append((b.sem, b.cnt, b.semq))
                b.sem = None
        self.sembufs = [b for b in self.sembufs if b.sem is not None]
        while len(self._ctx) > m:
            cm = self._ctx.pop()
            cm.__exit__(None, None, None)

    def sb(self, name, shape, dt=F32):
        cm = self.nc.sbuf_tensor("sb_" + name, list(shape), dt)
        t = cm.__enter__()
        b = Buf(t, name)
        b._scope = len(self._ctx)
        self._ctx.append(cm)
        return b

    def ps(self, name, shape, dt=F32):
        cm = self.nc.psum_tensor(name, list(shape), dt)
        t = cm.__enter__()
        self._ctx.append(cm)
        return t

    def dram(self, name, shape, dt=F32, kind="Internal"):
        t = self.nc.dram_tensor(name, list(shape), dt, kind=kind)
        return Buf(t, name, dram=True)

    def _get_sem(self, b, q):
        if b.sem is not None:
            assert b.semq == q, "buffer %s used by both DMA queue kinds" % b.name
        if b.sem is None:
            b.semq = q
            cand = [i for i, e in enumerate(self.sempool) if e[2] == q]
            if cand:
                b.sem, b.cnt, _ = self.sempool.pop(cand[-1])
            else:
                cm = self.nc.semaphore("s%d" % len(self.sems))
                b.sem = cm.__enter__()
                self._sem_ctx.append(cm)
                b.cnt = 0
                self.sems[id(b.sem)] = b.sem
            self.sembufs.append(b)

    def _waits(self, E, reads, writes, extra=None):
        deps = {}

        def add(d):
            for k, v in d.items():
                if deps.get(k, -1) < v:
                    deps[k] = v

        for b in reads:
            add(b.lw)
        for b in writes:
            add(b.lw)
            add(b.rd)
        if extra:
            add(extra)
        for k, v in deps.items():
            if k[0] == "E" and k[1] == E.name:
                if E.name in ("pe", "sp", "pool") or not SAME_ENG_SYNC:
                    continue
            if E.waited.get(k, -1) >= v:
                continue
            E.waited[k] = v
            if k[0] == "E":
                self.eng[k[1]].needed.add(v)
            E.ops.append(("w", k, v))

    def op(self, eng, fn, reads=(), writes=()):
        E = self.eng[eng]
        self._waits(E, reads, writes)
        idx = E.count
        E.count += 1
        E.ops.append(("o", fn, idx))
        key = ("E", eng)
        for b in writes:
            if b.dram:
                b.lw[key] = idx
            else:
                b.lw = {key: idx}
                b.rd = {}
        for b in reads:
            b.rd[key] = idx

    def dma(self, q, out, in_, reads, writes):
        E = self.eng[q]
        self._waits(E, reads, writes)
        cand = [b for b in list(writes) + list(reads) if not b.dram]
        sb = cand[0]
        self._get_sem(sb, q)
        sb.cnt += 16
        c = sb.cnt
        key = ("S", id(sb.sem))
        E.ops.append(("d", out, in_, sb.sem))
        for b in writes:
            if b.dram:
                b.lw[key] = c
            else:
                b.lw = {key: c}
                b.rd = {}
        for b in reads:
            b.rd[key] = c

    def barrier(self):
        ev = {}
        for n in ("pe", "act", "dve"):
            if self.eng[n].count > 0:
                ev[("E", n)] = self.eng[n].count - 1
        for b in self.sembufs:
            if b.sem is not None and b.cnt > 0:
                ev[("S", id(b.sem))] = b.cnt
        for n, E in self.eng.items():
            deps = dict(ev)
            for k, v in deps.items():
                if k[0] == "E" and k[1] == n:
                    continue
                if E.waited.get(k, -1) >= v:
                    continue
                E.waited[k] = v
                if k[0] == "E":
                    self.eng[k[1]].needed.add(v)
                E.ops.append(("w", k, v))

    def finalize(self):
        nc = self.nc
        engsems = {}
        rank = {}
        for name, E in self.eng.items():
            nd = sorted(E.needed)
            rank[name] = {idx: r for r, idx in enumerate(nd)}
            nsem = (len(nd) + EPOCH - 1) // EPOCH
            lst = []
            for i in range(nsem):
                cm = nc.semaphore("e_%s_%d" % (name, i))
                lst.append(cm.__enter__())
                self._sem_ctx.append(cm)
            engsems[name] = lst
        engobj = {"pe": nc.tensor, "act": nc.scalar, "dve": nc.vector, "pool": nc.gpsimd, "sp": nc.sync}

        def replay(E, e):
            for o in E.ops:
                if o[0] == "w":
                    k, v = o[1], o[2]
                    if k[0] == "E":
                        r = rank[k[1]][v]
                        e.wait_ge(engsems[k[1]][r // EPOCH], r % EPOCH + 1)
                    else:
                        e.wait_ge(self.sems[k[1]], v)
                elif o[0] == "o":
                    inst = o[1](e)
                    r = rank[E.name].get(o[2])
                    if r is not None:
                        inst.then_inc(engsems[E.name][r // EPOCH], 1)
                else:
                    e.dma_start(out=o[1], in_=o[2]).then_inc(o[3], 16)

        with nc.Block() as block:
            @block.tensor
            def _(e):
                replay(self.eng["pe"], e)

            @block.scalar
            def _(e):
                replay(self.eng["act"], e)

            @block.vector
            def _(e):
                replay(self.eng["dve"], e)

            @block.gpsimd
            def _(e):
                replay(self.eng["pool"], e)

            @block.sync
            def _(e):
                replay(self.eng["sp"], e)

    def close(self):
        while self._ctx:
            self._ctx.pop().__exit__(None, None, None)
        while self._sem_ctx:
            self._sem_ctx.pop().__exit__(None, None, None)


def AP(buf, offset, pairs):
    return bass.AP(buf.t if isinstance(buf, Buf) else buf, offset, [list(p) for p in pairs])


SPL = dict(gq=0, gk=512, gv=1024, gz=2048, lr=3072, dq=3104, dk=4128, dv=5152, dz=6176,
           ab=7200, g11=7232, g12=8256)
PIN = 9280
FFH = 2816


class Prog:
    def __init__(self, W, dbg=()):
        self.W = W
        self.L = 128 * W
        self.NT = CTX + self.L
        self.NCH = self.NT // 128
        self.dbg = set(dbg)
        nc = bass.Bass("TRN2", target_bir_lowering=False)
        self.nc = nc
        self.kb = KB(nc)
        kb = self.kb
        L, NT = self.L, self.NT
        ein = lambda n, s: kb.dram(n, s, F32, kind="ExternalInput")
        self.x = ein("x", [L, DM])
        self.ctx = ein("ctx", [CTX, DM])
        self.cT = ein("cT", [128, 8])
        self.cctxT = ein("cctxT", [128, 8])
        self.w_mod = ein("w_mod", [DM, 6 * DM])
        self.b_mod = ein("b_mod", [1, 6 * DM])
        self.n1bc = ein("n1bc", [128, DM])
        self.n2bc = ein("n2bc", [128, DM])
        self.fnbc = ein("fnbc", [128, DM])
        self.w_in = ein("w_in", [DM, PIN])
        self.lrw = ein("lrw", [2, 33, 512])
        self.glag = ein("glag", [128, DM])
        self.gdng = ein("gdng", [128, DM])
        self.cwT = ein("cwT", [128, 120])
        self.alog = ein("alog", [128, 16])
        self.dtb = ein("dtb", [128, 16])
        self.w_out = ein("w_out", [DM, DM])
        self.wg = ein("wg", [DM, FFH])
        self.wu = ein("wu", [DM, FFH])
        self.wd = ein("wd", [FFH, DM])
        self.identf_d = ein("identf", [128, 128])
        self.cumU = ein("cumU", [6, 128, 128])
        self.gmask = ein("gmask", [2, 128, 128])
        self.dmask = ein("dmask", [4, 128, 128])
        self.sel_d = ein("sel", [96, 32 * 128])
        self.lvlm = ein("lvlm", [14, 128, 128])
        self.out = kb.dram("out", [L, DM], F32, kind="ExternalOutput")

        def scr(n, s, dt):
            return kb.dram(n, s, dt, kind="ExternalOutput" if (n in self.dbg or n.startswith("dbg")) else "Internal")
        self.gq = [scr("gq%d" % d, [4, 128, NT], BF16) for d in range(2)]
        self.gk = [scr("gk%d" % d, [4, 128, NT], BF16) for d in range(2)]
        self.gkh = [scr("gkh%d" % d, [NT, 512], BF16) for d in range(2)]
        self.gv = scr("gv", [NT, DM], BF16)
        self.go = [scr("go%d" % d, [NT, DM], F32) for d in range(2)]
        self.dqT = scr("dqT", [8, 128, NT], BF16)
        self.dkt = scr("dkt", [NT, DM], BF16)
        self.du = [scr("du%d" % d, [NT, DM], F32) for d in range(2)]
        self.dw = [scr("dw%d" % d, [8, 128, NT], BF16) for d in range(2)]
        self.daq = [scr("daq%d" % d, [NT, DM], BF16) for d in range(2)]
        self.do = [scr("do%d" % d, [NT, DM], F32) for d in range(2)]
        self.dx1 = scr("dx1", [L, DM], F32)
        if "dbgX" in self.dbg:
            self.dbgX = scr("dbgX", [128, DM], BF16)
            self.dbgXT = scr("dbgXT", [128, DM], BF16)
            self.dbgMT = scr("dbgMT", [128, DM], BF16)
            self.dbg.update(["dbgXT", "dbgMT"])

        self.PP = [kb.ps("pp%d" % i, [128, 1024], F32) for i in range(4)]
        self.PPb = [p.bitcast(BF16) for p in self.PP]
        self.bank = [Buf(self.PP[i // 2], "bank%d" % i) for i in range(8)]

    def pf(self, b, c0, n, p=128):
        return AP(self.PP[b // 2], (b % 2) * 512 + c0, [[1024, p], [1, n]])

    def pb16(self, b, c0, n, p=128):
        return AP(self.PPb[b // 2], (b % 2) * 1024 + c0, [[2048, p], [1, n]])

    def mm(self, out, lhsT, rhs, start, stop, reads, writes):
        self.kb.op("pe", lambda e: e.matmul(out, lhsT=lhsT, rhs=rhs, start=start, stop=stop), reads, writes)

    def tr(self, out, in_, ident, reads, writes):
        self.kb.op("pe", lambda e: e.transpose(out=out, in_=in_, identity=ident), reads, writes)

    def act(self, out, in_, func, reads, writes, scale=None, bias=None, accum=None):
        kw = {}
        if scale is not None:
            kw["scale"] = scale
        if bias is not None:
            kw["bias"] = bias
        if accum is not None:
            kw["accum_out"] = accum
        self.kb.op("act", lambda e: e.activation(out=out, in_=in_, func=func, **kw), reads, writes)

    def tt(self, out, in0, in1, op, reads, writes):
        self.kb.op("dve", lambda e: e.tensor_tensor(out=out, in0=in0, in1=in1, op=op), reads, writes)

    def stt(self, out, in0, scalar, in1, op0, op1, reads, writes):
        self.kb.op("dve", lambda e: e.scalar_tensor_tensor(out=out, in0=in0, scalar=scalar, in1=in1,
                                                           op0=op0, op1=op1), reads, writes)

    def ts(self, out, in0, s1, s2, op0, op1, reads, writes):
        if s2 is None:
            self.kb.op("dve", lambda e: e.tensor_scalar(out=out, in0=in0, scalar1=s1, scalar2=None, op0=op0),
                       reads, writes)
        else:
            self.kb.op("dve", lambda e: e.tensor_scalar(out=out, in0=in0, scalar1=s1, scalar2=s2, op0=op0, op1=op1),
                       reads, writes)

    def cp(self, out, in_, reads, writes):
        self.kb.op("dve", lambda e: e.tensor_copy(out=out, in_=in_), reads, writes)

    def recip(self, out, in_, reads, writes):
        self.kb.op("dve", lambda e: e.reciprocal(out=out, in_=in_), reads, writes)

    def memset(self, ap, val, writes):
        self.kb.op("dve", lambda e: e.memset(ap, val), [], writes)

    def load(self, out, in_, src, dst, q="sp"):
        self.kb.dma(q, out, in_, [src], [dst])

    def store(self, out, in_, src, dst, q="pool"):
        self.kb.dma(q, out, in_, [src], [dst])

    def wload(self, dst, src, col0, ncols, nk=8, rowlen=None):
        rl = rowlen
        step = 4
        for k0 in range(0, nk, step):
            kn = min(step, nk - k0)
            self.kb.dma("pool", AP(dst, k0 * ncols, [[nk * ncols, 128], [ncols, kn], [1, ncols]]),
                        AP(src, k0 * 128 * rl + col0, [[rl, 128], [128 * rl, kn], [1, ncols]]), [src], [dst])

    def consts(self):
        kb = self.kb
        self.identf = kb.sb("identf", [128, 128], F32)
        self.identb = kb.sb("identb", [128, 128], BF16)
        self.load(self.identf[:, :], self.identf_d[:, :], self.identf_d, self.identf)
        self.kb.dma("pool", self.identb[:, :], self.identf_d[:, :], [self.identf_d], [self.identb])
        self.onesf = kb.sb("onesf", [128, 128], F32)
        self.onesb = kb.sb("onesb", [128, 128], BF16)
        self.memset(self.onesf[:, :], 1.0, [self.onesf])
        self.memset(self.onesb[:, :], 1.0, [self.onesb])
        self.junk = kb.sb("junk", [128, DM], BF16)
        self.ssq = kb.sb("ssq", [128, 16], F32)
        self.rsd = kb.sb("rsd", [128, 16], F32)
        self.ntmp = kb.sb("ntmp", [128, DM], F32)
        self.hnb = kb.sb("hnb", [128, DM], BF16)

    def phase0(self):
        kb = self.kb
        m = kb.mark()
        cT = kb.sb("cTs", [128, 8], F32)
        ccT = kb.sb("ccTs", [128, 8], F32)
        bm = kb.sb("bm", [1, 6 * DM], F32)
        n1 = kb.sb("n1", [128, DM], F32)
        n2 = kb.sb("n2", [128, DM], F32)
        SL = kb.sb("SL", [128, 8, 128], F32)
        SC = kb.sb("SC", [128, 8, 128], F32)
        wm = [kb.sb("wm%d" % i, [128, 8, 512], F32) for i in range(2)]
        self.load(cT[:, :], self.cT[:, :], self.cT, cT)
        self.load(ccT[:, :], self.cctxT[:, :], self.cctxT, ccT)
        self.load(bm[:, :], self.b_mod[:, :], self.b_mod, bm)
        self.load(n1[:, :], self.n1bc[:, :], self.n1bc, n1)
        self.load(n2[:, :], self.n2bc[:, :], self.n2bc, n2)
        for kc in range(8):
            self.act(SL[:, kc, :], self.onesf[:, :], AF.Silu, [self.onesf, cT], [SL], scale=cT[:, kc:kc + 1])
            self.act(SC[:, kc, :], self.onesf[:, :], AF.Silu, [self.onesf, ccT], [SC], scale=ccT[:, kc:kc + 1])
        for g in range(12):
            w = wm[g % 2]
            for k0 in (0, 4):
                self.load(AP(w, k0 * 512, [[8 * 512, 128], [512, 4], [1, 512]]),
                          AP(self.w_mod, k0 * 128 * 6 * DM + g * 512, [[6 * DM, 128], [128 * 6 * DM, 4], [1, 512]]),
                          self.w_mod, w)
            variants = [(SL, self.modL1 if g < 6 else self.modL2, 0, (g % 6) * 512)]
            if g < 4:
                variants.append((SC, self.modC, 1, g * 512))
            for (S_, dst, bi, dc0) in variants:
                bk = (g * 2 + bi) % 8
                for kc in range(8):
                    self.mm(self.pf(bk, 0, 512), S_[:, kc, :], w[:, kc, :], kc == 0, False, [S_, w], [self.bank[bk]])
                self.mm(self.pf(bk, 0, 512), self.onesf[0:1, :], bm[0:1, g * 512:(g + 1) * 512], False, True,
                        [self.onesf, bm], [self.bank[bk]])
                self.act(dst[:, dc0:dc0 + 512], self.pf(bk, 0, 512), AF.Copy, [self.bank[bk]], [dst])
        for (dst, nw, c0) in ((self.modL1, n1, DM), (self.modL2, n2, DM), (self.modC, n1, DM)):
            self.stt(dst[:, c0:c0 + DM], dst[:, c0:c0 + DM], 1.0, nw[:, :], ALU.add, ALU.mult, [dst, nw], [dst])
        kb.barrier()
        kb.release(m)
        ML, ML2, MC = self.modL1, self.modL2, self.modC
        self.modL = ML
        self.B1, self.A1, self.G1 = ML[:, 0:DM], ML[:, DM:2 * DM], ML[:, 2 * DM:3 * DM]
        self.B2, self.A2, self.G2 = ML2[:, 0:DM], ML2[:, DM:2 * DM], ML2[:, 2 * DM:3 * DM]
        self.B1c, self.A1c = MC[:, 0:DM], MC[:, DM:2 * DM]

    def norm_T(self, xt, xbuf, A, B, mbuf, hnT, Tn, sub, bk):
        ss, rs = self.ssq, self.rsd
        self.act(self.junk[:, :], xt, AF.Square, [xbuf], [self.junk, ss], accum=ss[:, 0:1])
        self.act(rs[:, 0:1], ss[:, 0:1], AF.Ln, [ss], [rs], scale=1.0 / DM, bias=self.epsb[:, 0:1])
        self.act(rs[:, 1:2], rs[:, 0:1], AF.Exp, [rs], [rs], scale=-0.5)
        self.stt(self.ntmp[:, :], xt, rs[:, 1:2], A, ALU.mult, ALU.mult, [xbuf, rs, mbuf], [self.ntmp])
        self.tt(self.hnb[:, :], self.ntmp[:, :], B, ALU.add, [self.ntmp, mbuf], [self.hnb])
        for kc in range(8):
            self.tr(self.pb16(bk, kc * 128, 128), self.hnb[:, kc * 128:(kc + 1) * 128], self.identb[:, :],
                    [self.hnb, self.identb], [self.bank[bk]])
        self.act(AP(hnT, sub * 128, [[8 * Tn, 128], [Tn, 8], [1, 128]]),
                 AP(self.PPb[bk // 2], (bk % 2) * 1024, [[2048, 128], [128, 8], [1, 128]]), AF.Copy,
                 [self.bank[bk]], [hnT])

    def tiles(self, colmajor):
        res = [(0, 0, True, None)]
        for i in range(self.L // TT):
            res.append((i + 1, CTX + i * TT, False, i))
        return res

    def x_src(self, is_ctx, li, colmajor):
        if is_ctx:
            return AP(self.ctx, 0, [[DM, 128], [128 * DM, 2], [1, DM]]), self.ctx
        if not colmajor:
            return AP(self.x, li * TT * DM, [[DM, 128], [128 * DM, 2], [1, DM]]), self.x
        c0 = li * 2
        return AP(self.x, c0 * DM, [[self.W * DM, 128], [DM, 2], [1, DM]]), self.x

    def phaseA(self):
        kb = self.kb
        m = kb.mark()
        NT = self.NT
        Wqk = kb.sb("Wqk", [128, 8, 1024], BF16)
        Wv = kb.sb("Wv", [128, 8, 1024], BF16)
        Wlr = kb.sb("Wlr", [128, 8, 32], BF16)
        self.wload(Wqk, self.w_in, SPL["gq"], 1024, rowlen=PIN)
        self.wload(Wv, self.w_in, SPL["gv"], 1024, rowlen=PIN)
        self.wload(Wlr, self.w_in, SPL["lr"], 32, rowlen=PIN)
        LRW = [kb.sb("LRW%d" % d, [33, 512], F32) for d in range(2)]
        U = [kb.sb("U%d" % d, [128, 128], F32) for d in range(2)]
        for d in range(2):
            self.load(LRW[d][:, :], AP(self.lrw, d * 33 * 512, [[512, 33], [1, 512]]), self.lrw, LRW[d])
            self.load(U[d][:, :], AP(self.cumU, d * 128 * 128, [[128, 128], [1, 128]]), self.cumU, U[d])
        xt = [kb.sb("xtA%d" % i, [128, 2, DM], F32) for i in range(2)]
        hnT = [kb.sb("hnTA%d" % i, [128, 8, TT], BF16) for i in range(2)]
        qkTs = [kb.sb("qkT%d" % i, [128, 8, TT], F32) for i in range(2)]
        vtok = [kb.sb("vtok%d" % i, [128, 2, DM], BF16) for i in range(2)]
        lraug = kb.sb("lraug", [33, TT], F32)
        self.memset(lraug[32:33, :], 1.0, [lraug])
        e1s = [kb.sb("e1_%d" % i, [128, 512], F32) for i in range(2)]
        sps = [kb.sb("sp_%d" % i, [128, 512], F32) for i in range(2)]
        eGs = [kb.sb("eG_%d" % i, [128, 512], F32) for i in range(2)]
        enGs = [kb.sb("enG_%d" % i, [128, 512], F32) for i in range(2)]
        dKs = [kb.sb("dK_%d" % i, [128, 512], F32) for i in range(2)]
        gends = [kb.sb("gend_%d" % i, [128, 4], F32) for i in range(2)]
        khTs = [kb.sb("khT_%d" % i, [128, 512], BF16) for i in range(2)]
        qs = [kb.sb("qs%d" % d, [128, 4, TT], BF16) for d in range(2)]
        ks = [kb.sb("ks%d" % d, [128, 4, TT], BF16) for d in range(2)]
        khs = [kb.sb("khs%d" % d, [128, 2, 512], BF16) for d in range(2)]
        print("phaseA sbuf remaining", self.nc.sbuf_bytes_remaining)
        tl = self.tiles(False)
        src, sbuf = self.x_src(tl[0][2], tl[0][3], False)
        self.load(xt[0][:, :, :], src, sbuf, xt[0])
        for ti, (idx, tok0, is_ctx, li) in enumerate(tl):
            X = xt[ti % 2]
            H = hnT[ti % 2]
            qkT = qkTs[ti % 2]
            if ti + 1 < len(tl):
                src, sbuf = self.x_src(tl[ti + 1][2], tl[ti + 1][3], False)
                self.load(xt[(ti + 1) % 2][:, :, :], src, sbuf, xt[(ti + 1) % 2])
            A, B, mb = (self.A1c, self.B1c, self.modC) if is_ctx else (self.A1, self.B1, self.modL)
            for sub in range(2):
                self.norm_T(X[:, sub, :], X, A, B, mb, H, TT, sub, 0)
            for fc in range(8):
                bk = 1 + fc % 2
                for kc in range(8):
                    self.mm(self.pf(bk, 0, TT), Wqk[:, kc, fc * 128:(fc + 1) * 128], H[:, kc, :], kc == 0, kc == 7,
                            [Wqk, H], [self.bank[bk]])
                self.act(qkT[:, fc, :], self.pf(bk, 0, TT), AF.Copy, [self.bank[bk]], [qkT],
                         scale=(128.0 ** -0.5 if fc < 4 else 1.0))
            VT = vtok[ti % 2]
            for sub in range(2):
                for half in range(2):
                    bk = 3 + half
                    for kc in range(8):
                        self.mm(self.pf(bk, 0, 512), H[:, kc, sub * 128:(sub + 1) * 128],
                                Wv[:, kc, half * 512:(half + 1) * 512], kc == 0, kc == 7, [H, Wv], [self.bank[bk]])
                    self.cp(VT[:, sub, half * 512:(half + 1) * 512], self.pf(bk, 0, 512), [self.bank[bk]], [VT])
            self.store(AP(self.gv, tok0 * DM, [[DM, 128], [128 * DM, 2], [1, DM]]), VT[:, :, :], VT, self.gv)
            for kc in range(8):
                self.mm(self.pf(5, 0, TT, 32), Wlr[:, kc, :], H[:, kc, :], kc == 0, kc == 7, [Wlr, H], [self.bank[5]])
            self.act(lraug[0:32, :], self.pf(5, 0, TT, 32), AF.Copy, [self.bank[5]], [lraug])
            for sub in range(2):
                ch = (tok0 // 128) + sub
                for d in range(2):
                    e1, sp, eG, enG, dK, gend, khT = (b[d] for b in (e1s, sps, eGs, enGs, dKs, gends, khTs))
                    self.mm(self.pf(6, 0, 512), lraug[0:33, sub * 128:(sub + 1) * 128], LRW[d][:, :], True, True,
                            [lraug, LRW[d]], [self.bank[6]])
                    self.act(e1[:, :], self.pf(6, 0, 512), AF.Exp, [self.bank[6]], [e1], scale=-1.0)
                    self.act(sp[:, :], e1[:, :], AF.Ln, [e1], [sp], bias=self.oneb[:, 0:1])
                    for h in range(4):
                        self.mm(self.pf(7, h * 128, 128), sp[:, h * 128:(h + 1) * 128], U[d][:, :], True, True,
                                [sp, U[d]], [self.bank[7]])
                    G = self.pf(7, 0, 512)
                    ecol = 127 if d == 0 else 0
                    self.act(eG[:, :], G, AF.Exp, [self.bank[7]], [eG])
                    self.act(enG[:, :], G, AF.Exp, [self.bank[7]], [enG], scale=-1.0)
                    self.act(gend[:, :], AP(self.PP[3], 512 + ecol, [[1024, 128], [128, 4]]), AF.Copy,
                             [self.bank[7]], [gend])
                    for h in range(4):
                        self.act(dK[:, h * 128:(h + 1) * 128], self.pf(7, h * 128, 128), AF.Exp,
                                 [self.bank[7], gend], [dK], scale=-1.0, bias=gend[:, h:h + 1])
                    self.cp(AP(self.Egla, (d * self.NCH + ch) * 4, [[2 * self.NCH * 4, 128], [1, 4]]),
                            AP(eG, ecol, [[512, 128], [128, 4]]), [eG], [self.Egla])
                    qv = AP(qkT, sub * 128, [[8 * TT, 128], [TT, 4], [1, 128]])
                    kv = AP(qkT, 4 * TT + sub * 128, [[8 * TT, 128], [TT, 4], [1, 128]])
                    g3 = lambda t: AP(t, 0, [[512, 128], [128, 4], [1, 128]])
                    self.tt(AP(qs[d], sub * 128, [[4 * TT, 128], [TT, 4], [1, 128]]), qv, g3(eG), ALU.mult,
                            [qkT, eG], [qs[d]])
                    self.tt(AP(ks[d], sub * 128, [[4 * TT, 128], [TT, 4], [1, 128]]), kv, g3(enG), ALU.mult,
                            [qkT, enG], [ks[d]])
                    self.tt(g3(khT), kv, g3(dK), ALU.mult, [qkT, dK], [khT])
                    for h in range(4):
                        self.tr(self.pb16(5, h * 128, 128), khT[:, h * 128:(h + 1) * 128], self.identb[:, :],
                                [khT, self.identb], [self.bank[5]])
                    self.cp(khs[d][:, sub, :], self.pb16(5, 0, 512), [self.bank[5]], [khs[d]])
            for d in range(2):
                self.store(AP(self.gq[d], tok0, [[NT, 128], [128 * NT, 4], [1, TT]]), qs[d][:, :, :], qs[d], self.gq[d])
                self.store(AP(self.gk[d], tok0, [[NT, 128], [128 * NT, 4], [1, TT]]), ks[d][:, :, :], ks[d], self.gk[d])
                self.store(AP(self.gkh[d], tok0 * 512, [[512, 128], [128 * 512, 2], [1, 512]]), khs[d][:, :, :],
                           khs[d], self.gkh[d])
        kb.barrier()
        kb.release(m)

    def gla_scan(self):
        kb = self.kb
        m = kb.mark()
        NT, NCH = self.NT, self.NCH
        S = [kb.sb("Sg%d" % d, [128, 4, 256], F32) for d in range(2)]
        Sb = [kb.sb("Sgb%d" % d, [128, 4, 256], BF16) for d in range(2)]
        msk = [kb.sb("gm%d" % d, [128, 128], F32) for d in range(2)]
        for d in range(2):
            self.memset(S[d][:, :, :], 0.0, [S[d]])
            self.memset(Sb[d][:, :, :], 0.0, [Sb[d]])
            self.load(msk[d][:, :], AP(self.gmask, d * 128 * 128, [[128, 128], [1, 128]]), self.gmask, msk[d])
        nb = 2
        qt = [[kb.sb("sq%d_%d" % (d, i), [128, 4, 128], BF16) for i in range(nb)] for d in range(2)]
        kt = [[kb.sb("sk%d_%d" % (d, i), [128, 4, 128], BF16) for i in range(nb)] for d in range(2)]
        kh = [[kb.sb("skh%d_%d" % (d, i), [128, 512], BF16) for i in range(nb)] for d in range(2)]
        vv = [[kb.sb("sv%d_%d" % (d, i), [128, DM], BF16) for i in range(nb)] for d in range(2)]
        AT = [kb.sb("AT%d" % d, [128, 4, 128], BF16) for d in range(2)]
        osb = [kb.sb("osb%d" % d, [128, DM], F32) for d in range(2)]
        order = [list(range(NCH)), [1, 0] + list(range(NCH - 1, 1, -1))]

        def issue_loads(s):
            for d in range(2):
                c = order[d][s]
                t0 = c * 128
                i = s % nb
                self.load(qt[d][i][:, :, :], AP(self.gq[d], t0, [[NT, 128], [128 * NT, 4], [1, 128]]), self.gq[d], qt[d][i])
                self.load(kt[d][i][:, :, :], AP(self.gk[d], t0, [[NT, 128], [128 * NT, 4], [1, 128]]), self.gk[d], kt[d][i])
                self.load(kh[d][i][:, :], AP(self.gkh[d], t0 * 512, [[512, 128], [1, 512]]), self.gkh[d], kh[d][i])
                self.load(vv[d][i][:, :], AP(self.gv, t0 * DM, [[DM, 128], [1, DM]]), self.gv, vv[d][i])

        issue_loads(0)
        for s in range(NCH):
            if s + 1 < NCH:
                issue_loads(s + 1)
            for d in range(2):
                c = order[d][s]
                i = s % nb
                Q, K_, KH, V = qt[d][i], kt[d][i], kh[d][i], vv[d][i]
                bA = d
                bO = 2 + 2 * d
                bS = 6
                for h in range(4):
                    self.mm(self.pf(bA, h * 128, 128), K_[:, h, :], Q[:, h, :], True, True, [K_, Q], [self.bank[bA]])
                self.tt(AT[d][:, :, :], AP(self.PP[bA // 2], (bA % 2) * 512, [[1024, 128], [128, 4], [1, 128]]),
                        AP(msk[d], 0, [[128, 128], [0, 4], [1, 128]]), ALU.mult, [self.bank[bA], msk[d]], [AT[d]])
                for h in range(4):
                    bk = bO + h // 2
                    o_ap = self.pf(bk, (h % 2) * 256, 256)
                    self.mm(o_ap, AT[d][:, h, :], V[:, h * 256:(h + 1) * 256], True, False, [AT[d], V], [self.bank[bk]])
                    self.mm(o_ap, Q[:, h, :], Sb[d][:, h, :], False, True, [Q, Sb[d]], [self.bank[bk]])
                self.act(osb[d][:, :], AP(self.PP[bO // 2], 0, [[1024, 128], [1, 1024]]), AF.Copy,
                         [self.bank[bO], self.bank[bO + 1]], [osb[d]])
                self.store(AP(self.go[d], c * 128 * DM, [[DM, 128], [1, DM]]), osb[d][:, :], osb[d], self.go[d])
                for h in range(4):
                    bk = bS + h // 2
                    self.mm(self.pf(bk, (h % 2) * 256, 256), KH[:, h * 128:(h + 1) * 128], V[:, h * 256:(h + 1) * 256],
                            True, True, [KH, V], [self.bank[bk]])
                for h in range(4):
                    bk = bS + h // 2
                    self.stt(S[d][:, h, :], S[d][:, h, :],
                             AP(self.Egla, (d * NCH + c) * 4 + h, [[2 * NCH * 4, 128], [1, 1]]),
                             self.pf(bk, (h % 2) * 256, 256), ALU.mult, ALU.add,
                             [S[d], self.Egla, self.bank[bk]], [S[d]])
                self.act(Sb[d][:, :, :], S[d][:, :, :], AF.Copy, [S[d]], [Sb[d]])
        kb.barrier()
        kb.release(m)

    def phaseB(self):
        kb = self.kb
        m = kb.mark()
        NT, NCH = self.NT, self.NCH
        Wgd = kb.sb("Wgd", [128, 8, 3072], BF16)
        Wab = kb.sb("Wab", [128, 8, 32], BF16)
        self.wload(Wgd, self.w_in, SPL["dq"], 3072, rowlen=PIN)
        self.wload(Wab, self.w_in, SPL["ab"], 32, rowlen=PIN)
        cw = kb.sb("cw", [128, 24, 5], F32)
        self.load(AP(cw, 0, [[120, 128], [1, 120]]), self.cwT[:, :], self.cwT, cw)
        negA = kb.sb("negA", [128, 16], F32)
        dtb = kb.sb("dtbs", [128, 16], F32)
        self.load(negA[:, :], self.alog[:, :], self.alog, negA)
        self.load(dtb[:, :], self.dtb[:, :], self.dtb, dtb)
        self.act(negA[:, :], negA[:, :], AF.Exp, [negA], [negA])
        self.kb.op("act", lambda e: e.mul(negA[:, :], negA[:, :], -1.0), [negA], [negA])
        CU = [kb.sb("CU%d" % i, [128, 128], F32) for i in range(4)]
        for i in range(4):
            self.load(CU[i][:, :], AP(self.cumU, (2 + i) * 128 * 128, [[128, 128], [1, 128]]), self.cumU, CU[i])
        UFi, UBi, UBs, UFs = CU
        MK = [kb.sb("MK%d" % i, [128, 128], BF16) for i in range(4)]
        for i in range(4):
            self.kb.dma("pool", MK[i][:, :], AP(self.dmask, i * 128 * 128, [[128, 128], [1, 128]]), [self.dmask], [MK[i]])
        Minc = [MK[0], MK[1]]
        Mlt, Mgt = MK[2], MK[3]
        sel = kb.sb("sel", [96, 32, 128], BF16)
        self.kb.dma("pool", AP(sel, 0, [[4096, 96], [1, 4096]]), self.sel_d[:, :], [self.sel_d], [sel])
        xt = [kb.sb("xtB%d" % i, [128, 2, DM], F32) for i in range(1)] * 2
        hnT = kb.sb("hnTB", [128, 8, TT], BF16)
        acc = [kb.sb("acc%d" % i, [128, TT], F32) for i in range(4)]
        sg = [kb.sb("sgb%d" % i, [128, TT], F32) for i in range(2)]
        sil = [kb.sb("sil%d" % i, [128, TT], F32) for i in range(4)]
        sq = [kb.sb("sqb%d" % i, [128, TT], BF16) for i in range(2)]
        rn = [kb.sb("rn%d" % i, [128, TT], F32) for i in range(2)]
        vTb = [kb.sb("vTb%d" % i, [128, TT], BF16) for i in range(2)]
        dqs = kb.sb("dqs", [128, 8, TT], BF16)
        dks = kb.sb("dks", [128, 8, TT], BF16)
        vtk = kb.sb("vtk", [128, 2, DM], BF16)
        ktk = kb.sb("ktk", [128, 2, DM], BF16)
        g16 = kb.sb("g16", [128, 16], F32)
        t16 = kb.sb("t16", [128, 16], F32)
        beta = kb.sb("beta", [128, 16], F32)
        lnb = kb.sb("lnb", [128, 16], F32)
        ba = kb.sb("ba", [128, 16], F32)
        R = kb.sb("R", [128, 32], F32)
        negG = kb.sb("negG", [128, 16], F32)
        R1 = kb.sb("R1", [128, 32], F32)
        Rs = kb.sb("Rs", [128, 96], BF16)
        rows = kb.sb("rows", [96, 128], BF16)
        nrows = kb.sb("nrows", [96, 128], BF16)
        Dm = kb.sb("Dm", [128, 8, 128], F32)
        E1 = kb.sb("E1m", [128, 8, 128], F32)
        E2 = Dm
        aqs = kb.sb("aqs", [128, 8, 128], BF16)
        X0 = kb.sb("X0n", [128, 8, 128], BF16)
        LT = [kb.sb("LTn%d" % d, [128, 8, 128], BF16) for d in range(2)]
        Dn = [[kb.sb("Dn%d_%d" % (d, g), [128, 4, 128], BF16) for g in range(2)] for d in range(2)]
        DTn = [[kb.sb("DTn%d_%d" % (d, g), [128, 4, 128], BF16) for g in range(2)] for d in range(2)]
        Wn = [[kb.sb("Wn%d_%d" % (d, g), [128, 4, 128], BF16) for g in range(2)] for d in range(2)]
        MT = [kb.sb("MTn%d" % d, [128, 8, 128], BF16) for d in range(2)]
        MaT = [kb.sb("MaTn%d" % d, [128, 8, 128], BF16) for d in range(2)]
        LM = kb.sb("LM", [128, 14, 128], BF16)
        self.kb.dma("pool", LM[:, :, :], AP(self.lvlm, 0, [[128, 128], [128 * 128, 14], [1, 128]]), [self.lvlm], [LM])
        lm = lambda i: AP(LM, i * 128, [[14 * 128, 128], [0, 8], [1, 128]])
        lm4 = lambda i: AP(LM, i * 128, [[14 * 128, 128], [0, 4], [1, 128]])
        nidb = kb.sb("nidb", [128, 128], BF16)
        self.kb.op("act", lambda e: e.mul(nidb[:, :], self.identb[:, :], -1.0), [self.identb], [nidb])
        us = kb.sb("us", [128, DM], F32)
        ws = kb.sb("wsn", [128, 8, 128], BF16)
        b3 = lambda t, c0: AP(t, c0, [[16, 128], [1, 8], [0, 128]])
        f3 = lambda t: AP(t, 0, [[1024, 128], [128, 8], [1, 128]])
        p3 = lambda k: AP(self.PP[k], 0, [[1024, 128], [128, 8], [1, 128]])
        pbk = lambda k: [self.bank[2 * k], self.bank[2 * k + 1]]
        print("phaseB sbuf remaining", self.nc.sbuf_bytes_remaining)
        tl = self.tiles(True)
        src, sbuf = self.x_src(tl[0][2], tl[0][3], True)
        self.load(xt[0][:, :, :], src, sbuf, xt[0])
        for ti, (idx, tok0, is_ctx, li) in enumerate(tl):
            Xt = xt[ti % 2]
            A, B, mb = (self.A1c, self.B1c, self.modC) if is_ctx else (self.A1, self.B1, self.modL)
            for sub in range(2):
                self.norm_T(Xt[:, sub, :], Xt, A, B, mb, hnT, TT, sub, 0)
            if ti + 1 < len(tl):
                src, sbuf = self.x_src(tl[ti + 1][2], tl[ti + 1][3], True)
                self.load(xt[(ti + 1) % 2][:, :, :], src, sbuf, xt[(ti + 1) % 2])
            nseg, ls = (1, TT) if is_ctx else (2, 128)

            def stA(fc):
                bk = 1 + fc % 3
                for kc in range(8):
                    self.mm(self.pf(bk, 0, TT), Wgd[:, kc, fc * 128:(fc + 1) * 128], hnT[:, kc, :], kc == 0, kc == 7,
                            [Wgd, hnT], [self.bank[bk]])

            def stB(fc):
                bk = 1 + fc % 3
                A_ = acc[fc % 4]
                self.ts(A_[:, :], self.pf(bk, 0, TT), cw[:, fc, 2:3], None, ALU.mult, None, [self.bank[bk], cw], [A_])
                for tap in (0, 1, 3, 4):
                    sh = tap - 2
                    n = ls - abs(sh)
                    o0, i0 = (0, sh) if sh > 0 else (-sh, 0)
                    oap = AP(A_, o0, [[TT, 128], [ls, nseg], [1, n]])
                    iap = AP(self.PP[bk // 2], (bk % 2) * 512 + i0, [[1024, 128], [ls, nseg], [1, n]])
                    self.stt(oap, iap, cw[:, fc, tap:tap + 1], oap, ALU.mult, ALU.add, [self.bank[bk], cw, A_], [A_])

            def stCE(fc_c, fc_e):
                okc = fc_c is not None and 0 <= fc_c < 24
                oke = fc_e is not None and 0 <= fc_e < 16
                if okc:
                    A_, G_ = acc[fc_c % 4], sg[fc_c % 2]
                    self.act(G_[:, :], A_[:, :], AF.Exp, [A_], [G_], scale=-1.0)
                if oke:
                    S_, Q_, R_ = sil[fc_e % 4], sq[fc_e % 2], rn[fc_e % 2]
                    bq = 6 + fc_e % 2
                    self.act(Q_[:, :], S_[:, :], AF.Square, [S_], [Q_])
                    self.mm(self.pf(bq, 0, TT), self.onesb[:, :], Q_[:, :], True, True, [self.onesb, Q_], [self.bank[bq]])
                if okc:
                    self.act(G_[:, :], G_[:, :], AF.Ln, [G_], [G_], bias=self.oneb[:, 0:1])
                if oke:
                    self.act(R_[:, :], self.pf(bq, 0, TT), AF.Ln, [self.bank[bq]], [R_], bias=self.epsb[:, 0:1])
                if okc:
                    self.act(G_[:, :], G_[:, :], AF.Exp, [G_], [G_], scale=-1.0)
                if oke:
                    self.act(R_[:, :], R_[:, :], AF.Exp, [R_], [R_], scale=-0.5)

            def stD(fc):
                A_, G_ = acc[fc % 4], sg[fc % 2]
                if fc < 16:
                    S_ = sil[fc % 4]
                    self.tt(S_[:, :], A_[:, :], G_[:, :], ALU.mult, [A_, G_], [S_])
                else:
                    h = fc - 16
                    V_ = vTb[fc % 2]
                    self.tt(V_[:, :], A_[:, :], G_[:, :], ALU.mult, [A_, G_], [V_])
                    for sub in range(2):
                        self.tr(self.pb16(4 + sub, h * 128, 128), V_[:, sub * 128:(sub + 1) * 128], self.identb[:, :],
                                [V_, self.identb], [self.bank[4 + sub]])

            def stF(fc):
                if fc < 16:
                    h = fc % 8
                    S_, R_ = sil[fc % 4], rn[fc % 2]
                    dst = dqs if fc < 8 else dks
                    self.stt(dst[:, h, :], S_[:, :], (128.0 ** -0.5 if fc < 8 else 1.0), R_[:, :], ALU.mult, ALU.mult,
                             [S_, R_], [dst])

            for it in range(24 + 5):
                if it < 24:
                    stA(it)
                if 0 <= it - 1 < 24:
                    stB(it - 1)
                stCE(it - 2, it - 4)
                if 0 <= it - 3 < 24:
                    stD(it - 3)
                if 0 <= it - 5 < 24:
                    stF(it - 5)
            for sub in range(2):
                self.cp(vtk[:, sub, :], self.pb16(4 + sub, 0, 1024), [self.bank[4 + sub]], [vtk])
            for sub in range(2):
                for h in range(8):
                    self.tr(self.pb16(4 + sub, h * 128, 128), dks[:, h, sub * 128:(sub + 1) * 128], self.identb[:, :],
                            [dks, self.identb], [self.bank[4 + sub]])
                self.cp(ktk[:, sub, :], self.pb16(4 + sub, 0, 1024), [self.bank[4 + sub]], [ktk])
            self.store(AP(self.dqT, tok0, [[NT, 128], [128 * NT, 8], [1, TT]]), dqs[:, :, :], dqs, self.dqT)
            self.store(AP(self.dkt, tok0 * DM, [[DM, 128], [128 * DM, 2], [1, DM]]), ktk[:, :, :], ktk, self.dkt)
            for sub in range(2):
                ch = tok0 // 128 + sub
                tsl = slice(sub * 128, (sub + 1) * 128)
                for kc in range(8):
                    self.mm(self.pf(6, 0, 32), hnT[:, kc, tsl], Wab[:, kc, :], kc == 0, kc == 7, [hnT, Wab], [self.bank[6]])
                self.tt(t16[:, :], self.pf(6, 0, 16), dtb[:, :], ALU.add, [self.bank[6], dtb], [t16])
                self.act(t16[:, :], t16[:, :], AF.Exp, [t16], [t16])
                self.act(t16[:, :], t16[:, :], AF.Ln, [t16], [t16], bias=self.oneb[:, 0:1])
                self.tt(g16[:, :], t16[:, :], negA[:, :], ALU.mult, [t16, negA], [g16])
                self.act(lnb[:, :], self.pf(6, 16, 16), AF.Exp, [self.bank[6]], [lnb], scale=-1.0)
                self.act(lnb[:, :], lnb[:, :], AF.Ln, [lnb], [lnb], bias=self.oneb[:, 0:1])
                self.kb.op("act", lambda e: e.mul(lnb[:, :], lnb[:, :], -1.0), [lnb], [lnb])
                self.act(beta[:, :], lnb[:, :], AF.Exp, [lnb], [beta])
                self.mm(self.pf(7, 0, 8), UFi[:, :], g16[:, 0:8], True, True, [UFi, g16], [self.bank[7]])
                self.mm(self.pf(7, 8, 8), UBi[:, :], g16[:, 8:16], True, True, [UBi, g16], [self.bank[7]])
                self.mm(self.pf(7, 16, 8), UBs[:, :], g16[:, 0:8], True, True, [UBs, g16], [self.bank[7]])
                self.mm(self.pf(7, 24, 8), UFs[:, :], g16[:, 8:16], True, True, [UFs, g16], [self.bank[7]])
                self.mm(self.pf(7, 32, 16), self.onesf[:, :], g16[:, :], True, True, [self.onesf, g16], [self.bank[7]])
                self.act(R[:, 0:16], self.pf(7, 0, 16), AF.Copy, [self.bank[7]], [R])
                self.act(AP(self.GDa, ch * 16, [[NCH * 16, 128], [1, 16]]), self.pf(7, 0, 16), AF.Exp,
                         [self.bank[7]], [self.GDa])
                self.act(AP(self.GDd, ch * 16, [[NCH * 16, 128], [1, 16]]), self.pf(7, 16, 16), AF.Exp,
                         [self.bank[7]], [self.GDd])
                self.act(AP(self.GDe, ch * 16, [[NCH * 16, 128], [1, 16]]), self.pf(7, 32, 16), AF.Exp,
                         [self.bank[7]], [self.GDe])
                self.tt(ba[:, :], beta[:, :], AP(self.GDa, ch * 16, [[NCH * 16, 128], [1, 16]]), ALU.mult,
                        [beta, self.GDa], [ba])
                self.tt(R[:, 16:32], R[:, 0:16], lnb[:, :], ALU.add, [R, lnb], [R])
                self.kb.op("act", lambda e: e.mul(negG[:, :], R[:, 0:16], -1.0), [R], [negG])
                self.cp(Rs[:, 0:32], R[:, :], [R], [Rs])
                self.tt(R1[:, :], R[:, :], Rs[:, 0:32], ALU.subtract, [R, Rs], [R1])
                self.cp(Rs[:, 32:64], R1[:, :], [R1], [Rs])
                self.tt(R1[:, :], R1[:, :], Rs[:, 32:64], ALU.subtract, [R1, Rs], [R1])
                self.cp(Rs[:, 64:96], R1[:, :], [R1], [Rs])
                self.tr(self.pb16(6, 0, 128, 96), Rs[:, :], self.identb[:, :], [Rs, self.identb], [self.bank[6]])
                self.act(rows[:, :], self.pb16(6, 0, 128, 96), AF.Copy, [self.bank[6]], [rows])
                self.kb.op("act", lambda e: e.mul(nrows[:, :], rows[:, :], -1.0), [rows], [nrows])
                for h in range(8):
                    bk = h // 4
                    self.mm(self.pf(bk, (h % 4) * 128, 128), dks[:, h, tsl], dks[:, h, tsl], True, True,
                            [dks], [self.bank[bk]])
                for h in range(8):
                    bk = 2 + h // 4
                    self.mm(self.pf(bk, (h % 4) * 128, 128), dks[:, h, tsl], dqs[:, h, tsl], True, True,
                            [dks, dqs], [self.bank[bk]])
                for d in range(2):
                    mstrT = Mlt if d == 0 else Mgt
                    mstr = Mgt if d == 0 else Mlt
                    specs = [(Dm, 0, rows, nrows, 0, Minc[d]), (E1, 16, rows, nrows, 0, mstrT), (E2, 0, nrows, rows, 16, mstr)]
                    for si, (dst, so, rr, pr, po, mk) in enumerate(specs):
                        pk = 2 + (si % 2)
                        for hg in range(2):
                            bk = 2 * pk + hg
                            c0 = d * 8 + hg * 4
                            self.mm(self.pf(bk, 0, 512), self.identb[:, :], AP(mk, 0, [[128, 128], [0, 4], [1, 128]]),
                                    True, False, [self.identb, mk], [self.bank[bk]])
                            self.mm(self.pf(bk, 0, 512), pr[:, :], AP(sel, (po + c0) * 128, [[32 * 128, 96], [1, 512]]),
                                    False, False, [sel, pr], [self.bank[bk]])
                            for h4i in range(4):
                                self.mm(self.pf(bk, h4i * 128, 128), sel[:, so + c0 + h4i, :], rr[:, :], False, True,
                                        [sel, rr], [self.bank[bk]])
                        for hg in range(2):
                            bk = 2 * pk + hg
                            self.act(AP(dst, hg * 512, [[1024, 128], [1, 512]]), self.pf(bk, 0, 512), AF.Exp,
                                     [self.bank[bk]], [dst])
                        if si == 0:
                            self.tt(f3(aqs), p3(1), f3(Dm), ALU.mult, pbk(1) + [Dm], [aqs])
                            self.store(AP(self.daq[d], ch * 128 * DM, [[DM, 128], [1, DM]]), aqs[:, :, :], aqs, self.daq[d])
                    self.tt(f3(LT[d]), p3(0), f3(E1), ALU.mult, pbk(0) + [E1], [LT[d]])
                    self.tt(f3(X0), p3(0), f3(E2), ALU.mult, pbk(0) + [E2], [X0])
                    if "dbgX" in self.dbg and ch == 2 and d == 0:
                        self.store(AP(self.dbgX, 0, [[DM, 128], [1, DM]]), X0[:, :, :], X0, self.dbgX)
                        self.store(AP(self.dbgXT, 0, [[DM, 128], [1, DM]]), LT[d][:, :, :], LT[d], self.dbgXT)
                    for hg in range(2):
                        h4 = lambda t: AP(t, 0, [[512, 128], [128, 4], [1, 128]])
                        idb = AP(self.identb, 0, [[128, 128], [0, 4], [1, 128]])
                        src4 = lambda t: AP(t, hg * 512, [[1024, 128], [128, 4], [1, 128]])
                        W_, D_, DT_ = Wn[d][hg], Dn[d][hg], DTn[d][hg]
                        self.tt(h4(W_), src4(X0), lm4(d * 7), ALU.mult, [X0, LM], [W_])
                        self.tt(h4(D_), h4(W_), idb, ALU.add, [W_, self.identb], [D_])
                        self.tt(h4(W_), src4(LT[d]), lm4((1 - d) * 7), ALU.mult, [LT[d], LM], [W_])
                        self.tt(h4(DT_), h4(W_), idb, ALU.add, [W_, self.identb], [DT_])
                h4 = lambda t: AP(t, 0, [[512, 128], [128, 4], [1, 128]])
                for lvl in range(1, 7):
                    for d in range(2):
                        for hg in range(2):
                            k = d * 2 + hg
                            ba_, bb_ = 2 * k, 2 * k + 1
                            W_, D_, DT_ = Wn[d][hg], Dn[d][hg], DTn[d][hg]
                            self.mm(self.pf(ba_, 0, 512), self.identb[:, :], AP(nidb, 0, [[128, 128], [0, 4], [1, 128]]),
                                    True, False, [self.identb, nidb], [self.bank[ba_]])
                            for h4i in range(4):
                                h = hg * 4 + h4i
                                self.mm(self.pf(ba_, h4i * 128, 128), LT[d][:, h, :], D_[:, h4i, :], False, True,
                                        [LT[d], D_], [self.bank[ba_]])
                            self.tt(h4(W_), AP(self.PP[ba_ // 2], (ba_ % 2) * 512, [[1024, 128], [128, 4], [1, 128]]),
                                    lm4(d * 7 + lvl), ALU.mult, [self.bank[ba_], LM], [W_])
                            if lvl < 6:
                                for h4i in range(4):
                                    self.mm(self.pf(bb_, h4i * 128, 128), DT_[:, h4i, :], W_[:, h4i, :], True, True,
                                            [DT_, W_], [self.bank[bb_]])
                            for h4i in range(4):
                                self.mm(self.pf(ba_, h4i * 128, 128), W_[:, h4i, :], DT_[:, h4i, :], True, True,
                                        [W_, DT_], [self.bank[ba_]])
                            if lvl < 6:
                                if hg == 0:
                                    self.act(AP(D_, 0, [[512, 128], [1, 512]]), self.pf(bb_, 0, 512), AF.Copy, [self.bank[bb_]], [D_])
                                    self.cp(AP(DT_, 0, [[512, 128], [1, 512]]), self.pf(ba_, 0, 512), [self.bank[ba_]], [DT_])
                                else:
                                    self.cp(AP(D_, 0, [[512, 128], [1, 512]]), self.pf(bb_, 0, 512), [self.bank[bb_]], [D_])
                                    self.act(AP(DT_, 0, [[512, 128], [1, 512]]), self.pf(ba_, 0, 512), AF.Copy, [self.bank[ba_]], [DT_])
                            else:
                                pv = AP(self.PP[ba_ // 2], (ba_ % 2) * 512, [[1024, 128], [128, 4], [1, 128]])
                                mo = lambda t: AP(t, hg * 512, [[1024, 128], [128, 4], [1, 128]])
                                b4 = lambda t, c0: AP(t, c0, [[16, 128], [1, 4], [0, 128]])
                                self.tt(mo(MT[d]), pv, b4(beta, d * 8 + hg * 4), ALU.mult, [self.bank[ba_], beta], [MT[d]])
                                self.tt(mo(MaT[d]), pv, b4(ba, d * 8 + hg * 4), ALU.mult, [self.bank[ba_], ba], [MaT[d]])
                for d in range(2):
                    pa, pb_ = 2 * d, 2 * d + 1
                    if "dbgX" in self.dbg and ch == 2 and d == 0:
                        self.store(AP(self.dbgMT, 0, [[DM, 128], [1, DM]]), MT[d][:, :, :], MT[d], self.dbgMT)
                    for h in range(8):
                        bk = 2 * pb_ + h // 4
                        self.mm(self.pf(bk, (h % 4) * 128, 128), MT[d][:, h, :], vtk[:, sub, h * 128:(h + 1) * 128], True, True,
                                [MT[d], vtk], [self.bank[bk]])
                    self.act(us[:, :], AP(self.PP[pb_], 0, [[1024, 128], [1, 1024]]), AF.Copy, pbk(pb_), [us])
                    self.store(AP(self.du[d], ch * 128 * DM, [[DM, 128], [1, DM]]), us[:, :], us, self.du[d])
                    for h in range(8):
                        bk = 2 * pa + h // 4
                        self.mm(self.pf(bk, (h % 4) * 128, 128), ktk[:, sub, h * 128:(h + 1) * 128], MaT[d][:, h, :], True, True,
                                [ktk, MaT[d]], [self.bank[bk]])
                    self.cp(f3(ws), p3(pa), pbk(pa), [ws])
                    self.store(AP(self.dw[d], ch * 128, [[NT, 128], [128 * NT, 8], [1, 128]]), ws[:, :, :], ws, self.dw[d])
        kb.barrier()
        kb.release(m)

    def gdn_scan(self):
        kb = self.kb
        m = kb.mark()
        NT, NCH = self.NT, self.NCH
        S = [kb.sb("Sd%d" % d, [128, 8, 128], F32) for d in range(2)]
        Sb = [kb.sb("Sdb%d" % d, [128, 8, 128], BF16) for d in range(2)]
        for d in range(2):
            self.memset(S[d][:, :, :], 0.0, [S[d]])
            self.memset(Sb[d][:, :, :], 0.0, [Sb[d]])
        nb = 2
        mk = lambda n, sh, dt: [[kb.sb("%s%d_%d" % (n, d, i), sh, dt) for i in range(nb)] for d in range(2)]
        wT = mk("lw", [128, 8, 128], BF16)
        qT = mk("lq", [128, 8, 128], BF16)
        uu = mk("lu", [128, DM], F32)
        aq = mk("la", [128, DM], BF16)
        kt = mk("lk", [128, DM], BF16)
        vn32 = kb.sb("vn32", [128, DM], F32)
        vnb = kb.sb("vnb", [128, DM], BF16)
        vnd = kb.sb("vnd", [128, DM], BF16)
        tq = kb.sb("tq", [128, DM], F32)
        osb = [kb.sb("odb%d" % d, [128, DM], F32) for d in range(2)]
        order = [list(range(NCH)), [1, 0] + list(range(NCH - 1, 1, -1))]
        b3 = lambda t, c0: AP(t, c0, [[NCH * 16, 128], [1, 8], [0, 128]])
        f3 = lambda t: AP(t, 0, [[1024, 128], [128, 8], [1, 128]])
        p3 = lambda k: AP(self.PP[k], 0, [[1024, 128], [128, 8], [1, 128]])
        pbk = lambda k: [self.bank[2 * k], self.bank[2 * k + 1]]

        def issue_loads(s):
            for d in range(2):
                c = order[d][s]
                t0 = c * 128
                i = s % nb
                self.load(wT[d][i][:, :, :], AP(self.dw[d], t0, [[NT, 128], [128 * NT, 8], [1, 128]]), self.dw[d], wT[d][i])
                self.load(qT[d][i][:, :, :], AP(self.dqT, t0, [[NT, 128], [128 * NT, 8], [1, 128]]), self.dqT, qT[d][i])
                self.load(uu[d][i][:, :], AP(self.du[d], t0 * DM, [[DM, 128], [1, DM]]), self.du[d], uu[d][i])
                self.load(aq[d][i][:, :], AP(self.daq[d], t0 * DM, [[DM, 128], [1, DM]]), self.daq[d], aq[d][i])
                self.load(kt[d][i][:, :], AP(self.dkt, t0 * DM, [[DM, 128], [1, DM]]), self.dkt, kt[d][i])

        issue_loads(0)
        for s in range(NCH):
            if s + 1 < NCH:
                issue_loads(s + 1)
            for d in range(2):
                c = order[d][s]
                i = s % nb
                Wt, Qt, Uu, Aq, Kt = wT[d][i], qT[d][i], uu[d][i], aq[d][i], kt[d][i]
                for h in range(8):
                    bk = h // 4
                    self.mm(self.pf(bk, (h % 4) * 128, 128), Wt[:, h, :], Sb[d][:, h, :], True, True, [Wt, Sb[d]], [self.bank[bk]])
                for h in range(8):
                    bk = 2 + h // 4
                    self.mm(self.pf(bk, (h % 4) * 128, 128), Qt[:, h, :], Sb[d][:, h, :], True, True, [Qt, Sb[d]], [self.bank[bk]])
                self.tt(vn32[:, :], Uu[:, :], AP(self.PP[0], 0, [[1024, 128], [1, 1024]]), ALU.subtract, [Uu] + pbk(0), [vn32])
                self.act(vnb[:, :], vn32[:, :], AF.Copy, [vn32], [vnb])
                self.tt(f3(vnd), f3(vn32), b3(self.GDd, c * 16 + d * 8), ALU.mult, [vn32, self.GDd], [vnd])
                for h in range(8):
                    bk = 4 + h // 4
                    hs = slice(h * 128, (h + 1) * 128)
                    self.mm(self.pf(bk, (h % 4) * 128, 128), Aq[:, hs], vnb[:, hs], True, True, [Aq, vnb], [self.bank[bk]])
                for h in range(8):
                    bk = 6 + h // 4
                    hs = slice(h * 128, (h + 1) * 128)
                    self.mm(self.pf(bk, (h % 4) * 128, 128), Kt[:, hs], vnd[:, hs], True, True, [Kt, vnd], [self.bank[bk]])
                self.tt(f3(tq), p3(1), b3(self.GDa, c * 16 + d * 8), ALU.mult, pbk(1) + [self.GDa], [tq])
                self.tt(osb[d][:, :], tq[:, :], AP(self.PP[2], 0, [[1024, 128], [1, 1024]]), ALU.add, [tq] + pbk(2), [osb[d]])
                self.store(AP(self.do[d], c * 128 * DM, [[DM, 128], [1, DM]]), osb[d][:, :], osb[d], self.do[d])
                self.tt(f3(S[d]), f3(S[d]), b3(self.GDe, c * 16 + d * 8), ALU.mult, [S[d], self.GDe], [S[d]])
                self.tt(f3(S[d]), f3(S[d]), p3(3), ALU.add, [S[d]] + pbk(3), [S[d]])
                self.act(Sb[d][:, :, :], S[d][:, :, :], AF.Copy, [S[d]], [Sb[d]])
        kb.barrier()
        kb.release(m)

    def phase3a(self):
        kb = self.kb
        m = kb.mark()
        W, NT = self.W, self.NT
        Wz = kb.sb("Wz", [128, 8, 4096], BF16)
        for gi, key in enumerate(("gz", "dz", "g11", "g12")):
            for k0 in (0, 4):
                self.kb.dma("pool", AP(Wz, k0 * 4096 + gi * 1024, [[8 * 4096, 128], [4096, 4], [1, 1024]]),
                            AP(self.w_in, k0 * 128 * PIN + SPL[key], [[PIN, 128], [128 * PIN, 4], [1, 1024]]),
                            [self.w_in], [Wz])
        Wo = kb.sb("Wo", [128, 8, DM], BF16)
        self.wload(Wo, self.w_out, 0, DM, rowlen=DM)
        gg = kb.sb("ggl", [128, DM], F32)
        gd = kb.sb("ggd", [128, DM], F32)
        self.load(gg[:, :], self.glag[:, :], self.glag, gg)
        self.load(gd[:, :], self.gdng[:, :], self.gdng, gd)
        xt = [kb.sb("xt3%d" % i, [128, DM], F32) for i in range(2)]
        ol = [[kb.sb("ol%d_%d" % (j, i), [128, DM], F32) for i in range(2)] for j in range(4)]
        hnT = [kb.sb("hnT3_%d" % i, [128, 8, 128], BF16) for i in range(2)]
        sz = [kb.sb("sz%d" % i, [128, 2048], BF16) for i in range(2)]
        sg = [kb.sb("sg%d" % i, [128, 2048], BF16) for i in range(2)]
        ss = kb.sb("ss3", [128, 12], F32)
        rs = kb.sb("rs3", [128, 12], F32)
        mb16 = [kb.sb("mb16_%d" % i, [128, DM], BF16) for i in range(2)]
        mT = [kb.sb("mT%d" % i, [128, 8, 128], BF16) for i in range(2)]
        x1 = [kb.sb("x1s%d" % i, [128, DM], F32) for i in range(2)]
        print("phase3a sbuf remaining", self.nc.sbuf_bytes_remaining)
        ntile = self.L // 128

        def issue_loads(t):
            i = t % 2
            self.load(xt[i][:, :], AP(self.x, t * 128 * DM, [[DM, 128], [1, DM]]), self.x, xt[i])
            g0 = (CTX + t * 128) * DM
            self.load(ol[0][i][:, :], AP(self.go[0], g0, [[DM, 128], [1, DM]]), self.go[0], ol[0][i])
            self.load(ol[1][i][:, :], AP(self.go[1], g0, [[DM, 128], [1, DM]]), self.go[1], ol[1][i])
            nr = 128 // W if W <= 128 else 1
            for j in (0, 1):
                for rr in range(128 // W):
                    r = t * (128 // W) + rr
                    self.load(AP(ol[2 + j][i], rr * W * DM, [[DM, W], [1, DM]]),
                              AP(self.do[j], (CTX + r) * DM, [[128 * DM, W], [1, DM]]), self.do[j], ol[2 + j][i])

        def stage_a(t):
            i = t % 2
            H, SZ, SG = hnT[i], sz[i], sg[i]
            self.norm_T(xt[i][:, :], xt[i], self.A1, self.B1, self.modL, H, 128, 0, 0)
            for g in range(8):
                bk = 1 + g % 4
                for kc in range(8):
                    self.mm(self.pf(bk, 0, 512), H[:, kc, :], Wz[:, kc, g * 512:(g + 1) * 512], kc == 0, kc == 7,
                            [H, Wz], [self.bank[bk]])
                if g < 4:
                    self.act(SZ[:, g * 512:(g + 1) * 512], self.pf(bk, 0, 512), AF.Silu, [self.bank[bk]], [SZ])
                else:
                    self.act(SG[:, (g - 4) * 512:(g - 3) * 512], self.pf(bk, 0, 512), AF.Sigmoid, [self.bank[bk]], [SG])

        def stage_b(t):
            i = t % 2
            SZ, SG, MB, MT_ = sz[i], sg[i], mb16[i], mT[i]
            og, od = ol[0][i], ol[2][i]
            self.tt(og[:, :], ol[0][i][:, :], ol[1][i][:, :], ALU.add, [ol[0][i], ol[1][i]], [og])
            self.tt(od[:, :], ol[2][i][:, :], ol[3][i][:, :], ALU.add, [ol[2][i], ol[3][i]], [od])
            for h in range(4):
                self.act(self.junk[:, 0:256], og[:, h * 256:(h + 1) * 256], AF.Square, [og], [self.junk, ss],
                         accum=ss[:, h:h + 1])
            for h in range(8):
                self.act(self.junk[:, 0:128], od[:, h * 128:(h + 1) * 128], AF.Square, [od], [self.junk, ss],
                         accum=ss[:, 4 + h:5 + h])
            self.act(rs[:, 0:4], ss[:, 0:4], AF.Ln, [ss], [rs], scale=1.0 / 256, bias=self.epsb[:, 0:1])
            self.act(rs[:, 4:12], ss[:, 4:12], AF.Ln, [ss], [rs], scale=1.0 / 128, bias=self.epsb[:, 0:1])
            self.act(rs[:, :], rs[:, :], AF.Exp, [rs], [rs], scale=-0.5)
            self.tt(AP(og, 0, [[DM, 128], [256, 4], [1, 256]]), AP(og, 0, [[DM, 128], [256, 4], [1, 256]]),
                    AP(rs, 0, [[12, 128], [1, 4], [0, 256]]), ALU.mult, [og, rs], [og])
            self.tt(AP(od, 0, [[DM, 128], [128, 8], [1, 128]]), AP(od, 0, [[DM, 128], [128, 8], [1, 128]]),
                    AP(rs, 4, [[12, 128], [1, 8], [0, 128]]), ALU.mult, [od, rs], [od])
            self.tt(og[:, :], og[:, :], gg[:, :], ALU.mult, [og, gg], [og])
            self.tt(od[:, :], od[:, :], gd[:, :], ALU.mult, [od, gd], [od])
            self.tt(og[:, :], og[:, :], SZ[:, 0:1024], ALU.mult, [og, SZ], [og])
            self.tt(od[:, :], od[:, :], SZ[:, 1024:2048], ALU.mult, [od, SZ], [od])
            self.tt(og[:, :], og[:, :], SG[:, 0:1024], ALU.mult, [og, SG], [og])
            self.tt(od[:, :], od[:, :], SG[:, 1024:2048], ALU.mult, [od, SG], [od])
            self.tt(MB[:, :], og[:, :], od[:, :], ALU.add, [og, od], [MB])
            for kc in range(8):
                self.tr(self.pb16(5, kc * 128, 128), MB[:, kc * 128:(kc + 1) * 128], self.identb[:, :],
                        [MB, self.identb], [self.bank[5]])
            self.act(AP(MT_, 0, [[1024, 128], [1, 1024]]), self.pb16(5, 0, 1024), AF.Copy, [self.bank[5]], [MT_])
            for half in range(2):
                bk = 6 + half
                for kc in range(8):
                    self.mm(self.pf(bk, 0, 512), MT_[:, kc, :], Wo[:, kc, half * 512:(half + 1) * 512], kc == 0, kc == 7,
                            [MT_, Wo], [self.bank[bk]])
            self.tt(x1[i][:, :], AP(self.PP[3], 0, [[1024, 128], [1, 1024]]), self.G1, ALU.mult,
                    [self.bank[6], self.bank[7], self.modL], [x1[i]])
            self.tt(x1[i][:, :], x1[i][:, :], xt[i][:, :], ALU.add, [x1[i], xt[i]], [x1[i]])
            self.store(AP(self.dx1, t * 128 * DM, [[DM, 128], [1, DM]]), x1[i][:, :], x1[i], self.dx1)

        issue_loads(0)
        if ntile > 1:
            issue_loads(1)
        stage_a(0)
        for t in range(ntile):
            if t + 1 < ntile:
                stage_a(t + 1)
            stage_b(t)
            if t + 2 < ntile:
                issue_loads(t + 2)
        kb.barrier()
        kb.release(m)

    def phase3b(self):
        kb = self.kb
        m = kb.mark()
        Wg = kb.sb("Wg", [128, 8, FFH], BF16)
        Wu = kb.sb("Wu", [128, 8, FFH], BF16)
        Wd = kb.sb("Wd", [128, 22, DM], BF16)
        for (dst, src) in ((Wg, self.wg), (Wu, self.wu)):
            for k0 in range(0, 8, 2):
                self.kb.dma("pool", AP(dst, k0 * FFH, [[8 * FFH, 128], [FFH, 2], [1, FFH]]),
                            AP(src, k0 * 128 * FFH, [[FFH, 128], [128 * FFH, 2], [1, FFH]]), [src], [dst])
        for k0 in range(0, 22, 2):
            self.kb.dma("pool", AP(Wd, k0 * DM, [[22 * DM, 128], [DM, 2], [1, DM]]),
                        AP(self.wd, k0 * 128 * DM, [[DM, 128], [128 * DM, 2], [1, DM]]), [self.wd], [Wd])
        fn = kb.sb("fnw", [128, DM], F32)
        self.load(fn[:, :], self.fnbc[:, :], self.fnbc, fn)
        xt = [kb.sb("x1t%d" % i, [128, 2, DM], F32) for i in range(1)]
        h2T = kb.sb("h2T", [128, 8, TT], BF16)
        sgl = kb.sb("sgl", [128, TT], F32)
        actT = kb.sb("actT", [128, 22, TT], BF16)
        ty = kb.sb("ty2", [128, 512], F32)
        x2s = [kb.sb("x2_%d" % i, [128, DM], F32) for i in range(2)]
        print("phase3b sbuf remaining", self.nc.sbuf_bytes_remaining)
        ss, rs = self.ssq, self.rsd
        ntile = self.L // TT

        def issue_load(t):
            self.load(xt[0][:, :, :], AP(self.dx1, t * TT * DM, [[DM, 128], [128 * DM, 2], [1, DM]]), self.dx1, xt[0])

        issue_load(0)
        oc = 0
        for t in range(ntile):
            X = xt[0]
            if t > 0:
                issue_load(t)
            for sub in range(2):
                self.norm_T(X[:, sub, :], X, self.A2, self.B2, self.modL2, h2T, TT, sub, 0)
            for hc in range(22):
                bg = 1 + (hc % 2) * 2
                bu = bg + 1
                for kc in range(8):
                    self.mm(self.pf(bg, 0, TT), Wg[:, kc, hc * 128:(hc + 1) * 128], h2T[:, kc, :], kc == 0, kc == 7,
                            [Wg, h2T], [self.bank[bg]])
                for kc in range(8):
                    self.mm(self.pf(bu, 0, TT), Wu[:, kc, hc * 128:(hc + 1) * 128], h2T[:, kc, :], kc == 0, kc == 7,
                            [Wu, h2T], [self.bank[bu]])
                self.act(sgl[:, :], self.pf(bg, 0, TT), AF.Silu, [self.bank[bg]], [sgl])
                self.tt(actT[:, hc, :], sgl[:, :], self.pf(bu, 0, TT), ALU.mult, [sgl, self.bank[bu]], [actT])
            for sub in range(2):
                x2 = x2s[oc % 2]
                for half in range(2):
                    bk = 5 + half
                    for hc in range(22):
                        self.mm(self.pf(bk, 0, 512), actT[:, hc, sub * 128:(sub + 1) * 128],
                                Wd[:, hc, half * 512:(half + 1) * 512], hc == 0, hc == 21, [actT, Wd], [self.bank[bk]])
                    hs = slice(half * 512, (half + 1) * 512)
                    self.tt(ty[:, :], self.pf(bk, 0, 512), AP(self.modL2, 2 * DM + half * 512, [[3 * DM, 128], [1, 512]]),
                            ALU.mult, [self.bank[bk], self.modL2], [ty])
                    self.tt(x2[:, hs], ty[:, :], X[:, sub, hs], ALU.add, [ty, X], [x2])
                self.act(self.junk[:, :], x2[:, :], AF.Square, [x2], [self.junk, ss], accum=ss[:, 2:3])
                self.act(rs[:, 2:3], ss[:, 2:3], AF.Sqrt, [ss], [rs], scale=1.0 / DM, bias=self.epsb[:, 0:1])
                self.recip(rs[:, 3:4], rs[:, 2:3], [rs], [rs])
                O = x2
                oc += 1
                self.stt(O[:, :], x2[:, :], rs[:, 3:4], fn[:, :], ALU.mult, ALU.mult, [x2, rs, fn], [O])
                self.store(AP(self.out, (t * TT + sub * 128) * DM, [[DM, 128], [1, DM]]), O[:, :], O, self.out)
        kb.barrier()
        kb.release(m)

    def build(self, phases="0ASBG3F"):
        kb = self.kb
        self.consts()
        self.epsb = kb.sb("epsb", [128, 1], F32)
        self.oneb = kb.sb("oneb", [128, 1], F32)
        self.memset(self.epsb[:, :], EPS, [self.epsb])
        self.memset(self.oneb[:, :], 1.0, [self.oneb])
        NCH = self.NCH
        self.modL2 = kb.sb("modL2", [128, 3 * DM], F32)
        ma = kb.mark()
        self.modL1 = kb.sb("modL1", [128, 3 * DM], F32)
        mb_ = kb.mark()
        self.modC = kb.sb("modC", [128, 2 * DM], F32)
        self.Egla = kb.sb("Egla", [128, 2 * NCH * 4], F32)
        self.GDa = kb.sb("GDa", [128, NCH * 16], F32)
        self.GDd = kb.sb("GDd", [128, NCH * 16], F32)
        self.GDe = kb.sb("GDe", [128, NCH * 16], F32)
        self.phase0()
        if "A" in phases:
            self.phaseA()
        if "S" in phases:
            self.gla_scan()
        if "B" in phases:
            self.phaseB()
        if "G" in phases:
            self.gdn_scan()
        kb.release(mb_)
        if "3" in phases:
            self.phase3a()
        kb.release(ma)
        if "F" in phases:
            self.phase3b()
        kb.barrier()
        kb.finalize()
        kb.close()
        return self.nc


def host_consts():
    p = np.arange(128)[:, None]
    f = np.arange(128)[None, :]
    le = (p <= f).astype(np.float32)
    ge = (p >= f).astype(np.float32)
    lt = (p < f).astype(np.float32)
    gt = (p > f).astype(np.float32)
    cumU = np.stack([le * (-1.0 / 16.0), ge * (-1.0 / 16.0), le, ge, gt, lt]).astype(np.float32)
    gmask = np.stack([le, ge]).astype(np.float32)
    dmask = np.stack([(1 - le) * NEG, (1 - ge) * NEG, (1 - lt) * NEG, (1 - gt) * NEG]).astype(np.float32)
    sel = np.zeros((96, 32, 128), np.float32)
    for c in range(32):
        for part in range(3):
            sel[part * 32 + c, c, :] = 1.0
    lv = np.zeros((14, 128, 128), np.float32)
    pi = np.arange(128)[:, None]
    fj = np.arange(128)[None, :]
    for s_ in range(7):
        b = 1 << s_
        mlow = ((pi // (2 * b)) == (fj // (2 * b))) & ((pi % (2 * b)) >= b) & ((fj % (2 * b)) < b)
        dg = np.eye(128, dtype=np.float32) if s_ >= 1 else 0.0
        lv[s_] = -mlow.astype(np.float32) - dg
        lv[7 + s_] = -mlow.T.astype(np.float32) - dg
    return dict(identf=np.eye(128, dtype=np.float32), cumU=cumU, gmask=gmask, dmask=dmask,
                sel=sel.reshape(96, 32 * 128), lvlm=lv)


def host_inputs(b, x, c, ctx, c_ctx, w_mod, b_mod, norm1_w, norm2_w, w_in, gla_lr_w, gla_lr_b, gla_norm_w,
                gdn_conv_w, gdn_a_log, gdn_dt_bias, gdn_norm_w, w_out, ffn_w_gate, ffn_w_up, ffn_w_down,
                final_norm_w, shared):
    f = lambda a: np.ascontiguousarray(a, dtype=np.float32)
    d = dict(shared)
    d["x"] = f(x[b])
    d["ctx"] = f(ctx[b])
    d["cT"] = f(np.asarray(c[b]).reshape(8, 128).T)
    return d


def shared_inputs(c_ctx, w_mod, b_mod, norm1_w, norm2_w, w_in, gla_lr_w, gla_lr_b, gla_norm_w,
                  gdn_conv_w, gdn_a_log, gdn_dt_bias, gdn_norm_w, w_out, ffn_w_gate, ffn_w_up, ffn_w_down,
                  final_norm_w):
    f = lambda a: np.ascontiguousarray(a, dtype=np.float32)
    bc = lambda v: f(np.broadcast_to(np.asarray(v).reshape(1, -1), (128, np.asarray(v).size)))
    d = host_consts()
    d["cctxT"] = f(np.asarray(c_ctx).reshape(8, 128).T)
    d["w_mod"] = f(w_mod[0])
    d["b_mod"] = f(np.asarray(b_mod[0]).reshape(1, -1))
    d["n1bc"] = bc(norm1_w[0])
    d["n2bc"] = bc(norm2_w[0])
    d["fnbc"] = bc(final_norm_w)
    d["w_in"] = f(w_in[0])
    lrw = np.zeros((2, 33, 512), np.float32)
    lrw[0, 0:16] = np.asarray(gla_lr_w[0, 0])
    lrw[1, 16:32] = np.asarray(gla_lr_w[0, 1])
    lrw[0, 32] = np.asarray(gla_lr_b[0, 0])
    lrw[1, 32] = np.asarray(gla_lr_b[0, 1])
    d["lrw"] = lrw
    d["glag"] = bc(np.asarray(gla_norm_w[0]).reshape(-1))
    d["gdng"] = bc(np.asarray(gdn_norm_w[0]).reshape(-1))
    cw = np.asarray(gdn_conv_w[0])
    d["cwT"] = f(cw.reshape(5, 24, 128).transpose(2, 1, 0).reshape(128, 120))
    d["alog"] = bc(np.asarray(gdn_a_log[0]).reshape(-1))
    d["dtb"] = bc(np.asarray(gdn_dt_bias[0]).reshape(-1))
    d["w_out"] = f(w_out[0])
    d["wg"] = f(ffn_w_gate[0])
    d["wu"] = f(ffn_w_up[0])
    d["wd"] = f(ffn_w_down[0])
    return d


_CACHE = {}


def kernel(x, c, ctx, c_ctx, w_mod, b_mod, norm1_w, norm2_w, w_in, gla_lr_w, gla_lr_b, gla_norm_w,
           gdn_conv_w, gdn_a_log, gdn_dt_bias, gdn_norm_w, w_out, ffn_w_gate, ffn_w_up, ffn_w_down,
           final_norm_w):
    x = np.asarray(x)
    B, L, _ = x.shape
    W = L // 128
    nc = Prog(W).build()
    shared = shared_inputs(c_ctx, w_mod, b_mod, norm1_w, norm2_w, w_in, gla_lr_w, gla_lr_b, gla_norm_w,
                           gdn_conv_w, gdn_a_log, gdn_dt_bias, gdn_norm_w, w_out, ffn_w_gate, ffn_w_up,
                           ffn_w_down, final_norm_w)
    f = lambda a: np.ascontiguousarray(a, dtype=np.float32)
    in_maps = []
    for b in range(B):
        d = dict(shared)
        d["x"] = f(x[b])
        d["ctx"] = f(np.asarray(ctx)[b])
        d["cT"] = f(np.asarray(c)[b].reshape(8, 128).T)
        in_maps.append(d)
    res = run_bass_kernel_spmd(nc, in_maps, core_ids=list(range(B)))
    return np.stack([np.asarray(r["out"], dtype=np.float32) for r in res.results], axis=0)
```

```python
import numpy as np
import ml_dtypes
import concourse.bass as bass
import concourse.mybir as mybir
from concourse.bass_utils import run_bass_kernel_spmd

F32 = mybir.dt.float32
BF16 = mybir.dt.bfloat16
AF = mybir.ActivationFunctionType
ALU = mybir.AluOpType

SAME_ENG_SYNC = True
NOSYNC_ENGS = ()
EPOCH = 20000
EPS = 1e-6
DM = 1024
CTX = 256
TT = 256
NEG = -30000.0


class Buf:
    __slots__ = ("t", "lw", "rd", "sem", "cnt", "name", "dram", "_scope", "semq")

    def __init__(self, t, name, dram=False):
        self.t = t
        self.name = name
        self.lw = {}
        self.rd = {}
        self.sem = None
        self.cnt = 0
        self.dram = dram
        self._scope = 0
        self.semq = None

    def __getitem__(self, idx):
        return self.t[idx]


class EngState:
    def __init__(self, name):
        self.name = name
        self.ops = []
        self.count = 0
        self.waited = {}
        self.needed = set()


class KB:
    def __init__(self, nc):
        self.nc = nc
        self.eng = {n: EngState(n) for n in ("pe", "act", "dve", "pool", "sp")}
        self._ctx = []
        self.sems = {}
        self.sempool = []
        self.sembufs = []
        self._sem_ctx = []

    def mark(self):
        return len(self._ctx)

    def release(self, m):
        for b in self.sembufs:
            if b.sem is not None and (not b.dram) and b._scope >= m:
                self.sempool.append((b.sem, b.cnt, b.semq))
                b.sem = None
        self.sembufs = [b for b in self.sembufs if b.sem is not None]
        while len(self._ctx) > m:
            cm = self._ctx.pop()
            cm.__exit__(None, None, None)

    def sb(self, name, shape, dt=F32):
        cm = self.nc.sbuf_tensor("sb_" + name, list(shape), dt)
        t = cm.__enter__()
        b = Buf(t, name)
        b._scope = len(self._ctx)
        self._ctx.append(cm)
        return b

    def ps(self, name, shape, dt=F32):
        cm = self.nc.psum_tensor(name, list(shape), dt)
        t = cm.__enter__()
        self._ctx.append(cm)
        return t

    def dram(self, name, shape, dt=F32, kind="Internal"):
        t = self.nc.dram_tensor(name, list(shape), dt, kind=kind)
        return Buf(t, name, dram=True)

    def _get_sem(self, b, q):
        if b.sem is not None:
            assert b.semq == q, "buffer %s used by both DMA queue kinds" % b.name
        if b.sem is None:
            b.semq = q
            cand = [i for i, e in enumerate(self.sempool) if e[2] == q]
            if cand:
                b.sem, b.cnt, _ = self.sempool.pop(cand[-1])
            else:
                cm = self.nc.semaphore("s%d" % len(self.sems))
                b.sem = cm.__enter__()
                self._sem_ctx.append(cm)
                b.cnt = 0
                self.sems[id(b.sem)] = b.sem
            self.sembufs.append(b)

    def _waits(self, E, reads, writes, extra=None):
        deps = {}

        def add(d):
            for k, v in d.items():
                if deps.get(k, -1) < v:
                    deps[k] = v

        for b in reads:
            add(b.lw)
        for b in writes:
            add(b.lw)
            add(b.rd)
        if extra:
            add(extra)
        for k, v in deps.items():
            if k[0] == "E" and k[1] == E.name:
                if E.name in ("pe", "sp") or not SAME_ENG_SYNC or E.name in NOSYNC_ENGS:
                    continue
            if E.waited.get(k, -1) >= v:
                continue
            E.waited[k] = v
            if k[0] == "E":
                self.eng[k[1]].needed.add(v)
            E.ops.append(("w", k, v))

    def op(self, eng, fn, reads=(), writes=()):
        E = self.eng[eng]
        self._waits(E, reads, writes)
        idx = E.count
        E.count += 1
        E.ops.append(("o", fn, idx))
        key = ("E", eng)
        for b in writes:
            if b.dram:
                b.lw[key] = idx
            else:
                b.lw = {key: idx}
                b.rd = {}
        for b in reads:
            b.rd[key] = idx

    def dma(self, q, out, in_, reads, writes):
        E = self.eng[q]
        self._waits(E, reads, writes)
        cand = [b for b in list(writes) + list(reads) if not b.dram]
        sb = cand[0]
        self._get_sem(sb, q)
        sb.cnt += 16
        c = sb.cnt
        key = ("S", id(sb.sem))
        E.ops.append(("d", out, in_, sb.sem))
        for b in writes:
            if b.dram:
                b.lw[key] = c
            else:
                b.lw = {key: c}
                b.rd = {}
        for b in reads:
            b.rd[key] = c

    def barrier(self):
        ev = {}
        for n in ("pe", "act", "dve", "pool"):
            if self.eng[n].count > 0:
                ev[("E", n)] = self.eng[n].count - 1
        for b in self.sembufs:
            if b.sem is not None and b.cnt > 0:
                ev[("S", id(b.sem))] = b.cnt
        for n, E in self.eng.items():
            deps = dict(ev)
            for k, v in deps.items():
                if k[0] == "E" and k[1] == n:
                    continue
                if E.waited.get(k, -1) >= v:
                    continue
                E.waited[k] = v
                if k[0] == "E":
                    self.eng[k[1]].needed.add(v)
                E.ops.append(("w", k, v))

    def finalize(self):
        nc = self.nc
        engsems = {}
        rank = {}
        for name, E in self.eng.items():
            nd = sorted(E.needed)
            rank[name] = {idx: r for r, idx in enumerate(nd)}
            nsem = (len(nd) + EPOCH - 1) // EPOCH
            lst = []
            for i in range(nsem):
                cm = nc.semaphore("e_%s_%d" % (name, i))
                lst.append(cm.__enter__())
                self._sem_ctx.append(cm)
            engsems[name] = lst
        engobj = {"pe": nc.tensor, "act": nc.scalar, "dve": nc.vector, "pool": nc.gpsimd, "sp": nc.sync}

        def replay(E, e):
            for o in E.ops:
                if o[0] == "w":
                    k, v = o[1], o[2]
                    if k[0] == "E":
                        r = rank[k[1]][v]
                        e.wait_ge(engsems[k[1]][r // EPOCH], r % EPOCH + 1)
                    else:
                        e.wait_ge(self.sems[k[1]], v)
                elif o[0] == "o":
                    inst = o[1](e)
                    r = rank[E.name].get(o[2])
                    if r is not None:
                        inst.then_inc(engsems[E.name][r // EPOCH], 1)
                else:
                    e.dma_start(out=o[1], in_=o[2]).then_inc(o[3], 16)

        with nc.Block() as block:
            @block.tensor
            def _(e):
                replay(self.eng["pe"], e)

            @block.scalar
            def _(e):
                replay(self.eng["act"], e)

            @block.vector
            def _(e):
                replay(self.eng["dve"], e)

            @block.gpsimd
            def _(e):
                replay(self.eng["pool"], e)

            @block.sync
            def _(e):
                replay(self.eng["sp"], e)

    def close(self):
        while self._ctx:
            self._ctx.pop().__exit__(None, None, None)
        while self._sem_ctx:
            self._sem_ctx.pop().__exit__(None, None, None)


def AP(buf, offset, pairs):
    return bass.AP(buf.t if isinstance(buf, Buf) else buf, offset, [list(p) for p in pairs])


SPL = dict(gq=0, gk=512, gv=1024, gz=2048, lr=3072, dq=3104, dk=4128, dv=5152, dz=6176,
           ab=7200, g11=7232, g12=8256)
PIN = 9280
FFH = 2816


class Prog:
    def __init__(self, W, dbg=()):
        self.W = W
        self.L = 128 * W
        self.NT = CTX + self.L
        self.NCH = self.NT // 128
        self.dbg = set(dbg)
        nc = bass.Bass("TRN2", target_bir_lowering=False)
        self.nc = nc
        self.kb = KB(nc)
        kb = self.kb
        L, NT = self.L, self.NT
        ein = lambda n, s: kb.dram(n, s, F32, kind="ExternalInput")
        self.x = ein("x", [L, DM])
        self.ctx = ein("ctx", [CTX, DM])
        self.cT = ein("cT", [128, 8])
        self.cctxT = ein("cctxT", [128, 8])
        self.w_mod = ein("w_mod", [DM, 6 * DM])
        self.b_mod = ein("b_mod", [1, 6 * DM])
        self.n1bc = ein("n1bc", [128, DM])
        self.n2bc = ein("n2bc", [128, DM])
        self.fnbc = ein("fnbc", [128, DM])
        self.w_in = ein("w_in", [DM, PIN])
        self.lrw = ein("lrw", [2, 33, 512])
        self.glag = ein("glag", [128, DM])
        self.gdng = ein("gdng", [128, DM])
        self.cwT = ein("cwT", [128, 120])
        self.alog = ein("alog", [128, 16])
        self.dtb = ein("dtb", [128, 16])
        self.w_out = ein("w_out", [DM, DM])
        self.wg = ein("wg", [DM, FFH])
        self.wu = ein("wu", [DM, FFH])
        self.wd = ein("wd", [FFH, DM])
        self.identf_d = ein("identf", [128, 128])
        self.cumU = ein("cumU", [6, 128, 128])
        self.gmask = ein("gmask", [2, 128, 128])
        self.dmask = ein("dmask", [4, 128, 128])
        self.sel_d = ein("sel", [96, 32 * 128])
        self.lvlm = ein("lvlm", [14, 128, 128])
        self.out = kb.dram("out", [L, DM], F32, kind="ExternalOutput")

        def scr(n, s, dt):
            return kb.dram(n, s, dt, kind="ExternalOutput" if (n in self.dbg or n.startswith("dbg")) else "Internal")
        self.gq = [scr("gq%d" % d, [4, 128, NT], BF16) for d in range(2)]
        self.gk = [scr("gk%d" % d, [4, 128, NT], BF16) for d in range(2)]
        self.gkh = [scr("gkh%d" % d, [NT, 512], BF16) for d in range(2)]
        self.gv = scr("gv", [NT, DM], BF16)
        self.go = [scr("go%d" % d, [NT, DM], F32) for d in range(2)]
        self.dqT = scr("dqT", [8, 128, NT], BF16)
        self.dkt = scr("dkt", [NT, DM], BF16)
        self.du = [scr("du%d" % d, [NT, DM], F32) for d in range(2)]
        self.dw = [scr("dw%d" % d, [8, 128, NT], BF16) for d in range(2)]
        self.daq = [scr("daq%d" % d, [NT, DM], BF16) for d in range(2)]
        self.do = [scr("do%d" % d, [NT, DM], F32) for d in range(2)]
        self.dx1 = scr("dx1", [L, DM], F32)
        if "dbgX" in self.dbg:
            self.dbgX = scr("dbgX", [128, DM], BF16)
            self.dbgXT = scr("dbgXT", [128, DM], BF16)
            self.dbgMT = scr("dbgMT", [128, DM], BF16)
            self.dbg.update(["dbgXT", "dbgMT"])

        self.PP = [kb.ps("pp%d" % i, [128, 1024], F32) for i in range(4)]
        self.PPb = [p.bitcast(BF16) for p in self.PP]
        self.bank = [Buf(self.PP[i // 2], "bank%d" % i) for i in range(8)]

    def pf(self, b, c0, n, p=128):
        return AP(self.PP[b // 2], (b % 2) * 512 + c0, [[1024, p], [1, n]])

    def pb16(self, b, c0, n, p=128):
        return AP(self.PPb[b // 2], (b % 2) * 1024 + c0, [[2048, p], [1, n]])

    def mm(self, out, lhsT, rhs, start, stop, reads, writes):
        self.kb.op("pe", lambda e: e.matmul(out, lhsT=lhsT, rhs=rhs, start=start, stop=stop), reads, writes)

    def tr(self, out, in_, ident, reads, writes):
        self.kb.op("pe", lambda e: e.transpose(out=out, in_=in_, identity=ident), reads, writes)

    def act(self, out, in_, func, reads, writes, scale=None, bias=None, accum=None):
        kw = {}
        if scale is not None:
            kw["scale"] = scale
        if bias is not None:
            kw["bias"] = bias
        if accum is not None:
            kw["accum_out"] = accum
        self.kb.op("act", lambda e: e.activation(out=out, in_=in_, func=func, **kw), reads, writes)

    def tt(self, out, in0, in1, op, reads, writes):
        self.kb.op("dve", lambda e: e.tensor_tensor(out=out, in0=in0, in1=in1, op=op), reads, writes)

    def ptt(self, out, in0, in1, op, reads, writes):
        self.kb.op("pool", lambda e: e.tensor_tensor(out=out, in0=in0, in1=in1, op=op), reads, writes)

    def stt(self, out, in0, scalar, in1, op0, op1, reads, writes):
        self.kb.op("dve", lambda e: e.scalar_tensor_tensor(out=out, in0=in0, scalar=scalar, in1=in1,
                                                           op0=op0, op1=op1), reads, writes)

    def ts(self, out, in0, s1, s2, op0, op1, reads, writes):
        if s2 is None:
            self.kb.op("dve", lambda e: e.tensor_scalar(out=out, in0=in0, scalar1=s1, scalar2=None, op0=op0),
                       reads, writes)
        else:
            self.kb.op("dve", lambda e: e.tensor_scalar(out=out, in0=in0, scalar1=s1, scalar2=s2, op0=op0, op1=op1),
                       reads, writes)

    def cp(self, out, in_, reads, writes):
        self.kb.op("dve", lambda e: e.tensor_copy(out=out, in_=in_), reads, writes)

    def recip(self, out, in_, reads, writes):
        self.kb.op("dve", lambda e: e.reciprocal(out=out, in_=in_), reads, writes)

    def memset(self, ap, val, writes):
        self.kb.op("dve", lambda e: e.memset(ap, val), [], writes)

    def load(self, out, in_, src, dst, q="sp"):
        self.kb.dma(q, out, in_, [src], [dst])

    def store(self, out, in_, src, dst, q="pool"):
        self.kb.dma(q, out, in_, [src], [dst])

    def wload(self, dst, src, col0, ncols, nk=8, rowlen=None):
        rl = rowlen
        step = 4
        for k0 in range(0, nk, step):
            kn = min(step, nk - k0)
            self.kb.dma("pool", AP(dst, k0 * ncols, [[nk * ncols, 128], [ncols, kn], [1, ncols]]),
                        AP(src, k0 * 128 * rl + col0, [[rl, 128], [128 * rl, kn], [1, ncols]]), [src], [dst])

    def consts(self):
        kb = self.kb
        self.identf = kb.sb("identf", [128, 128], F32)
        self.identb = kb.sb("identb", [128, 128], BF16)
        self.load(self.identf[:, :], self.identf_d[:, :], self.identf_d, self.identf)
        self.kb.dma("pool", self.identb[:, :], self.identf_d[:, :], [self.identf_d], [self.identb])
        self.onesf = kb.sb("onesf", [128, 128], F32)
        self.onesb = kb.sb("onesb", [128, 128], BF16)
        self.memset(self.onesf[:, :], 1.0, [self.onesf])
        self.memset(self.onesb[:, :], 1.0, [self.onesb])
        self.junk = kb.sb("junk", [128, DM], BF16)
        self.ssq = kb.sb("ssq", [128, 16], F32)
        self.rsd = kb.sb("rsd", [128, 16], F32)
        self.ntmp = kb.sb("ntmp", [128, DM], F32)
        self.hnb = kb.sb("hnb", [128, DM], BF16)

    def phase0(self):
        kb = self.kb
        m = kb.mark()
        cT = kb.sb("cTs", [128, 8], F32)
        ccT = kb.sb("ccTs", [128, 8], F32)
        bm = kb.sb("bm", [1, 6 * DM], F32)
        n1 = kb.sb("n1", [128, DM], F32)
        n2 = kb.sb("n2", [128, DM], F32)
        SL = kb.sb("SL", [128, 8, 128], F32)
        SC = kb.sb("SC", [128, 8, 128], F32)
        wm = [kb.sb("wm%d" % i, [128, 8, 512], F32) for i in range(2)]
        self.load(cT[:, :], self.cT[:, :], self.cT, cT)
        self.load(ccT[:, :], self.cctxT[:, :], self.cctxT, ccT)
        self.load(bm[:, :], self.b_mod[:, :], self.b_mod, bm)
        self.load(n1[:, :], self.n1bc[:, :], self.n1bc, n1)
        self.load(n2[:, :], self.n2bc[:, :], self.n2bc, n2)
        for kc in range(8):
            self.act(SL[:, kc, :], self.onesf[:, :], AF.Silu, [self.onesf, cT], [SL], scale=cT[:, kc:kc + 1])
            self.act(SC[:, kc, :], self.onesf[:, :], AF.Silu, [self.onesf, ccT], [SC], scale=ccT[:, kc:kc + 1])
        for g in range(12):
            w = wm[g % 2]
            for k0 in (0, 4):
                self.load(AP(w, k0 * 512, [[8 * 512, 128], [512, 4], [1, 512]]),
                          AP(self.w_mod, k0 * 128 * 6 * DM + g * 512, [[6 * DM, 128], [128 * 6 * DM, 4], [1, 512]]),
                          self.w_mod, w)
            variants = [(SL, self.modL1 if g < 6 else self.modL2, 0, (g % 6) * 512)]
            if g < 4:
                variants.append((SC, self.modC, 1, g * 512))
            for (S_, dst, bi, dc0) in variants:
                bk = (g * 2 + bi) % 8
                for kc in range(8):
                    self.mm(self.pf(bk, 0, 512), S_[:, kc, :], w[:, kc, :], kc == 0, False, [S_, w], [self.bank[bk]])
                self.mm(self.pf(bk, 0, 512), self.onesf[0:1, :], bm[0:1, g * 512:(g + 1) * 512], False, True,
                        [self.onesf, bm], [self.bank[bk]])
                self.act(dst[:, dc0:dc0 + 512], self.pf(bk, 0, 512), AF.Copy, [self.bank[bk]], [dst])
        for (dst, nw, c0) in ((self.modL1, n1, DM), (self.modL2, n2, DM), (self.modC, n1, DM)):
            self.stt(dst[:, c0:c0 + DM], dst[:, c0:c0 + DM], 1.0, nw[:, :], ALU.add, ALU.mult, [dst, nw], [dst])
        kb.barrier()
        kb.release(m)
        ML, ML2, MC = self.modL1, self.modL2, self.modC
        self.modL = ML
        self.B1, self.A1, self.G1 = ML[:, 0:DM], ML[:, DM:2 * DM], ML[:, 2 * DM:3 * DM]
        self.B2, self.A2, self.G2 = ML2[:, 0:DM], ML2[:, DM:2 * DM], ML2[:, 2 * DM:3 * DM]
        self.B1c, self.A1c = MC[:, 0:DM], MC[:, DM:2 * DM]

    def norm_T(self, xt, xbuf, A, B, mbuf, hnT, Tn, sub, bk):
        ss, rs = self.ssq, self.rsd
        self.act(self.junk[:, :], xt, AF.Square, [xbuf], [self.junk, ss], accum=ss[:, 0:1])
        self.act(rs[:, 0:1], ss[:, 0:1], AF.Ln, [ss], [rs], scale=1.0 / DM, bias=self.epsb[:, 0:1])
        self.act(rs[:, 1:2], rs[:, 0:1], AF.Exp, [rs], [rs], scale=-0.5)
        self.stt(self.ntmp[:, :], xt, rs[:, 1:2], A, ALU.mult, ALU.mult, [xbuf, rs, mbuf], [self.ntmp])
        self.tt(self.hnb[:, :], self.ntmp[:, :], B, ALU.add, [self.ntmp, mbuf], [self.hnb])
        for kc in range(8):
            self.tr(self.pb16(bk, kc * 128, 128), self.hnb[:, kc * 128:(kc + 1) * 128], self.identb[:, :],
                    [self.hnb, self.identb], [self.bank[bk]])
        self.act(AP(hnT, sub * 128, [[8 * Tn, 128], [Tn, 8], [1, 128]]),
                 AP(self.PPb[bk // 2], (bk % 2) * 1024, [[2048, 128], [128, 8], [1, 128]]), AF.Copy,
                 [self.bank[bk]], [hnT])

    def tiles(self, colmajor):
        res = [(0, 0, True, None)]
        for i in range(self.L // TT):
            res.append((i + 1, CTX + i * TT, False, i))
        return res

    def x_src(self, is_ctx, li, colmajor):
        if is_ctx:
            return AP(self.ctx, 0, [[DM, 128], [128 * DM, 2], [1, DM]]), self.ctx
        if not colmajor:
            return AP(self.x, li * TT * DM, [[DM, 128], [128 * DM, 2], [1, DM]]), self.x
        c0 = li * 2
        return AP(self.x, c0 * DM, [[self.W * DM, 128], [DM, 2], [1, DM]]), self.x

    def phaseA(self):
        kb = self.kb
        m = kb.mark()
        NT = self.NT
        Wqk = kb.sb("Wqk", [128, 8, 1024], BF16)
        Wv = kb.sb("Wv", [128, 8, 1024], BF16)
        Wlr = kb.sb("Wlr", [128, 8, 32], BF16)
        self.wload(Wqk, self.w_in, SPL["gq"], 1024, rowlen=PIN)
        self.wload(Wv, self.w_in, SPL["gv"], 1024, rowlen=PIN)
        self.wload(Wlr, self.w_in, SPL["lr"], 32, rowlen=PIN)
        LRW = [kb.sb("LRW%d" % d, [33, 512], F32) for d in range(2)]
        U = [kb.sb("U%d" % d, [128, 128], F32) for d in range(2)]
        for d in range(2):
            self.load(LRW[d][:, :], AP(self.lrw, d * 33 * 512, [[512, 33], [1, 512]]), self.lrw, LRW[d])
            self.load(U[d][:, :], AP(self.cumU, d * 128 * 128, [[128, 128], [1, 128]]), self.cumU, U[d])
        xt = [kb.sb("xtA%d" % i, [128, 2, DM], F32) for i in range(2)]
        hnT = [kb.sb("hnTA%d" % i, [128, 8, TT], BF16) for i in range(2)]
        qkTs = [kb.sb("qkT%d" % i, [128, 8, TT], F32) for i in range(2)]
        vtok = [kb.sb("vtok%d" % i, [128, 2, DM], BF16) for i in range(2)]
        lraug = kb.sb("lraug", [33, TT], F32)
        self.memset(lraug[32:33, :], 1.0, [lraug])
        e1s = [kb.sb("e1_%d" % i, [128, 512], F32) for i in range(2)]
        sps = [kb.sb("sp_%d" % i, [128, 512], F32) for i in range(2)]
        eGs = [kb.sb("eG_%d" % i, [128, 512], F32) for i in range(2)]
        enGs = [kb.sb("enG_%d" % i, [128, 512], F32) for i in range(2)]
        dKs = [kb.sb("dK_%d" % i, [128, 512], F32) for i in range(2)]
        gends = [kb.sb("gend_%d" % i, [128, 4], F32) for i in range(2)]
        khTs = [kb.sb("khT_%d" % i, [128, 512], BF16) for i in range(2)]
        qs = [kb.sb("qs%d" % d, [128, 4, TT], BF16) for d in range(2)]
        ks = [kb.sb("ks%d" % d, [128, 4, TT], BF16) for d in range(2)]
        khs = [kb.sb("khs%d" % d, [128, 2, 512], BF16) for d in range(2)]
        print("phaseA sbuf remaining", self.nc.sbuf_bytes_remaining)
        tl = self.tiles(False)
        src, sbuf = self.x_src(tl[0][2], tl[0][3], False)
        self.load(xt[0][:, :, :], src, sbuf, xt[0])
        def do_norm(tj):
            is_c = tl[tj][2]
            A, B, mb = (self.A1c, self.B1c, self.modC) if is_c else (self.A1, self.B1, self.modL)
            for sub in range(2):
                self.norm_T(xt[tj % 2][:, sub, :], xt[tj % 2], A, B, mb, hnT[tj % 2], TT, sub, 0)

        if len(tl) > 1:
            src, sbuf = self.x_src(tl[1][2], tl[1][3], False)
            self.load(xt[1][:, :, :], src, sbuf, xt[1])
        do_norm(0)
        for ti, (idx, tok0, is_ctx, li) in enumerate(tl):
            X = xt[ti % 2]
            H = hnT[ti % 2]
            qkT = qkTs[ti % 2]
            for fc in range(8):
                bk = 1 + fc % 2
                for kc in range(8):
                    self.mm(self.pf(bk, 0, TT), Wqk[:, kc, fc * 128:(fc + 1) * 128], H[:, kc, :], kc == 0, kc == 7,
                            [Wqk, H], [self.bank[bk]])
                self.act(qkT[:, fc, :], self.pf(bk, 0, TT), AF.Copy, [self.bank[bk]], [qkT],
                         scale=(128.0 ** -0.5 if fc < 4 else 1.0))
            VT = vtok[ti % 2]
            for sub in range(2):
                for half in range(2):
                    bk = 3 + half
                    for kc in range(8):
                        self.mm(self.pf(bk, 0, 512), H[:, kc, sub * 128:(sub + 1) * 128],
                                Wv[:, kc, half * 512:(half + 1) * 512], kc == 0, kc == 7, [H, Wv], [self.bank[bk]])
                    self.cp(VT[:, sub, half * 512:(half + 1) * 512], self.pf(bk, 0, 512), [self.bank[bk]], [VT])
            self.store(AP(self.gv, tok0 * DM, [[DM, 128], [128 * DM, 2], [1, DM]]), VT[:, :, :], VT, self.gv)
            for kc in range(8):
                self.mm(self.pf(5, 0, TT, 32), Wlr[:, kc, :], H[:, kc, :], kc == 0, kc == 7, [Wlr, H], [self.bank[5]])
            self.act(lraug[0:32, :], self.pf(5, 0, TT, 32), AF.Copy, [self.bank[5]], [lraug])
            if ti + 1 < len(tl):
                do_norm(ti + 1)
                if ti + 2 < len(tl):
                    src, sbuf = self.x_src(tl[ti + 2][2], tl[ti + 2][3], False)
                    self.load(xt[ti % 2][:, :, :], src, sbuf, xt[ti % 2])
            for sub in range(2):
                ch = (tok0 // 128) + sub
                for d in range(2):
                    e1, sp, eG, enG, dK, gend, khT = (b[d] for b in (e1s, sps, eGs, enGs, dKs, gends, khTs))
                    self.mm(self.pf(6, 0, 512), lraug[0:33, sub * 128:(sub + 1) * 128], LRW[d][:, :], True, True,
                            [lraug, LRW[d]], [self.bank[6]])
                    self.act(e1[:, :], self.pf(6, 0, 512), AF.Exp, [self.bank[6]], [e1], scale=-1.0)
                    self.act(sp[:, :], e1[:, :], AF.Ln, [e1], [sp], bias=self.oneb[:, 0:1])
                    for h in range(4):
                        self.mm(self.pf(7, h * 128, 128), sp[:, h * 128:(h + 1) * 128], U[d][:, :], True, True,
                                [sp, U[d]], [self.bank[7]])
                    G = self.pf(7, 0, 512)
                    ecol = 127 if d == 0 else 0
                    self.act(eG[:, :], G, AF.Exp, [self.bank[7]], [eG])
                    self.act(enG[:, :], G, AF.Exp, [self.bank[7]], [enG], scale=-1.0)
                    self.act(gend[:, :], AP(self.PP[3], 512 + ecol, [[1024, 128], [128, 4]]), AF.Copy,
                             [self.bank[7]], [gend])
                    for h in range(4):
                        self.act(dK[:, h * 128:(h + 1) * 128], self.pf(7, h * 128, 128), AF.Exp,
                                 [self.bank[7], gend], [dK], scale=-1.0, bias=gend[:, h:h + 1])
                    self.cp(AP(self.Egla, (d * self.NCH + ch) * 4, [[2 * self.NCH * 4, 128], [1, 4]]),
                            AP(eG, ecol, [[512, 128], [128, 4]]), [eG], [self.Egla])
                    qv = AP(qkT, sub * 128, [[8 * TT, 128], [TT, 4], [1, 128]])
                    kv = AP(qkT, 4 * TT + sub * 128, [[8 * TT, 128], [TT, 4], [1, 128]])
                    g3 = lambda t: AP(t, 0, [[512, 128], [128, 4], [1, 128]])
                    self.tt(AP(qs[d], sub * 128, [[4 * TT, 128], [TT, 4], [1, 128]]), qv, g3(eG), ALU.mult,
                            [qkT, eG], [qs[d]])
                    self.tt(AP(ks[d], sub * 128, [[4 * TT, 128], [TT, 4], [1, 128]]), kv, g3(enG), ALU.mult,
                            [qkT, enG], [ks[d]])
                    self.tt(g3(khT), kv, g3(dK), ALU.mult, [qkT, dK], [khT])
                    for h in range(4):
                        self.tr(self.pb16(5, h * 128, 128), khT[:, h * 128:(h + 1) * 128], self.identb[:, :],
                                [khT, self.identb], [self.bank[5]])
                    self.cp(khs[d][:, sub, :], self.pb16(5, 0, 512), [self.bank[5]], [khs[d]])
            for d in range(2):
                self.store(AP(self.gq[d], tok0, [[NT, 128], [128 * NT, 4], [1, TT]]), qs[d][:, :, :], qs[d], self.gq[d])
                self.store(AP(self.gk[d], tok0, [[NT, 128], [128 * NT, 4], [1, TT]]), ks[d][:, :, :], ks[d], self.gk[d])
                self.store(AP(self.gkh[d], tok0 * 512, [[512, 128], [128 * 512, 2], [1, 512]]), khs[d][:, :, :],
                           khs[d], self.gkh[d])
        kb.barrier()
        kb.release(m)

    def gla_scan(self):
        kb = self.kb
        m = kb.mark()
        NT, NCH = self.NT, self.NCH
        S = [kb.sb("Sg%d" % d, [128, 4, 256], F32) for d in range(2)]
        Sb = [kb.sb("Sgb%d" % d, [128, 4, 256], BF16) for d in range(2)]
        msk = [kb.sb("gm%d" % d, [128, 128], F32) for d in range(2)]
        for d in range(2):
            self.memset(S[d][:, :, :], 0.0, [S[d]])
            self.memset(Sb[d][:, :, :], 0.0, [Sb[d]])
            self.load(msk[d][:, :], AP(self.gmask, d * 128 * 128, [[128, 128], [1, 128]]), self.gmask, msk[d])
        nb = 2
        qt = [[kb.sb("sq%d_%d" % (d, i), [128, 4, 128], BF16) for i in range(nb)] for d in range(2)]
        kt = [[kb.sb("sk%d_%d" % (d, i), [128, 4, 128], BF16) for i in range(nb)] for d in range(2)]
        kh = [[kb.sb("skh%d_%d" % (d, i), [128, 512], BF16) for i in range(nb)] for d in range(2)]
        vv = [[kb.sb("sv%d_%d" % (d, i), [128, DM], BF16) for i in range(nb)] for d in range(2)]
        AT = [kb.sb("AT%d" % d, [128, 4, 128], BF16) for d in range(2)]
        osb = [kb.sb("osb%d" % d, [128, DM], F32) for d in range(2)]
        order = [list(range(NCH)), [1, 0] + list(range(NCH - 1, 1, -1))]

        def issue_loads(s):
            for d in range(2):
                c = order[d][s]
                t0 = c * 128
                i = s % nb
                self.load(qt[d][i][:, :, :], AP(self.gq[d], t0, [[NT, 128], [128 * NT, 4], [1, 128]]), self.gq[d], qt[d][i])
                self.load(kt[d][i][:, :, :], AP(self.gk[d], t0, [[NT, 128], [128 * NT, 4], [1, 128]]), self.gk[d], kt[d][i])
                self.load(kh[d][i][:, :], AP(self.gkh[d], t0 * 512, [[512, 128], [1, 512]]), self.gkh[d], kh[d][i])
                self.load(vv[d][i][:, :], AP(self.gv, t0 * DM, [[DM, 128], [1, DM]]), self.gv, vv[d][i])

        issue_loads(0)
        for s in range(NCH):
            if s + 1 < NCH:
                issue_loads(s + 1)
            for d in range(2):
                c = order[d][s]
                i = s % nb
                Q, K_, KH, V = qt[d][i], kt[d][i], kh[d][i], vv[d][i]
                bA = d
                bO = 2 + 2 * d
                bS = 6
                for h in range(4):
                    self.mm(self.pf(bA, h * 128, 128), K_[:, h, :], Q[:, h, :], True, True, [K_, Q], [self.bank[bA]])
                self.tt(AT[d][:, :, :], AP(self.PP[bA // 2], (bA % 2) * 512, [[1024, 128], [128, 4], [1, 128]]),
                        AP(msk[d], 0, [[128, 128], [0, 4], [1, 128]]), ALU.mult, [self.bank[bA], msk[d]], [AT[d]])
                for h in range(4):
                    bk = bO + h // 2
                    o_ap = self.pf(bk, (h % 2) * 256, 256)
                    self.mm(o_ap, AT[d][:, h, :], V[:, h * 256:(h + 1) * 256], True, False, [AT[d], V], [self.bank[bk]])
                    self.mm(o_ap, Q[:, h, :], Sb[d][:, h, :], False, True, [Q, Sb[d]], [self.bank[bk]])
                self.act(osb[d][:, :], AP(self.PP[bO // 2], 0, [[1024, 128], [1, 1024]]), AF.Copy,
                         [self.bank[bO], self.bank[bO + 1]], [osb[d]])
                self.store(AP(self.go[d], c * 128 * DM, [[DM, 128], [1, DM]]), osb[d][:, :], osb[d], self.go[d])
                for h in range(4):
                    bk = bS + h // 2
                    self.mm(self.pf(bk, (h % 2) * 256, 256), KH[:, h * 128:(h + 1) * 128], V[:, h * 256:(h + 1) * 256],
                            True, True, [KH, V], [self.bank[bk]])
                for h in range(4):
                    bk = bS + h // 2
                    self.stt(S[d][:, h, :], S[d][:, h, :],
                             AP(self.Egla, (d * NCH + c) * 4 + h, [[2 * NCH * 4, 128], [1, 1]]),
                             self.pf(bk, (h % 2) * 256, 256), ALU.mult, ALU.add,
                             [S[d], self.Egla, self.bank[bk]], [S[d]])
                self.act(Sb[d][:, :, :], S[d][:, :, :], AF.Copy, [S[d]], [Sb[d]])
        kb.barrier()
        kb.release(m)

    def phaseB(self):
        kb = self.kb
        m = kb.mark()
        NT, NCH = self.NT, self.NCH
        Wgd = kb.sb("Wgd", [128, 8, 3072], BF16)
        Wab = kb.sb("Wab", [128, 8, 32], BF16)
        self.wload(Wgd, self.w_in, SPL["dq"], 3072, rowlen=PIN)
        self.wload(Wab, self.w_in, SPL["ab"], 32, rowlen=PIN)
        cw = kb.sb("cw", [128, 24, 5], F32)
        self.load(AP(cw, 0, [[120, 128], [1, 120]]), self.cwT[:, :], self.cwT, cw)
        negA = kb.sb("negA", [128, 16], F32)
        dtb = kb.sb("dtbs", [128, 16], F32)
        self.load(negA[:, :], self.alog[:, :], self.alog, negA)
        self.load(dtb[:, :], self.dtb[:, :], self.dtb, dtb)
        self.act(negA[:, :], negA[:, :], AF.Exp, [negA], [negA])
        self.kb.op("act", lambda e: e.mul(negA[:, :], negA[:, :], -1.0), [negA], [negA])
        CU = [kb.sb("CU%d" % i, [128, 128], F32) for i in range(4)]
        for i in range(4):
            self.load(CU[i][:, :], AP(self.cumU, (2 + i) * 128 * 128, [[128, 128], [1, 128]]), self.cumU, CU[i])
        UFi, UBi, UBs, UFs = CU
        MK = [kb.sb("MK%d" % i, [128, 128], BF16) for i in range(4)]
        for i in range(4):
            self.kb.dma("pool", MK[i][:, :], AP(self.dmask, i * 128 * 128, [[128, 128], [1, 128]]), [self.dmask], [MK[i]])
        Minc = [MK[0], MK[1]]
        Mlt, Mgt = MK[2], MK[3]
        sel = kb.sb("sel", [96, 32, 128], BF16)
        self.kb.dma("pool", AP(sel, 0, [[4096, 96], [1, 4096]]), self.sel_d[:, :], [self.sel_d], [sel])
        xt = [kb.sb("xtB%d" % i, [128, 2, DM], F32) for i in range(1)] * 2
        hnT = kb.sb("hnTB", [128, 8, TT], BF16)
        acc = [kb.sb("acc%d" % i, [128, TT], F32) for i in range(4)]
        sg = [kb.sb("sgb%d" % i, [128, TT], F32) for i in range(2)]
        sil = [kb.sb("sil%d" % i, [128, TT], F32) for i in range(4)]
        sq = [kb.sb("sqb%d" % i, [128, TT], BF16) for i in range(2)]
        rn = [kb.sb("rn%d" % i, [128, TT], F32) for i in range(2)]
        vTb = [kb.sb("vTb%d" % i, [128, TT], BF16) for i in range(2)]
        dqs = kb.sb("dqs", [128, 8, TT], BF16)
        dks = kb.sb("dks", [128, 8, TT], BF16)
        vtk = kb.sb("vtk", [128, 2, DM], BF16)
        ktk = kb.sb("ktk", [128, 2, DM], BF16)
        g16 = kb.sb("g16", [128, 16], F32)
        t16 = kb.sb("t16", [128, 16], F32)
        beta = kb.sb("beta", [128, 16], F32)
        lnb = kb.sb("lnb", [128, 16], F32)
        ba = kb.sb("ba", [128, 16], F32)
        R = kb.sb("R", [128, 32], F32)
        negG = kb.sb("negG", [128, 16], F32)
        R1 = kb.sb("R1", [128, 32], F32)
        Rs = kb.sb("Rs", [128, 96], BF16)
        rows = kb.sb("rows", [96, 128], BF16)
        nrows = kb.sb("nrows", [96, 128], BF16)
        Dm = kb.sb("Dm", [128, 8, 128], F32)
        E1 = kb.sb("E1m", [128, 8, 128], F32)
        E2 = Dm
        aqs = kb.sb("aqs", [128, 8, 128], BF16)
        X0 = kb.sb("X0n", [128, 8, 128], BF16)
        LT = [kb.sb("LTn%d" % d, [128, 8, 128], BF16) for d in range(2)]
        Dn = [[kb.sb("Dn%d_%d" % (d, g), [128, 4, 128], BF16) for g in range(2)] for d in range(2)]
        DTn = [[kb.sb("DTn%d_%d" % (d, g), [128, 4, 128], BF16) for g in range(2)] for d in range(2)]
        Wn = [[kb.sb("Wn%d_%d" % (d, g), [128, 4, 128], BF16) for g in range(2)] for d in range(2)]
        MT = [kb.sb("MTn%d" % d, [128, 8, 128], BF16) for d in range(2)]
        MaT = [kb.sb("MaTn%d" % d, [128, 8, 128], BF16) for d in range(2)]
        LM = kb.sb("LM", [128, 14, 128], BF16)
        self.kb.dma("pool", LM[:, :, :], AP(self.lvlm, 0, [[128, 128], [128 * 128, 14], [1, 128]]), [self.lvlm], [LM])
        lm = lambda i: AP(LM, i * 128, [[14 * 128, 128], [0, 8], [1, 128]])
        lm4 = lambda i: AP(LM, i * 128, [[14 * 128, 128], [0, 4], [1, 128]])
        nidb = kb.sb("nidb", [128, 128], BF16)
        self.kb.op("act", lambda e: e.mul(nidb[:, :], self.identb[:, :], -1.0), [self.identb], [nidb])
        us = kb.sb("us", [128, DM], F32)
        ws = kb.sb("wsn", [128, 8, 128], BF16)
        b3 = lambda t, c0: AP(t, c0, [[16, 128], [1, 8], [0, 128]])
        f3 = lambda t: AP(t, 0, [[1024, 128], [128, 8], [1, 128]])
        p3 = lambda k: AP(self.PP[k], 0, [[1024, 128], [128, 8], [1, 128]])
        pbk = lambda k: [self.bank[2 * k], self.bank[2 * k + 1]]
        print("phaseB sbuf remaining", self.nc.sbuf_bytes_remaining)
        tl = self.tiles(True)
        src, sbuf = self.x_src(tl[0][2], tl[0][3], True)
        self.load(xt[0][:, :, :], src, sbuf, xt[0])
        for ti, (idx, tok0, is_ctx, li) in enumerate(tl):
            Xt = xt[ti % 2]
            A, B, mb = (self.A1c, self.B1c, self.modC) if is_ctx else (self.A1, self.B1, self.modL)
            for sub in range(2):
                self.norm_T(Xt[:, sub, :], Xt, A, B, mb, hnT, TT, sub, 0)
            if ti + 1 < len(tl):
                src, sbuf = self.x_src(tl[ti + 1][2], tl[ti + 1][3], True)
                self.load(xt[(ti + 1) % 2][:, :, :], src, sbuf, xt[(ti + 1) % 2])
            nseg, ls = (1, TT) if is_ctx else (2, 128)

            def stA(fc):
                bk = 1 + fc % 3
                for kc in range(8):
                    self.mm(self.pf(bk, 0, TT), Wgd[:, kc, fc * 128:(fc + 1) * 128], hnT[:, kc, :], kc == 0, kc == 7,
                            [Wgd, hnT], [self.bank[bk]])

            def stB(fc):
                bk = 1 + fc % 3
                A_ = acc[fc % 4]
                self.ts(A_[:, :], self.pf(bk, 0, TT), cw[:, fc, 2:3], None, ALU.mult, None, [self.bank[bk], cw], [A_])
                for tap in (0, 1, 3, 4):
                    sh = tap - 2
                    n = ls - abs(sh)
                    o0, i0 = (0, sh) if sh > 0 else (-sh, 0)
                    oap = AP(A_, o0, [[TT, 128], [ls, nseg], [1, n]])
                    iap = AP(self.PP[bk // 2], (bk % 2) * 512 + i0, [[1024, 128], [ls, nseg], [1, n]])
                    self.stt(oap, iap, cw[:, fc, tap:tap + 1], oap, ALU.mult, ALU.add, [self.bank[bk], cw, A_], [A_])

            def stCE(fc_c, fc_e):
                okc = fc_c is not None and 0 <= fc_c < 24
                oke = fc_e is not None and 0 <= fc_e < 16
                if okc:
                    A_, G_ = acc[fc_c % 4], sg[fc_c % 2]
                    self.act(G_[:, :], A_[:, :], AF.Exp, [A_], [G_], scale=-1.0)
                if oke:
                    S_, Q_, R_ = sil[fc_e % 4], sq[fc_e % 2], rn[fc_e % 2]
                    bq = 6 + fc_e % 2
                    self.act(Q_[:, :], S_[:, :], AF.Square, [S_], [Q_])
                    self.mm(self.pf(bq, 0, TT), self.onesb[:, :], Q_[:, :], True, True, [self.onesb, Q_], [self.bank[bq]])
                if okc:
                    self.act(G_[:, :], G_[:, :], AF.Ln, [G_], [G_], bias=self.oneb[:, 0:1])
                if oke:
                    self.act(R_[:, :], self.pf(bq, 0, TT), AF.Ln, [self.bank[bq]], [R_], bias=self.epsb[:, 0:1])
                if okc:
                    self.act(G_[:, :], G_[:, :], AF.Exp, [G_], [G_], scale=-1.0)
                if oke:
                    self.act(R_[:, :], R_[:, :], AF.Exp, [R_], [R_], scale=-0.5,
                             bias=(self.lncb[:, 0:1] if fc_e < 8 else self.zerob[:, 0:1]))

            def stD(fc):
                A_, G_ = acc[fc % 4], sg[fc % 2]
                if fc < 16:
                    S_ = sil[fc % 4]
                    self.ptt(S_[:, :], A_[:, :], G_[:, :], ALU.mult, [A_, G_], [S_])
                else:
                    h = fc - 16
                    V_ = vTb[fc % 2]
                    self.ptt(V_[:, :], A_[:, :], G_[:, :], ALU.mult, [A_, G_], [V_])
                    for sub in range(2):
                        self.tr(self.pb16(4 + sub, h * 128, 128), V_[:, sub * 128:(sub + 1) * 128], self.identb[:, :],
                                [V_, self.identb], [self.bank[4 + sub]])

            def stF(fc):
                if fc < 16:
                    h = fc % 8
                    S_, R_ = sil[fc % 4], rn[fc % 2]
                    dst = dqs if fc < 8 else dks
                    self.ptt(dst[:, h, :], S_[:, :], R_[:, :], ALU.mult, [S_, R_], [dst])

            for it in range(24 + 5):
                if it < 24:
                    stA(it)
                if 0 <= it - 1 < 24:
                    stB(it - 1)
                stCE(it - 2, it - 4)
                if 0 <= it - 3 < 24:
                    stD(it - 3)
                if 0 <= it - 5 < 24:
                    stF(it - 5)
            for sub in range(2):
                self.cp(vtk[:, sub, :], self.pb16(4 + sub, 0, 1024), [self.bank[4 + sub]], [vtk])
            for sub in range(2):
                for h in range(8):
                    self.tr(self.pb16(4 + sub, h * 128, 128), dks[:, h, sub * 128:(sub + 1) * 128], self.identb[:, :],
                            [dks, self.identb], [self.bank[4 + sub]])
                self.cp(ktk[:, sub, :], self.pb16(4 + sub, 0, 1024), [self.bank[4 + sub]], [ktk])
            self.store(AP(self.dqT, tok0, [[NT, 128], [128 * NT, 8], [1, TT]]), dqs[:, :, :], dqs, self.dqT)
            self.store(AP(self.dkt, tok0 * DM, [[DM, 128], [128 * DM, 2], [1, DM]]), ktk[:, :, :], ktk, self.dkt)
            for sub in range(2):
                ch = tok0 // 128 + sub
                tsl = slice(sub * 128, (sub + 1) * 128)
                for kc in range(8):
                    self.mm(self.pf(6, 0, 32), hnT[:, kc, tsl], Wab[:, kc, :], kc == 0, kc == 7, [hnT, Wab], [self.bank[6]])
                self.tt(t16[:, :], self.pf(6, 0, 16), dtb[:, :], ALU.add, [self.bank[6], dtb], [t16])
                self.act(t16[:, :], t16[:, :], AF.Exp, [t16], [t16])
                self.act(t16[:, :], t16[:, :], AF.Ln, [t16], [t16], bias=self.oneb[:, 0:1])
                self.tt(g16[:, :], t16[:, :], negA[:, :], ALU.mult, [t16, negA], [g16])
                self.act(lnb[:, :], self.pf(6, 16, 16), AF.Exp, [self.bank[6]], [lnb], scale=-1.0)
                self.act(lnb[:, :], lnb[:, :], AF.Ln, [lnb], [lnb], bias=self.oneb[:, 0:1])
                self.kb.op("act", lambda e: e.mul(lnb[:, :], lnb[:, :], -1.0), [lnb], [lnb])
                self.act(beta[:, :], lnb[:, :], AF.Exp, [lnb], [beta])
                self.mm(self.pf(7, 0, 8), UFi[:, :], g16[:, 0:8], True, True, [UFi, g16], [self.bank[7]])
                self.mm(self.pf(7, 8, 8), UBi[:, :], g16[:, 8:16], True, True, [UBi, g16], [self.bank[7]])
                self.mm(self.pf(7, 16, 8), UBs[:, :], g16[:, 0:8], True, True, [UBs, g16], [self.bank[7]])
                self.mm(self.pf(7, 24, 8), UFs[:, :], g16[:, 8:16], True, True, [UFs, g16], [self.bank[7]])
                self.mm(self.pf(7, 32, 16), self.onesf[:, :], g16[:, :], True, True, [self.onesf, g16], [self.bank[7]])
                self.act(R[:, 0:16], self.pf(7, 0, 16), AF.Copy, [self.bank[7]], [R])
                self.act(AP(self.GDa, ch * 16, [[NCH * 16, 128], [1, 16]]), self.pf(7, 0, 16), AF.Exp,
                         [self.bank[7]], [self.GDa])
                self.act(AP(self.GDd, ch * 16, [[NCH * 16, 128], [1, 16]]), self.pf(7, 16, 16), AF.Exp,
                         [self.bank[7]], [self.GDd])
                self.act(AP(self.GDe, ch * 16, [[NCH * 16, 128], [1, 16]]), self.pf(7, 32, 16), AF.Exp,
                         [self.bank[7]], [self.GDe])
                self.tt(ba[:, :], beta[:, :], AP(self.GDa, ch * 16, [[NCH * 16, 128], [1, 16]]), ALU.mult,
                        [beta, self.GDa], [ba])
                self.tt(R[:, 16:32], R[:, 0:16], lnb[:, :], ALU.add, [R, lnb], [R])
                self.kb.op("act", lambda e: e.mul(negG[:, :], R[:, 0:16], -1.0), [R], [negG])
                self.cp(Rs[:, 0:32], R[:, :], [R], [Rs])
                self.tt(R1[:, :], R[:, :], Rs[:, 0:32], ALU.subtract, [R, Rs], [R1])
                self.cp(Rs[:, 32:64], R1[:, :], [R1], [Rs])
                self.tt(R1[:, :], R1[:, :], Rs[:, 32:64], ALU.subtract, [R1, Rs], [R1])
                self.cp(Rs[:, 64:96], R1[:, :], [R1], [Rs])
                self.tr(self.pb16(6, 0, 128, 96), Rs[:, :], self.identb[:, :], [Rs, self.identb], [self.bank[6]])
                self.act(rows[:, :], self.pb16(6, 0, 128, 96), AF.Copy, [self.bank[6]], [rows])
                self.kb.op("act", lambda e: e.mul(nrows[:, :], rows[:, :], -1.0), [rows], [nrows])
                for h in range(8):
                    bk = h // 4
                    self.mm(self.pf(bk, (h % 4) * 128, 128), dks[:, h, tsl], dks[:, h, tsl], True, True,
                            [dks], [self.bank[bk]])
                for h in range(8):
                    bk = 2 + h // 4
                    self.mm(self.pf(bk, (h % 4) * 128, 128), dks[:, h, tsl], dqs[:, h, tsl], True, True,
                            [dks, dqs], [self.bank[bk]])
                for d in range(2):
                    mstrT = Mlt if d == 0 else Mgt
                    mstr = Mgt if d == 0 else Mlt
                    specs = [(Dm, 0, rows, nrows, 0, Minc[d]), (E1, 16, rows, nrows, 0, mstrT), (E2, 0, nrows, rows, 16, mstr)]
                    for si, (dst, so, rr, pr, po, mk) in enumerate(specs):
                        pk = 2 + (si % 2)
                        for hg in range(2):
                            bk = 2 * pk + hg
                            c0 = d * 8 + hg * 4
                            self.mm(self.pf(bk, 0, 512), self.identb[:, :], AP(mk, 0, [[128, 128], [0, 4], [1, 128]]),
                                    True, False, [self.identb, mk], [self.bank[bk]])
                            self.mm(self.pf(bk, 0, 512), pr[:, :], AP(sel, (po + c0) * 128, [[32 * 128, 96], [1, 512]]),
                                    False, False, [sel, pr], [self.bank[bk]])
                            for h4i in range(4):
                                self.mm(self.pf(bk, h4i * 128, 128), sel[:, so + c0 + h4i, :], rr[:, :], False, True,
                                        [sel, rr], [self.bank[bk]])
                        for hg in range(2):
                            bk = 2 * pk + hg
                            self.act(AP(dst, hg * 512, [[1024, 128], [1, 512]]), self.pf(bk, 0, 512), AF.Exp,
                                     [self.bank[bk]], [dst])
                        if si == 0:
                            self.tt(f3(aqs), p3(1), f3(Dm), ALU.mult, pbk(1) + [Dm], [aqs])
                            self.store(AP(self.daq[d], ch * 128 * DM, [[DM, 128], [1, DM]]), aqs[:, :, :], aqs, self.daq[d])
                    self.tt(f3(LT[d]), p3(0), f3(E1), ALU.mult, pbk(0) + [E1], [LT[d]])
                    self.tt(f3(X0), p3(0), f3(E2), ALU.mult, pbk(0) + [E2], [X0])
                    if "dbgX" in self.dbg and ch == 2 and d == 0:
                        self.store(AP(self.dbgX, 0, [[DM, 128], [1, DM]]), X0[:, :, :], X0, self.dbgX)
                        self.store(AP(self.dbgXT, 0, [[DM, 128], [1, DM]]), LT[d][:, :, :], LT[d], self.dbgXT)
                    for hg in range(2):
                        h4 = lambda t: AP(t, 0, [[512, 128], [128, 4], [1, 128]])
                        idb = AP(self.identb, 0, [[128, 128], [0, 4], [1, 128]])
                        src4 = lambda t: AP(t, hg * 512, [[1024, 128], [128, 4], [1, 128]])
                        W_, D_, DT_ = Wn[d][hg], Dn[d][hg], DTn[d][hg]
                        self.tt(h4(W_), src4(X0), lm4(d * 7), ALU.mult, [X0, LM], [W_])
                        self.tt(h4(D_), h4(W_), idb, ALU.add, [W_, self.identb], [D_])
                        self.tt(h4(W_), src4(LT[d]), lm4((1 - d) * 7), ALU.mult, [LT[d], LM], [W_])
                        self.tt(h4(DT_), h4(W_), idb, ALU.add, [W_, self.identb], [DT_])
                h4 = lambda t: AP(t, 0, [[512, 128], [128, 4], [1, 128]])
                for lvl in range(1, 7):
                    for d in range(2):
                        for hg in range(2):
                            k = d * 2 + hg
                            ba_, bb_ = 2 * k, 2 * k + 1
                            W_, D_, DT_ = Wn[d][hg], Dn[d][hg], DTn[d][hg]
                            self.mm(self.pf(ba_, 0, 512), self.identb[:, :], AP(nidb, 0, [[128, 128], [0, 4], [1, 128]]),
                                    True, False, [self.identb, nidb], [self.bank[ba_]])
                            for h4i in range(4):
                                h = hg * 4 + h4i
                                self.mm(self.pf(ba_, h4i * 128, 128), LT[d][:, h, :], D_[:, h4i, :], False, True,
                                        [LT[d], D_], [self.bank[ba_]])
                            self.tt(h4(W_), AP(self.PP[ba_ // 2], (ba_ % 2) * 512, [[1024, 128], [128, 4], [1, 128]]),
                                    lm4(d * 7 + lvl), ALU.mult, [self.bank[ba_], LM], [W_])
                            if lvl < 6:
                                for h4i in range(4):
                                    self.mm(self.pf(bb_, h4i * 128, 128), DT_[:, h4i, :], W_[:, h4i, :], True, True,
                                            [DT_, W_], [self.bank[bb_]])
                            for h4i in range(4):
                                self.mm(self.pf(ba_, h4i * 128, 128), W_[:, h4i, :], DT_[:, h4i, :], True, True,
                                        [W_, DT_], [self.bank[ba_]])
                            if lvl < 6:
                                if hg == 0:
                                    self.act(AP(D_, 0, [[512, 128], [1, 512]]), self.pf(bb_, 0, 512), AF.Copy, [self.bank[bb_]], [D_])
                                    self.cp(AP(DT_, 0, [[512, 128], [1, 512]]), self.pf(ba_, 0, 512), [self.bank[ba_]], [DT_])
                                else:
                                    self.cp(AP(D_, 0, [[512, 128], [1, 512]]), self.pf(bb_, 0, 512), [self.bank[bb_]], [D_])
                                    self.act(AP(DT_, 0, [[512, 128], [1, 512]]), self.pf(ba_, 0, 512), AF.Copy, [self.bank[ba_]], [DT_])
                            else:
                                pv = AP(self.PP[ba_ // 2], (ba_ % 2) * 512, [[1024, 128], [128, 4], [1, 128]])
                                mo = lambda t: AP(t, hg * 512, [[1024, 128], [128, 4], [1, 128]])
                                b4 = lambda t, c0: AP(t, c0, [[16, 128], [1, 4], [0, 128]])
                                self.tt(mo(MT[d]), pv, b4(beta, d * 8 + hg * 4), ALU.mult, [self.bank[ba_], beta], [MT[d]])
                                self.tt(mo(MaT[d]), pv, b4(ba, d * 8 + hg * 4), ALU.mult, [self.bank[ba_], ba], [MaT[d]])
                for d in range(2):
                    pa, pb_ = 2 * d, 2 * d + 1
                    if "dbgX" in self.dbg and ch == 2 and d == 0:
                        self.store(AP(self.dbgMT, 0, [[DM, 128], [1, DM]]), MT[d][:, :, :], MT[d], self.dbgMT)
                    for h in range(8):
                        bk = 2 * pb_ + h // 4
                        self.mm(self.pf(bk, (h % 4) * 128, 128), MT[d][:, h, :], vtk[:, sub, h * 128:(h + 1) * 128], True, True,
                                [MT[d], vtk], [self.bank[bk]])
                    self.act(us[:, :], AP(self.PP[pb_], 0, [[1024, 128], [1, 1024]]), AF.Copy, pbk(pb_), [us])
                    self.store(AP(self.du[d], ch * 128 * DM, [[DM, 128], [1, DM]]), us[:, :], us, self.du[d])
                    for h in range(8):
                        bk = 2 * pa + h // 4
                        self.mm(self.pf(bk, (h % 4) * 128, 128), ktk[:, sub, h * 128:(h + 1) * 128], MaT[d][:, h, :], True, True,
                                [ktk, MaT[d]], [self.bank[bk]])
                    self.cp(f3(ws), p3(pa), pbk(pa), [ws])
                    self.store(AP(self.dw[d], ch * 128, [[NT, 128], [128 * NT, 8], [1, 128]]), ws[:, :, :], ws, self.dw[d])
        kb.barrier()
        kb.release(m)

    def gdn_scan(self):
        kb = self.kb
        m = kb.mark()
        NT, NCH = self.NT, self.NCH
        S = [kb.sb("Sd%d" % d, [128, 8, 128], F32) for d in range(2)]
        Sb = [kb.sb("Sdb%d" % d, [128, 8, 128], BF16) for d in range(2)]
        for d in range(2):
            self.memset(S[d][:, :, :], 0.0, [S[d]])
            self.memset(Sb[d][:, :, :], 0.0, [Sb[d]])
        nb = 2
        mk = lambda n, sh, dt: [[kb.sb("%s%d_%d" % (n, d, i), sh, dt) for i in range(nb)] for d in range(2)]
        wT = mk("lw", [128, 8, 128], BF16)
        qT = mk("lq", [128, 8, 128], BF16)
        uu = mk("lu", [128, DM], F32)
        aq = mk("la", [128, DM], BF16)
        kt = mk("lk", [128, DM], BF16)
        vn32 = kb.sb("vn32", [128, DM], F32)
        vnb = kb.sb("vnb", [128, DM], BF16)
        vnd = kb.sb("vnd", [128, DM], BF16)
        tq = kb.sb("tq", [128, DM], F32)
        osb = [kb.sb("odb%d" % d, [128, DM], F32) for d in range(2)]
        order = [list(range(NCH)), [1, 0] + list(range(NCH - 1, 1, -1))]
        b3 = lambda t, c0: AP(t, c0, [[NCH * 16, 128], [1, 8], [0, 128]])
        f3 = lambda t: AP(t, 0, [[1024, 128], [128, 8], [1, 128]])
        p3 = lambda k: AP(self.PP[k], 0, [[1024, 128], [128, 8], [1, 128]])
        pbk = lambda k: [self.bank[2 * k], self.bank[2 * k + 1]]

        def issue_loads(s):
            for d in range(2):
                c = order[d][s]
                t0 = c * 128
                i = s % nb
                self.load(wT[d][i][:, :, :], AP(self.dw[d], t0, [[NT, 128], [128 * NT, 8], [1, 128]]), self.dw[d], wT[d][i])
                self.load(qT[d][i][:, :, :], AP(self.dqT, t0, [[NT, 128], [128 * NT, 8], [1, 128]]), self.dqT, qT[d][i])
                self.load(uu[d][i][:, :], AP(self.du[d], t0 * DM, [[DM, 128], [1, DM]]), self.du[d], uu[d][i])
                self.load(aq[d][i][:, :], AP(self.daq[d], t0 * DM, [[DM, 128], [1, DM]]), self.daq[d], aq[d][i])
                self.load(kt[d][i][:, :], AP(self.dkt, t0 * DM, [[DM, 128], [1, DM]]), self.dkt, kt[d][i])

        issue_loads(0)
        for s in range(NCH):
            if s + 1 < NCH:
                issue_loads(s + 1)
            for d in range(2):
                c = order[d][s]
                i = s % nb
                Wt, Qt, Uu, Aq, Kt = wT[d][i], qT[d][i], uu[d][i], aq[d][i], kt[d][i]
                for h in range(8):
                    bk = h // 4
                    self.mm(self.pf(bk, (h % 4) * 128, 128), Wt[:, h, :], Sb[d][:, h, :], True, True, [Wt, Sb[d]], [self.bank[bk]])
                for h in range(8):
                    bk = 2 + h // 4
                    self.mm(self.pf(bk, (h % 4) * 128, 128), Qt[:, h, :], Sb[d][:, h, :], True, True, [Qt, Sb[d]], [self.bank[bk]])
                self.tt(vn32[:, :], Uu[:, :], AP(self.PP[0], 0, [[1024, 128], [1, 1024]]), ALU.subtract, [Uu] + pbk(0), [vn32])
                self.act(vnb[:, :], vn32[:, :], AF.Copy, [vn32], [vnb])
                self.tt(f3(vnd), f3(vn32), b3(self.GDd, c * 16 + d * 8), ALU.mult, [vn32, self.GDd], [vnd])
                for h in range(8):
                    bk = 4 + h // 4
                    hs = slice(h * 128, (h + 1) * 128)
                    self.mm(self.pf(bk, (h % 4) * 128, 128), Aq[:, hs], vnb[:, hs], True, True, [Aq, vnb], [self.bank[bk]])
                for h in range(8):
                    bk = 6 + h // 4
                    hs = slice(h * 128, (h + 1) * 128)
                    self.mm(self.pf(bk, (h % 4) * 128, 128), Kt[:, hs], vnd[:, hs], True, True, [Kt, vnd], [self.bank[bk]])
                self.tt(f3(tq), p3(1), b3(self.GDa, c * 16 + d * 8), ALU.mult, pbk(1) + [self.GDa], [tq])
                self.tt(osb[d][:, :], tq[:, :], AP(self.PP[2], 0, [[1024, 128], [1, 1024]]), ALU.add, [tq] + pbk(2), [osb[d]])
                self.store(AP(self.do[d], c * 128 * DM, [[DM, 128], [1, DM]]), osb[d][:, :], osb[d], self.do[d])
                self.tt(f3(S[d]), f3(S[d]), b3(self.GDe, c * 16 + d * 8), ALU.mult, [S[d], self.GDe], [S[d]])
                self.tt(f3(S[d]), f3(S[d]), p3(3), ALU.add, [S[d]] + pbk(3), [S[d]])
                self.act(Sb[d][:, :, :], S[d][:, :, :], AF.Copy, [S[d]], [Sb[d]])
        kb.barrier()
        kb.release(m)

    def phase3a(self):
        kb = self.kb
        m = kb.mark()
        W, NT = self.W, self.NT
        Wz = kb.sb("Wz", [128, 8, 4096], BF16)
        for gi, key in enumerate(("gz", "dz", "g11", "g12")):
            for k0 in (0, 4):
                self.kb.dma("pool", AP(Wz, k0 * 4096 + gi * 1024, [[8 * 4096, 128], [4096, 4], [1, 1024]]),
                            AP(self.w_in, k0 * 128 * PIN + SPL[key], [[PIN, 128], [128 * PIN, 4], [1, 1024]]),
                            [self.w_in], [Wz])
        Wo = kb.sb("Wo", [128, 8, DM], BF16)
        self.wload(Wo, self.w_out, 0, DM, rowlen=DM)
        gg = kb.sb("ggl", [128, DM], F32)
        gd = kb.sb("ggd", [128, DM], F32)
        self.load(gg[:, :], self.glag[:, :], self.glag, gg)
        self.load(gd[:, :], self.gdng[:, :], self.gdng, gd)
        xt = [kb.sb("xt3%d" % i, [128, DM], F32) for i in range(2)]
        ol = [[kb.sb("ol%d_%d" % (j, i), [128, DM], F32) for i in range(2)] for j in range(4)]
        hnT = [kb.sb("hnT3_%d" % i, [128, 8, 128], BF16) for i in range(2)]
        sz = [kb.sb("sz%d" % i, [128, 2048], BF16) for i in range(2)]
        sg = [kb.sb("sg%d" % i, [128, 2048], BF16) for i in range(2)]
        ss = kb.sb("ss3", [128, 12], F32)
        rs = kb.sb("rs3", [128, 12], F32)
        mb16 = [kb.sb("mb16_%d" % i, [128, DM], BF16) for i in range(2)]
        mT = [kb.sb("mT%d" % i, [128, 8, 128], BF16) for i in range(2)]
        x1 = [kb.sb("x1s%d" % i, [128, DM], F32) for i in range(2)]
        print("phase3a sbuf remaining", self.nc.sbuf_bytes_remaining)
        ntile = self.L // 128

        def issue_loads(t):
            i = t % 2
            self.load(xt[i][:, :], AP(self.x, t * 128 * DM, [[DM, 128], [1, DM]]), self.x, xt[i])
            g0 = (CTX + t * 128) * DM
            self.load(ol[0][i][:, :], AP(self.go[0], g0, [[DM, 128], [1, DM]]), self.go[0], ol[0][i])
            self.load(ol[1][i][:, :], AP(self.go[1], g0, [[DM, 128], [1, DM]]), self.go[1], ol[1][i])
            nr = 128 // W if W <= 128 else 1
            for j in (0, 1):
                for rr in range(128 // W):
                    r = t * (128 // W) + rr
                    self.load(AP(ol[2 + j][i], rr * W * DM, [[DM, W], [1, DM]]),
                              AP(self.do[j], (CTX + r) * DM, [[128 * DM, W], [1, DM]]), self.do[j], ol[2 + j][i])

        def stage_a(t):
            i = t % 2
            H, SZ, SG = hnT[i], sz[i], sg[i]
            self.norm_T(xt[i][:, :], xt[i], self.A1, self.B1, self.modL, H, 128, 0, 0)
            for g in range(8):
                bk = 1 + g % 4
                for kc in range(8):
                    self.mm(self.pf(bk, 0, 512), H[:, kc, :], Wz[:, kc, g * 512:(g + 1) * 512], kc == 0, kc == 7,
                            [H, Wz], [self.bank[bk]])
                if g < 4:
                    self.act(SZ[:, g * 512:(g + 1) * 512], self.pf(bk, 0, 512), AF.Silu, [self.bank[bk]], [SZ])
                else:
                    self.act(SG[:, (g - 4) * 512:(g - 3) * 512], self.pf(bk, 0, 512), AF.Sigmoid, [self.bank[bk]], [SG])

        def stage_b(t):
            i = t % 2
            SZ, SG, MB, MT_ = sz[i], sg[i], mb16[i], mT[i]
            og, od = ol[0][i], ol[2][i]
            self.tt(og[:, :], ol[0][i][:, :], ol[1][i][:, :], ALU.add, [ol[0][i], ol[1][i]], [og])
            self.tt(od[:, :], ol[2][i][:, :], ol[3][i][:, :], ALU.add, [ol[2][i], ol[3][i]], [od])
            for h in range(4):
                self.act(self.junk[:, 0:256], og[:, h * 256:(h + 1) * 256], AF.Square, [og], [self.junk, ss],
                         accum=ss[:, h:h + 1])
            for h in range(8):
                self.act(self.junk[:, 0:128], od[:, h * 128:(h + 1) * 128], AF.Square, [od], [self.junk, ss],
                         accum=ss[:, 4 + h:5 + h])
            self.act(rs[:, 0:4], ss[:, 0:4], AF.Ln, [ss], [rs], scale=1.0 / 256, bias=self.epsb[:, 0:1])
            self.act(rs[:, 4:12], ss[:, 4:12], AF.Ln, [ss], [rs], scale=1.0 / 128, bias=self.epsb[:, 0:1])
            self.act(rs[:, :], rs[:, :], AF.Exp, [rs], [rs], scale=-0.5)
            self.tt(AP(og, 0, [[DM, 128], [256, 4], [1, 256]]), AP(og, 0, [[DM, 128], [256, 4], [1, 256]]),
                    AP(rs, 0, [[12, 128], [1, 4], [0, 256]]), ALU.mult, [og, rs], [og])
            self.tt(AP(od, 0, [[DM, 128], [128, 8], [1, 128]]), AP(od, 0, [[DM, 128], [128, 8], [1, 128]]),
                    AP(rs, 4, [[12, 128], [1, 8], [0, 128]]), ALU.mult, [od, rs], [od])
            self.tt(og[:, :], og[:, :], gg[:, :], ALU.mult, [og, gg], [og])
            self.tt(od[:, :], od[:, :], gd[:, :], ALU.mult, [od, gd], [od])
            self.tt(og[:, :], og[:, :], SZ[:, 0:1024], ALU.mult, [og, SZ], [og])
            self.tt(od[:, :], od[:, :], SZ[:, 1024:2048], ALU.mult, [od, SZ], [od])
            self.tt(og[:, :], og[:, :], SG[:, 0:1024], ALU.mult, [og, SG], [og])
            self.tt(od[:, :], od[:, :], SG[:, 1024:2048], ALU.mult, [od, SG], [od])
            self.tt(MB[:, :], og[:, :], od[:, :], ALU.add, [og, od], [MB])
            for kc in range(8):
                self.tr(self.pb16(5, kc * 128, 128), MB[:, kc * 128:(kc + 1) * 128], self.identb[:, :],
                        [MB, self.identb], [self.bank[5]])
            self.act(AP(MT_, 0, [[1024, 128], [1, 1024]]), self.pb16(5, 0, 1024), AF.Copy, [self.bank[5]], [MT_])
            for half in range(2):
                bk = 6 + half
                for kc in range(8):
                    self.mm(self.pf(bk, 0, 512), MT_[:, kc, :], Wo[:, kc, half * 512:(half + 1) * 512], kc == 0, kc == 7,
                            [MT_, Wo], [self.bank[bk]])
            self.tt(x1[i][:, :], AP(self.PP[3], 0, [[1024, 128], [1, 1024]]), self.G1, ALU.mult,
                    [self.bank[6], self.bank[7], self.modL], [x1[i]])
            self.tt(x1[i][:, :], x1[i][:, :], xt[i][:, :], ALU.add, [x1[i], xt[i]], [x1[i]])
            self.store(AP(self.dx1, t * 128 * DM, [[DM, 128], [1, DM]]), x1[i][:, :], x1[i], self.dx1)

        issue_loads(0)
        if ntile > 1:
            issue_loads(1)
        stage_a(0)
        for t in range(ntile):
            if t + 1 < ntile:
                stage_a(t + 1)
            stage_b(t)
            if t + 2 < ntile:
                issue_loads(t + 2)
        kb.barrier()
        kb.release(m)

    def phase3b(self):
        kb = self.kb
        m = kb.mark()
        Wg = kb.sb("Wg", [128, 8, FFH], BF16)
        Wu = kb.sb("Wu", [128, 8, FFH], BF16)
        Wd = kb.sb("Wd", [128, 22, DM], BF16)
        for (dst, src) in ((Wg, self.wg), (Wu, self.wu)):
            for k0 in range(0, 8, 2):
                self.kb.dma("pool", AP(dst, k0 * FFH, [[8 * FFH, 128], [FFH, 2], [1, FFH]]),
                            AP(src, k0 * 128 * FFH, [[FFH, 128], [128 * FFH, 2], [1, FFH]]), [src], [dst])
        for k0 in range(0, 22, 2):
            self.kb.dma("pool", AP(Wd, k0 * DM, [[22 * DM, 128], [DM, 2], [1, DM]]),
                        AP(self.wd, k0 * 128 * DM, [[DM, 128], [128 * DM, 2], [1, DM]]), [self.wd], [Wd])
        fn = kb.sb("fnw", [128, DM], F32)
        self.load(fn[:, :], self.fnbc[:, :], self.fnbc, fn)
        xt = [kb.sb("x1t%d" % i, [128, 2, DM], F32) for i in range(1)]
        h2T = kb.sb("h2T", [128, 8, TT], BF16)
        sgl = kb.sb("sgl", [128, TT], F32)
        actT = kb.sb("actT", [128, 22, TT], BF16)
        ty = kb.sb("ty2", [128, 512], F32)
        x2s = [kb.sb("x2_%d" % i, [128, DM], F32) for i in range(2)]
        print("phase3b sbuf remaining", self.nc.sbuf_bytes_remaining)
        ss, rs = self.ssq, self.rsd
        ntile = self.L // TT

        def issue_load(t):
            self.load(xt[0][:, :, :], AP(self.dx1, t * TT * DM, [[DM, 128], [128 * DM, 2], [1, DM]]), self.dx1, xt[0])

        issue_load(0)
        oc = 0
        for t in range(ntile):
            X = xt[0]
            if t > 0:
                issue_load(t)
            for sub in range(2):
                self.norm_T(X[:, sub, :], X, self.A2, self.B2, self.modL2, h2T, TT, sub, 0)
            for hc in range(22):
                bg = 1 + (hc % 2) * 2
                bu = bg + 1
                for kc in range(8):
                    self.mm(self.pf(bg, 0, TT), Wg[:, kc, hc * 128:(hc + 1) * 128], h2T[:, kc, :], kc == 0, kc == 7,
                            [Wg, h2T], [self.bank[bg]])
                for kc in range(8):
                    self.mm(self.pf(bu, 0, TT), Wu[:, kc, hc * 128:(hc + 1) * 128], h2T[:, kc, :], kc == 0, kc == 7,
                            [Wu, h2T], [self.bank[bu]])
                self.act(sgl[:, :], self.pf(bg, 0, TT), AF.Silu, [self.bank[bg]], [sgl])
                self.tt(actT[:, hc, :], sgl[:, :], self.pf(bu, 0, TT), ALU.mult, [sgl, self.bank[bu]], [actT])
            for sub in range(2):
                x2 = x2s[oc % 2]
                for half in range(2):
                    bk = 5 + half
                    for hc in range(22):
                        self.mm(self.pf(bk, 0, 512), actT[:, hc, sub * 128:(sub + 1) * 128],
                                Wd[:, hc, half * 512:(half + 1) * 512], hc == 0, hc == 21, [actT, Wd], [self.bank[bk]])
                    hs = slice(half * 512, (half + 1) * 512)
                    self.tt(ty[:, :], self.pf(bk, 0, 512), AP(self.modL2, 2 * DM + half * 512, [[3 * DM, 128], [1, 512]]),
                            ALU.mult, [self.bank[bk], self.modL2], [ty])
                    self.tt(x2[:, hs], ty[:, :], X[:, sub, hs], ALU.add, [ty, X], [x2])
                self.act(self.junk[:, :], x2[:, :], AF.Square, [x2], [self.junk, ss], accum=ss[:, 2:3])
                self.act(rs[:, 2:3], ss[:, 2:3], AF.Sqrt, [ss], [rs], scale=1.0 / DM, bias=self.epsb[:, 0:1])
                self.recip(rs[:, 3:4], rs[:, 2:3], [rs], [rs])
                O = x2
                oc += 1
                self.stt(O[:, :], x2[:, :], rs[:, 3:4], fn[:, :], ALU.mult, ALU.mult, [x2, rs, fn], [O])
                self.store(AP(self.out, (t * TT + sub * 128) * DM, [[DM, 128], [1, DM]]), O[:, :], O, self.out)
        kb.barrier()
        kb.release(m)

    def build(self, phases="0ASBG3F"):
        kb = self.kb
        self.consts()
        self.epsb = kb.sb("epsb", [128, 1], F32)
        self.oneb = kb.sb("oneb", [128, 1], F32)
        self.lncb = kb.sb("lncb", [128, 1], F32)
        self.zerob = kb.sb("zerob", [128, 1], F32)
        self.memset(self.lncb[:, :], -0.5 * float(np.log(128.0)), [self.lncb])
        self.memset(self.zerob[:, :], 0.0, [self.zerob])
        self.memset(self.epsb[:, :], EPS, [self.epsb])
        self.memset(self.oneb[:, :], 1.0, [self.oneb])
        NCH = self.NCH
        self.modL2 = kb.sb("modL2", [128, 3 * DM], F32)
        ma = kb.mark()
        self.modL1 = kb.sb("modL1", [128, 3 * DM], F32)
        mb_ = kb.mark()
        self.modC = kb.sb("modC", [128, 2 * DM], F32)
        self.Egla = kb.sb("Egla", [128, 2 * NCH * 4], F32)
        self.GDa = kb.sb("GDa", [128, NCH * 16], F32)
        self.GDd = kb.sb("GDd", [128, NCH * 16], F32)
        self.GDe = kb.sb("GDe", [128, NCH * 16], F32)
        self.phase0()
        if "A" in phases:
            self.phaseA()
        if "S" in phases:
            self.gla_scan()
        if "B" in phases:
            self.phaseB()
        if "G" in phases:
            self.gdn_scan()
        kb.release(mb_)
        if "3" in phases:
            self.phase3a()
        kb.release(ma)
        if "F" in phases:
            self.phase3b()
        kb.barrier()
        kb.finalize()
        kb.close()
        return self.nc


def host_consts():
    p = np.arange(128)[:, None]
    f = np.arange(128)[None, :]
    le = (p <= f).astype(np.float32)
    ge = (p >= f).astype(np.float32)
    lt = (p < f).astype(np.float32)
    gt = (p > f).astype(np.float32)
    cumU = np.stack([le * (-1.0 / 16.0), ge * (-1.0 / 16.0), le, ge, gt, lt]).astype(np.float32)
    gmask = np.stack([le, ge]).astype(np.float32)
    dmask = np.stack([(1 - le) * NEG, (1 - ge) * NEG, (1 - lt) * NEG, (1 - gt) * NEG]).astype(np.float32)
    sel = np.zeros((96, 32, 128), np.float32)
    for c in range(32):
        for part in range(3):
            sel[part * 32 + c, c, :] = 1.0
    lv = np.zeros((14, 128, 128), np.float32)
    pi = np.arange(128)[:, None]
    fj = np.arange(128)[None, :]
    for s_ in range(7):
        b = 1 << s_
        mlow = ((pi // (2 * b)) == (fj // (2 * b))) & ((pi % (2 * b)) >= b) & ((fj % (2 * b)) < b)
        dg = np.eye(128, dtype=np.float32) if s_ >= 1 else 0.0
        lv[s_] = -mlow.astype(np.float32) - dg
        lv[7 + s_] = -mlow.T.astype(np.float32) - dg
    return dict(identf=np.eye(128, dtype=np.float32), cumU=cumU, gmask=gmask, dmask=dmask,
                sel=sel.reshape(96, 32 * 128), lvlm=lv)


def host_inputs(b, x, c, ctx, c_ctx, w_mod, b_mod, norm1_w, norm2_w, w_in, gla_lr_w, gla_lr_b, gla_norm_w,
                gdn_conv_w, gdn_a_log, gdn_dt_bias, gdn_norm_w, w_out, ffn_w_gate, ffn_w_up, ffn_w_down,
                final_norm_w, shared):
    f = lambda a: np.ascontiguousarray(a, dtype=np.float32)
    d = dict(shared)
    d["x"] = f(x[b])
    d["ctx"] = f(ctx[b])
    d["cT"] = f(np.asarray(c[b]).reshape(8, 128).T)
    return d


def shared_inputs(c_ctx, w_mod, b_mod, norm1_w, norm2_w, w_in, gla_lr_w, gla_lr_b, gla_norm_w,
                  gdn_conv_w, gdn_a_log, gdn_dt_bias, gdn_norm_w, w_out, ffn_w_gate, ffn_w_up, ffn_w_down,
                  final_norm_w):
    f = lambda a: np.ascontiguousarray(a, dtype=np.float32)
    bc = lambda v: f(np.broadcast_to(np.asarray(v).reshape(1, -1), (128, np.asarray(v).size)))
    d = host_consts()
    d["cctxT"] = f(np.asarray(c_ctx).reshape(8, 128).T)
    d["w_mod"] = f(w_mod[0])
    d["b_mod"] = f(np.asarray(b_mod[0]).reshape(1, -1))
    d["n1bc"] = bc(norm1_w[0])
    d["n2bc"] = bc(norm2_w[0])
    d["fnbc"] = bc(final_norm_w)
    d["w_in"] = f(w_in[0])
    lrw = np.zeros((2, 33, 512), np.float32)
    lrw[0, 0:16] = np.asarray(gla_lr_w[0, 0])
    lrw[1, 16:32] = np.asarray(gla_lr_w[0, 1])
    lrw[0, 32] = np.asarray(gla_lr_b[0, 0])
    lrw[1, 32] = np.asarray(gla_lr_b[0, 1])
    d["lrw"] = lrw
    d["glag"] = bc(np.asarray(gla_norm_w[0]).reshape(-1))
    d["gdng"] = bc(np.asarray(gdn_norm_w[0]).reshape(-1))
    cw = np.asarray(gdn_conv_w[0])
    d["cwT"] = f(cw.reshape(5, 24, 128).transpose(2, 1, 0).reshape(128, 120))
    d["alog"] = bc(np.asarray(gdn_a_log[0]).reshape(-1))
    d["dtb"] = bc(np.asarray(gdn_dt_bias[0]).reshape(-1))
    d["w_out"] = f(w_out[0])
    d["wg"] = f(ffn_w_gate[0])
    d["wu"] = f(ffn_w_up[0])
    d["wd"] = f(ffn_w_down[0])
    return d


_CACHE = {}


def kernel(x, c, ctx, c_ctx, w_mod, b_mod, norm1_w, norm2_w, w_in, gla_lr_w, gla_lr_b, gla_norm_w,
           gdn_conv_w, gdn_a_log, gdn_dt_bias, gdn_norm_w, w_out, ffn_w_gate, ffn_w_up, ffn_w_down,
           final_norm_w):
    x = np.asarray(x)
    B, L, _ = x.shape
    W = L // 128
    nc = Prog(W).build()
    shared = shared_inputs(c_ctx, w_mod, b_mod, norm1_w, norm2_w, w_in, gla_lr_w, gla_lr_b, gla_norm_w,
                           gdn_conv_w, gdn_a_log, gdn_dt_bias, gdn_norm_w, w_out, ffn_w_gate, ffn_w_up,
                           ffn_w_down, final_norm_w)
    f = lambda a: np.ascontiguousarray(a, dtype=np.float32)
    in_maps = []
    for b in range(B):
        d = dict(shared)
        d["x"] = f(x[b])
        d["ctx"] = f(np.asarray(ctx)[b])
        d["cT"] = f(np.asarray(c)[b].reshape(8, 128).T)
        in_maps.append(d)
    res = run_bass_kernel_spmd(nc, in_maps, core_ids=list(range(B)))
    return np.stack([np.asarray(r["out"], dtype=np.float32) for r in res.results], axis=0)
```

```python
import numpy as np
import ml_dtypes
import concourse.bass as bass
import concourse.mybir as mybir
from concourse.bass_utils import run_bass_kernel_spmd

F32 = mybir.dt.float32
BF16 = mybir.dt.bfloat16
AF = mybir.ActivationFunctionType
ALU = mybir.AluOpType

SAME_ENG_SYNC = True
NOSYNC_ENGS = ()
EPOCH = 20000
EPS = 1e-6
DM = 1024
CTX = 256
TT = 256
NEG = -30000.0


class Buf:
    __slots__ = ("t", "lw", "rd", "sem", "cnt", "name", "dram", "_scope", "semq")

    def __init__(self, t, name, dram=False):
        self.t = t
        self.name = name
        self.lw = {}
        self.rd = {}
        self.sem = None
        self.cnt = 0
        self.dram = dram
        self._scope = 0
        self.semq = None

    def __getitem__(self, idx):
        return self.t[idx]


class EngState:
    def __init__(self, name):
        self.name = name
        self.ops = []
        self.count = 0
        self.waited = {}
        self.needed = set()


class KB:
    def __init__(self, nc):
        self.nc = nc
        self.eng = {n: EngState(n) for n in ("pe", "act", "dve", "pool", "sp")}
        self._ctx = []
        self.sems = {}
        self.sempool = []
        self.sembufs = []
        self._sem_ctx = []

    def mark(self):
        return len(self._ctx)

    def release(self, m):
        for b in self.sembufs:
            if b.sem is not None and (not b.dram) and b._scope >= m:
                self.sempool.append((b.sem, b.cnt, b.semq))
                b.sem = None
        self.sembufs = [b for b in self.sembufs if b.sem is not None]
        while len(self._ctx) > m:
            cm = self._ctx.pop()
            cm.__exit__(None, None, None)

    def sb(self, name, shape, dt=F32):
        cm = self.nc.sbuf_tensor("sb_" + name, list(shape), dt)
        t = cm.__enter__()
        b = Buf(t, name)
        b._scope = len(self._ctx)
        self._ctx.append(cm)
        return b

    def ps(self, name, shape, dt=F32):
        cm = self.nc.psum_tensor(name, list(shape), dt)
        t = cm.__enter__()
        self._ctx.append(cm)
        return t

    def dram(self, name, shape, dt=F32, kind="Internal"):
        t = self.nc.dram_tensor(name, list(shape), dt, kind=kind)
        return Buf(t, name, dram=True)

    def _get_sem(self, b, q):
        if b.sem is not None:
            assert b.semq == q, "buffer %s used by both DMA queue kinds" % b.name
        if b.sem is None:
            b.semq = q
            cand = [i for i, e in enumerate(self.sempool) if e[2] == q]
            if cand:
                b.sem, b.cnt, _ = self.sempool.pop(cand[-1])
            else:
                cm = self.nc.semaphore("s%d" % len(self.sems))
                b.sem = cm.__enter__()
                self._sem_ctx.append(cm)
                b.cnt = 0
                self.sems[id(b.sem)] = b.sem
            self.sembufs.append(b)

    def _waits(self, E, reads, writes, extra=None):
        deps = {}

        def add(d):
            for k, v in d.items():
                if deps.get(k, -1) < v:
                    deps[k] = v

        for b in reads:
            add(b.lw)
        for b in writes:
            add(b.lw)
            add(b.rd)
        if extra:
            add(extra)
        for k, v in deps.items():
            if k[0] == "E" and k[1] == E.name:
                if E.name in ("pe", "sp") or not SAME_ENG_SYNC or E.name in NOSYNC_ENGS:
                    continue
            if E.waited.get(k, -1) >= v:
                continue
            E.waited[k] = v
            if k[0] == "E":
                self.eng[k[1]].needed.add(v)
            E.ops.append(("w", k, v))

    def op(self, eng, fn, reads=(), writes=()):
        E = self.eng[eng]
        self._waits(E, reads, writes)
        idx = E.count
        E.count += 1
        E.ops.append(("o", fn, idx))
        key = ("E", eng)
        for b in writes:
            if b.dram:
                b.lw[key] = idx
            else:
                b.lw = {key: idx}
                b.rd = {}
        for b in reads:
            b.rd[key] = idx

    def dma(self, q, out, in_, reads, writes):
        E = self.eng[q]
        self._waits(E, reads, writes)
        cand = [b for b in list(writes) + list(reads) if not b.dram]
        sb = cand[0]
        self._get_sem(sb, q)
        sb.cnt += 16
        c = sb.cnt
        key = ("S", id(sb.sem))
        E.ops.append(("d", out, in_, sb.sem))
        for b in writes:
            if b.dram:
                b.lw[key] = c
            else:
                b.lw = {key: c}
                b.rd = {}
        for b in reads:
            b.rd[key] = c

    def barrier(self):
        ev = {}
        for n in ("pe", "act", "dve", "pool"):
            if self.eng[n].count > 0:
                ev[("E", n)] = self.eng[n].count - 1
        for b in self.sembufs:
            if b.sem is not None and b.cnt > 0:
                ev[("S", id(b.sem))] = b.cnt
        for n, E in self.eng.items():
            deps = dict(ev)
            for k, v in deps.items():
                if k[0] == "E" and k[1] == n:
                    continue
                if E.waited.get(k, -1) >= v:
                    continue
                E.waited[k] = v
                if k[0] == "E":
                    self.eng[k[1]].needed.add(v)
                E.ops.append(("w", k, v))

    def finalize(self):
        nc = self.nc
        engsems = {}
        rank = {}
        for name, E in self.eng.items():
            nd = sorted(E.needed)
            rank[name] = {idx: r for r, idx in enumerate(nd)}
            nsem = (len(nd) + EPOCH - 1) // EPOCH
            lst = []
            for i in range(nsem):
                cm = nc.semaphore("e_%s_%d" % (name, i))
                lst.append(cm.__enter__())
                self._sem_ctx.append(cm)
            engsems[name] = lst
        engobj = {"pe": nc.tensor, "act": nc.scalar, "dve": nc.vector, "pool": nc.gpsimd, "sp": nc.sync}

        def replay(E, e):
            for o in E.ops:
                if o[0] == "w":
                    k, v = o[1], o[2]
                    if k[0] == "E":
                        r = rank[k[1]][v]
                        e.wait_ge(engsems[k[1]][r // EPOCH], r % EPOCH + 1)
                    else:
                        e.wait_ge(self.sems[k[1]], v)
                elif o[0] == "o":
                    inst = o[1](e)
                    r = rank[E.name].get(o[2])
                    if r is not None:
                        inst.then_inc(engsems[E.name][r // EPOCH], 1)
                else:
                    e.dma_start(out=o[1], in_=o[2]).then_inc(o[3], 16)

        with nc.Block() as block:
            @block.tensor
            def _(e):
                replay(self.eng["pe"], e)

            @block.scalar
            def _(e):
                replay(self.eng["act"], e)

            @block.vector
            def _(e):
                replay(self.eng["dve"], e)

            @block.gpsimd
            def _(e):
                replay(self.eng["pool"], e)

            @block.sync
            def _(e):
                replay(self.eng["sp"], e)

    def close(self):
        while self._ctx:
            self._ctx.pop().__exit__(None, None, None)
        while self._sem_ctx:
            self._sem_ctx.pop().__exit__(None, None, None)


def AP(buf, offset, pairs):
    return bass.AP(buf.t if isinstance(buf, Buf) else buf, offset, [list(p) for p in pairs])


SPL = dict(gq=0, gk=512, gv=1024, gz=2048, lr=3072, dq=3104, dk=4128, dv=5152, dz=6176,
           ab=7200, g11=7232, g12=8256)
PIN = 9280
FFH = 2816


class Prog:
    def __init__(self, W, dbg=()):
        self.W = W
        self.L = 128 * W
        self.NT = CTX + self.L
        self.NCH = self.NT // 128
        self.dbg = set(dbg)
        nc = bass.Bass("TRN2", target_bir_lowering=False)
        self.nc = nc
        self.kb = KB(nc)
        kb = self.kb
        L, NT = self.L, self.NT
        ein = lambda n, s: kb.dram(n, s, F32, kind="ExternalInput")
        self.x = ein("x", [L, DM])
        self.ctx = ein("ctx", [CTX, DM])
        self.cT = ein("cT", [128, 8])
        self.cctxT = ein("cctxT", [128, 8])
        self.w_mod = ein("w_mod", [DM, 6 * DM])
        self.b_mod = ein("b_mod", [1, 6 * DM])
        self.n1bc = ein("n1bc", [128, DM])
        self.n2bc = ein("n2bc", [128, DM])
        self.fnbc = ein("fnbc", [128, DM])
        self.w_in = ein("w_in", [DM, PIN])
        self.lrw = ein("lrw", [2, 33, 512])
        self.glag = ein("glag", [128, DM])
        self.gdng = ein("gdng", [128, DM])
        self.cwT = ein("cwT", [128, 120])
        self.alog = ein("alog", [128, 16])
        self.dtb = ein("dtb", [128, 16])
        self.w_out = ein("w_out", [DM, DM])
        self.wg = ein("wg", [DM, FFH])
        self.wu = ein("wu", [DM, FFH])
        self.wd = ein("wd", [FFH, DM])
        self.identf_d = ein("identf", [128, 128])
        self.cumU = ein("cumU", [6, 128, 128])
        self.gmask = ein("gmask", [2, 128, 128])
        self.dmask = ein("dmask", [4, 128, 128])
        self.sel_d = ein("sel", [96, 32 * 128])
        self.lvlm = ein("lvlm", [14, 128, 128])
        self.out = kb.dram("out", [L, DM], F32, kind="ExternalOutput")

        def scr(n, s, dt):
            return kb.dram(n, s, dt, kind="ExternalOutput" if (n in self.dbg or n.startswith("dbg")) else "Internal")
        self.gq = [scr("gq%d" % d, [4, 128, NT], BF16) for d in range(2)]
        self.gk = [scr("gk%d" % d, [4, 128, NT], BF16) for d in range(2)]
        self.gkh = [scr("gkh%d" % d, [NT, 512], BF16) for d in range(2)]
        self.gv = scr("gv", [NT, DM], BF16)
        self.go = [scr("go%d" % d, [NT, DM], F32) for d in range(2)]
        self.dqT = scr("dqT", [8, 128, NT], BF16)
        self.dkt = scr("dkt", [NT, DM], BF16)
        self.du = [scr("du%d" % d, [NT, DM], F32) for d in range(2)]
        self.dw = [scr("dw%d" % d, [8, 128, NT], BF16) for d in range(2)]
        self.daq = [scr("daq%d" % d, [NT, DM], BF16) for d in range(2)]
        self.do = [scr("do%d" % d, [NT, DM], F32) for d in range(2)]
        self.dx1 = scr("dx1", [L, DM], F32)
        if "dbgX" in self.dbg:
            self.dbgX = scr("dbgX", [128, DM], BF16)
            self.dbgXT = scr("dbgXT", [128, DM], BF16)
            self.dbgMT = scr("dbgMT", [128, DM], BF16)
            self.dbg.update(["dbgXT", "dbgMT"])

        self.PP = [kb.ps("pp%d" % i, [128, 1024], F32) for i in range(4)]
        self.PPb = [p.bitcast(BF16) for p in self.PP]
        self.bank = [Buf(self.PP[i // 2], "bank%d" % i) for i in range(8)]

    def pf(self, b, c0, n, p=128):
        return AP(self.PP[b // 2], (b % 2) * 512 + c0, [[1024, p], [1, n]])

    def pb16(self, b, c0, n, p=128):
        return AP(self.PPb[b // 2], (b % 2) * 1024 + c0, [[2048, p], [1, n]])

    def mm(self, out, lhsT, rhs, start, stop, reads, writes):
        self.kb.op("pe", lambda e: e.matmul(out, lhsT=lhsT, rhs=rhs, start=start, stop=stop), reads, writes)

    def tr(self, out, in_, ident, reads, writes):
        self.kb.op("pe", lambda e: e.transpose(out=out, in_=in_, identity=ident), reads, writes)

    def act(self, out, in_, func, reads, writes, scale=None, bias=None, accum=None):
        kw = {}
        if scale is not None:
            kw["scale"] = scale
        if bias is not None:
            kw["bias"] = bias
        if accum is not None:
            kw["accum_out"] = accum
        self.kb.op("act", lambda e: e.activation(out=out, in_=in_, func=func, **kw), reads, writes)

    def tt(self, out, in0, in1, op, reads, writes):
        self.kb.op("dve", lambda e: e.tensor_tensor(out=out, in0=in0, in1=in1, op=op), reads, writes)

    def ptt(self, out, in0, in1, op, reads, writes):
        self.kb.op("pool", lambda e: e.tensor_tensor(out=out, in0=in0, in1=in1, op=op), reads, writes)

    def stt(self, out, in0, scalar, in1, op0, op1, reads, writes):
        self.kb.op("dve", lambda e: e.scalar_tensor_tensor(out=out, in0=in0, scalar=scalar, in1=in1,
                                                           op0=op0, op1=op1), reads, writes)

    def ts(self, out, in0, s1, s2, op0, op1, reads, writes):
        if s2 is None:
            self.kb.op("dve", lambda e: e.tensor_scalar(out=out, in0=in0, scalar1=s1, scalar2=None, op0=op0),
                       reads, writes)
        else:
            self.kb.op("dve", lambda e: e.tensor_scalar(out=out, in0=in0, scalar1=s1, scalar2=s2, op0=op0, op1=op1),
                       reads, writes)

    def cp(self, out, in_, reads, writes):
        self.kb.op("dve", lambda e: e.tensor_copy(out=out, in_=in_), reads, writes)

    def recip(self, out, in_, reads, writes):
        self.kb.op("dve", lambda e: e.reciprocal(out=out, in_=in_), reads, writes)

    def memset(self, ap, val, writes):
        self.kb.op("dve", lambda e: e.memset(ap, val), [], writes)

    def load(self, out, in_, src, dst, q="sp"):
        self.kb.dma(q, out, in_, [src], [dst])

    def store(self, out, in_, src, dst, q="pool"):
        self.kb.dma(q, out, in_, [src], [dst])

    def wload(self, dst, src, col0, ncols, nk=8, rowlen=None):
        rl = rowlen
        step = 4
        for k0 in range(0, nk, step):
            kn = min(step, nk - k0)
            self.kb.dma("pool", AP(dst, k0 * ncols, [[nk * ncols, 128], [ncols, kn], [1, ncols]]),
                        AP(src, k0 * 128 * rl + col0, [[rl, 128], [128 * rl, kn], [1, ncols]]), [src], [dst])

    def consts(self):
        kb = self.kb
        self.identf = kb.sb("identf", [128, 128], F32)
        self.identb = kb.sb("identb", [128, 128], BF16)
        self.load(self.identf[:, :], self.identf_d[:, :], self.identf_d, self.identf)
        self.kb.dma("pool", self.identb[:, :], self.identf_d[:, :], [self.identf_d], [self.identb])
        self.onesf = kb.sb("onesf", [128, 128], F32)
        self.onesb = kb.sb("onesb", [128, 128], BF16)
        self.memset(self.onesf[:, :], 1.0, [self.onesf])
        self.memset(self.onesb[:, :], 1.0, [self.onesb])
        self.junk = kb.sb("junk", [128, 256], BF16)
        self.ssq = kb.sb("ssq", [128, 16], F32)
        self.rsd = kb.sb("rsd", [128, 16], F32)
        self.ntmp = kb.sb("ntmp", [128, DM], F32)
        self.hnb = kb.sb("hnb", [128, DM], BF16)

    def phase0(self):
        kb = self.kb
        m = kb.mark()
        cT = kb.sb("cTs", [128, 8], F32)
        ccT = kb.sb("ccTs", [128, 8], F32)
        bm = kb.sb("bm", [1, 6 * DM], F32)
        n1 = kb.sb("n1", [128, DM], F32)
        n2 = kb.sb("n2", [128, DM], F32)
        SL = kb.sb("SL", [128, 8, 128], F32)
        SC = kb.sb("SC", [128, 8, 128], F32)
        wm = [kb.sb("wm%d" % i, [128, 8, 512], F32) for i in range(2)]
        self.load(cT[:, :], self.cT[:, :], self.cT, cT)
        self.load(ccT[:, :], self.cctxT[:, :], self.cctxT, ccT)
        self.load(bm[:, :], self.b_mod[:, :], self.b_mod, bm)
        self.load(n1[:, :], self.n1bc[:, :], self.n1bc, n1)
        self.load(n2[:, :], self.n2bc[:, :], self.n2bc, n2)
        for kc in range(8):
            self.act(SL[:, kc, :], self.onesf[:, :], AF.Silu, [self.onesf, cT], [SL], scale=cT[:, kc:kc + 1])
            self.act(SC[:, kc, :], self.onesf[:, :], AF.Silu, [self.onesf, ccT], [SC], scale=ccT[:, kc:kc + 1])
        for g in range(12):
            w = wm[g % 2]
            for k0 in (0, 4):
                self.load(AP(w, k0 * 512, [[8 * 512, 128], [512, 4], [1, 512]]),
                          AP(self.w_mod, k0 * 128 * 6 * DM + g * 512, [[6 * DM, 128], [128 * 6 * DM, 4], [1, 512]]),
                          self.w_mod, w)
            variants = [(SL, self.modL1 if g < 6 else self.modL2, 0, (g % 6) * 512)]
            if g < 4:
                variants.append((SC, self.modC, 1, g * 512))
            for (S_, dst, bi, dc0) in variants:
                bk = (g * 2 + bi) % 8
                for kc in range(8):
                    self.mm(self.pf(bk, 0, 512), S_[:, kc, :], w[:, kc, :], kc == 0, False, [S_, w], [self.bank[bk]])
                self.mm(self.pf(bk, 0, 512), self.onesf[0:1, :], bm[0:1, g * 512:(g + 1) * 512], False, True,
                        [self.onesf, bm], [self.bank[bk]])
                self.act(dst[:, dc0:dc0 + 512], self.pf(bk, 0, 512), AF.Copy, [self.bank[bk]], [dst])
        for (dst, nw, c0) in ((self.modL1, n1, DM), (self.modL2, n2, DM), (self.modC, n1, DM)):
            self.stt(dst[:, c0:c0 + DM], dst[:, c0:c0 + DM], 1.0, nw[:, :], ALU.add, ALU.mult, [dst, nw], [dst])
        kb.barrier()
        kb.release(m)
        ML, ML2, MC = self.modL1, self.modL2, self.modC
        self.modL = ML
        self.B1, self.A1, self.G1 = ML[:, 0:DM], ML[:, DM:2 * DM], ML[:, 2 * DM:3 * DM]
        self.B2, self.A2, self.G2 = ML2[:, 0:DM], ML2[:, DM:2 * DM], ML2[:, 2 * DM:3 * DM]
        self.B1c, self.A1c = MC[:, 0:DM], MC[:, DM:2 * DM]

    def norm_T(self, xt, xbuf, A, B, mbuf, hnT, Tn, sub, bk):
        ss, rs = self.ssq, self.rsd
        self.act(self.ntmp[:, :], xt, AF.Square, [xbuf], [self.ntmp, ss], accum=ss[:, 0:1])
        self.act(rs[:, 0:1], ss[:, 0:1], AF.Ln, [ss], [rs], scale=1.0 / DM, bias=self.epsb[:, 0:1])
        self.act(rs[:, 1:2], rs[:, 0:1], AF.Exp, [rs], [rs], scale=-0.5)
        self.stt(self.ntmp[:, :], xt, rs[:, 1:2], A, ALU.mult, ALU.mult, [xbuf, rs, mbuf], [self.ntmp])
        self.tt(self.hnb[:, :], self.ntmp[:, :], B, ALU.add, [self.ntmp, mbuf], [self.hnb])
        for kc in range(8):
            self.tr(self.pb16(bk, kc * 128, 128), self.hnb[:, kc * 128:(kc + 1) * 128], self.identb[:, :],
                    [self.hnb, self.identb], [self.bank[bk]])
        self.act(AP(hnT, sub * 128, [[8 * Tn, 128], [Tn, 8], [1, 128]]),
                 AP(self.PPb[bk // 2], (bk % 2) * 1024, [[2048, 128], [128, 8], [1, 128]]), AF.Copy,
                 [self.bank[bk]], [hnT])

    def tiles(self, colmajor):
        res = [(0, 0, True, None)]
        for i in range(self.L // TT):
            res.append((i + 1, CTX + i * TT, False, i))
        return res

    def x_src(self, is_ctx, li, colmajor):
        if is_ctx:
            return AP(self.ctx, 0, [[DM, 128], [128 * DM, 2], [1, DM]]), self.ctx
        if not colmajor:
            return AP(self.x, li * TT * DM, [[DM, 128], [128 * DM, 2], [1, DM]]), self.x
        c0 = li * 2
        return AP(self.x, c0 * DM, [[self.W * DM, 128], [DM, 2], [1, DM]]), self.x

    def phaseA(self):
        kb = self.kb
        m = kb.mark()
        NT = self.NT
        Wqk = kb.sb("Wqk", [128, 8, 1024], BF16)
        Wv = kb.sb("Wv", [128, 8, 1024], BF16)
        Wlr = kb.sb("Wlr", [128, 8, 32], BF16)
        self.wload(Wqk, self.w_in, SPL["gq"], 1024, rowlen=PIN)
        self.wload(Wv, self.w_in, SPL["gv"], 1024, rowlen=PIN)
        self.wload(Wlr, self.w_in, SPL["lr"], 32, rowlen=PIN)
        LRW = [kb.sb("LRW%d" % d, [33, 512], F32) for d in range(2)]
        U = [kb.sb("U%d" % d, [128, 128], F32) for d in range(2)]
        for d in range(2):
            self.load(LRW[d][:, :], AP(self.lrw, d * 33 * 512, [[512, 33], [1, 512]]), self.lrw, LRW[d])
            self.load(U[d][:, :], AP(self.cumU, d * 128 * 128, [[128, 128], [1, 128]]), self.cumU, U[d])
        xt = [kb.sb("xtA%d" % i, [128, 2, DM], F32) for i in range(2)]
        hnT = [kb.sb("hnTA%d" % i, [128, 8, TT], BF16) for i in range(2)]
        qkTs = [kb.sb("qkT%d" % i, [128, 8, TT], F32) for i in range(2)]
        vtok = [kb.sb("vtok%d" % i, [128, 2, DM], BF16) for i in range(2)]
        lraug = kb.sb("lraug", [33, TT], F32)
        self.memset(lraug[32:33, :], 1.0, [lraug])
        e1s = [kb.sb("e1_%d" % i, [128, 512], F32) for i in range(2)]
        sps = [kb.sb("sp_%d" % i, [128, 512], F32) for i in range(2)]
        eGs = [kb.sb("eG_%d" % i, [128, 512], F32) for i in range(2)]
        enGs = [kb.sb("enG_%d" % i, [128, 512], F32) for i in range(2)]
        dKs = [kb.sb("dK_%d" % i, [128, 512], F32) for i in range(2)]
        gends = [kb.sb("gend_%d" % i, [128, 4], F32) for i in range(2)]
        khTs = [kb.sb("khT_%d" % i, [128, 512], BF16) for i in range(2)]
        qs = [kb.sb("qs%d" % d, [128, 4, TT], BF16) for d in range(2)]
        ks = [kb.sb("ks%d" % d, [128, 4, TT], BF16) for d in range(2)]
        khs = [kb.sb("khs%d" % d, [128, 2, 512], BF16) for d in range(2)]
        print("phaseA sbuf remaining", self.nc.sbuf_bytes_remaining)
        tl = self.tiles(False)
        src, sbuf = self.x_src(tl[0][2], tl[0][3], False)
        self.load(xt[0][:, :, :], src, sbuf, xt[0])
        def do_norm(tj):
            is_c = tl[tj][2]
            A, B, mb = (self.A1c, self.B1c, self.modC) if is_c else (self.A1, self.B1, self.modL)
            for sub in range(2):
                self.norm_T(xt[tj % 2][:, sub, :], xt[tj % 2], A, B, mb, hnT[tj % 2], TT, sub, 0)

        if len(tl) > 1:
            src, sbuf = self.x_src(tl[1][2], tl[1][3], False)
            self.load(xt[1][:, :, :], src, sbuf, xt[1])
        do_norm(0)
        for ti, (idx, tok0, is_ctx, li) in enumerate(tl):
            X = xt[ti % 2]
            H = hnT[ti % 2]
            qkT = qkTs[ti % 2]
            for fc in range(8):
                bk = 1 + fc % 2
                for kc in range(8):
                    self.mm(self.pf(bk, 0, TT), Wqk[:, kc, fc * 128:(fc + 1) * 128], H[:, kc, :], kc == 0, kc == 7,
                            [Wqk, H], [self.bank[bk]])
                self.act(qkT[:, fc, :], self.pf(bk, 0, TT), AF.Copy, [self.bank[bk]], [qkT],
                         scale=(128.0 ** -0.5 if fc < 4 else 1.0))
            VT = vtok[ti % 2]
            for sub in range(2):
                for half in range(2):
                    bk = 3 + half
                    for kc in range(8):
                        self.mm(self.pf(bk, 0, 512), H[:, kc, sub * 128:(sub + 1) * 128],
                                Wv[:, kc, half * 512:(half + 1) * 512], kc == 0, kc == 7, [H, Wv], [self.bank[bk]])
                    self.cp(VT[:, sub, half * 512:(half + 1) * 512], self.pf(bk, 0, 512), [self.bank[bk]], [VT])
            self.store(AP(self.gv, tok0 * DM, [[DM, 128], [128 * DM, 2], [1, DM]]), VT[:, :, :], VT, self.gv)
            for kc in range(8):
                self.mm(self.pf(5, 0, TT, 32), Wlr[:, kc, :], H[:, kc, :], kc == 0, kc == 7, [Wlr, H], [self.bank[5]])
            self.act(lraug[0:32, :], self.pf(5, 0, TT, 32), AF.Copy, [self.bank[5]], [lraug])
            if ti + 1 < len(tl):
                do_norm(ti + 1)
                if ti + 2 < len(tl):
                    src, sbuf = self.x_src(tl[ti + 2][2], tl[ti + 2][3], False)
                    self.load(xt[ti % 2][:, :, :], src, sbuf, xt[ti % 2])
            for sub in range(2):
                ch = (tok0 // 128) + sub
                for d in range(2):
                    e1, sp, eG, enG, dK, gend, khT = (b[d] for b in (e1s, sps, eGs, enGs, dKs, gends, khTs))
                    self.mm(self.pf(6, 0, 512), lraug[0:33, sub * 128:(sub + 1) * 128], LRW[d][:, :], True, True,
                            [lraug, LRW[d]], [self.bank[6]])
                    self.act(e1[:, :], self.pf(6, 0, 512), AF.Exp, [self.bank[6]], [e1], scale=-1.0)
                    self.act(sp[:, :], e1[:, :], AF.Ln, [e1], [sp], bias=self.oneb[:, 0:1])
                    for h in range(4):
                        self.mm(self.pf(7, h * 128, 128), sp[:, h * 128:(h + 1) * 128], U[d][:, :], True, True,
                                [sp, U[d]], [self.bank[7]])
                    G = self.pf(7, 0, 512)
                    ecol = 127 if d == 0 else 0
                    self.act(eG[:, :], G, AF.Exp, [self.bank[7]], [eG])
                    self.act(enG[:, :], G, AF.Exp, [self.bank[7]], [enG], scale=-1.0)
                    self.act(gend[:, :], AP(self.PP[3], 512 + ecol, [[1024, 128], [128, 4]]), AF.Copy,
                             [self.bank[7]], [gend])
                    for h in range(4):
                        self.act(dK[:, h * 128:(h + 1) * 128], self.pf(7, h * 128, 128), AF.Exp,
                                 [self.bank[7], gend], [dK], scale=-1.0, bias=gend[:, h:h + 1])
                    self.cp(AP(self.Egla, (d * self.NCH + ch) * 4, [[2 * self.NCH * 4, 128], [1, 4]]),
                            AP(eG, ecol, [[512, 128], [128, 4]]), [eG], [self.Egla])
                    qv = AP(qkT, sub * 128, [[8 * TT, 128], [TT, 4], [1, 128]])
                    kv = AP(qkT, 4 * TT + sub * 128, [[8 * TT, 128], [TT, 4], [1, 128]])
                    g3 = lambda t: AP(t, 0, [[512, 128], [128, 4], [1, 128]])
                    self.tt(AP(qs[d], sub * 128, [[4 * TT, 128], [TT, 4], [1, 128]]), qv, g3(eG), ALU.mult,
                            [qkT, eG], [qs[d]])
                    self.tt(AP(ks[d], sub * 128, [[4 * TT, 128], [TT, 4], [1, 128]]), kv, g3(enG), ALU.mult,
                            [qkT, enG], [ks[d]])
                    self.tt(g3(khT), kv, g3(dK), ALU.mult, [qkT, dK], [khT])
                    for h in range(4):
                        self.tr(self.pb16(5, h * 128, 128), khT[:, h * 128:(h + 1) * 128], self.identb[:, :],
                                [khT, self.identb], [self.bank[5]])
                    self.cp(khs[d][:, sub, :], self.pb16(5, 0, 512), [self.bank[5]], [khs[d]])
            for d in range(2):
                self.store(AP(self.gq[d], tok0, [[NT, 128], [128 * NT, 4], [1, TT]]), qs[d][:, :, :], qs[d], self.gq[d])
                self.store(AP(self.gk[d], tok0, [[NT, 128], [128 * NT, 4], [1, TT]]), ks[d][:, :, :], ks[d], self.gk[d])
                self.store(AP(self.gkh[d], tok0 * 512, [[512, 128], [128 * 512, 2], [1, 512]]), khs[d][:, :, :],
                           khs[d], self.gkh[d])
        kb.barrier()
        kb.release(m)

    def gla_scan(self):
        kb = self.kb
        m = kb.mark()
        NT, NCH = self.NT, self.NCH
        S = [kb.sb("Sg%d" % d, [128, 4, 256], F32) for d in range(2)]
        Sb = [kb.sb("Sgb%d" % d, [128, 4, 256], BF16) for d in range(2)]
        msk = [kb.sb("gm%d" % d, [128, 128], F32) for d in range(2)]
        for d in range(2):
            self.memset(S[d][:, :, :], 0.0, [S[d]])
            self.memset(Sb[d][:, :, :], 0.0, [Sb[d]])
            self.load(msk[d][:, :], AP(self.gmask, d * 128 * 128, [[128, 128], [1, 128]]), self.gmask, msk[d])
        nb = 2
        qt = [[kb.sb("sq%d_%d" % (d, i), [128, 4, 128], BF16) for i in range(nb)] for d in range(2)]
        kt = [[kb.sb("sk%d_%d" % (d, i), [128, 4, 128], BF16) for i in range(nb)] for d in range(2)]
        kh = [[kb.sb("skh%d_%d" % (d, i), [128, 512], BF16) for i in range(nb)] for d in range(2)]
        vv = [[kb.sb("sv%d_%d" % (d, i), [128, DM], BF16) for i in range(nb)] for d in range(2)]
        AT = [kb.sb("AT%d" % d, [128, 4, 128], BF16) for d in range(2)]
        osb = [kb.sb("osb%d" % d, [128, DM], F32) for d in range(2)]
        order = [list(range(NCH)), [1, 0] + list(range(NCH - 1, 1, -1))]

        def issue_loads(s):
            for d in range(2):
                c = order[d][s]
                t0 = c * 128
                i = s % nb
                self.load(qt[d][i][:, :, :], AP(self.gq[d], t0, [[NT, 128], [128 * NT, 4], [1, 128]]), self.gq[d], qt[d][i])
                self.load(kt[d][i][:, :, :], AP(self.gk[d], t0, [[NT, 128], [128 * NT, 4], [1, 128]]), self.gk[d], kt[d][i])
                self.load(kh[d][i][:, :], AP(self.gkh[d], t0 * 512, [[512, 128], [1, 512]]), self.gkh[d], kh[d][i])
                self.load(vv[d][i][:, :], AP(self.gv, t0 * DM, [[DM, 128], [1, DM]]), self.gv, vv[d][i])

        issue_loads(0)
        for s in range(NCH):
            if s + 1 < NCH:
                issue_loads(s + 1)
            for d in range(2):
                c = order[d][s]
                i = s % nb
                Q, K_, KH, V = qt[d][i], kt[d][i], kh[d][i], vv[d][i]
                bA = d
                bO = 2 + 2 * d
                bS = 6
                for h in range(4):
                    self.mm(self.pf(bA, h * 128, 128), K_[:, h, :], Q[:, h, :], True, True, [K_, Q], [self.bank[bA]])
                self.tt(AT[d][:, :, :], AP(self.PP[bA // 2], (bA % 2) * 512, [[1024, 128], [128, 4], [1, 128]]),
                        AP(msk[d], 0, [[128, 128], [0, 4], [1, 128]]), ALU.mult, [self.bank[bA], msk[d]], [AT[d]])
                for h in range(4):
                    bk = bO + h // 2
                    o_ap = self.pf(bk, (h % 2) * 256, 256)
                    self.mm(o_ap, AT[d][:, h, :], V[:, h * 256:(h + 1) * 256], True, False, [AT[d], V], [self.bank[bk]])
                    self.mm(o_ap, Q[:, h, :], Sb[d][:, h, :], False, True, [Q, Sb[d]], [self.bank[bk]])
                self.act(osb[d][:, :], AP(self.PP[bO // 2], 0, [[1024, 128], [1, 1024]]), AF.Copy,
                         [self.bank[bO], self.bank[bO + 1]], [osb[d]])
                self.store(AP(self.go[d], c * 128 * DM, [[DM, 128], [1, DM]]), osb[d][:, :], osb[d], self.go[d])
                for h in range(4):
                    bk = bS + h // 2
                    self.mm(self.pf(bk, (h % 2) * 256, 256), KH[:, h * 128:(h + 1) * 128], V[:, h * 256:(h + 1) * 256],
                            True, True, [KH, V], [self.bank[bk]])
                for h in range(4):
                    bk = bS + h // 2
                    self.stt(S[d][:, h, :], S[d][:, h, :],
                             AP(self.Egla, (d * NCH + c) * 4 + h, [[2 * NCH * 4, 128], [1, 1]]),
                             self.pf(bk, (h % 2) * 256, 256), ALU.mult, ALU.add,
                             [S[d], self.Egla, self.bank[bk]], [S[d]])
                self.act(Sb[d][:, :, :], S[d][:, :, :], AF.Copy, [S[d]], [Sb[d]])
        kb.barrier()
        kb.release(m)

    def phaseB(self):
        kb = self.kb
        m = kb.mark()
        NT, NCH = self.NT, self.NCH
        Wgd = kb.sb("Wgd", [128, 8, 3072], BF16)
        Wab = kb.sb("Wab", [128, 8, 32], BF16)
        self.wload(Wgd, self.w_in, SPL["dq"], 3072, rowlen=PIN)
        self.wload(Wab, self.w_in, SPL["ab"], 32, rowlen=PIN)
        cw = kb.sb("cw", [128, 24, 5], F32)
        self.load(AP(cw, 0, [[120, 128], [1, 120]]), self.cwT[:, :], self.cwT, cw)
        negA = kb.sb("negA", [128, 16], F32)
        dtb = kb.sb("dtbs", [128, 16], F32)
        self.load(negA[:, :], self.alog[:, :], self.alog, negA)
        self.load(dtb[:, :], self.dtb[:, :], self.dtb, dtb)
        self.act(negA[:, :], negA[:, :], AF.Exp, [negA], [negA])
        self.kb.op("act", lambda e: e.mul(negA[:, :], negA[:, :], -1.0), [negA], [negA])
        CU = [kb.sb("CU%d" % i, [128, 128], F32) for i in range(4)]
        for i in range(4):
            self.load(CU[i][:, :], AP(self.cumU, (2 + i) * 128 * 128, [[128, 128], [1, 128]]), self.cumU, CU[i])
        UFi, UBi, UBs, UFs = CU
        MK = [kb.sb("MK%d" % i, [128, 128], BF16) for i in range(4)]
        for i in range(4):
            self.kb.dma("pool", MK[i][:, :], AP(self.dmask, i * 128 * 128, [[128, 128], [1, 128]]), [self.dmask], [MK[i]])
        Minc = [MK[0], MK[1]]
        Mlt, Mgt = MK[2], MK[3]
        sel = kb.sb("sel", [96, 32, 128], BF16)
        self.kb.dma("pool", AP(sel, 0, [[4096, 96], [1, 4096]]), self.sel_d[:, :], [self.sel_d], [sel])
        xt = [kb.sb("xtB%d" % i, [128, 2, DM], F32) for i in range(1)] * 2
        hnT = kb.sb("hnTB", [128, 8, TT], BF16)
        acc = [kb.sb("acc%d" % i, [128, TT], F32) for i in range(5)]
        sg = [kb.sb("sgb%d" % i, [128, TT], F32) for i in range(2)]
        sil = [kb.sb("sil%d" % i, [128, TT], F32) for i in range(4)]
        sq = [kb.sb("sqb%d" % i, [128, TT], BF16) for i in range(2)]
        rn = [kb.sb("rn%d" % i, [128, TT], F32) for i in range(2)]
        vTb = [kb.sb("vTb%d" % i, [128, TT], BF16) for i in range(2)]
        dqs = kb.sb("dqs", [128, 8, TT], BF16)
        dks = kb.sb("dks", [128, 8, TT], BF16)
        vtk = kb.sb("vtk", [128, 2, DM], BF16)
        ktk = kb.sb("ktk", [128, 2, DM], BF16)
        gbuf = []
        for gi in range(2):
            gbuf.append((kb.sb("g16_%d" % gi, [128, 16], F32), kb.sb("t16_%d" % gi, [128, 16], F32),
                         kb.sb("beta_%d" % gi, [128, 16], F32), kb.sb("lnb_%d" % gi, [128, 16], F32),
                         kb.sb("ba_%d" % gi, [128, 16], F32), kb.sb("R_%d" % gi, [128, 32], F32),
                         kb.sb("negG_%d" % gi, [128, 16], F32), kb.sb("R1_%d" % gi, [128, 32], F32),
                         kb.sb("Rs_%d" % gi, [128, 96], BF16), kb.sb("rows_%d" % gi, [96, 128], BF16),
                         kb.sb("nrows_%d" % gi, [96, 128], BF16)))
        Dm = kb.sb("Dm", [128, 8, 128], F32)
        E1 = kb.sb("E1m", [128, 8, 128], F32)
        E2 = Dm
        aqs = kb.sb("aqs", [128, 8, 128], BF16)
        X0 = kb.sb("X0n", [128, 8, 128], BF16)
        LT = [kb.sb("LTn%d" % d, [128, 8, 128], BF16) for d in range(2)]
        Dn = [[kb.sb("Dn%d_%d" % (d, g), [128, 4, 128], BF16) for g in range(2)] for d in range(2)]
        DTn = [[kb.sb("DTn%d_%d" % (d, g), [128, 4, 128], BF16) for g in range(2)] for d in range(2)]
        Wn = [[kb.sb("Wn%d_%d" % (d, g), [128, 4, 128], BF16) for g in range(2)] for d in range(2)]
        MT = [kb.sb("MTn%d" % d, [128, 8, 128], BF16) for d in range(2)]
        MaT = [kb.sb("MaTn%d" % d, [128, 8, 128], BF16) for d in range(2)]
        LM = kb.sb("LM", [128, 14, 128], BF16)
        self.kb.dma("pool", LM[:, :, :], AP(self.lvlm, 0, [[128, 128], [128 * 128, 14], [1, 128]]), [self.lvlm], [LM])
        lm = lambda i: AP(LM, i * 128, [[14 * 128, 128], [0, 8], [1, 128]])
        lm4 = lambda i: AP(LM, i * 128, [[14 * 128, 128], [0, 4], [1, 128]])
        nidb = kb.sb("nidb", [128, 128], BF16)
        self.kb.op("act", lambda e: e.mul(nidb[:, :], self.identb[:, :], -1.0), [self.identb], [nidb])
        us = kb.sb("us", [128, DM], F32)
        ws = kb.sb("wsn", [128, 8, 128], BF16)
        b3 = lambda t, c0: AP(t, c0, [[16, 128], [1, 8], [0, 128]])
        f3 = lambda t: AP(t, 0, [[1024, 128], [128, 8], [1, 128]])
        p3 = lambda k: AP(self.PP[k], 0, [[1024, 128], [128, 8], [1, 128]])
        pbk = lambda k: [self.bank[2 * k], self.bank[2 * k + 1]]
        print("phaseB sbuf remaining", self.nc.sbuf_bytes_remaining)
        tl = self.tiles(True)
        src, sbuf = self.x_src(tl[0][2], tl[0][3], True)
        self.load(xt[0][:, :, :], src, sbuf, xt[0])
        for ti, (idx, tok0, is_ctx, li) in enumerate(tl):
            Xt = xt[ti % 2]
            A, B, mb = (self.A1c, self.B1c, self.modC) if is_ctx else (self.A1, self.B1, self.modL)
            for sub in range(2):
                self.norm_T(Xt[:, sub, :], Xt, A, B, mb, hnT, TT, sub, 0)
            if ti + 1 < len(tl):
                src, sbuf = self.x_src(tl[ti + 1][2], tl[ti + 1][3], True)
                self.load(xt[(ti + 1) % 2][:, :, :], src, sbuf, xt[(ti + 1) % 2])
            def gates(sub):
                ch = tok0 // 128 + sub
                tsl = slice(sub * 128, (sub + 1) * 128)
                g16, t16, beta, lnb, ba, R, negG, R1, Rs, rows, nrows = gbuf[sub]
                for kc in range(8):
                    self.mm(self.pf(6, 0, 32), hnT[:, kc, tsl], Wab[:, kc, :], kc == 0, kc == 7, [hnT, Wab], [self.bank[6]])
                self.tt(t16[:, :], self.pf(6, 0, 16), dtb[:, :], ALU.add, [self.bank[6], dtb], [t16])
                self.act(t16[:, :], t16[:, :], AF.Exp, [t16], [t16])
                self.act(t16[:, :], t16[:, :], AF.Ln, [t16], [t16], bias=self.oneb[:, 0:1])
                self.tt(g16[:, :], t16[:, :], negA[:, :], ALU.mult, [t16, negA], [g16])
                self.act(lnb[:, :], self.pf(6, 16, 16), AF.Exp, [self.bank[6]], [lnb], scale=-1.0)
                self.act(lnb[:, :], lnb[:, :], AF.Ln, [lnb], [lnb], bias=self.oneb[:, 0:1])
                self.kb.op("act", lambda e: e.mul(lnb[:, :], lnb[:, :], -1.0), [lnb], [lnb])
                self.act(beta[:, :], lnb[:, :], AF.Exp, [lnb], [beta])
                self.mm(self.pf(7, 0, 8), UFi[:, :], g16[:, 0:8], True, True, [UFi, g16], [self.bank[7]])
                self.mm(self.pf(7, 8, 8), UBi[:, :], g16[:, 8:16], True, True, [UBi, g16], [self.bank[7]])
                self.mm(self.pf(7, 16, 8), UBs[:, :], g16[:, 0:8], True, True, [UBs, g16], [self.bank[7]])
                self.mm(self.pf(7, 24, 8), UFs[:, :], g16[:, 8:16], True, True, [UFs, g16], [self.bank[7]])
                self.mm(self.pf(7, 32, 16), self.onesf[:, :], g16[:, :], True, True, [self.onesf, g16], [self.bank[7]])
                self.act(R[:, 0:16], self.pf(7, 0, 16), AF.Copy, [self.bank[7]], [R])
                self.act(AP(self.GDa, ch * 16, [[NCH * 16, 128], [1, 16]]), self.pf(7, 0, 16), AF.Exp,
                         [self.bank[7]], [self.GDa])
                self.act(AP(self.GDd, ch * 16, [[NCH * 16, 128], [1, 16]]), self.pf(7, 16, 16), AF.Exp,
                         [self.bank[7]], [self.GDd])
                self.act(AP(self.GDe, ch * 16, [[NCH * 16, 128], [1, 16]]), self.pf(7, 32, 16), AF.Exp,
                         [self.bank[7]], [self.GDe])
                self.tt(ba[:, :], beta[:, :], AP(self.GDa, ch * 16, [[NCH * 16, 128], [1, 16]]), ALU.mult,
                        [beta, self.GDa], [ba])
                self.tt(R[:, 16:32], R[:, 0:16], lnb[:, :], ALU.add, [R, lnb], [R])
                self.kb.op("act", lambda e: e.mul(negG[:, :], R[:, 0:16], -1.0), [R], [negG])
                self.cp(Rs[:, 0:32], R[:, :], [R], [Rs])
                self.tt(R1[:, :], R[:, :], Rs[:, 0:32], ALU.subtract, [R, Rs], [R1])
                self.cp(Rs[:, 32:64], R1[:, :], [R1], [Rs])
                self.tt(R1[:, :], R1[:, :], Rs[:, 32:64], ALU.subtract, [R1, Rs], [R1])
                self.cp(Rs[:, 64:96], R1[:, :], [R1], [Rs])
                self.tr(self.pb16(6, 0, 128, 96), Rs[:, :], self.identb[:, :], [Rs, self.identb], [self.bank[6]])
                self.act(rows[:, :], self.pb16(6, 0, 128, 96), AF.Copy, [self.bank[6]], [rows])
                self.kb.op("act", lambda e: e.mul(nrows[:, :], rows[:, :], -1.0), [rows], [nrows])
            for sub in range(2):
                gates(sub)
            nseg, ls = (1, TT) if is_ctx else (2, 128)

            def stA(fc):
                bk = fc % 4
                for kc in range(8):
                    self.mm(self.pf(bk, 0, TT), Wgd[:, kc, fc * 128:(fc + 1) * 128], hnT[:, kc, :], kc == 0, kc == 7,
                            [Wgd, hnT], [self.bank[bk]])

            def conv_ops(fc):
                bk = fc % 4
                A_ = acc[fc % 5]
                ops = [lambda: self.ts(A_[:, :], self.pf(bk, 0, TT), cw[:, fc, 2:3], None, ALU.mult, None,
                                       [self.bank[bk], cw], [A_])]
                for tap in (0, 1, 3, 4):
                    sh = tap - 2
                    n = ls - abs(sh)
                    o0, i0 = (0, sh) if sh > 0 else (-sh, 0)
                    oap = AP(A_, o0, [[TT, 128], [ls, nseg], [1, n]])
                    iap = AP(self.PP[bk // 2], (bk % 2) * 512 + i0, [[1024, 128], [ls, nseg], [1, n]])
                    ops.append(lambda oap=oap, iap=iap, tap=tap: self.stt(oap, iap, cw[:, fc, tap:tap + 1], oap,
                                                                        ALU.mult, ALU.add, [self.bank[bk], cw, A_], [A_]))
                return ops

            def stB(fa, fb):
                oa = conv_ops(fa) if (fa is not None and 0 <= fa < 24) else None
                ob = conv_ops(fb) if (fb is not None and 0 <= fb < 24) else None
                seq = [(ob, 2), (oa, 0), (ob, 3), (oa, 1), (ob, 4)]
                for (o, i) in seq:
                    if o is not None:
                        o[i]()


            def stCE(fc_c, fc_e):
                okc = fc_c is not None and 0 <= fc_c < 24
                oke = fc_e is not None and 0 <= fc_e < 16
                if okc:
                    A_, G_ = acc[fc_c % 5], sg[fc_c % 2]
                    self.act(G_[:, :], A_[:, :], AF.Exp, [A_], [G_], scale=-1.0)
                if oke:
                    S_, Q_, R_ = sil[fc_e % 4], sq[fc_e % 2], rn[fc_e % 2]
                    bq = 6 + fc_e % 2
                    self.act(Q_[:, :], S_[:, :], AF.Square, [S_], [Q_])
                    self.mm(self.pf(bq, 0, TT), self.onesb[:, :], Q_[:, :], True, True, [self.onesb, Q_], [self.bank[bq]])
                if okc:
                    self.act(G_[:, :], G_[:, :], AF.Ln, [G_], [G_], bias=self.oneb[:, 0:1])
                if oke:
                    self.act(R_[:, :], self.pf(bq, 0, TT), AF.Ln, [self.bank[bq]], [R_], bias=self.epsb[:, 0:1])
                if okc:
                    self.act(G_[:, :], G_[:, :], AF.Exp, [G_], [G_], scale=-1.0)
                if oke:
                    self.act(R_[:, :], R_[:, :], AF.Exp, [R_], [R_], scale=-0.5,
                             bias=(self.lncb[:, 0:1] if fc_e < 8 else self.zerob[:, 0:1]))

            def stD(fc):
                A_, G_ = acc[fc % 5], sg[fc % 2]
                if fc < 16:
                    S_ = sil[fc % 4]
                    self.ptt(S_[:, :], A_[:, :], G_[:, :], ALU.mult, [A_, G_], [S_])
                else:
                    h = fc - 16
                    V_ = vTb[fc % 2]
                    self.ptt(V_[:, :], A_[:, :], G_[:, :], ALU.mult, [A_, G_], [V_])
                    for sub in range(2):
                        self.tr(self.pb16(4 + sub, h * 128, 128), V_[:, sub * 128:(sub + 1) * 128], self.identb[:, :],
                                [V_, self.identb], [self.bank[4 + sub]])

            def stF(fc):
                if fc < 16:
                    h = fc % 8
                    S_, R_ = sil[fc % 4], rn[fc % 2]
                    dst = dqs if fc < 8 else dks
                    self.ptt(dst[:, h, :], S_[:, :], R_[:, :], ALU.mult, [S_, R_], [dst])

            for it in range(24 + 6):
                if it < 24:
                    stA(it)
                stB(it - 1, it - 2)
                stCE(it - 3, it - 5)
                if 0 <= it - 4 < 24:
                    stD(it - 4)
                if 0 <= it - 6 < 24:
                    stF(it - 6)
            for sub in range(2):
                self.cp(vtk[:, sub, :], self.pb16(4 + sub, 0, 1024), [self.bank[4 + sub]], [vtk])
            for sub in range(2):
                for h in range(8):
                    self.tr(self.pb16(4 + sub, h * 128, 128), dks[:, h, sub * 128:(sub + 1) * 128], self.identb[:, :],
                            [dks, self.identb], [self.bank[4 + sub]])
                self.cp(ktk[:, sub, :], self.pb16(4 + sub, 0, 1024), [self.bank[4 + sub]], [ktk])
            self.store(AP(self.dqT, tok0, [[NT, 128], [128 * NT, 8], [1, TT]]), dqs[:, :, :], dqs, self.dqT)
            self.store(AP(self.dkt, tok0 * DM, [[DM, 128], [128 * DM, 2], [1, DM]]), ktk[:, :, :], ktk, self.dkt)
            for sub in range(2):
                ch = tok0 // 128 + sub
                tsl = slice(sub * 128, (sub + 1) * 128)
                g16, t16, beta, lnb, ba, R, negG, R1, Rs, rows, nrows = gbuf[sub]
                for h in range(8):
                    bk = h // 4
                    self.mm(self.pf(bk, (h % 4) * 128, 128), dks[:, h, tsl], dks[:, h, tsl], True, True,
                            [dks], [self.bank[bk]])
                for h in range(8):
                    bk = 2 + h // 4
                    self.mm(self.pf(bk, (h % 4) * 128, 128), dks[:, h, tsl], dqs[:, h, tsl], True, True,
                            [dks, dqs], [self.bank[bk]])
                for d in range(2):
                    mstrT = Mlt if d == 0 else Mgt
                    mstr = Mgt if d == 0 else Mlt
                    specs = [(Dm, 0, rows, nrows, 0, Minc[d]), (E1, 16, rows, nrows, 0, mstrT), (E2, 0, nrows, rows, 16, mstr)]
                    for si, (dst, so, rr, pr, po, mk) in enumerate(specs):
                        pk = 2 + (si % 2)
                        for hg in range(2):
                            bk = 2 * pk + hg
                            c0 = d * 8 + hg * 4
                            self.mm(self.pf(bk, 0, 512), self.identb[:, :], AP(mk, 0, [[128, 128], [0, 4], [1, 128]]),
                                    True, False, [self.identb, mk], [self.bank[bk]])
                            self.mm(self.pf(bk, 0, 512), pr[:, :], AP(sel, (po + c0) * 128, [[32 * 128, 96], [1, 512]]),
                                    False, False, [sel, pr], [self.bank[bk]])
                            for h4i in range(4):
                                self.mm(self.pf(bk, h4i * 128, 128), sel[:, so + c0 + h4i, :], rr[:, :], False, h4i == 3,
                                        [sel, rr], [self.bank[bk]])
                        for hg in range(2):
                            bk = 2 * pk + hg
                            self.act(AP(dst, hg * 512, [[1024, 128], [1, 512]]), self.pf(bk, 0, 512), AF.Exp,
                                     [self.bank[bk]], [dst])
                        if si == 0:
                            self.tt(f3(aqs), p3(1), f3(Dm), ALU.mult, pbk(1) + [Dm], [aqs])
                            self.store(AP(self.daq[d], ch * 128 * DM, [[DM, 128], [1, DM]]), aqs[:, :, :], aqs, self.daq[d])
                    self.tt(f3(LT[d]), p3(0), f3(E1), ALU.mult, pbk(0) + [E1], [LT[d]])
                    self.tt(f3(X0), p3(0), f3(E2), ALU.mult, pbk(0) + [E2], [X0])
                    if "dbgX" in self.dbg and ch == 2 and d == 0:
                        self.store(AP(self.dbgX, 0, [[DM, 128], [1, DM]]), X0[:, :, :], X0, self.dbgX)
                        self.store(AP(self.dbgXT, 0, [[DM, 128], [1, DM]]), LT[d][:, :, :], LT[d], self.dbgXT)
                    for hg in range(2):
                        h4 = lambda t: AP(t, 0, [[512, 128], [128, 4], [1, 128]])
                        idb = AP(self.identb, 0, [[128, 128], [0, 4], [1, 128]])
                        src4 = lambda t: AP(t, hg * 512, [[1024, 128], [128, 4], [1, 128]])
                        W_, D_, DT_ = Wn[d][hg], Dn[d][hg], DTn[d][hg]
                        self.tt(h4(W_), src4(X0), lm4(d * 7), ALU.mult, [X0, LM], [W_])
                        self.tt(h4(D_), h4(W_), idb, ALU.add, [W_, self.identb], [D_])
                        self.tt(h4(W_), src4(LT[d]), lm4((1 - d) * 7), ALU.mult, [LT[d], LM], [W_])
                        self.tt(h4(DT_), h4(W_), idb, ALU.add, [W_, self.identb], [DT_])
                h4 = lambda t: AP(t, 0, [[512, 128], [128, 4], [1, 128]])
                chains = [(d, hg) for d in range(2) for hg in range(2)]
                for lvl in range(1, 7):
                    for (d, hg) in chains:
                        k = d * 2 + hg
                        ba_ = 2 * k
                        W_, D_ = Wn[d][hg], Dn[d][hg]
                        self.mm(self.pf(ba_, 0, 512), self.identb[:, :], AP(nidb, 0, [[128, 128], [0, 4], [1, 128]]),
                                True, False, [self.identb, nidb], [self.bank[ba_]])
                        for h4i in range(4):
                            h = hg * 4 + h4i
                            self.mm(self.pf(ba_, h4i * 128, 128), LT[d][:, h, :], D_[:, h4i, :], False, h4i == 3,
                                    [LT[d], D_], [self.bank[ba_]])
                        self.tt(h4(W_), AP(self.PP[ba_ // 2], (ba_ % 2) * 512, [[1024, 128], [128, 4], [1, 128]]),
                                lm4(d * 7 + lvl), ALU.mult, [self.bank[ba_], LM], [W_])
                    for (d, hg) in chains:
                        k = d * 2 + hg
                        ba_, bb_ = 2 * k, 2 * k + 1
                        W_, D_, DT_ = Wn[d][hg], Dn[d][hg], DTn[d][hg]
                        if lvl < 6:
                            for h4i in range(4):
                                self.mm(self.pf(bb_, h4i * 128, 128), DT_[:, h4i, :], W_[:, h4i, :], True, True,
                                        [DT_, W_], [self.bank[bb_]])
                        for h4i in range(4):
                            self.mm(self.pf(ba_, h4i * 128, 128), W_[:, h4i, :], DT_[:, h4i, :], True, True,
                                    [W_, DT_], [self.bank[ba_]])
                        if lvl < 6:
                            if hg == 0:
                                self.act(AP(D_, 0, [[512, 128], [1, 512]]), self.pf(bb_, 0, 512), AF.Copy, [self.bank[bb_]], [D_])
                                self.cp(AP(DT_, 0, [[512, 128], [1, 512]]), self.pf(ba_, 0, 512), [self.bank[ba_]], [DT_])
                            else:
                                self.cp(AP(D_, 0, [[512, 128], [1, 512]]), self.pf(bb_, 0, 512), [self.bank[bb_]], [D_])
                                self.act(AP(DT_, 0, [[512, 128], [1, 512]]), self.pf(ba_, 0, 512), AF.Copy, [self.bank[ba_]], [DT_])
                        else:
                            pv = AP(self.PP[ba_ // 2], (ba_ % 2) * 512, [[1024, 128], [128, 4], [1, 128]])
                            mo = lambda t: AP(t, hg * 512, [[1024, 128], [128, 4], [1, 128]])
                            b4 = lambda t, c0: AP(t, c0, [[16, 128], [1, 4], [0, 128]])
                            self.tt(mo(MT[d]), pv, b4(beta, d * 8 + hg * 4), ALU.mult, [self.bank[ba_], beta], [MT[d]])
                            self.tt(mo(MaT[d]), pv, b4(ba, d * 8 + hg * 4), ALU.mult, [self.bank[ba_], ba], [MaT[d]])
                for d in range(2):
                    pa, pb_ = 2 * d, 2 * d + 1
                    if "dbgX" in self.dbg and ch == 2 and d == 0:
                        self.store(AP(self.dbgMT, 0, [[DM, 128], [1, DM]]), MT[d][:, :, :], MT[d], self.dbgMT)
                    for h in range(8):
                        bk = 2 * pb_ + h // 4
                        self.mm(self.pf(bk, (h % 4) * 128, 128), MT[d][:, h, :], vtk[:, sub, h * 128:(h + 1) * 128], True, True,
                                [MT[d], vtk], [self.bank[bk]])
                    self.act(us[:, :], AP(self.PP[pb_], 0, [[1024, 128], [1, 1024]]), AF.Copy, pbk(pb_), [us])
                    self.store(AP(self.du[d], ch * 128 * DM, [[DM, 128], [1, DM]]), us[:, :], us, self.du[d])
                    for h in range(8):
                        bk = 2 * pa + h // 4
                        self.mm(self.pf(bk, (h % 4) * 128, 128), ktk[:, sub, h * 128:(h + 1) * 128], MaT[d][:, h, :], True, True,
                                [ktk, MaT[d]], [self.bank[bk]])
                    self.cp(f3(ws), p3(pa), pbk(pa), [ws])
                    self.store(AP(self.dw[d], ch * 128, [[NT, 128], [128 * NT, 8], [1, 128]]), ws[:, :, :], ws, self.dw[d])
        kb.barrier()
        kb.release(m)

    def gdn_scan(self):
        kb = self.kb
        m = kb.mark()
        NT, NCH = self.NT, self.NCH
        S = [kb.sb("Sd%d" % d, [128, 8, 128], F32) for d in range(2)]
        Sb = [kb.sb("Sdb%d" % d, [128, 8, 128], BF16) for d in range(2)]
        for d in range(2):
            self.memset(S[d][:, :, :], 0.0, [S[d]])
            self.memset(Sb[d][:, :, :], 0.0, [Sb[d]])
        nb = 2
        mk = lambda n, sh, dt: [[kb.sb("%s%d_%d" % (n, d, i), sh, dt) for i in range(nb)] for d in range(2)]
        wT = mk("lw", [128, 8, 128], BF16)
        qT = mk("lq", [128, 8, 128], BF16)
        uu = mk("lu", [128, DM], F32)
        aq = mk("la", [128, DM], BF16)
        kt = mk("lk", [128, DM], BF16)
        vn32 = kb.sb("vn32", [128, DM], F32)
        vnb = kb.sb("vnb", [128, DM], BF16)
        vnd = kb.sb("vnd", [128, DM], BF16)
        tq = kb.sb("tq", [128, DM], F32)
        osb = [kb.sb("odb%d" % d, [128, DM], F32) for d in range(2)]
        order = [list(range(NCH)), [1, 0] + list(range(NCH - 1, 1, -1))]
        b3 = lambda t, c0: AP(t, c0, [[NCH * 16, 128], [1, 8], [0, 128]])
        f3 = lambda t: AP(t, 0, [[1024, 128], [128, 8], [1, 128]])
        p3 = lambda k: AP(self.PP[k], 0, [[1024, 128], [128, 8], [1, 128]])
        pbk = lambda k: [self.bank[2 * k], self.bank[2 * k + 1]]

        def issue_loads(s):
            for d in range(2):
                c = order[d][s]
                t0 = c * 128
                i = s % nb
                self.load(wT[d][i][:, :, :], AP(self.dw[d], t0, [[NT, 128], [128 * NT, 8], [1, 128]]), self.dw[d], wT[d][i])
                self.load(qT[d][i][:, :, :], AP(self.dqT, t0, [[NT, 128], [128 * NT, 8], [1, 128]]), self.dqT, qT[d][i])
                self.load(uu[d][i][:, :], AP(self.du[d], t0 * DM, [[DM, 128], [1, DM]]), self.du[d], uu[d][i])
                self.load(aq[d][i][:, :], AP(self.daq[d], t0 * DM, [[DM, 128], [1, DM]]), self.daq[d], aq[d][i])
                self.load(kt[d][i][:, :], AP(self.dkt, t0 * DM, [[DM, 128], [1, DM]]), self.dkt, kt[d][i])

        issue_loads(0)
        for s in range(NCH):
            if s + 1 < NCH:
                issue_loads(s + 1)
            for d in range(2):
                c = order[d][s]
                i = s % nb
                Wt, Qt, Uu, Aq, Kt = wT[d][i], qT[d][i], uu[d][i], aq[d][i], kt[d][i]
                for h in range(8):
                    bk = h // 4
                    self.mm(self.pf(bk, (h % 4) * 128, 128), Wt[:, h, :], Sb[d][:, h, :], True, True, [Wt, Sb[d]], [self.bank[bk]])
                for h in range(8):
                    bk = 2 + h // 4
                    self.mm(self.pf(bk, (h % 4) * 128, 128), Qt[:, h, :], Sb[d][:, h, :], True, True, [Qt, Sb[d]], [self.bank[bk]])
                self.tt(vn32[:, :], Uu[:, :], AP(self.PP[0], 0, [[1024, 128], [1, 1024]]), ALU.subtract, [Uu] + pbk(0), [vn32])
                self.act(vnb[:, :], vn32[:, :], AF.Copy, [vn32], [vnb])
                self.tt(f3(vnd), f3(vn32), b3(self.GDd, c * 16 + d * 8), ALU.mult, [vn32, self.GDd], [vnd])
                for h in range(8):
                    bk = 4 + h // 4
                    hs = slice(h * 128, (h + 1) * 128)
                    self.mm(self.pf(bk, (h % 4) * 128, 128), Aq[:, hs], vnb[:, hs], True, True, [Aq, vnb], [self.bank[bk]])
                for h in range(8):
                    bk = 6 + h // 4
                    hs = slice(h * 128, (h + 1) * 128)
                    self.mm(self.pf(bk, (h % 4) * 128, 128), Kt[:, hs], vnd[:, hs], True, True, [Kt, vnd], [self.bank[bk]])
                self.tt(f3(tq), p3(1), b3(self.GDa, c * 16 + d * 8), ALU.mult, pbk(1) + [self.GDa], [tq])
                self.tt(osb[d][:, :], tq[:, :], AP(self.PP[2], 0, [[1024, 128], [1, 1024]]), ALU.add, [tq] + pbk(2), [osb[d]])
                self.store(AP(self.do[d], c * 128 * DM, [[DM, 128], [1, DM]]), osb[d][:, :], osb[d], self.do[d])
                self.tt(f3(S[d]), f3(S[d]), b3(self.GDe, c * 16 + d * 8), ALU.mult, [S[d], self.GDe], [S[d]])
                self.tt(f3(S[d]), f3(S[d]), p3(3), ALU.add, [S[d]] + pbk(3), [S[d]])
                self.act(Sb[d][:, :, :], S[d][:, :, :], AF.Copy, [S[d]], [Sb[d]])
        kb.barrier()
        kb.release(m)

    def phase3a(self):
        kb = self.kb
        m = kb.mark()
        W, NT = self.W, self.NT
        Wz = kb.sb("Wz", [128, 8, 4096], BF16)
        for gi, key in enumerate(("gz", "dz", "g11", "g12")):
            for k0 in (0, 4):
                self.kb.dma("pool", AP(Wz, k0 * 4096 + gi * 1024, [[8 * 4096, 128], [4096, 4], [1, 1024]]),
                            AP(self.w_in, k0 * 128 * PIN + SPL[key], [[PIN, 128], [128 * PIN, 4], [1, 1024]]),
                            [self.w_in], [Wz])
        Wo = kb.sb("Wo", [128, 8, DM], BF16)
        self.wload(Wo, self.w_out, 0, DM, rowlen=DM)
        gg = kb.sb("ggl", [128, DM], F32)
        gd = kb.sb("ggd", [128, DM], F32)
        self.load(gg[:, :], self.glag[:, :], self.glag, gg)
        self.load(gd[:, :], self.gdng[:, :], self.gdng, gd)
        xt = [kb.sb("xt3%d" % i, [128, DM], F32) for i in range(3)]
        ol = [[kb.sb("ol%d_%d" % (j, i), [128, DM], F32) for i in range(2)] for j in range(4)]
        hnT = [kb.sb("hnT3_%d" % i, [128, 8, 128], BF16) for i in range(2)]
        sz = [kb.sb("sz%d" % i, [128, 2048], BF16) for i in range(2)]
        sg = [kb.sb("sg%d" % i, [128, 2048], BF16) for i in range(2)]
        ss = kb.sb("ss3", [128, 12], F32)
        rs = kb.sb("rs3", [128, 12], F32)
        mb16 = [kb.sb("mb16_%d" % i, [128, DM], BF16) for i in range(2)]
        mT = [kb.sb("mT%d" % i, [128, 8, 128], BF16) for i in range(2)]
        x1 = [kb.sb("x1s%d" % i, [128, DM], F32) for i in range(2)]
        print("phase3a sbuf remaining", self.nc.sbuf_bytes_remaining)
        ntile = self.L // 128

        def issue_loads(t):
            i = t % 2
            self.load(xt[t % 3][:, :], AP(self.x, t * 128 * DM, [[DM, 128], [1, DM]]), self.x, xt[t % 3])
            g0 = (CTX + t * 128) * DM
            self.load(ol[0][i][:, :], AP(self.go[0], g0, [[DM, 128], [1, DM]]), self.go[0], ol[0][i])
            self.load(ol[1][i][:, :], AP(self.go[1], g0, [[DM, 128], [1, DM]]), self.go[1], ol[1][i])
            nr = 128 // W if W <= 128 else 1
            for j in (0, 1):
                for rr in range(128 // W):
                    r = t * (128 // W) + rr
                    self.load(AP(ol[2 + j][i], rr * W * DM, [[DM, W], [1, DM]]),
                              AP(self.do[j], (CTX + r) * DM, [[128 * DM, W], [1, DM]]), self.do[j], ol[2 + j][i])

        def stage_a(t):
            i = t % 2
            H, SZ, SG = hnT[i], sz[i], sg[i]
            self.norm_T(xt[t % 3][:, :], xt[t % 3], self.A1, self.B1, self.modL, H, 128, 0, 0)
            for g in range(8):
                bk = 1 + g % 4
                for kc in range(8):
                    self.mm(self.pf(bk, 0, 512), H[:, kc, :], Wz[:, kc, g * 512:(g + 1) * 512], kc == 0, kc == 7,
                            [H, Wz], [self.bank[bk]])
                if g < 4:
                    self.act(SZ[:, g * 512:(g + 1) * 512], self.pf(bk, 0, 512), AF.Silu, [self.bank[bk]], [SZ])
                else:
                    self.act(SG[:, (g - 4) * 512:(g - 3) * 512], self.pf(bk, 0, 512), AF.Sigmoid, [self.bank[bk]], [SG])

        def stage_b(t):
            i = t % 2
            SZ, SG, MB, MT_ = sz[i], sg[i], mb16[i], mT[i]
            og, od = ol[0][i], ol[2][i]
            self.tt(og[:, :], ol[0][i][:, :], ol[1][i][:, :], ALU.add, [ol[0][i], ol[1][i]], [og])
            self.tt(od[:, :], ol[2][i][:, :], ol[3][i][:, :], ALU.add, [ol[2][i], ol[3][i]], [od])
            for h in range(4):
                self.act(self.junk[:, 0:256], og[:, h * 256:(h + 1) * 256], AF.Square, [og], [self.junk, ss],
                         accum=ss[:, h:h + 1])
            for h in range(8):
                self.act(self.junk[:, 0:128], od[:, h * 128:(h + 1) * 128], AF.Square, [od], [self.junk, ss],
                         accum=ss[:, 4 + h:5 + h])
            self.act(rs[:, 0:4], ss[:, 0:4], AF.Ln, [ss], [rs], scale=1.0 / 256, bias=self.epsb[:, 0:1])
            self.act(rs[:, 4:12], ss[:, 4:12], AF.Ln, [ss], [rs], scale=1.0 / 128, bias=self.epsb[:, 0:1])
            self.act(rs[:, :], rs[:, :], AF.Exp, [rs], [rs], scale=-0.5)
            self.tt(AP(og, 0, [[DM, 128], [256, 4], [1, 256]]), AP(og, 0, [[DM, 128], [256, 4], [1, 256]]),
                    AP(rs, 0, [[12, 128], [1, 4], [0, 256]]), ALU.mult, [og, rs], [og])
            self.tt(AP(od, 0, [[DM, 128], [128, 8], [1, 128]]), AP(od, 0, [[DM, 128], [128, 8], [1, 128]]),
                    AP(rs, 4, [[12, 128], [1, 8], [0, 128]]), ALU.mult, [od, rs], [od])
            self.tt(og[:, :], og[:, :], gg[:, :], ALU.mult, [og, gg], [og])
            self.tt(od[:, :], od[:, :], gd[:, :], ALU.mult, [od, gd], [od])
            self.tt(og[:, :], og[:, :], SZ[:, 0:1024], ALU.mult, [og, SZ], [og])
            self.tt(od[:, :], od[:, :], SZ[:, 1024:2048], ALU.mult, [od, SZ], [od])
            self.tt(og[:, :], og[:, :], SG[:, 0:1024], ALU.mult, [og, SG], [og])
            self.tt(od[:, :], od[:, :], SG[:, 1024:2048], ALU.mult, [od, SG], [od])
            self.tt(MB[:, :], og[:, :], od[:, :], ALU.add, [og, od], [MB])
            for kc in range(8):
                self.tr(self.pb16(5, kc * 128, 128), MB[:, kc * 128:(kc + 1) * 128], self.identb[:, :],
                        [MB, self.identb], [self.bank[5]])
            self.act(AP(MT_, 0, [[1024, 128], [1, 1024]]), self.pb16(5, 0, 1024), AF.Copy, [self.bank[5]], [MT_])

        def stage_b2(t):
            i = t % 2
            MT_ = mT[i]
            for half in range(2):
                bk = 6 + half
                for kc in range(8):
                    self.mm(self.pf(bk, 0, 512), MT_[:, kc, :], Wo[:, kc, half * 512:(half + 1) * 512], kc == 0, kc == 7,
                            [MT_, Wo], [self.bank[bk]])

        def stage_c(t):
            i = t % 2
            X = xt[t % 3]
            self.tt(x1[i][:, :], AP(self.PP[3], 0, [[1024, 128], [1, 1024]]), self.G1, ALU.mult,
                    [self.bank[6], self.bank[7], self.modL], [x1[i]])
            self.tt(x1[i][:, :], x1[i][:, :], X[:, :], ALU.add, [x1[i], X], [x1[i]])
            self.store(AP(self.dx1, t * 128 * DM, [[DM, 128], [1, DM]]), x1[i][:, :], x1[i], self.dx1)

        issue_loads(0)
        if ntile > 1:
            issue_loads(1)
        stage_a(0)
        for t in range(ntile):
            if t + 1 < ntile:
                stage_a(t + 1)
            stage_b(t)
            if t >= 1:
                stage_c(t - 1)
            stage_b2(t)
            if t + 2 < ntile:
                issue_loads(t + 2)
        stage_c(ntile - 1)
        kb.barrier()
        kb.release(m)

    def phase3b(self):
        kb = self.kb
        m = kb.mark()
        Wg = kb.sb("Wg", [128, 8, FFH], BF16)
        Wu = kb.sb("Wu", [128, 8, FFH], BF16)
        Wd = kb.sb("Wd", [128, 22, DM], BF16)
        for (dst, src) in ((Wg, self.wg), (Wu, self.wu)):
            for k0 in range(0, 8, 2):
                self.kb.dma("pool", AP(dst, k0 * FFH, [[8 * FFH, 128], [FFH, 2], [1, FFH]]),
                            AP(src, k0 * 128 * FFH, [[FFH, 128], [128 * FFH, 2], [1, FFH]]), [src], [dst])
        for k0 in range(0, 22, 2):
            self.kb.dma("pool", AP(Wd, k0 * DM, [[22 * DM, 128], [DM, 2], [1, DM]]),
                        AP(self.wd, k0 * 128 * DM, [[DM, 128], [128 * DM, 2], [1, DM]]), [self.wd], [Wd])
        fn = kb.sb("fnw", [128, DM], F32)
        self.load(fn[:, :], self.fnbc[:, :], self.fnbc, fn)
        xt = [kb.sb("x1t%d" % i, [128, 2, DM], F32) for i in range(1)]
        h2T = kb.sb("h2T", [128, 8, TT], BF16)
        sgl = kb.sb("sgl", [128, TT], F32)
        actT = kb.sb("actT", [128, 22, TT], BF16)
        ty = kb.sb("ty2", [128, 512], F32)
        x2s = [kb.sb("x2_%d" % i, [128, DM], F32) for i in range(2)]
        print("phase3b sbuf remaining", self.nc.sbuf_bytes_remaining)
        ss, rs = self.ssq, self.rsd
        ntile = self.L // TT

        def issue_load(t):
            self.load(xt[0][:, :, :], AP(self.dx1, t * TT * DM, [[DM, 128], [128 * DM, 2], [1, DM]]), self.dx1, xt[0])

        issue_load(0)
        oc = 0
        for t in range(ntile):
            X = xt[0]
            if t > 0:
                issue_load(t)
            for sub in range(2):
                self.norm_T(X[:, sub, :], X, self.A2, self.B2, self.modL2, h2T, TT, sub, 0)
            for hc in range(22):
                bg = 1 + (hc % 2) * 2
                bu = bg + 1
                for kc in range(8):
                    self.mm(self.pf(bg, 0, TT), Wg[:, kc, hc * 128:(hc + 1) * 128], h2T[:, kc, :], kc == 0, kc == 7,
                            [Wg, h2T], [self.bank[bg]])
                for kc in range(8):
                    self.mm(self.pf(bu, 0, TT), Wu[:, kc, hc * 128:(hc + 1) * 128], h2T[:, kc, :], kc == 0, kc == 7,
                            [Wu, h2T], [self.bank[bu]])
                self.act(sgl[:, :], self.pf(bg, 0, TT), AF.Silu, [self.bank[bg]], [sgl])
                self.tt(actT[:, hc, :], sgl[:, :], self.pf(bu, 0, TT), ALU.mult, [sgl, self.bank[bu]], [actT])
            for sub in range(2):
                x2 = x2s[oc % 2]
                for half in range(2):
                    bk = 5 + half
                    for hc in range(22):
                        self.mm(self.pf(bk, 0, 512), actT[:, hc, sub * 128:(sub + 1) * 128],
                                Wd[:, hc, half * 512:(half + 1) * 512], hc == 0, hc == 21, [actT, Wd], [self.bank[bk]])
                    hs = slice(half * 512, (half + 1) * 512)
                    self.tt(ty[:, :], self.pf(bk, 0, 512), AP(self.modL2, 2 * DM + half * 512, [[3 * DM, 128], [1, 512]]),
                            ALU.mult, [self.bank[bk], self.modL2], [ty])
                    self.tt(x2[:, hs], ty[:, :], X[:, sub, hs], ALU.add, [ty, X], [x2])
                self.act(self.ntmp[:, :], x2[:, :], AF.Square, [x2], [self.ntmp, ss], accum=ss[:, 2:3])
                self.act(rs[:, 2:3], ss[:, 2:3], AF.Sqrt, [ss], [rs], scale=1.0 / DM, bias=self.epsb[:, 0:1])
                self.recip(rs[:, 3:4], rs[:, 2:3], [rs], [rs])
                O = x2
                oc += 1
                self.stt(O[:, :], x2[:, :], rs[:, 3:4], fn[:, :], ALU.mult, ALU.mult, [x2, rs, fn], [O])
                self.store(AP(self.out, (t * TT + sub * 128) * DM, [[DM, 128], [1, DM]]), O[:, :], O, self.out)
        kb.barrier()
        kb.release(m)

    def build(self, phases="0ASBG3F"):
        kb = self.kb
        self.consts()
        self.epsb = kb.sb("epsb", [128, 1], F32)
        self.oneb = kb.sb("oneb", [128, 1], F32)
        self.lncb = kb.sb("lncb", [128, 1], F32)
        self.zerob = kb.sb("zerob", [128, 1], F32)
        self.memset(self.lncb[:, :], -0.5 * float(np.log(128.0)), [self.lncb])
        self.memset(self.zerob[:, :], 0.0, [self.zerob])
        self.memset(self.epsb[:, :], EPS, [self.epsb])
        self.memset(self.oneb[:, :], 1.0, [self.oneb])
        NCH = self.NCH
        self.modL2 = kb.sb("modL2", [128, 3 * DM], F32)
        ma = kb.mark()
        self.modL1 = kb.sb("modL1", [128, 3 * DM], F32)
        mb_ = kb.mark()
        self.modC = kb.sb("modC", [128, 2 * DM], F32)
        self.Egla = kb.sb("Egla", [128, 2 * NCH * 4], F32)
        self.GDa = kb.sb("GDa", [128, NCH * 16], F32)
        self.GDd = kb.sb("GDd", [128, NCH * 16], F32)
        self.GDe = kb.sb("GDe", [128, NCH * 16], F32)
        self.phase0()
        if "A" in phases:
            self.phaseA()
        if "S" in phases:
            self.gla_scan()
        if "B" in phases:
            self.phaseB()
        if "G" in phases:
            self.gdn_scan()
        kb.release(mb_)
        if "3" in phases:
            self.phase3a()
        kb.release(ma)
        if "F" in phases:
            self.phase3b()
        kb.barrier()
        kb.finalize()
        kb.close()
        return self.nc


def host_consts():
    p = np.arange(128)[:, None]
    f = np.arange(128)[None, :]
    le = (p <= f).astype(np.float32)
    ge = (p >= f).astype(np.float32)
    lt = (p < f).astype(np.float32)
    gt = (p > f).astype(np.float32)
    cumU = np.stack([le * (-1.0 / 16.0), ge * (-1.0 / 16.0), le, ge, gt, lt]).astype(np.float32)
    gmask = np.stack([le, ge]).astype(np.float32)
    dmask = np.stack([(1 - le) * NEG, (1 - ge) * NEG, (1 - lt) * NEG, (1 - gt) * NEG]).astype(np.float32)
    sel = np.zeros((96, 32, 128), np.float32)
    for c in range(32):
        for part in range(3):
            sel[part * 32 + c, c, :] = 1.0
    lv = np.zeros((14, 128, 128), np.float32)
    pi = np.arange(128)[:, None]
    fj = np.arange(128)[None, :]
    for s_ in range(7):
        b = 1 << s_
        mlow = ((pi // (2 * b)) == (fj // (2 * b))) & ((pi % (2 * b)) >= b) & ((fj % (2 * b)) < b)
        dg = np.eye(128, dtype=np.float32) if s_ >= 1 else 0.0
        lv[s_] = -mlow.astype(np.float32) - dg
        lv[7 + s_] = -mlow.T.astype(np.float32) - dg
    return dict(identf=np.eye(128, dtype=np.float32), cumU=cumU, gmask=gmask, dmask=dmask,
                sel=sel.reshape(96, 32 * 128), lvlm=lv)


def host_inputs(b, x, c, ctx, c_ctx, w_mod, b_mod, norm1_w, norm2_w, w_in, gla_lr_w, gla_lr_b, gla_norm_w,
                gdn_conv_w, gdn_a_log, gdn_dt_bias, gdn_norm_w, w_out, ffn_w_gate, ffn_w_up, ffn_w_down,
                final_norm_w, shared):
    f = lambda a: np.ascontiguousarray(a, dtype=np.float32)
    d = dict(shared)
    d["x"] = f(x[b])
    d["ctx"] = f(ctx[b])
    d["cT"] = f(np.asarray(c[b]).reshape(8, 128).T)
    return d


def shared_inputs(c_ctx, w_mod, b_mod, norm1_w, norm2_w, w_in, gla_lr_w, gla_lr_b, gla_norm_w,
                  gdn_conv_w, gdn_a_log, gdn_dt_bias, gdn_norm_w, w_out, ffn_w_gate, ffn_w_up, ffn_w_down,
                  final_norm_w):
    f = lambda a: np.ascontiguousarray(a, dtype=np.float32)
    bc = lambda v: f(np.broadcast_to(np.asarray(v).reshape(1, -1), (128, np.asarray(v).size)))
    d = host_consts()
    d["cctxT"] = f(np.asarray(c_ctx).reshape(8, 128).T)
    d["w_mod"] = f(w_mod[0])
    d["b_mod"] = f(np.asarray(b_mod[0]).reshape(1, -1))
    d["n1bc"] = bc(norm1_w[0])
    d["n2bc"] = bc(norm2_w[0])
    d["fnbc"] = bc(final_norm_w)
    d["w_in"] = f(w_in[0])
    lrw = np.zeros((2, 33, 512), np.float32)
    lrw[0, 0:16] = np.asarray(gla_lr_w[0, 0])
    lrw[1, 16:32] = np.asarray(gla_lr_w[0, 1])
    lrw[0, 32] = np.asarray(gla_lr_b[0, 0])
    lrw[1, 32] = np.asarray(gla_lr_b[0, 1])
    d["lrw"] = lrw
    d["glag"] = bc(np.asarray(gla_norm_w[0]).reshape(-1))
    d["gdng"] = bc(np.asarray(gdn_norm_w[0]).reshape(-1))
    cw = np.asarray(gdn_conv_w[0])
    d["cwT"] = f(cw.reshape(5, 24, 128).transpose(2, 1, 0).reshape(128, 120))
    d["alog"] = bc(np.asarray(gdn_a_log[0]).reshape(-1))
    d["dtb"] = bc(np.asarray(gdn_dt_bias[0]).reshape(-1))
    d["w_out"] = f(w_out[0])
    d["wg"] = f(ffn_w_gate[0])
    d["wu"] = f(ffn_w_up[0])
    d["wd"] = f(ffn_w_down[0])
    return d


_CACHE = {}


def kernel(x, c, ctx, c_ctx, w_mod, b_mod, norm1_w, norm2_w, w_in, gla_lr_w, gla_lr_b, gla_norm_w,
           gdn_conv_w, gdn_a_log, gdn_dt_bias, gdn_norm_w, w_out, ffn_w_gate, ffn_w_up, ffn_w_down,
           final_norm_w):
    x = np.asarray(x)
    B, L, _ = x.shape
    W = L // 128
    nc = Prog(W).build()
    shared = shared_inputs(c_ctx, w_mod, b_mod, norm1_w, norm2_w, w_in, gla_lr_w, gla_lr_b, gla_norm_w,
                           gdn_conv_w, gdn_a_log, gdn_dt_bias, gdn_norm_w, w_out, ffn_w_gate, ffn_w_up,
                           ffn_w_down, final_norm_w)
    f = lambda a: np.ascontiguousarray(a, dtype=np.float32)
    in_maps = []
    for b in range(B):
        d = dict(shared)
        d["x"] = f(x[b])
        d["ctx"] = f(np.asarray(ctx)[b])
        d["cT"] = f(np.asarray(c)[b].reshape(8, 128).T)
        in_maps.append(d)
    res = run_bass_kernel_spmd(nc, in_maps, core_ids=list(range(B)))
    return np.stack([np.asarray(r["out"], dtype=np.float32) for r in res.results], axis=0)
```

```python
import numpy as np
import ml_dtypes
import concourse.bass as bass
import concourse.mybir as mybir
from concourse.bass_utils import run_bass_kernel_spmd

F32 = mybir.dt.float32
BF16 = mybir.dt.bfloat16
AF = mybir.ActivationFunctionType
ALU = mybir.AluOpType

SAME_ENG_SYNC = True
NOSYNC_ENGS = ()
EPOCH = 20000
EPS = 1e-6
DM = 1024
CTX = 256
TT = 256
NEG = -30000.0


class Buf:
    __slots__ = ("t", "lw", "rd", "sem", "cnt", "name", "dram", "_scope", "semq")

    def __init__(self, t, name, dram=False):
        self.t = t
        self.name = name
        self.lw = {}
        self.rd = {}
        self.sem = None
        self.cnt = 0
        self.dram = dram
        self._scope = 0
        self.semq = None

    def __getitem__(self, idx):
        return self.t[idx]


class EngState:
    def __init__(self, name):
        self.name = name
        self.ops = []
        self.count = 0
        self.waited = {}
        self.needed = set()


class KB:
    def __init__(self, nc):
        self.nc = nc
        self.eng = {n: EngState(n) for n in ("pe", "act", "dve", "pool", "sp")}
        self._ctx = []
        self.sems = {}
        self.sempool = []
        self.sembufs = []
        self._sem_ctx = []

    def mark(self):
        return len(self._ctx)

    def release(self, m):
        for b in self.sembufs:
            if b.sem is not None and (not b.dram) and b._scope >= m:
                self.sempool.append((b.sem, b.cnt, b.semq))
                b.sem = None
        self.sembufs = [b for b in self.sembufs if b.sem is not None]
        while len(self._ctx) > m:
            cm = self._ctx.pop()
            cm.__exit__(None, None, None)

    def sb(self, name, shape, dt=F32):
        cm = self.nc.sbuf_tensor("sb_" + name, list(shape), dt)
        t = cm.__enter__()
        b = Buf(t, name)
        b._scope = len(self._ctx)
        self._ctx.append(cm)
        return b

    def ps(self, name, shape, dt=F32):
        cm = self.nc.psum_tensor(name, list(shape), dt)
        t = cm.__enter__()
        self._ctx.append(cm)
        return t

    def dram(self, name, shape, dt=F32, kind="Internal"):
        t = self.nc.dram_tensor(name, list(shape), dt, kind=kind)
        return Buf(t, name, dram=True)

    def _get_sem(self, b, q):
        if b.sem is not None:
            assert b.semq == q, "buffer %s used by both DMA queue kinds" % b.name
        if b.sem is None:
            b.semq = q
            cand = [i for i, e in enumerate(self.sempool) if e[2] == q]
            if cand:
                b.sem, b.cnt, _ = self.sempool.pop(cand[-1])
            else:
                cm = self.nc.semaphore("s%d" % len(self.sems))
                b.sem = cm.__enter__()
                self._sem_ctx.append(cm)
                b.cnt = 0
                self.sems[id(b.sem)] = b.sem
            self.sembufs.append(b)

    def _waits(self, E, reads, writes, extra=None):
        deps = {}

        def add(d):
            for k, v in d.items():
                if deps.get(k, -1) < v:
                    deps[k] = v

        for b in reads:
            add(b.lw)
        for b in writes:
            add(b.lw)
            add(b.rd)
        if extra:
            add(extra)
        for k, v in deps.items():
            if k[0] == "E" and k[1] == E.name:
                if E.name in ("pe", "sp") or not SAME_ENG_SYNC or E.name in NOSYNC_ENGS:
                    continue
            if E.waited.get(k, -1) >= v:
                continue
            E.waited[k] = v
            if k[0] == "E":
                self.eng[k[1]].needed.add(v)
            E.ops.append(("w", k, v))

    def op(self, eng, fn, reads=(), writes=()):
        E = self.eng[eng]
        self._waits(E, reads, writes)
        idx = E.count
        E.count += 1
        E.ops.append(("o", fn, idx))
        key = ("E", eng)
        for b in writes:
            if b.dram:
                b.lw[key] = idx
            else:
                b.lw = {key: idx}
                b.rd = {}
        for b in reads:
            b.rd[key] = idx

    def dma(self, q, out, in_, reads, writes):
        E = self.eng[q]
        self._waits(E, reads, writes)
        cand = [b for b in list(writes) + list(reads) if not b.dram]
        sb = cand[0]
        self._get_sem(sb, q)
        sb.cnt += 16
        c = sb.cnt
        key = ("S", id(sb.sem))
        E.ops.append(("d", out, in_, sb.sem))
        for b in writes:
            if b.dram:
                b.lw[key] = c
            else:
                b.lw = {key: c}
                b.rd = {}
        for b in reads:
            b.rd[key] = c

    def barrier(self):
        ev = {}
        for n in ("pe", "act", "dve", "pool"):
            if self.eng[n].count > 0:
                ev[("E", n)] = self.eng[n].count - 1
        for b in self.sembufs:
            if b.sem is not None and b.cnt > 0:
                ev[("S", id(b.sem))] = b.cnt
        for n, E in self.eng.items():
            deps = dict(ev)
            for k, v in deps.items():
                if k[0] == "E" and k[1] == n:
                    continue
                if E.waited.get(k, -1) >= v:
                    continue
                E.waited[k] = v
                if k[0] == "E":
                    self.eng[k[1]].needed.add(v)
                E.ops.append(("w", k, v))

    def finalize(self):
        nc = self.nc
        engsems = {}
        rank = {}
        for name, E in self.eng.items():
            nd = sorted(E.needed)
            rank[name] = {idx: r for r, idx in enumerate(nd)}
            nsem = (len(nd) + EPOCH - 1) // EPOCH
            lst = []
            for i in range(nsem):
                cm = nc.semaphore("e_%s_%d" % (name, i))
                lst.append(cm.__enter__())
                self._sem_ctx.append(cm)
            engsems[name] = lst
        engobj = {"pe": nc.tensor, "act": nc.scalar, "dve": nc.vector, "pool": nc.gpsimd, "sp": nc.sync}

        def replay(E, e):
            for o in E.ops:
                if o[0] == "w":
                    k, v = o[1], o[2]
                    if k[0] == "E":
                        r = rank[k[1]][v]
                        e.wait_ge(engsems[k[1]][r // EPOCH], r % EPOCH + 1)
                    else:
                        e.wait_ge(self.sems[k[1]], v)
                elif o[0] == "o":
                    inst = o[1](e)
                    r = rank[E.name].get(o[2])
                    if r is not None:
                        inst.then_inc(engsems[E.name][r // EPOCH], 1)
                else:
                    e.dma_start(out=o[1], in_=o[2]).then_inc(o[3], 16)

        with nc.Block() as block:
            @block.tensor
            def _(e):
                replay(self.eng["pe"], e)

            @block.scalar
            def _(e):
                replay(self.eng["act"], e)

            @block.vector
            def _(e):
                replay(self.eng["dve"], e)

            @block.gpsimd
            def _(e):
                replay(self.eng["pool"], e)

            @block.sync
            def _(e):
                replay(self.eng["sp"], e)

    def close(self):
        while self._ctx:
            self._ctx.pop().__exit__(None, None, None)
        while self._sem_ctx:
            self._sem_ctx.pop().__exit__(None, None, None)


def AP(buf, offset, pairs):
    return bass.AP(buf.t if isinstance(buf, Buf) else buf, offset, [list(p) for p in pairs])


SPL = dict(gq=0, gk=512, gv=1024, gz=2048, lr=3072, dq=3104, dk=4128, dv=5152, dz=6176,
           ab=7200, g11=7232, g12=8256)
PIN = 9280
FFH = 2816


class Prog:
    def __init__(self, W, dbg=()):
        self.W = W
        self.L = 128 * W
        self.NT = CTX + self.L
        self.NCH = self.NT // 128
        self.dbg = set(dbg)
        nc = bass.Bass("TRN2", target_bir_lowering=False)
        self.nc = nc
        self.kb = KB(nc)
        kb = self.kb
        L, NT = self.L, self.NT
        ein = lambda n, s: kb.dram(n, s, F32, kind="ExternalInput")
        self.x = ein("x", [L, DM])
        self.ctx = ein("ctx", [CTX, DM])
        self.cT = ein("cT", [128, 8])
        self.cctxT = ein("cctxT", [128, 8])
        self.w_mod = ein("w_mod", [DM, 6 * DM])
        self.b_mod = ein("b_mod", [1, 6 * DM])
        self.n1bc = ein("n1bc", [128, DM])
        self.n2bc = ein("n2bc", [128, DM])
        self.fnbc = ein("fnbc", [128, DM])
        self.w_in = ein("w_in", [DM, PIN])
        self.lrw = ein("lrw", [2, 33, 512])
        self.glag = ein("glag", [128, DM])
        self.gdng = ein("gdng", [128, DM])
        self.cwT = ein("cwT", [128, 120])
        self.alog = ein("alog", [128, 16])
        self.dtb = ein("dtb", [128, 16])
        self.w_out = ein("w_out", [DM, DM])
        self.wg = ein("wg", [DM, FFH])
        self.wu = ein("wu", [DM, FFH])
        self.wd = ein("wd", [FFH, DM])
        self.identf_d = ein("identf", [128, 128])
        self.cumU = ein("cumU", [6, 128, 128])
        self.gmask = ein("gmask", [2, 128, 128])
        self.dmask = ein("dmask", [4, 128, 128])
        self.sel_d = ein("sel", [96, 32 * 128])
        self.lvlm = ein("lvlm", [14, 128, 128])
        self.out = kb.dram("out", [L, DM], F32, kind="ExternalOutput")

        def scr(n, s, dt):
            return kb.dram(n, s, dt, kind="ExternalOutput" if (n in self.dbg or n.startswith("dbg")) else "Internal")
        self.gq = [scr("gq%d" % d, [4, 128, NT], BF16) for d in range(2)]
        self.gk = [scr("gk%d" % d, [4, 128, NT], BF16) for d in range(2)]
        self.gkh = [scr("gkh%d" % d, [NT, 512], BF16) for d in range(2)]
        self.gv = scr("gv", [NT, DM], BF16)
        self.go = [scr("go%d" % d, [NT, DM], F32) for d in range(2)]
        self.dqT = scr("dqT", [8, 128, NT], BF16)
        self.dkt = scr("dkt", [NT, DM], BF16)
        self.du = [scr("du%d" % d, [NT, DM], F32) for d in range(2)]
        self.dw = [scr("dw%d" % d, [8, 128, NT], BF16) for d in range(2)]
        self.daq = [scr("daq%d" % d, [NT, DM], BF16) for d in range(2)]
        self.do = [scr("do%d" % d, [NT, DM], F32) for d in range(2)]
        self.dx1 = scr("dx1", [L, DM], F32)
        if "dbgX" in self.dbg:
            self.dbgX = scr("dbgX", [128, DM], BF16)
            self.dbgXT = scr("dbgXT", [128, DM], BF16)
            self.dbgMT = scr("dbgMT", [128, DM], BF16)
            self.dbg.update(["dbgXT", "dbgMT"])

        self.PP = [kb.ps("pp%d" % i, [128, 1024], F32) for i in range(4)]
        self.PPb = [p.bitcast(BF16) for p in self.PP]
        self.bank = [Buf(self.PP[i // 2], "bank%d" % i) for i in range(8)]

    def pf(self, b, c0, n, p=128):
        return AP(self.PP[b // 2], (b % 2) * 512 + c0, [[1024, p], [1, n]])

    def pb16(self, b, c0, n, p=128):
        return AP(self.PPb[b // 2], (b % 2) * 1024 + c0, [[2048, p], [1, n]])

    def mm(self, out, lhsT, rhs, start, stop, reads, writes):
        self.kb.op("pe", lambda e: e.matmul(out, lhsT=lhsT, rhs=rhs, start=start, stop=stop), reads, writes)

    def tr(self, out, in_, ident, reads, writes):
        self.kb.op("pe", lambda e: e.transpose(out=out, in_=in_, identity=ident), reads, writes)

    def act(self, out, in_, func, reads, writes, scale=None, bias=None, accum=None):
        kw = {}
        if scale is not None:
            kw["scale"] = scale
        if bias is not None:
            kw["bias"] = bias
        if accum is not None:
            kw["accum_out"] = accum
        self.kb.op("act", lambda e: e.activation(out=out, in_=in_, func=func, **kw), reads, writes)

    def tt(self, out, in0, in1, op, reads, writes):
        self.kb.op("dve", lambda e: e.tensor_tensor(out=out, in0=in0, in1=in1, op=op), reads, writes)

    def ptt(self, out, in0, in1, op, reads, writes):
        self.kb.op("pool", lambda e: e.tensor_tensor(out=out, in0=in0, in1=in1, op=op), reads, writes)

    def stt(self, out, in0, scalar, in1, op0, op1, reads, writes):
        self.kb.op("dve", lambda e: e.scalar_tensor_tensor(out=out, in0=in0, scalar=scalar, in1=in1,
                                                           op0=op0, op1=op1), reads, writes)

    def ts(self, out, in0, s1, s2, op0, op1, reads, writes):
        if s2 is None:
            self.kb.op("dve", lambda e: e.tensor_scalar(out=out, in0=in0, scalar1=s1, scalar2=None, op0=op0),
                       reads, writes)
        else:
            self.kb.op("dve", lambda e: e.tensor_scalar(out=out, in0=in0, scalar1=s1, scalar2=s2, op0=op0, op1=op1),
                       reads, writes)

    def cp(self, out, in_, reads, writes):
        self.kb.op("dve", lambda e: e.tensor_copy(out=out, in_=in_), reads, writes)

    def recip(self, out, in_, reads, writes):
        self.kb.op("dve", lambda e: e.reciprocal(out=out, in_=in_), reads, writes)

    def memset(self, ap, val, writes):
        self.kb.op("dve", lambda e: e.memset(ap, val), [], writes)

    def load(self, out, in_, src, dst, q="sp"):
        self.kb.dma(q, out, in_, [src], [dst])

    def store(self, out, in_, src, dst, q="pool"):
        self.kb.dma(q, out, in_, [src], [dst])

    def wload(self, dst, src, col0, ncols, nk=8, rowlen=None):
        rl = rowlen
        step = 4
        for k0 in range(0, nk, step):
            kn = min(step, nk - k0)
            self.kb.dma("pool", AP(dst, k0 * ncols, [[nk * ncols, 128], [ncols, kn], [1, ncols]]),
                        AP(src, k0 * 128 * rl + col0, [[rl, 128], [128 * rl, kn], [1, ncols]]), [src], [dst])

    def consts(self):
        kb = self.kb
        self.identf = kb.sb("identf", [128, 128], F32)
        self.identb = kb.sb("identb", [128, 128], BF16)
        self.load(self.identf[:, :], self.identf_d[:, :], self.identf_d, self.identf)
        self.kb.dma("pool", self.identb[:, :], self.identf_d[:, :], [self.identf_d], [self.identb])
        self.onesf = kb.sb("onesf", [128, 128], F32)
        self.onesb = kb.sb("onesb", [128, 128], BF16)
        self.memset(self.onesf[:, :], 1.0, [self.onesf])
        self.memset(self.onesb[:, :], 1.0, [self.onesb])
        self.junk = kb.sb("junk", [128, 256], BF16)
        self.ssq = kb.sb("ssq", [128, 16], F32)
        self.rsd = kb.sb("rsd", [128, 16], F32)
        self.ntmp = kb.sb("ntmp", [128, DM], F32)
        self.hnb = kb.sb("hnb", [128, DM], BF16)

    def phase0(self):
        kb = self.kb
        m = kb.mark()
        cT = kb.sb("cTs", [128, 8], F32)
        ccT = kb.sb("ccTs", [128, 8], F32)
        bm = kb.sb("bm", [1, 6 * DM], F32)
        n1 = kb.sb("n1", [128, DM], F32)
        n2 = kb.sb("n2", [128, DM], F32)
        SL = kb.sb("SL", [128, 8, 128], F32)
        SC = kb.sb("SC", [128, 8, 128], F32)
        wm = [kb.sb("wm%d" % i, [128, 8, 512], F32) for i in range(2)]
        self.load(cT[:, :], self.cT[:, :], self.cT, cT)
        self.load(ccT[:, :], self.cctxT[:, :], self.cctxT, ccT)
        self.load(bm[:, :], self.b_mod[:, :], self.b_mod, bm)
        self.load(n1[:, :], self.n1bc[:, :], self.n1bc, n1)
        self.load(n2[:, :], self.n2bc[:, :], self.n2bc, n2)
        for kc in range(8):
            self.act(SL[:, kc, :], self.onesf[:, :], AF.Silu, [self.onesf, cT], [SL], scale=cT[:, kc:kc + 1])
            self.act(SC[:, kc, :], self.onesf[:, :], AF.Silu, [self.onesf, ccT], [SC], scale=ccT[:, kc:kc + 1])
        for g in range(12):
            w = wm[g % 2]
            for k0 in (0, 4):
                self.load(AP(w, k0 * 512, [[8 * 512, 128], [512, 4], [1, 512]]),
                          AP(self.w_mod, k0 * 128 * 6 * DM + g * 512, [[6 * DM, 128], [128 * 6 * DM, 4], [1, 512]]),
                          self.w_mod, w)
            variants = [(SL, self.modL1 if g < 6 else self.modL2, 0, (g % 6) * 512)]
            if g < 4:
                variants.append((SC, self.modC, 1, g * 512))
            for (S_, dst, bi, dc0) in variants:
                bk = (g * 2 + bi) % 8
                for kc in range(8):
                    self.mm(self.pf(bk, 0, 512), S_[:, kc, :], w[:, kc, :], kc == 0, False, [S_, w], [self.bank[bk]])
                self.mm(self.pf(bk, 0, 512), self.onesf[0:1, :], bm[0:1, g * 512:(g + 1) * 512], False, True,
                        [self.onesf, bm], [self.bank[bk]])
                self.act(dst[:, dc0:dc0 + 512], self.pf(bk, 0, 512), AF.Copy, [self.bank[bk]], [dst])
        for (dst, nw, c0) in ((self.modL1, n1, DM), (self.modL2, n2, DM), (self.modC, n1, DM)):
            self.stt(dst[:, c0:c0 + DM], dst[:, c0:c0 + DM], 1.0, nw[:, :], ALU.add, ALU.mult, [dst, nw], [dst])
        kb.barrier()
        kb.release(m)
        ML, ML2, MC = self.modL1, self.modL2, self.modC
        self.modL = ML
        self.B1, self.A1, self.G1 = ML[:, 0:DM], ML[:, DM:2 * DM], ML[:, 2 * DM:3 * DM]
        self.B2, self.A2, self.G2 = ML2[:, 0:DM], ML2[:, DM:2 * DM], ML2[:, 2 * DM:3 * DM]
        self.B1c, self.A1c = MC[:, 0:DM], MC[:, DM:2 * DM]

    def norm_T(self, xt, xbuf, A, B, mbuf, hnT, Tn, sub, bk):
        ss, rs = self.ssq, self.rsd
        self.act(self.ntmp[:, :], xt, AF.Square, [xbuf], [self.ntmp, ss], accum=ss[:, 0:1])
        self.act(rs[:, 0:1], ss[:, 0:1], AF.Ln, [ss], [rs], scale=1.0 / DM, bias=self.epsb[:, 0:1])
        self.act(rs[:, 1:2], rs[:, 0:1], AF.Exp, [rs], [rs], scale=-0.5)
        self.stt(self.ntmp[:, :], xt, rs[:, 1:2], A, ALU.mult, ALU.mult, [xbuf, rs, mbuf], [self.ntmp])
        self.tt(self.hnb[:, :], self.ntmp[:, :], B, ALU.add, [self.ntmp, mbuf], [self.hnb])
        for kc in range(8):
            self.tr(self.pb16(bk, kc * 128, 128), self.hnb[:, kc * 128:(kc + 1) * 128], self.identb[:, :],
                    [self.hnb, self.identb], [self.bank[bk]])
        self.act(AP(hnT, sub * 128, [[8 * Tn, 128], [Tn, 8], [1, 128]]),
                 AP(self.PPb[bk // 2], (bk % 2) * 1024, [[2048, 128], [128, 8], [1, 128]]), AF.Copy,
                 [self.bank[bk]], [hnT])

    def tiles(self, colmajor):
        res = [(0, 0, True, None)]
        for i in range(self.L // TT):
            res.append((i + 1, CTX + i * TT, False, i))
        return res

    def x_src(self, is_ctx, li, colmajor):
        if is_ctx:
            return AP(self.ctx, 0, [[DM, 128], [128 * DM, 2], [1, DM]]), self.ctx
        if not colmajor:
            return AP(self.x, li * TT * DM, [[DM, 128], [128 * DM, 2], [1, DM]]), self.x
        c0 = li * 2
        return AP(self.x, c0 * DM, [[self.W * DM, 128], [DM, 2], [1, DM]]), self.x

    def phaseA(self):
        kb = self.kb
        m = kb.mark()
        NT = self.NT
        Wqk = kb.sb("Wqk", [128, 8, 1024], BF16)
        Wv = kb.sb("Wv", [128, 8, 1024], BF16)
        Wlr = kb.sb("Wlr", [128, 8, 32], BF16)
        self.wload(Wqk, self.w_in, SPL["gq"], 1024, rowlen=PIN)
        self.wload(Wv, self.w_in, SPL["gv"], 1024, rowlen=PIN)
        self.wload(Wlr, self.w_in, SPL["lr"], 32, rowlen=PIN)
        LRW = [kb.sb("LRW%d" % d, [33, 512], F32) for d in range(2)]
        U = [kb.sb("U%d" % d, [128, 128], F32) for d in range(2)]
        for d in range(2):
            self.load(LRW[d][:, :], AP(self.lrw, d * 33 * 512, [[512, 33], [1, 512]]), self.lrw, LRW[d])
            self.load(U[d][:, :], AP(self.cumU, d * 128 * 128, [[128, 128], [1, 128]]), self.cumU, U[d])
        xt = [kb.sb("xtA%d" % i, [128, 2, DM], F32) for i in range(2)]
        hnT = [kb.sb("hnTA%d" % i, [128, 8, TT], BF16) for i in range(2)]
        qkTs = [kb.sb("qkT%d" % i, [128, 8, TT], F32) for i in range(2)]
        vtok = [kb.sb("vtok%d" % i, [128, 2, DM], BF16) for i in range(2)]
        lraug = kb.sb("lraug", [33, TT], F32)
        self.memset(lraug[32:33, :], 1.0, [lraug])
        e1s = [kb.sb("e1_%d" % i, [128, 512], F32) for i in range(2)]
        sps = [kb.sb("sp_%d" % i, [128, 512], F32) for i in range(2)]
        eGs = [kb.sb("eG_%d" % i, [128, 512], F32) for i in range(2)]
        enGs = [kb.sb("enG_%d" % i, [128, 512], F32) for i in range(2)]
        dKs = [kb.sb("dK_%d" % i, [128, 512], F32) for i in range(2)]
        gends = [kb.sb("gend_%d" % i, [128, 4], F32) for i in range(2)]
        khTs = [kb.sb("khT_%d" % i, [128, 512], BF16) for i in range(2)]
        qs = [kb.sb("qs%d" % d, [128, 4, TT], BF16) for d in range(2)]
        ks = [kb.sb("ks%d" % d, [128, 4, TT], BF16) for d in range(2)]
        khs = [kb.sb("khs%d" % d, [128, 2, 512], BF16) for d in range(2)]
        print("phaseA sbuf remaining", self.nc.sbuf_bytes_remaining)
        tl = self.tiles(False)
        src, sbuf = self.x_src(tl[0][2], tl[0][3], False)
        self.load(xt[0][:, :, :], src, sbuf, xt[0])
        def do_norm(tj):
            is_c = tl[tj][2]
            A, B, mb = (self.A1c, self.B1c, self.modC) if is_c else (self.A1, self.B1, self.modL)
            for sub in range(2):
                self.norm_T(xt[tj % 2][:, sub, :], xt[tj % 2], A, B, mb, hnT[tj % 2], TT, sub, 0)

        if len(tl) > 1:
            src, sbuf = self.x_src(tl[1][2], tl[1][3], False)
            self.load(xt[1][:, :, :], src, sbuf, xt[1])
        do_norm(0)
        for ti, (idx, tok0, is_ctx, li) in enumerate(tl):
            X = xt[ti % 2]
            H = hnT[ti % 2]
            qkT = qkTs[ti % 2]
            for fc in range(8):
                bk = 1 + fc % 2
                for kc in range(8):
                    self.mm(self.pf(bk, 0, TT), Wqk[:, kc, fc * 128:(fc + 1) * 128], H[:, kc, :], kc == 0, kc == 7,
                            [Wqk, H], [self.bank[bk]])
                self.act(qkT[:, fc, :], self.pf(bk, 0, TT), AF.Copy, [self.bank[bk]], [qkT],
                         scale=(128.0 ** -0.5 if fc < 4 else 1.0))
            VT = vtok[ti % 2]
            for sub in range(2):
                for half in range(2):
                    bk = 3 + half
                    for kc in range(8):
                        self.mm(self.pf(bk, 0, 512), H[:, kc, sub * 128:(sub + 1) * 128],
                                Wv[:, kc, half * 512:(half + 1) * 512], kc == 0, kc == 7, [H, Wv], [self.bank[bk]])
                    self.cp(VT[:, sub, half * 512:(half + 1) * 512], self.pf(bk, 0, 512), [self.bank[bk]], [VT])
            self.store(AP(self.gv, tok0 * DM, [[DM, 128], [128 * DM, 2], [1, DM]]), VT[:, :, :], VT, self.gv)
            for kc in range(8):
                self.mm(self.pf(5, 0, TT, 32), Wlr[:, kc, :], H[:, kc, :], kc == 0, kc == 7, [Wlr, H], [self.bank[5]])
            self.act(lraug[0:32, :], self.pf(5, 0, TT, 32), AF.Copy, [self.bank[5]], [lraug])
            if ti + 1 < len(tl):
                do_norm(ti + 1)
                if ti + 2 < len(tl):
                    src, sbuf = self.x_src(tl[ti + 2][2], tl[ti + 2][3], False)
                    self.load(xt[ti % 2][:, :, :], src, sbuf, xt[ti % 2])
            def s1(sub, d):
                e1, sp = e1s[d], sps[d]
                bx = 6 if d == 0 else 1
                self.mm(self.pf(bx, 0, 512), lraug[0:33, sub * 128:(sub + 1) * 128], LRW[d][:, :], True, True,
                        [lraug, LRW[d]], [self.bank[bx]])
                self.act(e1[:, :], self.pf(bx, 0, 512), AF.Exp, [self.bank[bx]], [e1], scale=-1.0)
                self.act(sp[:, :], e1[:, :], AF.Ln, [e1], [sp], bias=self.oneb[:, 0:1])

            def s2(sub, d):
                ch = (tok0 // 128) + sub
                sp, eG, enG, dK, gend = sps[d], eGs[d], enGs[d], dKs[d], gends[d]
                bg = 7 if d == 0 else 2
                for h in range(4):
                    self.mm(self.pf(bg, h * 128, 128), sp[:, h * 128:(h + 1) * 128], U[d][:, :], True, True,
                            [sp, U[d]], [self.bank[bg]])
                G = self.pf(bg, 0, 512)
                ecol = 127 if d == 0 else 0
                self.act(eG[:, :], G, AF.Exp, [self.bank[bg]], [eG])
                self.act(gend[:, :], AP(self.PP[bg // 2], (bg % 2) * 512 + ecol, [[1024, 128], [128, 4]]), AF.Copy,
                         [self.bank[bg]], [gend])
                self.act(enG[:, :], G, AF.Exp, [self.bank[bg]], [enG], scale=-1.0)
                for h in range(4):
                    self.act(dK[:, h * 128:(h + 1) * 128], self.pf(bg, h * 128, 128), AF.Exp,
                             [self.bank[bg], gend], [dK], scale=-1.0, bias=gend[:, h:h + 1])
                self.cp(AP(self.Egla, (d * self.NCH + ch) * 4, [[2 * self.NCH * 4, 128], [1, 4]]),
                        AP(eG, ecol, [[512, 128], [128, 4]]), [eG], [self.Egla])

            def s3(sub, d):
                eG, enG, dK, khT = eGs[d], enGs[d], dKs[d], khTs[d]
                bt = 5 if d == 0 else 3
                qv = AP(qkT, sub * 128, [[8 * TT, 128], [TT, 4], [1, 128]])
                kv = AP(qkT, 4 * TT + sub * 128, [[8 * TT, 128], [TT, 4], [1, 128]])
                g3 = lambda t: AP(t, 0, [[512, 128], [128, 4], [1, 128]])
                self.tt(AP(qs[d], sub * 128, [[4 * TT, 128], [TT, 4], [1, 128]]), qv, g3(eG), ALU.mult,
                        [qkT, eG], [qs[d]])
                self.tt(AP(ks[d], sub * 128, [[4 * TT, 128], [TT, 4], [1, 128]]), kv, g3(enG), ALU.mult,
                        [qkT, enG], [ks[d]])
                self.tt(g3(khT), kv, g3(dK), ALU.mult, [qkT, dK], [khT])
                for h in range(4):
                    self.tr(self.pb16(bt, h * 128, 128), khT[:, h * 128:(h + 1) * 128], self.identb[:, :],
                            [khT, self.identb], [self.bank[bt]])
                self.cp(khs[d][:, sub, :], self.pb16(bt, 0, 512), [self.bank[bt]], [khs[d]])

            combos = [(sub, d) for sub in range(2) for d in range(2)]
            for k in range(len(combos) + 2):
                if k < len(combos):
                    s1(*combos[k])
                if 0 <= k - 1 < len(combos):
                    s2(*combos[k - 1])
                if 0 <= k - 2 < len(combos):
                    s3(*combos[k - 2])
            for d in range(2):
                self.store(AP(self.gq[d], tok0, [[NT, 128], [128 * NT, 4], [1, TT]]), qs[d][:, :, :], qs[d], self.gq[d])
                self.store(AP(self.gk[d], tok0, [[NT, 128], [128 * NT, 4], [1, TT]]), ks[d][:, :, :], ks[d], self.gk[d])
                self.store(AP(self.gkh[d], tok0 * 512, [[512, 128], [128 * 512, 2], [1, 512]]), khs[d][:, :, :],
                           khs[d], self.gkh[d])
        kb.barrier()
        kb.release(m)

    def gla_scan(self):
        kb = self.kb
        m = kb.mark()
        NT, NCH = self.NT, self.NCH
        S = [kb.sb("Sg%d" % d, [128, 4, 256], F32) for d in range(2)]
        Sb = [kb.sb("Sgb%d" % d, [128, 4, 256], BF16) for d in range(2)]
        msk = [kb.sb("gm%d" % d, [128, 128], F32) for d in range(2)]
        for d in range(2):
            self.memset(S[d][:, :, :], 0.0, [S[d]])
            self.memset(Sb[d][:, :, :], 0.0, [Sb[d]])
            self.load(msk[d][:, :], AP(self.gmask, d * 128 * 128, [[128, 128], [1, 128]]), self.gmask, msk[d])
        nb = 2
        qt = [[kb.sb("sq%d_%d" % (d, i), [128, 4, 128], BF16) for i in range(nb)] for d in range(2)]
        kt = [[kb.sb("sk%d_%d" % (d, i), [128, 4, 128], BF16) for i in range(nb)] for d in range(2)]
        kh = [[kb.sb("skh%d_%d" % (d, i), [128, 512], BF16) for i in range(nb)] for d in range(2)]
        vv = [[kb.sb("sv%d_%d" % (d, i), [128, DM], BF16) for i in range(nb)] for d in range(2)]
        AT = [kb.sb("AT%d" % d, [128, 4, 128], BF16) for d in range(2)]
        osb = [kb.sb("osb%d" % d, [128, DM], F32) for d in range(2)]
        order = [list(range(NCH)), [1, 0] + list(range(NCH - 1, 1, -1))]

        def issue_loads(s):
            for d in range(2):
                c = order[d][s]
                t0 = c * 128
                i = s % nb
                self.load(qt[d][i][:, :, :], AP(self.gq[d], t0, [[NT, 128], [128 * NT, 4], [1, 128]]), self.gq[d], qt[d][i])
                self.load(kt[d][i][:, :, :], AP(self.gk[d], t0, [[NT, 128], [128 * NT, 4], [1, 128]]), self.gk[d], kt[d][i])
                self.load(kh[d][i][:, :], AP(self.gkh[d], t0 * 512, [[512, 128], [1, 512]]), self.gkh[d], kh[d][i])
                self.load(vv[d][i][:, :], AP(self.gv, t0 * DM, [[DM, 128], [1, DM]]), self.gv, vv[d][i])

        issue_loads(0)
        for s in range(NCH):
            if s + 1 < NCH:
                issue_loads(s + 1)
            for d in range(2):
                c = order[d][s]
                i = s % nb
                Q, K_, KH, V = qt[d][i], kt[d][i], kh[d][i], vv[d][i]
                bA = d
                bO = 2 + 2 * d
                bS = 6
                for h in range(4):
                    self.mm(self.pf(bA, h * 128, 128), K_[:, h, :], Q[:, h, :], True, True, [K_, Q], [self.bank[bA]])
                self.tt(AT[d][:, :, :], AP(self.PP[bA // 2], (bA % 2) * 512, [[1024, 128], [128, 4], [1, 128]]),
                        AP(msk[d], 0, [[128, 128], [0, 4], [1, 128]]), ALU.mult, [self.bank[bA], msk[d]], [AT[d]])
                for h in range(4):
                    bk = bO + h // 2
                    o_ap = self.pf(bk, (h % 2) * 256, 256)
                    self.mm(o_ap, AT[d][:, h, :], V[:, h * 256:(h + 1) * 256], True, False, [AT[d], V], [self.bank[bk]])
                    self.mm(o_ap, Q[:, h, :], Sb[d][:, h, :], False, True, [Q, Sb[d]], [self.bank[bk]])
                self.act(osb[d][:, :], AP(self.PP[bO // 2], 0, [[1024, 128], [1, 1024]]), AF.Copy,
                         [self.bank[bO], self.bank[bO + 1]], [osb[d]])
                self.store(AP(self.go[d], c * 128 * DM, [[DM, 128], [1, DM]]), osb[d][:, :], osb[d], self.go[d])
                for h in range(4):
                    bk = bS + h // 2
                    self.mm(self.pf(bk, (h % 2) * 256, 256), KH[:, h * 128:(h + 1) * 128], V[:, h * 256:(h + 1) * 256],
                            True, True, [KH, V], [self.bank[bk]])
                for h in range(4):
                    bk = bS + h // 2
                    self.stt(S[d][:, h, :], S[d][:, h, :],
                             AP(self.Egla, (d * NCH + c) * 4 + h, [[2 * NCH * 4, 128], [1, 1]]),
                             self.pf(bk, (h % 2) * 256, 256), ALU.mult, ALU.add,
                             [S[d], self.Egla, self.bank[bk]], [S[d]])
                self.act(Sb[d][:, :, :], S[d][:, :, :], AF.Copy, [S[d]], [Sb[d]])
        kb.barrier()
        kb.release(m)

    def phaseB(self):
        kb = self.kb
        m = kb.mark()
        NT, NCH = self.NT, self.NCH
        Wgd = kb.sb("Wgd", [128, 8, 3072], BF16)
        Wab = kb.sb("Wab", [128, 8, 32], BF16)
        self.wload(Wgd, self.w_in, SPL["dq"], 3072, rowlen=PIN)
        self.wload(Wab, self.w_in, SPL["ab"], 32, rowlen=PIN)
        cw = kb.sb("cw", [128, 24, 5], F32)
        self.load(AP(cw, 0, [[120, 128], [1, 120]]), self.cwT[:, :], self.cwT, cw)
        negA = kb.sb("negA", [128, 16], F32)
        dtb = kb.sb("dtbs", [128, 16], F32)
        self.load(negA[:, :], self.alog[:, :], self.alog, negA)
        self.load(dtb[:, :], self.dtb[:, :], self.dtb, dtb)
        self.act(negA[:, :], negA[:, :], AF.Exp, [negA], [negA])
        self.kb.op("act", lambda e: e.mul(negA[:, :], negA[:, :], -1.0), [negA], [negA])
        CU = [kb.sb("CU%d" % i, [128, 128], F32) for i in range(4)]
        for i in range(4):
            self.load(CU[i][:, :], AP(self.cumU, (2 + i) * 128 * 128, [[128, 128], [1, 128]]), self.cumU, CU[i])
        UFi, UBi, UBs, UFs = CU
        MK = [kb.sb("MK%d" % i, [128, 128], BF16) for i in range(4)]
        for i in range(4):
            self.kb.dma("pool", MK[i][:, :], AP(self.dmask, i * 128 * 128, [[128, 128], [1, 128]]), [self.dmask], [MK[i]])
        Minc = [MK[0], MK[1]]
        Mlt, Mgt = MK[2], MK[3]
        sel = kb.sb("sel", [96, 32, 128], BF16)
        self.kb.dma("pool", AP(sel, 0, [[4096, 96], [1, 4096]]), self.sel_d[:, :], [self.sel_d], [sel])
        xt = [kb.sb("xtB%d" % i, [128, 2, DM], F32) for i in range(1)] * 2
        hnT = kb.sb("hnTB", [128, 8, TT], BF16)
        acc = [kb.sb("acc%d" % i, [128, TT], F32) for i in range(5)]
        sg = [kb.sb("sgb%d" % i, [128, TT], F32) for i in range(2)]
        sil = [kb.sb("sil%d" % i, [128, TT], F32) for i in range(4)]
        sq = [kb.sb("sqb%d" % i, [128, TT], BF16) for i in range(2)]
        rn = [kb.sb("rn%d" % i, [128, TT], F32) for i in range(2)]
        vTb = [kb.sb("vTb%d" % i, [128, TT], BF16) for i in range(2)]
        dqs = kb.sb("dqs", [128, 8, TT], BF16)
        dks = kb.sb("dks", [128, 8, TT], BF16)
        vtk = kb.sb("vtk", [128, 2, DM], BF16)
        ktk = kb.sb("ktk", [128, 2, DM], BF16)
        gbuf = []
        for gi in range(2):
            gbuf.append((kb.sb("g16_%d" % gi, [128, 16], F32), kb.sb("t16_%d" % gi, [128, 16], F32),
                         kb.sb("beta_%d" % gi, [128, 16], F32), kb.sb("lnb_%d" % gi, [128, 16], F32),
                         kb.sb("ba_%d" % gi, [128, 16], F32), kb.sb("R_%d" % gi, [128, 32], F32),
                         kb.sb("negG_%d" % gi, [128, 16], F32), kb.sb("R1_%d" % gi, [128, 32], F32),
                         kb.sb("Rs_%d" % gi, [128, 96], BF16), kb.sb("rows_%d" % gi, [96, 128], BF16),
                         kb.sb("nrows_%d" % gi, [96, 128], BF16)))
        Dm = kb.sb("Dm", [128, 8, 128], F32)
        E1 = kb.sb("E1m", [128, 8, 128], F32)
        E2 = Dm
        aqs = kb.sb("aqs", [128, 8, 128], BF16)
        X0 = kb.sb("X0n", [128, 8, 128], BF16)
        LT = [kb.sb("LTn%d" % d, [128, 8, 128], BF16) for d in range(2)]
        Dn = [[kb.sb("Dn%d_%d" % (d, g), [128, 4, 128], BF16) for g in range(2)] for d in range(2)]
        DTn = [[kb.sb("DTn%d_%d" % (d, g), [128, 4, 128], BF16) for g in range(2)] for d in range(2)]
        Wn = [[kb.sb("Wn%d_%d" % (d, g), [128, 4, 128], BF16) for g in range(2)] for d in range(2)]
        MT = [kb.sb("MTn%d" % d, [128, 8, 128], BF16) for d in range(2)]
        MaT = [kb.sb("MaTn%d" % d, [128, 8, 128], BF16) for d in range(2)]
        LM = kb.sb("LM", [128, 14, 128], BF16)
        self.kb.dma("pool", LM[:, :, :], AP(self.lvlm, 0, [[128, 128], [128 * 128, 14], [1, 128]]), [self.lvlm], [LM])
        lm = lambda i: AP(LM, i * 128, [[14 * 128, 128], [0, 8], [1, 128]])
        lm4 = lambda i: AP(LM, i * 128, [[14 * 128, 128], [0, 4], [1, 128]])
        nidb = kb.sb("nidb", [128, 128], BF16)
        self.kb.op("act", lambda e: e.mul(nidb[:, :], self.identb[:, :], -1.0), [self.identb], [nidb])
        us = kb.sb("us", [128, DM], F32)
        ws = kb.sb("wsn", [128, 8, 128], BF16)
        b3 = lambda t, c0: AP(t, c0, [[16, 128], [1, 8], [0, 128]])
        f3 = lambda t: AP(t, 0, [[1024, 128], [128, 8], [1, 128]])
        p3 = lambda k: AP(self.PP[k], 0, [[1024, 128], [128, 8], [1, 128]])
        pbk = lambda k: [self.bank[2 * k], self.bank[2 * k + 1]]
        print("phaseB sbuf remaining", self.nc.sbuf_bytes_remaining)
        tl = self.tiles(True)
        src, sbuf = self.x_src(tl[0][2], tl[0][3], True)
        self.load(xt[0][:, :, :], src, sbuf, xt[0])
        for ti, (idx, tok0, is_ctx, li) in enumerate(tl):
            Xt = xt[ti % 2]
            A, B, mb = (self.A1c, self.B1c, self.modC) if is_ctx else (self.A1, self.B1, self.modL)
            for sub in range(2):
                self.norm_T(Xt[:, sub, :], Xt, A, B, mb, hnT, TT, sub, 0)
            if ti + 1 < len(tl):
                src, sbuf = self.x_src(tl[ti + 1][2], tl[ti + 1][3], True)
                self.load(xt[(ti + 1) % 2][:, :, :], src, sbuf, xt[(ti + 1) % 2])
            def gates(sub):
                ch = tok0 // 128 + sub
                tsl = slice(sub * 128, (sub + 1) * 128)
                g16, t16, beta, lnb, ba, R, negG, R1, Rs, rows, nrows = gbuf[sub]
                for kc in range(8):
                    self.mm(self.pf(6, 0, 32), hnT[:, kc, tsl], Wab[:, kc, :], kc == 0, kc == 7, [hnT, Wab], [self.bank[6]])
                self.tt(t16[:, :], self.pf(6, 0, 16), dtb[:, :], ALU.add, [self.bank[6], dtb], [t16])
                self.act(t16[:, :], t16[:, :], AF.Exp, [t16], [t16])
                self.act(t16[:, :], t16[:, :], AF.Ln, [t16], [t16], bias=self.oneb[:, 0:1])
                self.tt(g16[:, :], t16[:, :], negA[:, :], ALU.mult, [t16, negA], [g16])
                self.act(lnb[:, :], self.pf(6, 16, 16), AF.Exp, [self.bank[6]], [lnb], scale=-1.0)
                self.act(lnb[:, :], lnb[:, :], AF.Ln, [lnb], [lnb], bias=self.oneb[:, 0:1])
                self.kb.op("act", lambda e: e.mul(lnb[:, :], lnb[:, :], -1.0), [lnb], [lnb])
                self.act(beta[:, :], lnb[:, :], AF.Exp, [lnb], [beta])
                self.mm(self.pf(7, 0, 8), UFi[:, :], g16[:, 0:8], True, True, [UFi, g16], [self.bank[7]])
                self.mm(self.pf(7, 8, 8), UBi[:, :], g16[:, 8:16], True, True, [UBi, g16], [self.bank[7]])
                self.mm(self.pf(7, 16, 8), UBs[:, :], g16[:, 0:8], True, True, [UBs, g16], [self.bank[7]])
                self.mm(self.pf(7, 24, 8), UFs[:, :], g16[:, 8:16], True, True, [UFs, g16], [self.bank[7]])
                self.mm(self.pf(7, 32, 16), self.onesf[:, :], g16[:, :], True, True, [self.onesf, g16], [self.bank[7]])
                self.act(R[:, 0:16], self.pf(7, 0, 16), AF.Copy, [self.bank[7]], [R])
                self.act(AP(self.GDa, ch * 16, [[NCH * 16, 128], [1, 16]]), self.pf(7, 0, 16), AF.Exp,
                         [self.bank[7]], [self.GDa])
                self.act(AP(self.GDd, ch * 16, [[NCH * 16, 128], [1, 16]]), self.pf(7, 16, 16), AF.Exp,
                         [self.bank[7]], [self.GDd])
                self.act(AP(self.GDe, ch * 16, [[NCH * 16, 128], [1, 16]]), self.pf(7, 32, 16), AF.Exp,
                         [self.bank[7]], [self.GDe])
                self.tt(ba[:, :], beta[:, :], AP(self.GDa, ch * 16, [[NCH * 16, 128], [1, 16]]), ALU.mult,
                        [beta, self.GDa], [ba])
                self.tt(R[:, 16:32], R[:, 0:16], lnb[:, :], ALU.add, [R, lnb], [R])
                self.kb.op("act", lambda e: e.mul(negG[:, :], R[:, 0:16], -1.0), [R], [negG])
                self.cp(Rs[:, 0:32], R[:, :], [R], [Rs])
                self.tt(R1[:, :], R[:, :], Rs[:, 0:32], ALU.subtract, [R, Rs], [R1])
                self.cp(Rs[:, 32:64], R1[:, :], [R1], [Rs])
                self.tt(R1[:, :], R1[:, :], Rs[:, 32:64], ALU.subtract, [R1, Rs], [R1])
                self.cp(Rs[:, 64:96], R1[:, :], [R1], [Rs])
                self.tr(self.pb16(6, 0, 128, 96), Rs[:, :], self.identb[:, :], [Rs, self.identb], [self.bank[6]])
                self.act(rows[:, :], self.pb16(6, 0, 128, 96), AF.Copy, [self.bank[6]], [rows])
                self.kb.op("act", lambda e: e.mul(nrows[:, :], rows[:, :], -1.0), [rows], [nrows])
            for sub in range(2):
                gates(sub)
            nseg, ls = (1, TT) if is_ctx else (2, 128)

            def stA(fc):
                bk = fc % 4
                for kc in range(8):
                    self.mm(self.pf(bk, 0, TT), Wgd[:, kc, fc * 128:(fc + 1) * 128], hnT[:, kc, :], kc == 0, kc == 7,
                            [Wgd, hnT], [self.bank[bk]])

            def conv_ops(fc):
                bk = fc % 4
                A_ = acc[fc % 5]
                ops = [lambda: self.ts(A_[:, :], self.pf(bk, 0, TT), cw[:, fc, 2:3], None, ALU.mult, None,
                                       [self.bank[bk], cw], [A_])]
                for tap in (0, 1, 3, 4):
                    sh = tap - 2
                    n = ls - abs(sh)
                    o0, i0 = (0, sh) if sh > 0 else (-sh, 0)
                    oap = AP(A_, o0, [[TT, 128], [ls, nseg], [1, n]])
                    iap = AP(self.PP[bk // 2], (bk % 2) * 512 + i0, [[1024, 128], [ls, nseg], [1, n]])
                    ops.append(lambda oap=oap, iap=iap, tap=tap: self.stt(oap, iap, cw[:, fc, tap:tap + 1], oap,
                                                                        ALU.mult, ALU.add, [self.bank[bk], cw, A_], [A_]))
                return ops

            def stB(fa, fb):
                oa = conv_ops(fa) if (fa is not None and 0 <= fa < 24) else None
                ob = conv_ops(fb) if (fb is not None and 0 <= fb < 24) else None
                seq = [(ob, 2), (oa, 0), (ob, 3), (oa, 1), (ob, 4)]
                for (o, i) in seq:
                    if o is not None:
                        o[i]()


            def stCE(fc_c, fc_e):
                okc = fc_c is not None and 0 <= fc_c < 24
                oke = fc_e is not None and 0 <= fc_e < 16
                if okc:
                    A_, G_ = acc[fc_c % 5], sg[fc_c % 2]
                    self.act(G_[:, :], A_[:, :], AF.Exp, [A_], [G_], scale=-1.0)
                if oke:
                    S_, Q_, R_ = sil[fc_e % 4], sq[fc_e % 2], rn[fc_e % 2]
                    bq = 6 + fc_e % 2
                    self.act(Q_[:, :], S_[:, :], AF.Square, [S_], [Q_])
                    self.mm(self.pf(bq, 0, TT), self.onesb[:, :], Q_[:, :], True, True, [self.onesb, Q_], [self.bank[bq]])
                if okc:
                    self.act(G_[:, :], G_[:, :], AF.Ln, [G_], [G_], bias=self.oneb[:, 0:1])
                if oke:
                    self.act(R_[:, :], self.pf(bq, 0, TT), AF.Ln, [self.bank[bq]], [R_], bias=self.epsb[:, 0:1])
                if okc:
                    self.act(G_[:, :], G_[:, :], AF.Exp, [G_], [G_], scale=-1.0)
                if oke:
                    self.act(R_[:, :], R_[:, :], AF.Exp, [R_], [R_], scale=-0.5,
                             bias=(self.lncb[:, 0:1] if fc_e < 8 else self.zerob[:, 0:1]))

            def stD(fc):
                A_, G_ = acc[fc % 5], sg[fc % 2]
                if fc < 16:
                    S_ = sil[fc % 4]
                    self.ptt(S_[:, :], A_[:, :], G_[:, :], ALU.mult, [A_, G_], [S_])
                else:
                    h = fc - 16
                    V_ = vTb[fc % 2]
                    self.ptt(V_[:, :], A_[:, :], G_[:, :], ALU.mult, [A_, G_], [V_])
                    for sub in range(2):
                        self.tr(self.pb16(4 + sub, h * 128, 128), V_[:, sub * 128:(sub + 1) * 128], self.identb[:, :],
                                [V_, self.identb], [self.bank[4 + sub]])

            def stF(fc):
                if fc < 16:
                    h = fc % 8
                    S_, R_ = sil[fc % 4], rn[fc % 2]
                    dst = dqs if fc < 8 else dks
                    self.ptt(dst[:, h, :], S_[:, :], R_[:, :], ALU.mult, [S_, R_], [dst])

            for it in range(24 + 6):
                if it < 24:
                    stA(it)
                stB(it - 1, it - 2)
                stCE(it - 3, it - 5)
                if 0 <= it - 4 < 24:
                    stD(it - 4)
                if 0 <= it - 6 < 24:
                    stF(it - 6)
            for sub in range(2):
                self.cp(vtk[:, sub, :], self.pb16(4 + sub, 0, 1024), [self.bank[4 + sub]], [vtk])
            for sub in range(2):
                for h in range(8):
                    self.tr(self.pb16(4 + sub, h * 128, 128), dks[:, h, sub * 128:(sub + 1) * 128], self.identb[:, :],
                            [dks, self.identb], [self.bank[4 + sub]])
                self.cp(ktk[:, sub, :], self.pb16(4 + sub, 0, 1024), [self.bank[4 + sub]], [ktk])
            self.store(AP(self.dqT, tok0, [[NT, 128], [128 * NT, 8], [1, TT]]), dqs[:, :, :], dqs, self.dqT)
            self.store(AP(self.dkt, tok0 * DM, [[DM, 128], [128 * DM, 2], [1, DM]]), ktk[:, :, :], ktk, self.dkt)
            for sub in range(2):
                ch = tok0 // 128 + sub
                tsl = slice(sub * 128, (sub + 1) * 128)
                g16, t16, beta, lnb, ba, R, negG, R1, Rs, rows, nrows = gbuf[sub]
                for h in range(8):
                    bk = h // 4
                    self.mm(self.pf(bk, (h % 4) * 128, 128), dks[:, h, tsl], dks[:, h, tsl], True, True,
                            [dks], [self.bank[bk]])
                for h in range(8):
                    bk = 2 + h // 4
                    self.mm(self.pf(bk, (h % 4) * 128, 128), dks[:, h, tsl], dqs[:, h, tsl], True, True,
                            [dks, dqs], [self.bank[bk]])
                for d in range(2):
                    mstrT = Mlt if d == 0 else Mgt
                    mstr = Mgt if d == 0 else Mlt
                    specs = [(Dm, 0, rows, nrows, 0, Minc[d]), (E1, 16, rows, nrows, 0, mstrT), (E2, 0, nrows, rows, 16, mstr)]
                    for si, (dst, so, rr, pr, po, mk) in enumerate(specs):
                        pk = 2 + (si % 2)
                        for hg in range(2):
                            bk = 2 * pk + hg
                            c0 = d * 8 + hg * 4
                            self.mm(self.pf(bk, 0, 512), self.identb[:, :], AP(mk, 0, [[128, 128], [0, 4], [1, 128]]),
                                    True, False, [self.identb, mk], [self.bank[bk]])
                            self.mm(self.pf(bk, 0, 512), pr[:, :], AP(sel, (po + c0) * 128, [[32 * 128, 96], [1, 512]]),
                                    False, False, [sel, pr], [self.bank[bk]])
                            for h4i in range(4):
                                self.mm(self.pf(bk, h4i * 128, 128), sel[:, so + c0 + h4i, :], rr[:, :], False, h4i == 3,
                                        [sel, rr], [self.bank[bk]])
                        for hg in range(2):
                            bk = 2 * pk + hg
                            self.act(AP(dst, hg * 512, [[1024, 128], [1, 512]]), self.pf(bk, 0, 512), AF.Exp,
                                     [self.bank[bk]], [dst])
                        if si == 0:
                            self.tt(f3(aqs), p3(1), f3(Dm), ALU.mult, pbk(1) + [Dm], [aqs])
                            self.store(AP(self.daq[d], ch * 128 * DM, [[DM, 128], [1, DM]]), aqs[:, :, :], aqs, self.daq[d])
                    self.tt(f3(LT[d]), p3(0), f3(E1), ALU.mult, pbk(0) + [E1], [LT[d]])
                    self.tt(f3(X0), p3(0), f3(E2), ALU.mult, pbk(0) + [E2], [X0])
                    if "dbgX" in self.dbg and ch == 2 and d == 0:
                        self.store(AP(self.dbgX, 0, [[DM, 128], [1, DM]]), X0[:, :, :], X0, self.dbgX)
                        self.store(AP(self.dbgXT, 0, [[DM, 128], [1, DM]]), LT[d][:, :, :], LT[d], self.dbgXT)
                    for hg in range(2):
                        h4 = lambda t: AP(t, 0, [[512, 128], [128, 4], [1, 128]])
                        idb = AP(self.identb, 0, [[128, 128], [0, 4], [1, 128]])
                        src4 = lambda t: AP(t, hg * 512, [[1024, 128], [128, 4], [1, 128]])
                        W_, D_, DT_ = Wn[d][hg], Dn[d][hg], DTn[d][hg]
                        self.tt(h4(W_), src4(X0), lm4(d * 7), ALU.mult, [X0, LM], [W_])
                        self.tt(h4(D_), h4(W_), idb, ALU.add, [W_, self.identb], [D_])
                        self.tt(h4(W_), src4(LT[d]), lm4((1 - d) * 7), ALU.mult, [LT[d], LM], [W_])
                        self.tt(h4(DT_), h4(W_), idb, ALU.add, [W_, self.identb], [DT_])
                h4 = lambda t: AP(t, 0, [[512, 128], [128, 4], [1, 128]])
                chains = [(d, hg) for d in range(2) for hg in range(2)]
                for lvl in range(1, 7):
                    for (d, hg) in chains:
                        k = d * 2 + hg
                        ba_ = 2 * k
                        W_, D_ = Wn[d][hg], Dn[d][hg]
                        self.mm(self.pf(ba_, 0, 512), self.identb[:, :], AP(nidb, 0, [[128, 128], [0, 4], [1, 128]]),
                                True, False, [self.identb, nidb], [self.bank[ba_]])
                        for h4i in range(4):
                            h = hg * 4 + h4i
                            self.mm(self.pf(ba_, h4i * 128, 128), LT[d][:, h, :], D_[:, h4i, :], False, h4i == 3,
                                    [LT[d], D_], [self.bank[ba_]])
                        self.tt(h4(W_), AP(self.PP[ba_ // 2], (ba_ % 2) * 512, [[1024, 128], [128, 4], [1, 128]]),
                                lm4(d * 7 + lvl), ALU.mult, [self.bank[ba_], LM], [W_])
                    for (d, hg) in chains:
                        k = d * 2 + hg
                        ba_, bb_ = 2 * k, 2 * k + 1
                        W_, D_, DT_ = Wn[d][hg], Dn[d][hg], DTn[d][hg]
                        if lvl < 6:
                            for h4i in range(4):
                                self.mm(self.pf(bb_, h4i * 128, 128), DT_[:, h4i, :], W_[:, h4i, :], True, True,
                                        [DT_, W_], [self.bank[bb_]])
                        for h4i in range(4):
                            self.mm(self.pf(ba_, h4i * 128, 128), W_[:, h4i, :], DT_[:, h4i, :], True, True,
                                    [W_, DT_], [self.bank[ba_]])
                        if lvl < 6:
                            if hg == 0:
                                self.act(AP(D_, 0, [[512, 128], [1, 512]]), self.pf(bb_, 0, 512), AF.Copy, [self.bank[bb_]], [D_])
                                self.cp(AP(DT_, 0, [[512, 128], [1, 512]]), self.pf(ba_, 0, 512), [self.bank[ba_]], [DT_])
                            else:
                                self.cp(AP(D_, 0, [[512, 128], [1, 512]]), self.pf(bb_, 0, 512), [self.bank[bb_]], [D_])
                                self.act(AP(DT_, 0, [[512, 128], [1, 512]]), self.pf(ba_, 0, 512), AF.Copy, [self.bank[ba_]], [DT_])
                        else:
                            pv = AP(self.PP[ba_ // 2], (ba_ % 2) * 512, [[1024, 128], [128, 4], [1, 128]])
                            mo = lambda t: AP(t, hg * 512, [[1024, 128], [128, 4], [1, 128]])
                            b4 = lambda t, c0: AP(t, c0, [[16, 128], [1, 4], [0, 128]])
                            self.tt(mo(MT[d]), pv, b4(beta, d * 8 + hg * 4), ALU.mult, [self.bank[ba_], beta], [MT[d]])
                            self.tt(mo(MaT[d]), pv, b4(ba, d * 8 + hg * 4), ALU.mult, [self.bank[ba_], ba], [MaT[d]])
                for d in range(2):
                    pa, pb_ = 2 * d, 2 * d + 1
                    if "dbgX" in self.dbg and ch == 2 and d == 0:
                        self.store(AP(self.dbgMT, 0, [[DM, 128], [1, DM]]), MT[d][:, :, :], MT[d], self.dbgMT)
                    for h in range(8):
                        bk = 2 * pb_ + h // 4
                        self.mm(self.pf(bk, (h % 4) * 128, 128), MT[d][:, h, :], vtk[:, sub, h * 128:(h + 1) * 128], True, True,
                                [MT[d], vtk], [self.bank[bk]])
                    self.act(us[:, :], AP(self.PP[pb_], 0, [[1024, 128], [1, 1024]]), AF.Copy, pbk(pb_), [us])
                    self.store(AP(self.du[d], ch * 128 * DM, [[DM, 128], [1, DM]]), us[:, :], us, self.du[d])
                    for h in range(8):
                        bk = 2 * pa + h // 4
                        self.mm(self.pf(bk, (h % 4) * 128, 128), ktk[:, sub, h * 128:(h + 1) * 128], MaT[d][:, h, :], True, True,
                                [ktk, MaT[d]], [self.bank[bk]])
                    self.cp(f3(ws), p3(pa), pbk(pa), [ws])
                    self.store(AP(self.dw[d], ch * 128, [[NT, 128], [128 * NT, 8], [1, 128]]), ws[:, :, :], ws, self.dw[d])
        kb.barrier()
        kb.release(m)

    def gdn_scan(self):
        kb = self.kb
        m = kb.mark()
        NT, NCH = self.NT, self.NCH
        S = [kb.sb("Sd%d" % d, [128, 8, 128], F32) for d in range(2)]
        Sb = [kb.sb("Sdb%d" % d, [128, 8, 128], BF16) for d in range(2)]
        for d in range(2):
            self.memset(S[d][:, :, :], 0.0, [S[d]])
            self.memset(Sb[d][:, :, :], 0.0, [Sb[d]])
        nb = 2
        mk = lambda n, sh, dt: [[kb.sb("%s%d_%d" % (n, d, i), sh, dt) for i in range(nb)] for d in range(2)]
        wT = mk("lw", [128, 8, 128], BF16)
        qT = mk("lq", [128, 8, 128], BF16)
        uu = mk("lu", [128, DM], F32)
        aq = mk("la", [128, DM], BF16)
        kt = mk("lk", [128, DM], BF16)
        vn32 = kb.sb("vn32", [128, DM], F32)
        vnb = kb.sb("vnb", [128, DM], BF16)
        vnd = kb.sb("vnd", [128, DM], BF16)
        tq = kb.sb("tq", [128, DM], F32)
        osb = [kb.sb("odb%d" % d, [128, DM], F32) for d in range(2)]
        order = [list(range(NCH)), [1, 0] + list(range(NCH - 1, 1, -1))]
        b3 = lambda t, c0: AP(t, c0, [[NCH * 16, 128], [1, 8], [0, 128]])
        f3 = lambda t: AP(t, 0, [[1024, 128], [128, 8], [1, 128]])
        p3 = lambda k: AP(self.PP[k], 0, [[1024, 128], [128, 8], [1, 128]])
        pbk = lambda k: [self.bank[2 * k], self.bank[2 * k + 1]]

        def issue_loads(s):
            for d in range(2):
                c = order[d][s]
                t0 = c * 128
                i = s % nb
                self.load(wT[d][i][:, :, :], AP(self.dw[d], t0, [[NT, 128], [128 * NT, 8], [1, 128]]), self.dw[d], wT[d][i])
                self.load(qT[d][i][:, :, :], AP(self.dqT, t0, [[NT, 128], [128 * NT, 8], [1, 128]]), self.dqT, qT[d][i])
                self.load(uu[d][i][:, :], AP(self.du[d], t0 * DM, [[DM, 128], [1, DM]]), self.du[d], uu[d][i])
                self.load(aq[d][i][:, :], AP(self.daq[d], t0 * DM, [[DM, 128], [1, DM]]), self.daq[d], aq[d][i])
                self.load(kt[d][i][:, :], AP(self.dkt, t0 * DM, [[DM, 128], [1, DM]]), self.dkt, kt[d][i])

        issue_loads(0)
        for s in range(NCH):
            if s + 1 < NCH:
                issue_loads(s + 1)
            for d in range(2):
                c = order[d][s]
                i = s % nb
                Wt, Qt, Uu, Aq, Kt = wT[d][i], qT[d][i], uu[d][i], aq[d][i], kt[d][i]
                for h in range(8):
                    bk = h // 4
                    self.mm(self.pf(bk, (h % 4) * 128, 128), Wt[:, h, :], Sb[d][:, h, :], True, True, [Wt, Sb[d]], [self.bank[bk]])
                for h in range(8):
                    bk = 2 + h // 4
                    self.mm(self.pf(bk, (h % 4) * 128, 128), Qt[:, h, :], Sb[d][:, h, :], True, True, [Qt, Sb[d]], [self.bank[bk]])
                self.tt(vn32[:, :], Uu[:, :], AP(self.PP[0], 0, [[1024, 128], [1, 1024]]), ALU.subtract, [Uu] + pbk(0), [vn32])
                self.act(vnb[:, :], vn32[:, :], AF.Copy, [vn32], [vnb])
                self.tt(f3(vnd), f3(vn32), b3(self.GDd, c * 16 + d * 8), ALU.mult, [vn32, self.GDd], [vnd])
                for h in range(8):
                    bk = 4 + h // 4
                    hs = slice(h * 128, (h + 1) * 128)
                    self.mm(self.pf(bk, (h % 4) * 128, 128), Aq[:, hs], vnb[:, hs], True, True, [Aq, vnb], [self.bank[bk]])
                for h in range(8):
                    bk = 6 + h // 4
                    hs = slice(h * 128, (h + 1) * 128)
                    self.mm(self.pf(bk, (h % 4) * 128, 128), Kt[:, hs], vnd[:, hs], True, True, [Kt, vnd], [self.bank[bk]])
                self.tt(f3(tq), p3(1), b3(self.GDa, c * 16 + d * 8), ALU.mult, pbk(1) + [self.GDa], [tq])
                self.tt(osb[d][:, :], tq[:, :], AP(self.PP[2], 0, [[1024, 128], [1, 1024]]), ALU.add, [tq] + pbk(2), [osb[d]])
                self.store(AP(self.do[d], c * 128 * DM, [[DM, 128], [1, DM]]), osb[d][:, :], osb[d], self.do[d])
                self.tt(f3(S[d]), f3(S[d]), b3(self.GDe, c * 16 + d * 8), ALU.mult, [S[d], self.GDe], [S[d]])
                self.tt(f3(S[d]), f3(S[d]), p3(3), ALU.add, [S[d]] + pbk(3), [S[d]])
                self.act(Sb[d][:, :, :], S[d][:, :, :], AF.Copy, [S[d]], [Sb[d]])
        kb.barrier()
        kb.release(m)

    def phase3a(self):
        kb = self.kb
        m = kb.mark()
        W, NT = self.W, self.NT
        Wz = kb.sb("Wz", [128, 8, 4096], BF16)
        for gi, key in enumerate(("gz", "dz", "g11", "g12")):
            for k0 in (0, 4):
                self.kb.dma("pool", AP(Wz, k0 * 4096 + gi * 1024, [[8 * 4096, 128], [4096, 4], [1, 1024]]),
                            AP(self.w_in, k0 * 128 * PIN + SPL[key], [[PIN, 128], [128 * PIN, 4], [1, 1024]]),
                            [self.w_in], [Wz])
        Wo = kb.sb("Wo", [128, 8, DM], BF16)
        self.wload(Wo, self.w_out, 0, DM, rowlen=DM)
        gg = kb.sb("ggl", [128, DM], F32)
        gd = kb.sb("ggd", [128, DM], F32)
        self.load(gg[:, :], self.glag[:, :], self.glag, gg)
        self.load(gd[:, :], self.gdng[:, :], self.gdng, gd)
        xt = [kb.sb("xt3%d" % i, [128, DM], F32) for i in range(3)]
        ol = [[kb.sb("ol%d_%d" % (j, i), [128, DM], F32) for i in range(2)] for j in range(4)]
        hnT = [kb.sb("hnT3_%d" % i, [128, 8, 128], BF16) for i in range(2)]
        sz = [kb.sb("sz%d" % i, [128, 2048], BF16) for i in range(2)]
        sg = [kb.sb("sg%d" % i, [128, 2048], BF16) for i in range(2)]
        ss = kb.sb("ss3", [128, 12], F32)
        rs = kb.sb("rs3", [128, 12], F32)
        mb16 = [kb.sb("mb16_%d" % i, [128, DM], BF16) for i in range(2)]
        mT = [kb.sb("mT%d" % i, [128, 8, 128], BF16) for i in range(2)]
        x1 = [kb.sb("x1s%d" % i, [128, DM], F32) for i in range(2)]
        print("phase3a sbuf remaining", self.nc.sbuf_bytes_remaining)
        ntile = self.L // 128

        def issue_loads(t):
            i = t % 2
            self.load(xt[t % 3][:, :], AP(self.x, t * 128 * DM, [[DM, 128], [1, DM]]), self.x, xt[t % 3])
            g0 = (CTX + t * 128) * DM
            self.load(ol[0][i][:, :], AP(self.go[0], g0, [[DM, 128], [1, DM]]), self.go[0], ol[0][i])
            self.load(ol[1][i][:, :], AP(self.go[1], g0, [[DM, 128], [1, DM]]), self.go[1], ol[1][i])
            nr = 128 // W if W <= 128 else 1
            for j in (0, 1):
                for rr in range(128 // W):
                    r = t * (128 // W) + rr
                    self.load(AP(ol[2 + j][i], rr * W * DM, [[DM, W], [1, DM]]),
                              AP(self.do[j], (CTX + r) * DM, [[128 * DM, W], [1, DM]]), self.do[j], ol[2 + j][i])

        def stage_a(t):
            i = t % 2
            H, SZ, SG = hnT[i], sz[i], sg[i]
            self.norm_T(xt[t % 3][:, :], xt[t % 3], self.A1, self.B1, self.modL, H, 128, 0, 0)
            for g in range(8):
                bk = 1 + g % 4
                for kc in range(8):
                    self.mm(self.pf(bk, 0, 512), H[:, kc, :], Wz[:, kc, g * 512:(g + 1) * 512], kc == 0, kc == 7,
                            [H, Wz], [self.bank[bk]])
                if g < 4:
                    self.act(SZ[:, g * 512:(g + 1) * 512], self.pf(bk, 0, 512), AF.Silu, [self.bank[bk]], [SZ])
                else:
                    self.act(SG[:, (g - 4) * 512:(g - 3) * 512], self.pf(bk, 0, 512), AF.Sigmoid, [self.bank[bk]], [SG])

        def stage_b(t):
            i = t % 2
            SZ, SG, MB, MT_ = sz[i], sg[i], mb16[i], mT[i]
            og, od = ol[0][i], ol[2][i]
            self.tt(og[:, :], ol[0][i][:, :], ol[1][i][:, :], ALU.add, [ol[0][i], ol[1][i]], [og])
            self.tt(od[:, :], ol[2][i][:, :], ol[3][i][:, :], ALU.add, [ol[2][i], ol[3][i]], [od])
            for h in range(4):
                self.act(self.junk[:, 0:256], og[:, h * 256:(h + 1) * 256], AF.Square, [og], [self.junk, ss],
                         accum=ss[:, h:h + 1])
            for h in range(8):
                self.act(self.junk[:, 0:128], od[:, h * 128:(h + 1) * 128], AF.Square, [od], [self.junk, ss],
                         accum=ss[:, 4 + h:5 + h])
            self.act(rs[:, 0:4], ss[:, 0:4], AF.Ln, [ss], [rs], scale=1.0 / 256, bias=self.epsb[:, 0:1])
            self.act(rs[:, 4:12], ss[:, 4:12], AF.Ln, [ss], [rs], scale=1.0 / 128, bias=self.epsb[:, 0:1])
            self.act(rs[:, :], rs[:, :], AF.Exp, [rs], [rs], scale=-0.5)
            self.tt(AP(og, 0, [[DM, 128], [256, 4], [1, 256]]), AP(og, 0, [[DM, 128], [256, 4], [1, 256]]),
                    AP(rs, 0, [[12, 128], [1, 4], [0, 256]]), ALU.mult, [og, rs], [og])
            self.tt(AP(od, 0, [[DM, 128], [128, 8], [1, 128]]), AP(od, 0, [[DM, 128], [128, 8], [1, 128]]),
                    AP(rs, 4, [[12, 128], [1, 8], [0, 128]]), ALU.mult, [od, rs], [od])
            self.tt(og[:, :], og[:, :], gg[:, :], ALU.mult, [og, gg], [og])
            self.tt(od[:, :], od[:, :], gd[:, :], ALU.mult, [od, gd], [od])
            self.tt(og[:, :], og[:, :], SZ[:, 0:1024], ALU.mult, [og, SZ], [og])
            self.tt(od[:, :], od[:, :], SZ[:, 1024:2048], ALU.mult, [od, SZ], [od])
            self.tt(og[:, :], og[:, :], SG[:, 0:1024], ALU.mult, [og, SG], [og])
            self.tt(od[:, :], od[:, :], SG[:, 1024:2048], ALU.mult, [od, SG], [od])
            self.tt(MB[:, :], og[:, :], od[:, :], ALU.add, [og, od], [MB])
            for kc in range(8):
                self.tr(self.pb16(5, kc * 128, 128), MB[:, kc * 128:(kc + 1) * 128], self.identb[:, :],
                        [MB, self.identb], [self.bank[5]])
            self.act(AP(MT_, 0, [[1024, 128], [1, 1024]]), self.pb16(5, 0, 1024), AF.Copy, [self.bank[5]], [MT_])

        def stage_b2(t):
            i = t % 2
            MT_ = mT[i]
            for half in range(2):
                bk = 6 + half
                for kc in range(8):
                    self.mm(self.pf(bk, 0, 512), MT_[:, kc, :], Wo[:, kc, half * 512:(half + 1) * 512], kc == 0, kc == 7,
                            [MT_, Wo], [self.bank[bk]])

        def stage_c(t):
            i = t % 2
            X = xt[t % 3]
            self.tt(x1[i][:, :], AP(self.PP[3], 0, [[1024, 128], [1, 1024]]), self.G1, ALU.mult,
                    [self.bank[6], self.bank[7], self.modL], [x1[i]])
            self.tt(x1[i][:, :], x1[i][:, :], X[:, :], ALU.add, [x1[i], X], [x1[i]])
            self.store(AP(self.dx1, t * 128 * DM, [[DM, 128], [1, DM]]), x1[i][:, :], x1[i], self.dx1)

        issue_loads(0)
        if ntile > 1:
            issue_loads(1)
        stage_a(0)
        for t in range(ntile):
            if t + 1 < ntile:
                stage_a(t + 1)
            stage_b(t)
            if t >= 1:
                stage_c(t - 1)
            stage_b2(t)
            if t + 2 < ntile:
                issue_loads(t + 2)
        stage_c(ntile - 1)
        kb.barrier()
        kb.release(m)

    def phase3b(self):
        kb = self.kb
        m = kb.mark()
        Wg = kb.sb("Wg", [128, 8, FFH], BF16)
        Wu = kb.sb("Wu", [128, 8, FFH], BF16)
        Wd = kb.sb("Wd", [128, 22, DM], BF16)
        for (dst, src) in ((Wg, self.wg), (Wu, self.wu)):
            for k0 in range(0, 8, 2):
                self.kb.dma("pool", AP(dst, k0 * FFH, [[8 * FFH, 128], [FFH, 2], [1, FFH]]),
                            AP(src, k0 * 128 * FFH, [[FFH, 128], [128 * FFH, 2], [1, FFH]]), [src], [dst])
        for k0 in range(0, 22, 2):
            self.kb.dma("pool", AP(Wd, k0 * DM, [[22 * DM, 128], [DM, 2], [1, DM]]),
                        AP(self.wd, k0 * 128 * DM, [[DM, 128], [128 * DM, 2], [1, DM]]), [self.wd], [Wd])
        fn = kb.sb("fnw", [128, DM], F32)
        self.load(fn[:, :], self.fnbc[:, :], self.fnbc, fn)
        xt = [kb.sb("x1t%d" % i, [128, 2, DM], F32) for i in range(1)]
        h2T = kb.sb("h2T", [128, 8, TT], BF16)
        sgl = kb.sb("sgl", [128, TT], F32)
        actT = kb.sb("actT", [128, 22, TT], BF16)
        ty = kb.sb("ty2", [128, 512], F32)
        x2s = [kb.sb("x2_%d" % i, [128, DM], F32) for i in range(2)]
        print("phase3b sbuf remaining", self.nc.sbuf_bytes_remaining)
        ss, rs = self.ssq, self.rsd
        ntile = self.L // TT

        def issue_load(t):
            self.load(xt[0][:, :, :], AP(self.dx1, t * TT * DM, [[DM, 128], [128 * DM, 2], [1, DM]]), self.dx1, xt[0])

        issue_load(0)
        oc = 0
        for t in range(ntile):
            X = xt[0]
            if t > 0:
                issue_load(t)
            for sub in range(2):
                self.norm_T(X[:, sub, :], X, self.A2, self.B2, self.modL2, h2T, TT, sub, 0)
            for hc in range(22):
                bg = 1 + (hc % 2) * 2
                bu = bg + 1
                for kc in range(8):
                    self.mm(self.pf(bg, 0, TT), Wg[:, kc, hc * 128:(hc + 1) * 128], h2T[:, kc, :], kc == 0, kc == 7,
                            [Wg, h2T], [self.bank[bg]])
                for kc in range(8):
                    self.mm(self.pf(bu, 0, TT), Wu[:, kc, hc * 128:(hc + 1) * 128], h2T[:, kc, :], kc == 0, kc == 7,
                            [Wu, h2T], [self.bank[bu]])
                self.act(sgl[:, :], self.pf(bg, 0, TT), AF.Silu, [self.bank[bg]], [sgl])
                self.tt(actT[:, hc, :], sgl[:, :], self.pf(bu, 0, TT), ALU.mult, [sgl, self.bank[bu]], [actT])
            for sub in range(2):
                x2 = x2s[oc % 2]
                for half in range(2):
                    bk = 5 + half
                    for hc in range(22):
                        self.mm(self.pf(bk, 0, 512), actT[:, hc, sub * 128:(sub + 1) * 128],
                                Wd[:, hc, half * 512:(half + 1) * 512], hc == 0, hc == 21, [actT, Wd], [self.bank[bk]])
                    hs = slice(half * 512, (half + 1) * 512)
                    self.tt(ty[:, :], self.pf(bk, 0, 512), AP(self.modL2, 2 * DM + half * 512, [[3 * DM, 128], [1, 512]]),
                            ALU.mult, [self.bank[bk], self.modL2], [ty])
                    self.tt(x2[:, hs], ty[:, :], X[:, sub, hs], ALU.add, [ty, X], [x2])
                self.act(self.ntmp[:, :], x2[:, :], AF.Square, [x2], [self.ntmp, ss], accum=ss[:, 2:3])
                self.act(rs[:, 2:3], ss[:, 2:3], AF.Sqrt, [ss], [rs], scale=1.0 / DM, bias=self.epsb[:, 0:1])
                self.recip(rs[:, 3:4], rs[:, 2:3], [rs], [rs])
                O = x2
                oc += 1
                self.stt(O[:, :], x2[:, :], rs[:, 3:4], fn[:, :], ALU.mult, ALU.mult, [x2, rs, fn], [O])
                self.store(AP(self.out, (t * TT + sub * 128) * DM, [[DM, 128], [1, DM]]), O[:, :], O, self.out)
        kb.barrier()
        kb.release(m)

    def build(self, phases="0ASBG3F"):
        kb = self.kb
        self.consts()
        self.epsb = kb.sb("epsb", [128, 1], F32)
        self.oneb = kb.sb("oneb", [128, 1], F32)
        self.lncb = kb.sb("lncb", [128, 1], F32)
        self.zerob = kb.sb("zerob", [128, 1], F32)
        self.memset(self.lncb[:, :], -0.5 * float(np.log(128.0)), [self.lncb])
        self.memset(self.zerob[:, :], 0.0, [self.zerob])
        self.memset(self.epsb[:, :], EPS, [self.epsb])
        self.memset(self.oneb[:, :], 1.0, [self.oneb])
        NCH = self.NCH
        self.modL2 = kb.sb("modL2", [128, 3 * DM], F32)
        ma = kb.mark()
        self.modL1 = kb.sb("modL1", [128, 3 * DM], F32)
        mb_ = kb.mark()
        self.modC = kb.sb("modC", [128, 2 * DM], F32)
        self.Egla = kb.sb("Egla", [128, 2 * NCH * 4], F32)
        self.GDa = kb.sb("GDa", [128, NCH * 16], F32)
        self.GDd = kb.sb("GDd", [128, NCH * 16], F32)
        self.GDe = kb.sb("GDe", [128, NCH * 16], F32)
        self.phase0()
        if "A" in phases:
            self.phaseA()
        if "S" in phases:
            self.gla_scan()
        if "B" in phases:
            self.phaseB()
        if "G" in phases:
            self.gdn_scan()
        kb.release(mb_)
        if "3" in phases:
            self.phase3a()
        kb.release(ma)
        if "F" in phases:
            self.phase3b()
        kb.barrier()
        kb.finalize()
        kb.close()
        return self.nc


def host_consts():
    p = np.arange(128)[:, None]
    f = np.arange(128)[None, :]
    le = (p <= f).astype(np.float32)
    ge = (p >= f).astype(np.float32)
    lt = (p < f).astype(np.float32)
    gt = (p > f).astype(np.float32)
    cumU = np.stack([le * (-1.0 / 16.0), ge * (-1.0 / 16.0), le, ge, gt, lt]).astype(np.float32)
    gmask = np.stack([le, ge]).astype(np.float32)
    dmask = np.stack([(1 - le) * NEG, (1 - ge) * NEG, (1 - lt) * NEG, (1 - gt) * NEG]).astype(np.float32)
    sel = np.zeros((96, 32, 128), np.float32)
    for c in range(32):
        for part in range(3):
            sel[part * 32 + c, c, :] = 1.0
    lv = np.zeros((14, 128, 128), np.float32)
    pi = np.arange(128)[:, None]
    fj = np.arange(128)[None, :]
    for s_ in range(7):
        b = 1 << s_
        mlow = ((pi // (2 * b)) == (fj // (2 * b))) & ((pi % (2 * b)) >= b) & ((fj % (2 * b)) < b)
        dg = np.eye(128, dtype=np.float32) if s_ >= 1 else 0.0
        lv[s_] = -mlow.astype(np.float32) - dg
        lv[7 + s_] = -mlow.T.astype(np.float32) - dg
    return dict(identf=np.eye(128, dtype=np.float32), cumU=cumU, gmask=gmask, dmask=dmask,
                sel=sel.reshape(96, 32 * 128), lvlm=lv)


def host_inputs(b, x, c, ctx, c_ctx, w_mod, b_mod, norm1_w, norm2_w, w_in, gla_lr_w, gla_lr_b, gla_norm_w,
                gdn_conv_w, gdn_a_log, gdn_dt_bias, gdn_norm_w, w_out, ffn_w_gate, ffn_w_up, ffn_w_down,
                final_norm_w, shared):
    f = lambda a: np.ascontiguousarray(a, dtype=np.float32)
    d = dict(shared)
    d["x"] = f(x[b])
    d["ctx"] = f(ctx[b])
    d["cT"] = f(np.asarray(c[b]).reshape(8, 128).T)
    return d


def shared_inputs(c_ctx, w_mod, b_mod, norm1_w, norm2_w, w_in, gla_lr_w, gla_lr_b, gla_norm_w,
                  gdn_conv_w, gdn_a_log, gdn_dt_bias, gdn_norm_w, w_out, ffn_w_gate, ffn_w_up, ffn_w_down,
                  final_norm_w):
    f = lambda a: np.ascontiguousarray(a, dtype=np.float32)
    bc = lambda v: f(np.broadcast_to(np.asarray(v).reshape(1, -1), (128, np.asarray(v).size)))
    d = host_consts()
    d["cctxT"] = f(np.asarray(c_ctx).reshape(8, 128).T)
    d["w_mod"] = f(w_mod[0])
    d["b_mod"] = f(np.asarray(b_mod[0]).reshape(1, -1))
    d["n1bc"] = bc(norm1_w[0])
    d["n2bc"] = bc(norm2_w[0])
    d["fnbc"] = bc(final_norm_w)
    d["w_in"] = f(w_in[0])
    lrw = np.zeros((2, 33, 512), np.float32)
    lrw[0, 0:16] = np.asarray(gla_lr_w[0, 0])
    lrw[1, 16:32] = np.asarray(gla_lr_w[0, 1])
    lrw[0, 32] = np.asarray(gla_lr_b[0, 0])
    lrw[1, 32] = np.asarray(gla_lr_b[0, 1])
    d["lrw"] = lrw
    d["glag"] = bc(np.asarray(gla_norm_w[0]).reshape(-1))
    d["gdng"] = bc(np.asarray(gdn_norm_w[0]).reshape(-1))
    cw = np.asarray(gdn_conv_w[0])
    d["cwT"] = f(cw.reshape(5, 24, 128).transpose(2, 1, 0).reshape(128, 120))
    d["alog"] = bc(np.asarray(gdn_a_log[0]).reshape(-1))
    d["dtb"] = bc(np.asarray(gdn_dt_bias[0]).reshape(-1))
    d["w_out"] = f(w_out[0])
    d["wg"] = f(ffn_w_gate[0])
    d["wu"] = f(ffn_w_up[0])
    d["wd"] = f(ffn_w_down[0])
    return d


_CACHE = {}


def kernel(x, c, ctx, c_ctx, w_mod, b_mod, norm1_w, norm2_w, w_in, gla_lr_w, gla_lr_b, gla_norm_w,
           gdn_conv_w, gdn_a_log, gdn_dt_bias, gdn_norm_w, w_out, ffn_w_gate, ffn_w_up, ffn_w_down,
           final_norm_w):
    x = np.asarray(x)
    B, L, _ = x.shape
    W = L // 128
    nc = Prog(W).build()
    shared = shared_inputs(c_ctx, w_mod, b_mod, norm1_w, norm2_w, w_in, gla_lr_w, gla_lr_b, gla_norm_w,
                           gdn_conv_w, gdn_a_log, gdn_dt_bias, gdn_norm_w, w_out, ffn_w_gate, ffn_w_up,
                           ffn_w_down, final_norm_w)
    f = lambda a: np.ascontiguousarray(a, dtype=np.float32)
    in_maps = []
    for b in range(B):
        d = dict(shared)
        d["x"] = f(x[b])
        d["ctx"] = f(np.asarray(ctx)[b])
        d["cT"] = f(np.asarray(c)[b].reshape(8, 128).T)
        in_maps.append(d)
    res = run_bass_kernel_spmd(nc, in_maps, core_ids=list(range(B)))
    return np.stack([np.asarray(r["out"], dtype=np.float32) for r in res.results], axis=0)
```

```python
import numpy as np
import ml_dtypes
import concourse.bass as bass
import concourse.mybir as mybir
from concourse.bass_utils import run_bass_kernel_spmd

F32 = mybir.dt.float32
BF16 = mybir.dt.bfloat16
AF = mybir.ActivationFunctionType
ALU = mybir.AluOpType

SAME_ENG_SYNC = True
NOSYNC_ENGS = ()
EPOCH = 20000
EPS = 1e-6
DM = 1024
CTX = 256
TT = 256
NEG = -30000.0


class Buf:
    __slots__ = ("t", "lw", "rd", "sem", "cnt", "name", "dram", "_scope", "semq")

    def __init__(self, t, name, dram=False):
        self.t = t
        self.name = name
        self.lw = {}
        self.rd = {}
        self.sem = None
        self.cnt = 0
        self.dram = dram
        self._scope = 0
        self.semq = None

    def __getitem__(self, idx):
        return self.t[idx]


class EngState:
    def __init__(self, name):
        self.name = name
        self.ops = []
        self.count = 0
        self.waited = {}
        self.needed = set()


class KB:
    def __init__(self, nc):
        self.nc = nc
        self.eng = {n: EngState(n) for n in ("pe", "act", "dve", "pool", "sp")}
        self._ctx = []
        self.sems = {}
        self.sempool = []
        self.sembufs = []
        self._sem_ctx = []

    def mark(self):
        return len(self._ctx)

    def release(self, m):
        for b in self.sembufs:
            if b.sem is not None and (not b.dram) and b._scope >= m:
                self.sempool.append((b.sem, b.cnt, b.semq))
                b.sem = None
        self.sembufs = [b for b in self.sembufs if b.sem is not None]
        while len(self._ctx) > m:
            cm = self._ctx.pop()
            cm.__exit__(None, None, None)

    def sb(self, name, shape, dt=F32):
        cm = self.nc.sbuf_tensor("sb_" + name, list(shape), dt)
        t = cm.__enter__()
        b = Buf(t, name)
        b._scope = len(self._ctx)
        self._ctx.append(cm)
        return b

    def ps(self, name, shape, dt=F32):
        cm = self.nc.psum_tensor(name, list(shape), dt)
        t = cm.__enter__()
        self._ctx.append(cm)
        return t

    def dram(self, name, shape, dt=F32, kind="Internal"):
        t = self.nc.dram_tensor(name, list(shape), dt, kind=kind)
        return Buf(t, name, dram=True)

    def _get_sem(self, b, q):
        if b.sem is not None:
            assert b.semq == q, "buffer %s used by both DMA queue kinds" % b.name
        if b.sem is None:
            b.semq = q
            cand = [i for i, e in enumerate(self.sempool) if e[2] == q]
            if cand:
                b.sem, b.cnt, _ = self.sempool.pop(cand[-1])
            else:
                cm = self.nc.semaphore("s%d" % len(self.sems))
                b.sem = cm.__enter__()
                self._sem_ctx.append(cm)
                b.cnt = 0
                self.sems[id(b.sem)] = b.sem
            self.sembufs.append(b)

    def _waits(self, E, reads, writes, extra=None):
        deps = {}

        def add(d):
            for k, v in d.items():
                if deps.get(k, -1) < v:
                    deps[k] = v

        for b in reads:
            add(b.lw)
        for b in writes:
            add(b.lw)
            add(b.rd)
        if extra:
            add(extra)
        for k, v in deps.items():
            if k[0] == "E" and k[1] == E.name:
                if E.name in ("pe", "sp") or not SAME_ENG_SYNC or E.name in NOSYNC_ENGS:
                    continue
            if E.waited.get(k, -1) >= v:
                continue
            E.waited[k] = v
            if k[0] == "E":
                self.eng[k[1]].needed.add(v)
            E.ops.append(("w", k, v))

    def op(self, eng, fn, reads=(), writes=()):
        E = self.eng[eng]
        self._waits(E, reads, writes)
        idx = E.count
        E.count += 1
        E.ops.append(("o", fn, idx))
        key = ("E", eng)
        for b in writes:
            if b.dram:
                b.lw[key] = idx
            else:
                b.lw = {key: idx}
                b.rd = {}
        for b in reads:
            b.rd[key] = idx

    def dma(self, q, out, in_, reads, writes):
        E = self.eng[q]
        self._waits(E, reads, writes)
        cand = [b for b in list(writes) + list(reads) if not b.dram]
        sb = cand[0]
        self._get_sem(sb, q)
        sb.cnt += 16
        c = sb.cnt
        key = ("S", id(sb.sem))
        E.ops.append(("d", out, in_, sb.sem))
        for b in writes:
            if b.dram:
                b.lw[key] = c
            else:
                b.lw = {key: c}
                b.rd = {}
        for b in reads:
            b.rd[key] = c

    def barrier(self):
        ev = {}
        for n in ("pe", "act", "dve", "pool"):
            if self.eng[n].count > 0:
                ev[("E", n)] = self.eng[n].count - 1
        for b in self.sembufs:
            if b.sem is not None and b.cnt > 0:
                ev[("S", id(b.sem))] = b.cnt
        for n, E in self.eng.items():
            deps = dict(ev)
            for k, v in deps.items():
                if k[0] == "E" and k[1] == n:
                    continue
                if E.waited.get(k, -1) >= v:
                    continue
                E.waited[k] = v
                if k[0] == "E":
                    self.eng[k[1]].needed.add(v)
                E.ops.append(("w", k, v))

    def finalize(self):
        nc = self.nc
        engsems = {}
        rank = {}
        for name, E in self.eng.items():
            nd = sorted(E.needed)
            rank[name] = {idx: r for r, idx in enumerate(nd)}
            nsem = (len(nd) + EPOCH - 1) // EPOCH
            lst = []
            for i in range(nsem):
                cm = nc.semaphore("e_%s_%d" % (name, i))
                lst.append(cm.__enter__())
                self._sem_ctx.append(cm)
            engsems[name] = lst
        engobj = {"pe": nc.tensor, "act": nc.scalar, "dve": nc.vector, "pool": nc.gpsimd, "sp": nc.sync}

        def replay(E, e):
            for o in E.ops:
                if o[0] == "w":
                    k, v = o[1], o[2]
                    if k[0] == "E":
                        r = rank[k[1]][v]
                        e.wait_ge(engsems[k[1]][r // EPOCH], r % EPOCH + 1)
                    else:
                        e.wait_ge(self.sems[k[1]], v)
                elif o[0] == "o":
                    inst = o[1](e)
                    r = rank[E.name].get(o[2])
                    if r is not None:
                        inst.then_inc(engsems[E.name][r // EPOCH], 1)
                else:
                    e.dma_start(out=o[1], in_=o[2]).then_inc(o[3], 16)

        with nc.Block() as block:
            @block.tensor
            def _(e):
                replay(self.eng["pe"], e)

            @block.scalar
            def _(e):
                replay(self.eng["act"], e)

            @block.vector
            def _(e):
                replay(self.eng["dve"], e)

            @block.gpsimd
            def _(e):
                replay(self.eng["pool"], e)

            @block.sync
            def _(e):
                replay(self.eng["sp"], e)

    def close(self):
        while self._ctx:
            self._ctx.pop().__exit__(None, None, None)
        while self._sem_ctx:
            self._sem_ctx.pop().__exit__(None, None, None)


def AP(buf, offset, pairs):
    return bass.AP(buf.t if isinstance(buf, Buf) else buf, offset, [list(p) for p in pairs])


SPL = dict(gq=0, gk=512, gv=1024, gz=2048, lr=3072, dq=3104, dk=4128, dv=5152, dz=6176,
           ab=7200, g11=7232, g12=8256)
PIN = 9280
FFH = 2816


class Prog:
    def __init__(self, W, dbg=()):
        self.W = W
        self.L = 128 * W
        self.NT = CTX + self.L
        self.NCH = self.NT // 128
        self.dbg = set(dbg)
        nc = bass.Bass("TRN2", target_bir_lowering=False)
        self.nc = nc
        self.kb = KB(nc)
        kb = self.kb
        L, NT = self.L, self.NT
        ein = lambda n, s: kb.dram(n, s, F32, kind="ExternalInput")
        self.x = ein("x", [L, DM])
        self.ctx = ein("ctx", [CTX, DM])
        self.cT = ein("cT", [128, 8])
        self.cctxT = ein("cctxT", [128, 8])
        self.w_mod = ein("w_mod", [DM, 6 * DM])
        self.b_mod = ein("b_mod", [1, 6 * DM])
        self.n1bc = ein("n1bc", [128, DM])
        self.n2bc = ein("n2bc", [128, DM])
        self.fnbc = ein("fnbc", [128, DM])
        self.w_in = ein("w_in", [DM, PIN])
        self.lrw = ein("lrw", [2, 33, 512])
        self.glag = ein("glag", [128, DM])
        self.gdng = ein("gdng", [128, DM])
        self.cwT = ein("cwT", [128, 120])
        self.alog = ein("alog", [128, 16])
        self.dtb = ein("dtb", [128, 16])
        self.w_out = ein("w_out", [DM, DM])
        self.wg = ein("wg", [DM, FFH])
        self.wu = ein("wu", [DM, FFH])
        self.wd = ein("wd", [FFH, DM])
        self.identf_d = ein("identf", [128, 128])
        self.cumU = ein("cumU", [6, 128, 128])
        self.gmask = ein("gmask", [2, 128, 128])
        self.dmask = ein("dmask", [4, 128, 128])
        self.sel_d = ein("sel", [96, 32 * 128])
        self.lvlm = ein("lvlm", [14, 128, 128])
        self.out = kb.dram("out", [L, DM], F32, kind="ExternalOutput")

        def scr(n, s, dt):
            return kb.dram(n, s, dt, kind="ExternalOutput" if (n in self.dbg or n.startswith("dbg")) else "Internal")
        self.gq = [scr("gq%d" % d, [4, 128, NT], BF16) for d in range(2)]
        self.gk = [scr("gk%d" % d, [4, 128, NT], BF16) for d in range(2)]
        self.gkh = [scr("gkh%d" % d, [NT, 512], BF16) for d in range(2)]
        self.gv = scr("gv", [NT, DM], BF16)
        self.go = [scr("go%d" % d, [NT, DM], F32) for d in range(2)]
        self.dqT = scr("dqT", [8, 128, NT], BF16)
        self.dkt = scr("dkt", [NT, DM], BF16)
        self.du = [scr("du%d" % d, [NT, DM], F32) for d in range(2)]
        self.dw = [scr("dw%d" % d, [8, 128, NT], BF16) for d in range(2)]
        self.daq = [scr("daq%d" % d, [NT, DM], BF16) for d in range(2)]
        self.do = [scr("do%d" % d, [NT, DM], F32) for d in range(2)]
        self.dx1 = scr("dx1", [L, DM], F32)
        if "dbgX" in self.dbg:
            self.dbgX = scr("dbgX", [128, DM], BF16)
            self.dbgXT = scr("dbgXT", [128, DM], BF16)
            self.dbgMT = scr("dbgMT", [128, DM], BF16)
            self.dbg.update(["dbgXT", "dbgMT"])

        self.PP = [kb.ps("pp%d" % i, [128, 1024], F32) for i in range(4)]
        self.PPb = [p.bitcast(BF16) for p in self.PP]
        self.bank = [Buf(self.PP[i // 2], "bank%d" % i) for i in range(8)]

    def pf(self, b, c0, n, p=128):
        return AP(self.PP[b // 2], (b % 2) * 512 + c0, [[1024, p], [1, n]])

    def pb16(self, b, c0, n, p=128):
        return AP(self.PPb[b // 2], (b % 2) * 1024 + c0, [[2048, p], [1, n]])

    def mm(self, out, lhsT, rhs, start, stop, reads, writes):
        self.kb.op("pe", lambda e: e.matmul(out, lhsT=lhsT, rhs=rhs, start=start, stop=stop), reads, writes)

    def tr(self, out, in_, ident, reads, writes):
        self.kb.op("pe", lambda e: e.transpose(out=out, in_=in_, identity=ident), reads, writes)

    def act(self, out, in_, func, reads, writes, scale=None, bias=None, accum=None):
        kw = {}
        if scale is not None:
            kw["scale"] = scale
        if bias is not None:
            kw["bias"] = bias
        if accum is not None:
            kw["accum_out"] = accum
        self.kb.op("act", lambda e: e.activation(out=out, in_=in_, func=func, **kw), reads, writes)

    def tt(self, out, in0, in1, op, reads, writes):
        self.kb.op("dve", lambda e: e.tensor_tensor(out=out, in0=in0, in1=in1, op=op), reads, writes)

    def ptt(self, out, in0, in1, op, reads, writes):
        self.kb.op("pool", lambda e: e.tensor_tensor(out=out, in0=in0, in1=in1, op=op), reads, writes)

    def stt(self, out, in0, scalar, in1, op0, op1, reads, writes):
        self.kb.op("dve", lambda e: e.scalar_tensor_tensor(out=out, in0=in0, scalar=scalar, in1=in1,
                                                           op0=op0, op1=op1), reads, writes)

    def ts(self, out, in0, s1, s2, op0, op1, reads, writes):
        if s2 is None:
            self.kb.op("dve", lambda e: e.tensor_scalar(out=out, in0=in0, scalar1=s1, scalar2=None, op0=op0),
                       reads, writes)
        else:
            self.kb.op("dve", lambda e: e.tensor_scalar(out=out, in0=in0, scalar1=s1, scalar2=s2, op0=op0, op1=op1),
                       reads, writes)

    def cp(self, out, in_, reads, writes):
        self.kb.op("dve", lambda e: e.tensor_copy(out=out, in_=in_), reads, writes)

    def recip(self, out, in_, reads, writes):
        self.kb.op("dve", lambda e: e.reciprocal(out=out, in_=in_), reads, writes)

    def memset(self, ap, val, writes):
        self.kb.op("dve", lambda e: e.memset(ap, val), [], writes)

    def load(self, out, in_, src, dst, q="sp"):
        self.kb.dma(q, out, in_, [src], [dst])

    def store(self, out, in_, src, dst, q="pool"):
        self.kb.dma(q, out, in_, [src], [dst])

    def wload(self, dst, src, col0, ncols, nk=8, rowlen=None):
        rl = rowlen
        step = 4
        for k0 in range(0, nk, step):
            kn = min(step, nk - k0)
            self.kb.dma("pool", AP(dst, k0 * ncols, [[nk * ncols, 128], [ncols, kn], [1, ncols]]),
                        AP(src, k0 * 128 * rl + col0, [[rl, 128], [128 * rl, kn], [1, ncols]]), [src], [dst])

    def consts(self):
        kb = self.kb
        self.identf = kb.sb("identf", [128, 128], F32)
        self.identb = kb.sb("identb", [128, 128], BF16)
        self.load(self.identf[:, :], self.identf_d[:, :], self.identf_d, self.identf)
        self.kb.dma("pool", self.identb[:, :], self.identf_d[:, :], [self.identf_d], [self.identb])
        self.onesf = kb.sb("onesf", [128, 128], F32)
        self.onesb = kb.sb("onesb", [128, 128], BF16)
        self.memset(self.onesf[:, :], 1.0, [self.onesf])
        self.memset(self.onesb[:, :], 1.0, [self.onesb])
        self.junk = kb.sb("junk", [128, 256], BF16)
        self.ssq = kb.sb("ssq", [128, 16], F32)
        self.rsd = kb.sb("rsd", [128, 16], F32)
        self.ntmp = kb.sb("ntmp", [128, DM], F32)
        self.hnb = kb.sb("hnb", [128, DM], BF16)

    def phase0(self):
        kb = self.kb
        m = kb.mark()
        cT = kb.sb("cTs", [128, 8], F32)
        ccT = kb.sb("ccTs", [128, 8], F32)
        bm = kb.sb("bm", [1, 6 * DM], F32)
        n1 = kb.sb("n1", [128, DM], F32)
        n2 = kb.sb("n2", [128, DM], F32)
        SL = kb.sb("SL", [128, 8, 128], F32)
        SC = kb.sb("SC", [128, 8, 128], F32)
        wm = [kb.sb("wm%d" % i, [128, 8, 512], F32) for i in range(2)]
        self.load(cT[:, :], self.cT[:, :], self.cT, cT)
        self.load(ccT[:, :], self.cctxT[:, :], self.cctxT, ccT)
        self.load(bm[:, :], self.b_mod[:, :], self.b_mod, bm)
        self.load(n1[:, :], self.n1bc[:, :], self.n1bc, n1)
        self.load(n2[:, :], self.n2bc[:, :], self.n2bc, n2)
        for kc in range(8):
            self.act(SL[:, kc, :], self.onesf[:, :], AF.Silu, [self.onesf, cT], [SL], scale=cT[:, kc:kc + 1])
            self.act(SC[:, kc, :], self.onesf[:, :], AF.Silu, [self.onesf, ccT], [SC], scale=ccT[:, kc:kc + 1])
        for g in range(12):
            w = wm[g % 2]
            for k0 in (0, 4):
                self.load(AP(w, k0 * 512, [[8 * 512, 128], [512, 4], [1, 512]]),
                          AP(self.w_mod, k0 * 128 * 6 * DM + g * 512, [[6 * DM, 128], [128 * 6 * DM, 4], [1, 512]]),
                          self.w_mod, w)
            variants = [(SL, self.modL1 if g < 6 else self.modL2, 0, (g % 6) * 512)]
            if g < 4:
                variants.append((SC, self.modC, 1, g * 512))
            for (S_, dst, bi, dc0) in variants:
                bk = (g * 2 + bi) % 8
                for kc in range(8):
                    self.mm(self.pf(bk, 0, 512), S_[:, kc, :], w[:, kc, :], kc == 0, False, [S_, w], [self.bank[bk]])
                self.mm(self.pf(bk, 0, 512), self.onesf[0:1, :], bm[0:1, g * 512:(g + 1) * 512], False, True,
                        [self.onesf, bm], [self.bank[bk]])
                self.act(dst[:, dc0:dc0 + 512], self.pf(bk, 0, 512), AF.Copy, [self.bank[bk]], [dst])
        for (dst, nw, c0) in ((self.modL1, n1, DM), (self.modL2, n2, DM), (self.modC, n1, DM)):
            self.stt(dst[:, c0:c0 + DM], dst[:, c0:c0 + DM], 1.0, nw[:, :], ALU.add, ALU.mult, [dst, nw], [dst])
        kb.barrier()
        kb.release(m)
        ML, ML2, MC = self.modL1, self.modL2, self.modC
        self.modL = ML
        self.B1, self.A1, self.G1 = ML[:, 0:DM], ML[:, DM:2 * DM], ML[:, 2 * DM:3 * DM]
        self.B2, self.A2, self.G2 = ML2[:, 0:DM], ML2[:, DM:2 * DM], ML2[:, 2 * DM:3 * DM]
        self.B1c, self.A1c = MC[:, 0:DM], MC[:, DM:2 * DM]

    def norm_T(self, xt, xbuf, A, B, mbuf, hnT, Tn, sub, bk):
        ss, rs = self.ssq, self.rsd
        self.act(self.ntmp[:, :], xt, AF.Square, [xbuf], [self.ntmp, ss], accum=ss[:, 0:1])
        self.act(rs[:, 0:1], ss[:, 0:1], AF.Ln, [ss], [rs], scale=1.0 / DM, bias=self.epsb[:, 0:1])
        self.act(rs[:, 1:2], rs[:, 0:1], AF.Exp, [rs], [rs], scale=-0.5)
        self.stt(self.ntmp[:, :], xt, rs[:, 1:2], A, ALU.mult, ALU.mult, [xbuf, rs, mbuf], [self.ntmp])
        self.tt(self.hnb[:, :], self.ntmp[:, :], B, ALU.add, [self.ntmp, mbuf], [self.hnb])
        for kc in range(8):
            self.tr(self.pb16(bk, kc * 128, 128), self.hnb[:, kc * 128:(kc + 1) * 128], self.identb[:, :],
                    [self.hnb, self.identb], [self.bank[bk]])
        self.act(AP(hnT, sub * 128, [[8 * Tn, 128], [Tn, 8], [1, 128]]),
                 AP(self.PPb[bk // 2], (bk % 2) * 1024, [[2048, 128], [128, 8], [1, 128]]), AF.Copy,
                 [self.bank[bk]], [hnT])

    def tiles(self, colmajor):
        res = [(0, 0, True, None)]
        for i in range(self.L // TT):
            res.append((i + 1, CTX + i * TT, False, i))
        return res

    def x_src(self, is_ctx, li, colmajor):
        if is_ctx:
            return AP(self.ctx, 0, [[DM, 128], [128 * DM, 2], [1, DM]]), self.ctx
        if not colmajor:
            return AP(self.x, li * TT * DM, [[DM, 128], [128 * DM, 2], [1, DM]]), self.x
        c0 = li * 2
        return AP(self.x, c0 * DM, [[self.W * DM, 128], [DM, 2], [1, DM]]), self.x

    def phaseA(self):
        kb = self.kb
        m = kb.mark()
        NT = self.NT
        Wqk = kb.sb("Wqk", [128, 8, 1024], BF16)
        Wv = kb.sb("Wv", [128, 8, 1024], BF16)
        Wlr = kb.sb("Wlr", [128, 8, 32], BF16)
        self.wload(Wqk, self.w_in, SPL["gq"], 1024, rowlen=PIN)
        self.wload(Wv, self.w_in, SPL["gv"], 1024, rowlen=PIN)
        self.wload(Wlr, self.w_in, SPL["lr"], 32, rowlen=PIN)
        LRW = [kb.sb("LRW%d" % d, [33, 512], F32) for d in range(2)]
        U = [kb.sb("U%d" % d, [128, 128], F32) for d in range(2)]
        for d in range(2):
            self.load(LRW[d][:, :], AP(self.lrw, d * 33 * 512, [[512, 33], [1, 512]]), self.lrw, LRW[d])
            self.load(U[d][:, :], AP(self.cumU, d * 128 * 128, [[128, 128], [1, 128]]), self.cumU, U[d])
        xt = [kb.sb("xtA%d" % i, [128, 2, DM], F32) for i in range(2)]
        hnT = [kb.sb("hnTA%d" % i, [128, 8, TT], BF16) for i in range(2)]
        qkTs = [kb.sb("qkT%d" % i, [128, 8, TT], F32) for i in range(2)]
        vtok = [kb.sb("vtok%d" % i, [128, 2, DM], BF16) for i in range(2)]
        lraug = kb.sb("lraug", [33, TT], F32)
        self.memset(lraug[32:33, :], 1.0, [lraug])
        e1s = [kb.sb("e1_%d" % i, [128, 512], F32) for i in range(2)]
        sps = [kb.sb("sp_%d" % i, [128, 512], F32) for i in range(2)]
        eGs = [kb.sb("eG_%d" % i, [128, 512], F32) for i in range(2)]
        enGs = [kb.sb("enG_%d" % i, [128, 512], F32) for i in range(2)]
        dKs = [kb.sb("dK_%d" % i, [128, 512], F32) for i in range(2)]
        gends = [kb.sb("gend_%d" % i, [128, 4], F32) for i in range(2)]
        khTs = [kb.sb("khT_%d" % i, [128, 512], BF16) for i in range(2)]
        qs = [kb.sb("qs%d" % d, [128, 4, TT], BF16) for d in range(2)]
        ks = [kb.sb("ks%d" % d, [128, 4, TT], BF16) for d in range(2)]
        khs = [kb.sb("khs%d" % d, [128, 2, 512], BF16) for d in range(2)]
        print("phaseA sbuf remaining", self.nc.sbuf_bytes_remaining)
        tl = self.tiles(False)
        src, sbuf = self.x_src(tl[0][2], tl[0][3], False)
        self.load(xt[0][:, :, :], src, sbuf, xt[0])
        def do_norm(tj):
            is_c = tl[tj][2]
            A, B, mb = (self.A1c, self.B1c, self.modC) if is_c else (self.A1, self.B1, self.modL)
            for sub in range(2):
                self.norm_T(xt[tj % 2][:, sub, :], xt[tj % 2], A, B, mb, hnT[tj % 2], TT, sub, 0)

        if len(tl) > 1:
            src, sbuf = self.x_src(tl[1][2], tl[1][3], False)
            self.load(xt[1][:, :, :], src, sbuf, xt[1])
        do_norm(0)
        for ti, (idx, tok0, is_ctx, li) in enumerate(tl):
            X = xt[ti % 2]
            H = hnT[ti % 2]
            qkT = qkTs[ti % 2]
            for fc in range(8):
                bk = 1 + fc % 2
                for kc in range(8):
                    self.mm(self.pf(bk, 0, TT), Wqk[:, kc, fc * 128:(fc + 1) * 128], H[:, kc, :], kc == 0, kc == 7,
                            [Wqk, H], [self.bank[bk]])
                self.act(qkT[:, fc, :], self.pf(bk, 0, TT), AF.Copy, [self.bank[bk]], [qkT],
                         scale=(128.0 ** -0.5 if fc < 4 else 1.0))
            VT = vtok[ti % 2]
            for sub in range(2):
                for half in range(2):
                    bk = 3 + half
                    for kc in range(8):
                        self.mm(self.pf(bk, 0, 512), H[:, kc, sub * 128:(sub + 1) * 128],
                                Wv[:, kc, half * 512:(half + 1) * 512], kc == 0, kc == 7, [H, Wv], [self.bank[bk]])
                    self.cp(VT[:, sub, half * 512:(half + 1) * 512], self.pf(bk, 0, 512), [self.bank[bk]], [VT])
            self.store(AP(self.gv, tok0 * DM, [[DM, 128], [128 * DM, 2], [1, DM]]), VT[:, :, :], VT, self.gv)
            for kc in range(8):
                self.mm(self.pf(5, 0, TT, 32), Wlr[:, kc, :], H[:, kc, :], kc == 0, kc == 7, [Wlr, H], [self.bank[5]])
            self.act(lraug[0:32, :], self.pf(5, 0, TT, 32), AF.Copy, [self.bank[5]], [lraug])
            if ti + 1 < len(tl):
                do_norm(ti + 1)
                if ti + 2 < len(tl):
                    src, sbuf = self.x_src(tl[ti + 2][2], tl[ti + 2][3], False)
                    self.load(xt[ti % 2][:, :, :], src, sbuf, xt[ti % 2])
            def s1(sub, d):
                e1, sp = e1s[d], sps[d]
                bx = 6 if d == 0 else 1
                self.mm(self.pf(bx, 0, 512), lraug[0:33, sub * 128:(sub + 1) * 128], LRW[d][:, :], True, True,
                        [lraug, LRW[d]], [self.bank[bx]])
                self.act(e1[:, :], self.pf(bx, 0, 512), AF.Exp, [self.bank[bx]], [e1], scale=-1.0)
                self.act(sp[:, :], e1[:, :], AF.Ln, [e1], [sp], bias=self.oneb[:, 0:1])

            def s2(sub, d):
                ch = (tok0 // 128) + sub
                sp, eG, enG, dK, gend = sps[d], eGs[d], enGs[d], dKs[d], gends[d]
                bg = 7 if d == 0 else 2
                for h in range(4):
                    self.mm(self.pf(bg, h * 128, 128), sp[:, h * 128:(h + 1) * 128], U[d][:, :], True, True,
                            [sp, U[d]], [self.bank[bg]])
                G = self.pf(bg, 0, 512)
                ecol = 127 if d == 0 else 0
                self.act(eG[:, :], G, AF.Exp, [self.bank[bg]], [eG])
                self.act(gend[:, :], AP(self.PP[bg // 2], (bg % 2) * 512 + ecol, [[1024, 128], [128, 4]]), AF.Copy,
                         [self.bank[bg]], [gend])
                self.act(enG[:, :], G, AF.Exp, [self.bank[bg]], [enG], scale=-1.0)
                for h in range(4):
                    self.act(dK[:, h * 128:(h + 1) * 128], self.pf(bg, h * 128, 128), AF.Exp,
                             [self.bank[bg], gend], [dK], scale=-1.0, bias=gend[:, h:h + 1])
                self.cp(AP(self.Egla, (d * self.NCH + ch) * 4, [[2 * self.NCH * 4, 128], [1, 4]]),
                        AP(eG, ecol, [[512, 128], [128, 4]]), [eG], [self.Egla])

            def s3(sub, d):
                eG, enG, dK, khT = eGs[d], enGs[d], dKs[d], khTs[d]
                bt = 5 if d == 0 else 3
                qv = AP(qkT, sub * 128, [[8 * TT, 128], [TT, 4], [1, 128]])
                kv = AP(qkT, 4 * TT + sub * 128, [[8 * TT, 128], [TT, 4], [1, 128]])
                g3 = lambda t: AP(t, 0, [[512, 128], [128, 4], [1, 128]])
                self.tt(AP(qs[d], sub * 128, [[4 * TT, 128], [TT, 4], [1, 128]]), qv, g3(eG), ALU.mult,
                        [qkT, eG], [qs[d]])
                self.tt(AP(ks[d], sub * 128, [[4 * TT, 128], [TT, 4], [1, 128]]), kv, g3(enG), ALU.mult,
                        [qkT, enG], [ks[d]])
                self.tt(g3(khT), kv, g3(dK), ALU.mult, [qkT, dK], [khT])
                for h in range(4):
                    self.tr(self.pb16(bt, h * 128, 128), khT[:, h * 128:(h + 1) * 128], self.identb[:, :],
                            [khT, self.identb], [self.bank[bt]])
                self.cp(khs[d][:, sub, :], self.pb16(bt, 0, 512), [self.bank[bt]], [khs[d]])

            combos = [(sub, d) for sub in range(2) for d in range(2)]
            for k in range(len(combos) + 2):
                if k < len(combos):
                    s1(*combos[k])
                if 0 <= k - 1 < len(combos):
                    s2(*combos[k - 1])
                if 0 <= k - 2 < len(combos):
                    s3(*combos[k - 2])
            for d in range(2):
                self.store(AP(self.gq[d], tok0, [[NT, 128], [128 * NT, 4], [1, TT]]), qs[d][:, :, :], qs[d], self.gq[d])
                self.store(AP(self.gk[d], tok0, [[NT, 128], [128 * NT, 4], [1, TT]]), ks[d][:, :, :], ks[d], self.gk[d])
                self.store(AP(self.gkh[d], tok0 * 512, [[512, 128], [128 * 512, 2], [1, 512]]), khs[d][:, :, :],
                           khs[d], self.gkh[d])
        kb.barrier()
        kb.release(m)

    def gla_scan(self):
        kb = self.kb
        m = kb.mark()
        NT, NCH = self.NT, self.NCH
        S = [kb.sb("Sg%d" % d, [128, 4, 256], F32) for d in range(2)]
        Sb = [kb.sb("Sgb%d" % d, [128, 4, 256], BF16) for d in range(2)]
        msk = [kb.sb("gm%d" % d, [128, 128], F32) for d in range(2)]
        for d in range(2):
            self.memset(S[d][:, :, :], 0.0, [S[d]])
            self.memset(Sb[d][:, :, :], 0.0, [Sb[d]])
            self.load(msk[d][:, :], AP(self.gmask, d * 128 * 128, [[128, 128], [1, 128]]), self.gmask, msk[d])
        nb = 2
        qt = [[kb.sb("sq%d_%d" % (d, i), [128, 4, 128], BF16) for i in range(nb)] for d in range(2)]
        kt = [[kb.sb("sk%d_%d" % (d, i), [128, 4, 128], BF16) for i in range(nb)] for d in range(2)]
        kh = [[kb.sb("skh%d_%d" % (d, i), [128, 512], BF16) for i in range(nb)] for d in range(2)]
        vv = [[kb.sb("sv%d_%d" % (d, i), [128, DM], BF16) for i in range(nb)] for d in range(2)]
        AT = [kb.sb("AT%d" % d, [128, 4, 128], BF16) for d in range(2)]
        osb = [kb.sb("osb%d" % d, [128, DM], F32) for d in range(2)]
        order = [list(range(NCH)), [1, 0] + list(range(NCH - 1, 1, -1))]

        def issue_loads(s):
            for d in range(2):
                c = order[d][s]
                t0 = c * 128
                i = s % nb
                self.load(qt[d][i][:, :, :], AP(self.gq[d], t0, [[NT, 128], [128 * NT, 4], [1, 128]]), self.gq[d], qt[d][i])
                self.load(kt[d][i][:, :, :], AP(self.gk[d], t0, [[NT, 128], [128 * NT, 4], [1, 128]]), self.gk[d], kt[d][i])
                self.load(kh[d][i][:, :], AP(self.gkh[d], t0 * 512, [[512, 128], [1, 512]]), self.gkh[d], kh[d][i])
                self.load(vv[d][i][:, :], AP(self.gv, t0 * DM, [[DM, 128], [1, DM]]), self.gv, vv[d][i])

        issue_loads(0)
        for s in range(NCH):
            if s + 1 < NCH:
                issue_loads(s + 1)
            for d in range(2):
                c = order[d][s]
                i = s % nb
                Q, K_, KH, V = qt[d][i], kt[d][i], kh[d][i], vv[d][i]
                bA = d
                bO = 2 + 2 * d
                bS = 6
                for h in range(4):
                    self.mm(self.pf(bA, h * 128, 128), K_[:, h, :], Q[:, h, :], True, True, [K_, Q], [self.bank[bA]])
                self.tt(AT[d][:, :, :], AP(self.PP[bA // 2], (bA % 2) * 512, [[1024, 128], [128, 4], [1, 128]]),
                        AP(msk[d], 0, [[128, 128], [0, 4], [1, 128]]), ALU.mult, [self.bank[bA], msk[d]], [AT[d]])
                for h in range(4):
                    bk = bO + h // 2
                    o_ap = self.pf(bk, (h % 2) * 256, 256)
                    self.mm(o_ap, AT[d][:, h, :], V[:, h * 256:(h + 1) * 256], True, False, [AT[d], V], [self.bank[bk]])
                    self.mm(o_ap, Q[:, h, :], Sb[d][:, h, :], False, True, [Q, Sb[d]], [self.bank[bk]])
                self.act(osb[d][:, :], AP(self.PP[bO // 2], 0, [[1024, 128], [1, 1024]]), AF.Copy,
                         [self.bank[bO], self.bank[bO + 1]], [osb[d]])
                self.store(AP(self.go[d], c * 128 * DM, [[DM, 128], [1, DM]]), osb[d][:, :], osb[d], self.go[d])
                for h in range(4):
                    bk = bS + h // 2
                    self.mm(self.pf(bk, (h % 2) * 256, 256), KH[:, h * 128:(h + 1) * 128], V[:, h * 256:(h + 1) * 256],
                            True, True, [KH, V], [self.bank[bk]])
                for h in range(4):
                    bk = bS + h // 2
                    self.stt(S[d][:, h, :], S[d][:, h, :],
                             AP(self.Egla, (d * NCH + c) * 4 + h, [[2 * NCH * 4, 128], [1, 1]]),
                             self.pf(bk, (h % 2) * 256, 256), ALU.mult, ALU.add,
                             [S[d], self.Egla, self.bank[bk]], [S[d]])
                self.act(Sb[d][:, :, :], S[d][:, :, :], AF.Copy, [S[d]], [Sb[d]])
        kb.barrier()
        kb.release(m)

    def phaseB(self):
        kb = self.kb
        m = kb.mark()
        NT, NCH = self.NT, self.NCH
        Wgd = kb.sb("Wgd", [128, 8, 3072], BF16)
        Wab = kb.sb("Wab", [128, 8, 32], BF16)
        self.wload(Wgd, self.w_in, SPL["dq"], 3072, rowlen=PIN)
        self.wload(Wab, self.w_in, SPL["ab"], 32, rowlen=PIN)
        cw = kb.sb("cw", [128, 24, 5], F32)
        self.load(AP(cw, 0, [[120, 128], [1, 120]]), self.cwT[:, :], self.cwT, cw)
        negA = kb.sb("negA", [128, 16], F32)
        dtb = kb.sb("dtbs", [128, 16], F32)
        self.load(negA[:, :], self.alog[:, :], self.alog, negA)
        self.load(dtb[:, :], self.dtb[:, :], self.dtb, dtb)
        self.act(negA[:, :], negA[:, :], AF.Exp, [negA], [negA])
        self.kb.op("act", lambda e: e.mul(negA[:, :], negA[:, :], -1.0), [negA], [negA])
        CU = [kb.sb("CU%d" % i, [128, 128], F32) for i in range(4)]
        for i in range(4):
            self.load(CU[i][:, :], AP(self.cumU, (2 + i) * 128 * 128, [[128, 128], [1, 128]]), self.cumU, CU[i])
        UFi, UBi, UBs, UFs = CU
        MK = [kb.sb("MK%d" % i, [128, 128], BF16) for i in range(4)]
        for i in range(4):
            self.kb.dma("pool", MK[i][:, :], AP(self.dmask, i * 128 * 128, [[128, 128], [1, 128]]), [self.dmask], [MK[i]])
        Minc = [MK[0], MK[1]]
        Mlt, Mgt = MK[2], MK[3]
        sel = kb.sb("sel", [96, 32, 128], BF16)
        self.kb.dma("pool", AP(sel, 0, [[4096, 96], [1, 4096]]), self.sel_d[:, :], [self.sel_d], [sel])
        xt = [kb.sb("xtB%d" % i, [128, 2, DM], F32) for i in range(1)] * 2
        hnT = kb.sb("hnTB", [128, 8, TT], BF16)
        acc = [kb.sb("acc%d" % i, [128, TT], F32) for i in range(5)]
        sg = [kb.sb("sgb%d" % i, [128, TT], F32) for i in range(2)]
        sil = [kb.sb("sil%d" % i, [128, TT], F32) for i in range(4)]
        sq = [kb.sb("sqb%d" % i, [128, TT], BF16) for i in range(2)]
        rn = [kb.sb("rn%d" % i, [128, TT], F32) for i in range(2)]
        vTb = [kb.sb("vTb%d" % i, [128, TT], BF16) for i in range(2)]
        dqs = kb.sb("dqs", [128, 8, TT], BF16)
        dks = kb.sb("dks", [128, 8, TT], BF16)
        vtk = kb.sb("vtk", [128, 2, DM], BF16)
        ktk = kb.sb("ktk", [128, 2, DM], BF16)
        gbuf = []
        for gi in range(2):
            gbuf.append((kb.sb("g16_%d" % gi, [128, 16], F32), kb.sb("t16_%d" % gi, [128, 16], F32),
                         kb.sb("beta_%d" % gi, [128, 16], F32), kb.sb("lnb_%d" % gi, [128, 16], F32),
                         kb.sb("ba_%d" % gi, [128, 16], F32), kb.sb("R_%d" % gi, [128, 32], F32),
                         kb.sb("negG_%d" % gi, [128, 16], F32), kb.sb("R1_%d" % gi, [128, 32], F32),
                         kb.sb("Rs_%d" % gi, [128, 96], BF16), kb.sb("rows_%d" % gi, [96, 128], BF16),
                         kb.sb("nrows_%d" % gi, [96, 128], BF16)))
        Dm = kb.sb("Dm", [128, 8, 128], F32)
        E1 = kb.sb("E1m", [128, 8, 128], F32)
        E2 = Dm
        aqs = kb.sb("aqs", [128, 8, 128], BF16)
        X0 = kb.sb("X0n", [128, 8, 128], BF16)
        LT = [kb.sb("LTn%d" % d, [128, 8, 128], BF16) for d in range(2)]
        Dn = [[kb.sb("Dn%d_%d" % (d, g), [128, 4, 128], BF16) for g in range(2)] for d in range(2)]
        DTn = [[kb.sb("DTn%d_%d" % (d, g), [128, 4, 128], BF16) for g in range(2)] for d in range(2)]
        Wn = [[kb.sb("Wn%d_%d" % (d, g), [128, 4, 128], BF16) for g in range(2)] for d in range(2)]
        MT = [kb.sb("MTn%d" % d, [128, 8, 128], BF16) for d in range(2)]
        MaT = [kb.sb("MaTn%d" % d, [128, 8, 128], BF16) for d in range(2)]
        LM = kb.sb("LM", [128, 14, 128], BF16)
        self.kb.dma("pool", LM[:, :, :], AP(self.lvlm, 0, [[128, 128], [128 * 128, 14], [1, 128]]), [self.lvlm], [LM])
        lm = lambda i: AP(LM, i * 128, [[14 * 128, 128], [0, 8], [1, 128]])
        lm4 = lambda i: AP(LM, i * 128, [[14 * 128, 128], [0, 4], [1, 128]])
        nidb = kb.sb("nidb", [128, 128], BF16)
        self.kb.op("act", lambda e: e.mul(nidb[:, :], self.identb[:, :], -1.0), [self.identb], [nidb])
        us = kb.sb("us", [128, DM], F32)
        ws = kb.sb("wsn", [128, 8, 128], BF16)
        b3 = lambda t, c0: AP(t, c0, [[16, 128], [1, 8], [0, 128]])
        f3 = lambda t: AP(t, 0, [[1024, 128], [128, 8], [1, 128]])
        p3 = lambda k: AP(self.PP[k], 0, [[1024, 128], [128, 8], [1, 128]])
        pbk = lambda k: [self.bank[2 * k], self.bank[2 * k + 1]]
        print("phaseB sbuf remaining", self.nc.sbuf_bytes_remaining)
        tl = self.tiles(True)
        src, sbuf = self.x_src(tl[0][2], tl[0][3], True)
        self.load(xt[0][:, :, :], src, sbuf, xt[0])
        for ti, (idx, tok0, is_ctx, li) in enumerate(tl):
            Xt = xt[ti % 2]
            A, B, mb = (self.A1c, self.B1c, self.modC) if is_ctx else (self.A1, self.B1, self.modL)
            for sub in range(2):
                self.norm_T(Xt[:, sub, :], Xt, A, B, mb, hnT, TT, sub, 0)
            if ti + 1 < len(tl):
                src, sbuf = self.x_src(tl[ti + 1][2], tl[ti + 1][3], True)
                self.load(xt[(ti + 1) % 2][:, :, :], src, sbuf, xt[(ti + 1) % 2])
            def gates(sub):
                ch = tok0 // 128 + sub
                tsl = slice(sub * 128, (sub + 1) * 128)
                g16, t16, beta, lnb, ba, R, negG, R1, Rs, rows, nrows = gbuf[sub]
                for kc in range(8):
                    self.mm(self.pf(6, 0, 32), hnT[:, kc, tsl], Wab[:, kc, :], kc == 0, kc == 7, [hnT, Wab], [self.bank[6]])
                self.tt(t16[:, :], self.pf(6, 0, 16), dtb[:, :], ALU.add, [self.bank[6], dtb], [t16])
                self.act(t16[:, :], t16[:, :], AF.Exp, [t16], [t16])
                self.act(t16[:, :], t16[:, :], AF.Ln, [t16], [t16], bias=self.oneb[:, 0:1])
                self.tt(g16[:, :], t16[:, :], negA[:, :], ALU.mult, [t16, negA], [g16])
                self.act(lnb[:, :], self.pf(6, 16, 16), AF.Exp, [self.bank[6]], [lnb], scale=-1.0)
                self.act(lnb[:, :], lnb[:, :], AF.Ln, [lnb], [lnb], bias=self.oneb[:, 0:1])
                self.kb.op("act", lambda e: e.mul(lnb[:, :], lnb[:, :], -1.0), [lnb], [lnb])
                self.act(beta[:, :], lnb[:, :], AF.Exp, [lnb], [beta])
                self.mm(self.pf(7, 0, 8), UFi[:, :], g16[:, 0:8], True, True, [UFi, g16], [self.bank[7]])
                self.mm(self.pf(7, 8, 8), UBi[:, :], g16[:, 8:16], True, True, [UBi, g16], [self.bank[7]])
                self.mm(self.pf(7, 16, 8), UBs[:, :], g16[:, 0:8], True, True, [UBs, g16], [self.bank[7]])
                self.mm(self.pf(7, 24, 8), UFs[:, :], g16[:, 8:16], True, True, [UFs, g16], [self.bank[7]])
                self.mm(self.pf(7, 32, 16), self.onesf[:, :], g16[:, :], True, True, [self.onesf, g16], [self.bank[7]])
                self.act(R[:, 0:16], self.pf(7, 0, 16), AF.Copy, [self.bank[7]], [R])
                self.act(AP(self.GDa, ch * 16, [[NCH * 16, 128], [1, 16]]), self.pf(7, 0, 16), AF.Exp,
                         [self.bank[7]], [self.GDa])
                self.act(AP(self.GDd, ch * 16, [[NCH * 16, 128], [1, 16]]), self.pf(7, 16, 16), AF.Exp,
                         [self.bank[7]], [self.GDd])
                self.act(AP(self.GDe, ch * 16, [[NCH * 16, 128], [1, 16]]), self.pf(7, 32, 16), AF.Exp,
                         [self.bank[7]], [self.GDe])
                self.tt(ba[:, :], beta[:, :], AP(self.GDa, ch * 16, [[NCH * 16, 128], [1, 16]]), ALU.mult,
                        [beta, self.GDa], [ba])
                self.tt(R[:, 16:32], R[:, 0:16], lnb[:, :], ALU.add, [R, lnb], [R])
                self.kb.op("act", lambda e: e.mul(negG[:, :], R[:, 0:16], -1.0), [R], [negG])
                self.cp(Rs[:, 0:32], R[:, :], [R], [Rs])
                self.tt(R1[:, :], R[:, :], Rs[:, 0:32], ALU.subtract, [R, Rs], [R1])
                self.cp(Rs[:, 32:64], R1[:, :], [R1], [Rs])
                self.tt(R1[:, :], R1[:, :], Rs[:, 32:64], ALU.subtract, [R1, Rs], [R1])
                self.cp(Rs[:, 64:96], R1[:, :], [R1], [Rs])
                self.tr(self.pb16(6, 0, 128, 96), Rs[:, :], self.identb[:, :], [Rs, self.identb], [self.bank[6]])
                self.act(rows[:, :], self.pb16(6, 0, 128, 96), AF.Copy, [self.bank[6]], [rows])
                self.kb.op("act", lambda e: e.mul(nrows[:, :], rows[:, :], -1.0), [rows], [nrows])
            for sub in range(2):
                gates(sub)
            nseg, ls = (1, TT) if is_ctx else (2, 128)

            def stA(fc):
                bk = fc % 4
                for kc in range(8):
                    self.mm(self.pf(bk, 0, TT), Wgd[:, kc, fc * 128:(fc + 1) * 128], hnT[:, kc, :], kc == 0, kc == 7,
                            [Wgd, hnT], [self.bank[bk]])

            def conv_ops(fc):
                bk = fc % 4
                A_ = acc[fc % 5]
                ops = [lambda: self.ts(A_[:, :], self.pf(bk, 0, TT), cw[:, fc, 2:3], None, ALU.mult, None,
                                       [self.bank[bk], cw], [A_])]
                for tap in (0, 1, 3, 4):
                    sh = tap - 2
                    n = ls - abs(sh)
                    o0, i0 = (0, sh) if sh > 0 else (-sh, 0)
                    oap = AP(A_, o0, [[TT, 128], [ls, nseg], [1, n]])
                    iap = AP(self.PP[bk // 2], (bk % 2) * 512 + i0, [[1024, 128], [ls, nseg], [1, n]])
                    ops.append(lambda oap=oap, iap=iap, tap=tap: self.stt(oap, iap, cw[:, fc, tap:tap + 1], oap,
                                                                        ALU.mult, ALU.add, [self.bank[bk], cw, A_], [A_]))
                return ops

            def stB(fa, fb):
                oa = conv_ops(fa) if (fa is not None and 0 <= fa < 24) else None
                ob = conv_ops(fb) if (fb is not None and 0 <= fb < 24) else None
                seq = [(ob, 2), (oa, 0), (ob, 3), (oa, 1), (ob, 4)]
                for (o, i) in seq:
                    if o is not None:
                        o[i]()


            def stCE(fc_c, fc_e):
                okc = fc_c is not None and 0 <= fc_c < 24
                oke = fc_e is not None and 0 <= fc_e < 16
                if okc:
                    A_, G_ = acc[fc_c % 5], sg[fc_c % 2]
                    self.act(G_[:, :], A_[:, :], AF.Exp, [A_], [G_], scale=-1.0)
                if oke:
                    S_, Q_, R_ = sil[fc_e % 4], sq[fc_e % 2], rn[fc_e % 2]
                    bq = 6 + fc_e % 2
                    self.act(Q_[:, :], S_[:, :], AF.Square, [S_], [Q_])
                    self.mm(self.pf(bq, 0, TT), self.onesb[:, :], Q_[:, :], True, True, [self.onesb, Q_], [self.bank[bq]])
                if okc:
                    self.act(G_[:, :], G_[:, :], AF.Ln, [G_], [G_], bias=self.oneb[:, 0:1])
                if oke:
                    self.act(R_[:, :], self.pf(bq, 0, TT), AF.Ln, [self.bank[bq]], [R_], bias=self.epsb[:, 0:1])
                if okc:
                    self.act(G_[:, :], G_[:, :], AF.Exp, [G_], [G_], scale=-1.0)
                if oke:
                    self.act(R_[:, :], R_[:, :], AF.Exp, [R_], [R_], scale=-0.5,
                             bias=(self.lncb[:, 0:1] if fc_e < 8 else self.zerob[:, 0:1]))

            def stD(fc):
                A_, G_ = acc[fc % 5], sg[fc % 2]
                if fc < 16:
                    S_ = sil[fc % 4]
                    self.ptt(S_[:, :], A_[:, :], G_[:, :], ALU.mult, [A_, G_], [S_])
                else:
                    h = fc - 16
                    V_ = vTb[fc % 2]
                    self.ptt(V_[:, :], A_[:, :], G_[:, :], ALU.mult, [A_, G_], [V_])
                    for sub in range(2):
                        self.tr(self.pb16(4 + sub, h * 128, 128), V_[:, sub * 128:(sub + 1) * 128], self.identb[:, :],
                                [V_, self.identb], [self.bank[4 + sub]])

            def stF(fc):
                if fc < 16:
                    h = fc % 8
                    S_, R_ = sil[fc % 4], rn[fc % 2]
                    dst = dqs if fc < 8 else dks
                    self.ptt(dst[:, h, :], S_[:, :], R_[:, :], ALU.mult, [S_, R_], [dst])

            for it in range(24 + 6):
                if it < 24:
                    stA(it)
                stB(it - 1, it - 2)
                stCE(it - 3, it - 5)
                if 0 <= it - 4 < 24:
                    stD(it - 4)
                if 0 <= it - 6 < 24:
                    stF(it - 6)
            for sub in range(2):
                self.cp(vtk[:, sub, :], self.pb16(4 + sub, 0, 1024), [self.bank[4 + sub]], [vtk])
            for sub in range(2):
                for h in range(8):
                    self.tr(self.pb16(4 + sub, h * 128, 128), dks[:, h, sub * 128:(sub + 1) * 128], self.identb[:, :],
                            [dks, self.identb], [self.bank[4 + sub]])
                self.cp(ktk[:, sub, :], self.pb16(4 + sub, 0, 1024), [self.bank[4 + sub]], [ktk])
            self.store(AP(self.dqT, tok0, [[NT, 128], [128 * NT, 8], [1, TT]]), dqs[:, :, :], dqs, self.dqT)
            self.store(AP(self.dkt, tok0 * DM, [[DM, 128], [128 * DM, 2], [1, DM]]), ktk[:, :, :], ktk, self.dkt)
            for sub in range(2):
                ch = tok0 // 128 + sub
                tsl = slice(sub * 128, (sub + 1) * 128)
                g16, t16, beta, lnb, ba, R, negG, R1, Rs, rows, nrows = gbuf[sub]
                for h in range(8):
                    bk = h // 4
                    self.mm(self.pf(bk, (h % 4) * 128, 128), dks[:, h, tsl], dks[:, h, tsl], True, True,
                            [dks], [self.bank[bk]])
                for h in range(8):
                    bk = 2 + h // 4
                    self.mm(self.pf(bk, (h % 4) * 128, 128), dks[:, h, tsl], dqs[:, h, tsl], True, True,
                            [dks, dqs], [self.bank[bk]])
                for d in range(2):
                    mstrT = Mlt if d == 0 else Mgt
                    mstr = Mgt if d == 0 else Mlt
                    specs = [(Dm, 0, rows, nrows, 0, Minc[d]), (E1, 16, rows, nrows, 0, mstrT), (E2, 0, nrows, rows, 16, mstr)]
                    for si, (dst, so, rr, pr, po, mk) in enumerate(specs):
                        pk = 2 + (si % 2)
                        for hg in range(2):
                            bk = 2 * pk + hg
                            c0 = d * 8 + hg * 4
                            self.mm(self.pf(bk, 0, 512), self.identb[:, :], AP(mk, 0, [[128, 128], [0, 4], [1, 128]]),
                                    True, False, [self.identb, mk], [self.bank[bk]])
                            self.mm(self.pf(bk, 0, 512), pr[:, :], AP(sel, (po + c0) * 128, [[32 * 128, 96], [1, 512]]),
                                    False, False, [sel, pr], [self.bank[bk]])
                            for h4i in range(4):
                                self.mm(self.pf(bk, h4i * 128, 128), sel[:, so + c0 + h4i, :], rr[:, :], False, h4i == 3,
                                        [sel, rr], [self.bank[bk]])
                        for hg in range(2):
                            bk = 2 * pk + hg
                            self.act(AP(dst, hg * 512, [[1024, 128], [1, 512]]), self.pf(bk, 0, 512), AF.Exp,
                                     [self.bank[bk]], [dst])
                        if si == 0:
                            self.tt(f3(aqs), p3(1), f3(Dm), ALU.mult, pbk(1) + [Dm], [aqs])
                            self.store(AP(self.daq[d], ch * 128 * DM, [[DM, 128], [1, DM]]), aqs[:, :, :], aqs, self.daq[d])
                    self.tt(f3(LT[d]), p3(0), f3(E1), ALU.mult, pbk(0) + [E1], [LT[d]])
                    self.tt(f3(X0), p3(0), f3(E2), ALU.mult, pbk(0) + [E2], [X0])
                    if "dbgX" in self.dbg and ch == 2 and d == 0:
                        self.store(AP(self.dbgX, 0, [[DM, 128], [1, DM]]), X0[:, :, :], X0, self.dbgX)
                        self.store(AP(self.dbgXT, 0, [[DM, 128], [1, DM]]), LT[d][:, :, :], LT[d], self.dbgXT)
                    for hg in range(2):
                        h4 = lambda t: AP(t, 0, [[512, 128], [128, 4], [1, 128]])
                        idb = AP(self.identb, 0, [[128, 128], [0, 4], [1, 128]])
                        src4 = lambda t: AP(t, hg * 512, [[1024, 128], [128, 4], [1, 128]])
                        W_, D_, DT_ = Wn[d][hg], Dn[d][hg], DTn[d][hg]
                        self.tt(h4(W_), src4(X0), lm4(d * 7), ALU.mult, [X0, LM], [W_])
                        self.tt(h4(D_), h4(W_), idb, ALU.add, [W_, self.identb], [D_])
                        self.tt(h4(W_), src4(LT[d]), lm4((1 - d) * 7), ALU.mult, [LT[d], LM], [W_])
                        self.tt(h4(DT_), h4(W_), idb, ALU.add, [W_, self.identb], [DT_])
                h4 = lambda t: AP(t, 0, [[512, 128], [128, 4], [1, 128]])
                chains = [(d, hg) for d in range(2) for hg in range(2)]
                for lvl in range(1, 7):
                    for (d, hg) in chains:
                        k = d * 2 + hg
                        ba_ = 2 * k
                        W_, D_ = Wn[d][hg], Dn[d][hg]
                        self.mm(self.pf(ba_, 0, 512), self.identb[:, :], AP(nidb, 0, [[128, 128], [0, 4], [1, 128]]),
                                True, False, [self.identb, nidb], [self.bank[ba_]])
                        for h4i in range(4):
                            h = hg * 4 + h4i
                            self.mm(self.pf(ba_, h4i * 128, 128), LT[d][:, h, :], D_[:, h4i, :], False, h4i == 3,
                                    [LT[d], D_], [self.bank[ba_]])
                        self.tt(h4(W_), AP(self.PP[ba_ // 2], (ba_ % 2) * 512, [[1024, 128], [128, 4], [1, 128]]),
                                lm4(d * 7 + lvl), ALU.mult, [self.bank[ba_], LM], [W_])
                    for (d, hg) in chains:
                        k = d * 2 + hg
                        ba_, bb_ = 2 * k, 2 * k + 1
                        W_, D_, DT_ = Wn[d][hg], Dn[d][hg], DTn[d][hg]
                        if lvl < 6:
                            for h4i in range(4):
                                self.mm(self.pf(bb_, h4i * 128, 128), DT_[:, h4i, :], W_[:, h4i, :], True, True,
                                        [DT_, W_], [self.bank[bb_]])
                        for h4i in range(4):
                            self.mm(self.pf(ba_, h4i * 128, 128), W_[:, h4i, :], DT_[:, h4i, :], True, True,
                                    [W_, DT_], [self.bank[ba_]])
                        if lvl < 6:
                            if hg == 0:
                                self.act(AP(D_, 0, [[512, 128], [1, 512]]), self.pf(bb_, 0, 512), AF.Copy, [self.bank[bb_]], [D_])
                                self.cp(AP(DT_, 0, [[512, 128], [1, 512]]), self.pf(ba_, 0, 512), [self.bank[ba_]], [DT_])
                            else:
                                self.cp(AP(D_, 0, [[512, 128], [1, 512]]), self.pf(bb_, 0, 512), [self.bank[bb_]], [D_])
                                self.act(AP(DT_, 0, [[512, 128], [1, 512]]), self.pf(ba_, 0, 512), AF.Copy, [self.bank[ba_]], [DT_])
                        else:
                            pv = AP(self.PP[ba_ // 2], (ba_ % 2) * 512, [[1024, 128], [128, 4], [1, 128]])
                            mo = lambda t: AP(t, hg * 512, [[1024, 128], [128, 4], [1, 128]])
                            b4 = lambda t, c0: AP(t, c0, [[16, 128], [1, 4], [0, 128]])
                            self.tt(mo(MT[d]), pv, b4(beta, d * 8 + hg * 4), ALU.mult, [self.bank[ba_], beta], [MT[d]])
                            self.tt(mo(MaT[d]), pv, b4(ba, d * 8 + hg * 4), ALU.mult, [self.bank[ba_], ba], [MaT[d]])
                for d in range(2):
                    pa, pb_ = 2 * d, 2 * d + 1
                    if "dbgX" in self.dbg and ch == 2 and d == 0:
                        self.store(AP(self.dbgMT, 0, [[DM, 128], [1, DM]]), MT[d][:, :, :], MT[d], self.dbgMT)
                    for h in range(8):
                        bk = 2 * pb_ + h // 4
                        self.mm(self.pf(bk, (h % 4) * 128, 128), MT[d][:, h, :], vtk[:, sub, h * 128:(h + 1) * 128], True, True,
                                [MT[d], vtk], [self.bank[bk]])
                    self.act(us[:, :], AP(self.PP[pb_], 0, [[1024, 128], [1, 1024]]), AF.Copy, pbk(pb_), [us])
                    self.store(AP(self.du[d], ch * 128 * DM, [[DM, 128], [1, DM]]), us[:, :], us, self.du[d])
                    for h in range(8):
                        bk = 2 * pa + h // 4
                        self.mm(self.pf(bk, (h % 4) * 128, 128), ktk[:, sub, h * 128:(h + 1) * 128], MaT[d][:, h, :], True, True,
                                [ktk, MaT[d]], [self.bank[bk]])
                    self.cp(f3(ws), p3(pa), pbk(pa), [ws])
                    self.store(AP(self.dw[d], ch * 128, [[NT, 128], [128 * NT, 8], [1, 128]]), ws[:, :, :], ws, self.dw[d])
        kb.barrier()
        kb.release(m)

    def gdn_scan(self):
        kb = self.kb
        m = kb.mark()
        NT, NCH = self.NT, self.NCH
        S = [kb.sb("Sd%d" % d, [128, 8, 128], F32) for d in range(2)]
        Sb = [kb.sb("Sdb%d" % d, [128, 8, 128], BF16) for d in range(2)]
        for d in range(2):
            self.memset(S[d][:, :, :], 0.0, [S[d]])
            self.memset(Sb[d][:, :, :], 0.0, [Sb[d]])
        nb = 2
        mk = lambda n, sh, dt: [[kb.sb("%s%d_%d" % (n, d, i), sh, dt) for i in range(nb)] for d in range(2)]
        wT = mk("lw", [128, 8, 128], BF16)
        qT = mk("lq", [128, 8, 128], BF16)
        uu = mk("lu", [128, DM], F32)
        aq = mk("la", [128, DM], BF16)
        kt = mk("lk", [128, DM], BF16)
        vn32 = kb.sb("vn32", [128, DM], F32)
        vnb = kb.sb("vnb", [128, DM], BF16)
        vnd = kb.sb("vnd", [128, DM], BF16)
        tq = kb.sb("tq", [128, DM], F32)
        osb = [kb.sb("odb%d" % d, [128, DM], F32) for d in range(2)]
        order = [list(range(NCH)), [1, 0] + list(range(NCH - 1, 1, -1))]
        b3 = lambda t, c0: AP(t, c0, [[NCH * 16, 128], [1, 8], [0, 128]])
        f3 = lambda t: AP(t, 0, [[1024, 128], [128, 8], [1, 128]])
        p3 = lambda k: AP(self.PP[k], 0, [[1024, 128], [128, 8], [1, 128]])
        pbk = lambda k: [self.bank[2 * k], self.bank[2 * k + 1]]

        def issue_loads(s):
            for d in range(2):
                c = order[d][s]
                t0 = c * 128
                i = s % nb
                self.load(wT[d][i][:, :, :], AP(self.dw[d], t0, [[NT, 128], [128 * NT, 8], [1, 128]]), self.dw[d], wT[d][i])
                self.load(qT[d][i][:, :, :], AP(self.dqT, t0, [[NT, 128], [128 * NT, 8], [1, 128]]), self.dqT, qT[d][i])
                self.load(uu[d][i][:, :], AP(self.du[d], t0 * DM, [[DM, 128], [1, DM]]), self.du[d], uu[d][i])
                self.load(aq[d][i][:, :], AP(self.daq[d], t0 * DM, [[DM, 128], [1, DM]]), self.daq[d], aq[d][i])
                self.load(kt[d][i][:, :], AP(self.dkt, t0 * DM, [[DM, 128], [1, DM]]), self.dkt, kt[d][i])

        issue_loads(0)
        for s in range(NCH):
            if s + 1 < NCH:
                issue_loads(s + 1)
            for d in range(2):
                c = order[d][s]
                i = s % nb
                Wt, Qt, Uu, Aq, Kt = wT[d][i], qT[d][i], uu[d][i], aq[d][i], kt[d][i]
                for h in range(8):
                    bk = h // 4
                    self.mm(self.pf(bk, (h % 4) * 128, 128), Wt[:, h, :], Sb[d][:, h, :], True, True, [Wt, Sb[d]], [self.bank[bk]])
                for h in range(8):
                    bk = 2 + h // 4
                    self.mm(self.pf(bk, (h % 4) * 128, 128), Qt[:, h, :], Sb[d][:, h, :], True, True, [Qt, Sb[d]], [self.bank[bk]])
                self.tt(vn32[:, :], Uu[:, :], AP(self.PP[0], 0, [[1024, 128], [1, 1024]]), ALU.subtract, [Uu] + pbk(0), [vn32])
                self.act(vnb[:, :], vn32[:, :], AF.Copy, [vn32], [vnb])
                self.tt(f3(vnd), f3(vn32), b3(self.GDd, c * 16 + d * 8), ALU.mult, [vn32, self.GDd], [vnd])
                for h in range(8):
                    bk = 4 + h // 4
                    hs = slice(h * 128, (h + 1) * 128)
                    self.mm(self.pf(bk, (h % 4) * 128, 128), Aq[:, hs], vnb[:, hs], True, True, [Aq, vnb], [self.bank[bk]])
                for h in range(8):
                    bk = 6 + h // 4
                    hs = slice(h * 128, (h + 1) * 128)
                    self.mm(self.pf(bk, (h % 4) * 128, 128), Kt[:, hs], vnd[:, hs], True, True, [Kt, vnd], [self.bank[bk]])
                self.tt(f3(tq), p3(1), b3(self.GDa, c * 16 + d * 8), ALU.mult, pbk(1) + [self.GDa], [tq])
                self.tt(osb[d][:, :], tq[:, :], AP(self.PP[2], 0, [[1024, 128], [1, 1024]]), ALU.add, [tq] + pbk(2), [osb[d]])
                self.store(AP(self.do[d], c * 128 * DM, [[DM, 128], [1, DM]]), osb[d][:, :], osb[d], self.do[d])
                self.tt(f3(S[d]), f3(S[d]), b3(self.GDe, c * 16 + d * 8), ALU.mult, [S[d], self.GDe], [S[d]])
                self.tt(f3(S[d]), f3(S[d]), p3(3), ALU.add, [S[d]] + pbk(3), [S[d]])
                self.act(Sb[d][:, :, :], S[d][:, :, :], AF.Copy, [S[d]], [Sb[d]])
        kb.barrier()
        kb.release(m)

    def phase3a(self):
        kb = self.kb
        m = kb.mark()
        W, NT = self.W, self.NT
        Wz = kb.sb("Wz", [128, 8, 4096], BF16)
        for gi, key in enumerate(("gz", "dz", "g11", "g12")):
            for k0 in (0, 4):
                self.kb.dma("pool", AP(Wz, k0 * 4096 + gi * 1024, [[8 * 4096, 128], [4096, 4], [1, 1024]]),
                            AP(self.w_in, k0 * 128 * PIN + SPL[key], [[PIN, 128], [128 * PIN, 4], [1, 1024]]),
                            [self.w_in], [Wz])
        Wo = kb.sb("Wo", [128, 8, DM], BF16)
        self.wload(Wo, self.w_out, 0, DM, rowlen=DM)
        gg = kb.sb("ggl", [128, DM], F32)
        gd = kb.sb("ggd", [128, DM], F32)
        self.load(gg[:, :], self.glag[:, :], self.glag, gg)
        self.load(gd[:, :], self.gdng[:, :], self.gdng, gd)
        xt = [kb.sb("xt3%d" % i, [128, DM], F32) for i in range(3)]
        ol = [[kb.sb("ol%d_%d" % (j, i), [128, DM], F32) for i in range(2)] for j in range(4)]
        hnT = [kb.sb("hnT3_%d" % i, [128, 8, 128], BF16) for i in range(2)]
        sz = [kb.sb("sz%d" % i, [128, 2048], BF16) for i in range(2)]
        sg = [kb.sb("sg%d" % i, [128, 2048], BF16) for i in range(2)]
        ssl = [kb.sb("ss3_%d" % i, [128, 12], F32) for i in range(2)]
        rsl = [kb.sb("rs3_%d" % i, [128, 12], F32) for i in range(2)]
        mb16 = [kb.sb("mb16_%d" % i, [128, DM], BF16) for i in range(2)]
        mT = [kb.sb("mT%d" % i, [128, 8, 128], BF16) for i in range(2)]
        x1 = [kb.sb("x1s%d" % i, [128, DM], F32) for i in range(2)]
        print("phase3a sbuf remaining", self.nc.sbuf_bytes_remaining)
        ntile = self.L // 128

        def issue_loads(t):
            i = t % 2
            self.load(xt[t % 3][:, :], AP(self.x, t * 128 * DM, [[DM, 128], [1, DM]]), self.x, xt[t % 3])
            g0 = (CTX + t * 128) * DM
            self.load(ol[0][i][:, :], AP(self.go[0], g0, [[DM, 128], [1, DM]]), self.go[0], ol[0][i])
            self.load(ol[1][i][:, :], AP(self.go[1], g0, [[DM, 128], [1, DM]]), self.go[1], ol[1][i])
            nr = 128 // W if W <= 128 else 1
            for j in (0, 1):
                for rr in range(128 // W):
                    r = t * (128 // W) + rr
                    self.load(AP(ol[2 + j][i], rr * W * DM, [[DM, W], [1, DM]]),
                              AP(self.do[j], (CTX + r) * DM, [[128 * DM, W], [1, DM]]), self.do[j], ol[2 + j][i])

        def stage_a0(t):
            self.norm_T(xt[t % 3][:, :], xt[t % 3], self.A1, self.B1, self.modL, hnT[t % 2], 128, 0, 0)

        def stage_a(t):
            i = t % 2
            H, SZ, SG = hnT[i], sz[i], sg[i]
            for g in range(8):
                bk = 1 + g % 4
                for kc in range(8):
                    self.mm(self.pf(bk, 0, 512), H[:, kc, :], Wz[:, kc, g * 512:(g + 1) * 512], kc == 0, kc == 7,
                            [H, Wz], [self.bank[bk]])
                if g < 4:
                    self.act(SZ[:, g * 512:(g + 1) * 512], self.pf(bk, 0, 512), AF.Silu, [self.bank[bk]], [SZ])
                else:
                    self.act(SG[:, (g - 4) * 512:(g - 3) * 512], self.pf(bk, 0, 512), AF.Sigmoid, [self.bank[bk]], [SG])

        def stage_b(t):
            i = t % 2
            og, od = ol[0][i], ol[2][i]
            ss, rs = ssl[i], rsl[i]
            self.tt(og[:, :], ol[0][i][:, :], ol[1][i][:, :], ALU.add, [ol[0][i], ol[1][i]], [og])
            self.tt(od[:, :], ol[2][i][:, :], ol[3][i][:, :], ALU.add, [ol[2][i], ol[3][i]], [od])
            for h in range(4):
                self.act(self.junk[:, 0:256], og[:, h * 256:(h + 1) * 256], AF.Square, [og], [self.junk, ss],
                         accum=ss[:, h:h + 1])
            for h in range(8):
                self.act(self.junk[:, 0:128], od[:, h * 128:(h + 1) * 128], AF.Square, [od], [self.junk, ss],
                         accum=ss[:, 4 + h:5 + h])
            self.act(rs[:, 0:4], ss[:, 0:4], AF.Ln, [ss], [rs], scale=1.0 / 256, bias=self.epsb[:, 0:1])
            self.act(rs[:, 4:12], ss[:, 4:12], AF.Ln, [ss], [rs], scale=1.0 / 128, bias=self.epsb[:, 0:1])
            self.act(rs[:, :], rs[:, :], AF.Exp, [rs], [rs], scale=-0.5)

        def stage_b1(t):
            i = t % 2
            SZ, SG, MB, MT_ = sz[i], sg[i], mb16[i], mT[i]
            og, od = ol[0][i], ol[2][i]
            ss, rs = ssl[i], rsl[i]
            self.tt(AP(og, 0, [[DM, 128], [256, 4], [1, 256]]), AP(og, 0, [[DM, 128], [256, 4], [1, 256]]),
                    AP(rs, 0, [[12, 128], [1, 4], [0, 256]]), ALU.mult, [og, rs], [og])
            self.tt(AP(od, 0, [[DM, 128], [128, 8], [1, 128]]), AP(od, 0, [[DM, 128], [128, 8], [1, 128]]),
                    AP(rs, 4, [[12, 128], [1, 8], [0, 128]]), ALU.mult, [od, rs], [od])
            self.tt(og[:, :], og[:, :], gg[:, :], ALU.mult, [og, gg], [og])
            self.tt(od[:, :], od[:, :], gd[:, :], ALU.mult, [od, gd], [od])
            self.tt(og[:, :], og[:, :], SZ[:, 0:1024], ALU.mult, [og, SZ], [og])
            self.tt(od[:, :], od[:, :], SZ[:, 1024:2048], ALU.mult, [od, SZ], [od])
            self.tt(og[:, :], og[:, :], SG[:, 0:1024], ALU.mult, [og, SG], [og])
            self.tt(od[:, :], od[:, :], SG[:, 1024:2048], ALU.mult, [od, SG], [od])
            self.tt(MB[:, :], og[:, :], od[:, :], ALU.add, [og, od], [MB])
            for kc in range(8):
                self.tr(self.pb16(5, kc * 128, 128), MB[:, kc * 128:(kc + 1) * 128], self.identb[:, :],
                        [MB, self.identb], [self.bank[5]])
            self.act(AP(MT_, 0, [[1024, 128], [1, 1024]]), self.pb16(5, 0, 1024), AF.Copy, [self.bank[5]], [MT_])

        def stage_b2(t):
            i = t % 2
            MT_ = mT[i]
            for half in range(2):
                bk = 6 + half
                for kc in range(8):
                    self.mm(self.pf(bk, 0, 512), MT_[:, kc, :], Wo[:, kc, half * 512:(half + 1) * 512], kc == 0, kc == 7,
                            [MT_, Wo], [self.bank[bk]])

        def stage_c(t):
            i = t % 2
            X = xt[t % 3]
            self.tt(x1[i][:, :], AP(self.PP[3], 0, [[1024, 128], [1, 1024]]), self.G1, ALU.mult,
                    [self.bank[6], self.bank[7], self.modL], [x1[i]])
            self.tt(x1[i][:, :], x1[i][:, :], X[:, :], ALU.add, [x1[i], X], [x1[i]])
            self.store(AP(self.dx1, t * 128 * DM, [[DM, 128], [1, DM]]), x1[i][:, :], x1[i], self.dx1)

        issue_loads(0)
        if ntile > 1:
            issue_loads(1)
        stage_a0(0)
        stage_a(0)
        for t in range(ntile):
            if t + 1 < ntile:
                stage_a0(t + 1)
            stage_b(t)
            if t + 1 < ntile:
                stage_a(t + 1)
            stage_b1(t)
            if t >= 1:
                stage_c(t - 1)
            stage_b2(t)
            if t + 2 < ntile:
                issue_loads(t + 2)
        stage_c(ntile - 1)
        kb.barrier()
        kb.release(m)

    def phase3b(self):
        kb = self.kb
        m = kb.mark()
        Wg = kb.sb("Wg", [128, 8, FFH], BF16)
        Wu = kb.sb("Wu", [128, 8, FFH], BF16)
        Wd = kb.sb("Wd", [128, 22, DM], BF16)
        for (dst, src) in ((Wg, self.wg), (Wu, self.wu)):
            for k0 in range(0, 8, 2):
                self.kb.dma("pool", AP(dst, k0 * FFH, [[8 * FFH, 128], [FFH, 2], [1, FFH]]),
                            AP(src, k0 * 128 * FFH, [[FFH, 128], [128 * FFH, 2], [1, FFH]]), [src], [dst])
        for k0 in range(0, 22, 2):
            self.kb.dma("pool", AP(Wd, k0 * DM, [[22 * DM, 128], [DM, 2], [1, DM]]),
                        AP(self.wd, k0 * 128 * DM, [[DM, 128], [128 * DM, 2], [1, DM]]), [self.wd], [Wd])
        fn = kb.sb("fnw", [128, DM], F32)
        self.load(fn[:, :], self.fnbc[:, :], self.fnbc, fn)
        xt = [kb.sb("x1t%d" % i, [128, 2, DM], F32) for i in range(1)]
        h2T = kb.sb("h2T", [128, 8, TT], BF16)
        sgl = kb.sb("sgl", [128, TT], F32)
        actT = kb.sb("actT", [128, 22, TT], BF16)
        ty = kb.sb("ty2", [128, 512], F32)
        x2s = [kb.sb("x2_%d" % i, [128, DM], F32) for i in range(2)]
        print("phase3b sbuf remaining", self.nc.sbuf_bytes_remaining)
        ss, rs = self.ssq, self.rsd
        ntile = self.L // TT

        def issue_load(t):
            self.load(xt[0][:, :, :], AP(self.dx1, t * TT * DM, [[DM, 128], [128 * DM, 2], [1, DM]]), self.dx1, xt[0])

        issue_load(0)
        oc = 0
        for t in range(ntile):
            X = xt[0]
            if t > 0:
                issue_load(t)
            for sub in range(2):
                self.norm_T(X[:, sub, :], X, self.A2, self.B2, self.modL2, h2T, TT, sub, 0)
            for hc in range(22):
                bg = 1 + (hc % 2) * 2
                bu = bg + 1
                for kc in range(8):
                    self.mm(self.pf(bg, 0, TT), Wg[:, kc, hc * 128:(hc + 1) * 128], h2T[:, kc, :], kc == 0, kc == 7,
                            [Wg, h2T], [self.bank[bg]])
                for kc in range(8):
                    self.mm(self.pf(bu, 0, TT), Wu[:, kc, hc * 128:(hc + 1) * 128], h2T[:, kc, :], kc == 0, kc == 7,
                            [Wu, h2T], [self.bank[bu]])
                self.act(sgl[:, :], self.pf(bg, 0, TT), AF.Silu, [self.bank[bg]], [sgl])
                self.tt(actT[:, hc, :], sgl[:, :], self.pf(bu, 0, TT), ALU.mult, [sgl, self.bank[bu]], [actT])
            for sub in range(2):
                x2 = x2s[oc % 2]
                for half in range(2):
                    bk = 5 + half
                    for hc in range(22):
                        self.mm(self.pf(bk, 0, 512), actT[:, hc, sub * 128:(sub + 1) * 128],
                                Wd[:, hc, half * 512:(half + 1) * 512], hc == 0, hc == 21, [actT, Wd], [self.bank[bk]])
                    hs = slice(half * 512, (half + 1) * 512)
                    self.tt(ty[:, :], self.pf(bk, 0, 512), AP(self.modL2, 2 * DM + half * 512, [[3 * DM, 128], [1, 512]]),
                            ALU.mult, [self.bank[bk], self.modL2], [ty])
                    self.tt(x2[:, hs], ty[:, :], X[:, sub, hs], ALU.add, [ty, X], [x2])
                self.act(self.ntmp[:, :], x2[:, :], AF.Square, [x2], [self.ntmp, ss], accum=ss[:, 2:3])
                self.act(rs[:, 2:3], ss[:, 2:3], AF.Sqrt, [ss], [rs], scale=1.0 / DM, bias=self.epsb[:, 0:1])
                self.recip(rs[:, 3:4], rs[:, 2:3], [rs], [rs])
                O = x2
                oc += 1
                self.stt(O[:, :], x2[:, :], rs[:, 3:4], fn[:, :], ALU.mult, ALU.mult, [x2, rs, fn], [O])
                self.store(AP(self.out, (t * TT + sub * 128) * DM, [[DM, 128], [1, DM]]), O[:, :], O, self.out)
        kb.barrier()
        kb.release(m)

    def build(self, phases="0ASBG3F"):
        kb = self.kb
        self.consts()
        self.epsb = kb.sb("epsb", [128, 1], F32)
        self.oneb = kb.sb("oneb", [128, 1], F32)
        self.lncb = kb.sb("lncb", [128, 1], F32)
        self.zerob = kb.sb("zerob", [128, 1], F32)
        self.memset(self.lncb[:, :], -0.5 * float(np.log(128.0)), [self.lncb])
        self.memset(self.zerob[:, :], 0.0, [self.zerob])
        self.memset(self.epsb[:, :], EPS, [self.epsb])
        self.memset(self.oneb[:, :], 1.0, [self.oneb])
        NCH = self.NCH
        self.modL2 = kb.sb("modL2", [128, 3 * DM], F32)
        ma = kb.mark()
        self.modL1 = kb.sb("modL1", [128, 3 * DM], F32)
        mb_ = kb.mark()
        self.modC = kb.sb("modC", [128, 2 * DM], F32)
        self.Egla = kb.sb("Egla", [128, 2 * NCH * 4], F32)
        self.GDa = kb.sb("GDa", [128, NCH * 16], F32)
        self.GDd = kb.sb("GDd", [128, NCH * 16], F32)
        self.GDe = kb.sb("GDe", [128, NCH * 16], F32)
        self.phase0()
        if "A" in phases:
            self.phaseA()
        if "S" in phases:
            self.gla_scan()
        if "B" in phases:
            self.phaseB()
        if "G" in phases:
            self.gdn_scan()
        kb.release(mb_)
        if "3" in phases:
            self.phase3a()
        kb.release(ma)
        if "F" in phases:
            self.phase3b()
        kb.barrier()
        kb.finalize()
        kb.close()
        return self.nc


def host_consts():
    p = np.arange(128)[:, None]
    f = np.arange(128)[None, :]
    le = (p <= f).astype(np.float32)
    ge = (p >= f).astype(np.float32)
    lt = (p < f).astype(np.float32)
    gt = (p > f).astype(np.float32)
    cumU = np.stack([le * (-1.0 / 16.0), ge * (-1.0 / 16.0), le, ge, gt, lt]).astype(np.float32)
    gmask = np.stack([le, ge]).astype(np.float32)
    dmask = np.stack([(1 - le) * NEG, (1 - ge) * NEG, (1 - lt) * NEG, (1 - gt) * NEG]).astype(np.float32)
    sel = np.zeros((96, 32, 128), np.float32)
    for c in range(32):
        for part in range(3):
            sel[part * 32 + c, c, :] = 1.0
    lv = np.zeros((14, 128, 128), np.float32)
    pi = np.arange(128)[:, None]
    fj = np.arange(128)[None, :]
    for s_ in range(7):
        b = 1 << s_
        mlow = ((pi // (2 * b)) == (fj // (2 * b))) & ((pi % (2 * b)) >= b) & ((fj % (2 * b)) < b)
        dg = np.eye(128, dtype=np.float32) if s_ >= 1 else 0.0
        lv[s_] = -mlow.astype(np.float32) - dg
        lv[7 + s_] = -mlow.T.astype(np.float32) - dg
    return dict(identf=np.eye(128, dtype=np.float32), cumU=cumU, gmask=gmask, dmask=dmask,
                sel=sel.reshape(96, 32 * 128), lvlm=lv)


def host_inputs(b, x, c, ctx, c_ctx, w_mod, b_mod, norm1_w, norm2_w, w_in, gla_lr_w, gla_lr_b, gla_norm_w,
                gdn_conv_w, gdn_a_log, gdn_dt_bias, gdn_norm_w, w_out, ffn_w_gate, ffn_w_up, ffn_w_down,
                final_norm_w, shared):
    f = lambda a: np.ascontiguousarray(a, dtype=np.float32)
    d = dict(shared)
    d["x"] = f(x[b])
    d["ctx"] = f(ctx[b])
    d["cT"] = f(np.asarray(c[b]).reshape(8, 128).T)
    return d


def shared_inputs(c_ctx, w_mod, b_mod, norm1_w, norm2_w, w_in, gla_lr_w, gla_lr_b, gla_norm_w,
                  gdn_conv_w, gdn_a_log, gdn_dt_bias, gdn_norm_w, w_out, ffn_w_gate, ffn_w_up, ffn_w_down,
                  final_norm_w):
    f = lambda a: np.ascontiguousarray(a, dtype=np.float32)
    bc = lambda v: f(np.broadcast_to(np.asarray(v).reshape(1, -1), (128, np.asarray(v).size)))
    d = host_consts()
    d["cctxT"] = f(np.asarray(c_ctx).reshape(8, 128).T)
    d["w_mod"] = f(w_mod[0])
    d["b_mod"] = f(np.asarray(b_mod[0]).reshape(1, -1))
    d["n1bc"] = bc(norm1_w[0])
    d["n2bc"] = bc(norm2_w[0])
    d["fnbc"] = bc(final_norm_w)
    d["w_in"] = f(w_in[0])
    lrw = np.zeros((2, 33, 512), np.float32)
    lrw[0, 0:16] = np.asarray(gla_lr_w[0, 0])
    lrw[1, 16:32] = np.asarray(gla_lr_w[0, 1])
    lrw[0, 32] = np.asarray(gla_lr_b[0, 0])
    lrw[1, 32] = np.asarray(gla_lr_b[0, 1])
    d["lrw"] = lrw
    d["glag"] = bc(np.asarray(gla_norm_w[0]).reshape(-1))
    d["gdng"] = bc(np.asarray(gdn_norm_w[0]).reshape(-1))
    cw = np.asarray(gdn_conv_w[0])
    d["cwT"] = f(cw.reshape(5, 24, 128).transpose(2, 1, 0).reshape(128, 120))
    d["alog"] = bc(np.asarray(gdn_a_log[0]).reshape(-1))
    d["dtb"] = bc(np.asarray(gdn_dt_bias[0]).reshape(-1))
    d["w_out"] = f(w_out[0])
    d["wg"] = f(ffn_w_gate[0])
    d["wu"] = f(ffn_w_up[0])
    d["wd"] = f(ffn_w_down[0])
    return d


_CACHE = {}


def kernel(x, c, ctx, c_ctx, w_mod, b_mod, norm1_w, norm2_w, w_in, gla_lr_w, gla_lr_b, gla_norm_w,
           gdn_conv_w, gdn_a_log, gdn_dt_bias, gdn_norm_w, w_out, ffn_w_gate, ffn_w_up, ffn_w_down,
           final_norm_w):
    x = np.asarray(x)
    B, L, _ = x.shape
    W = L // 128
    nc = Prog(W).build()
    shared = shared_inputs(c_ctx, w_mod, b_mod, norm1_w, norm2_w, w_in, gla_lr_w, gla_lr_b, gla_norm_w,
                           gdn_conv_w, gdn_a_log, gdn_dt_bias, gdn_norm_w, w_out, ffn_w_gate, ffn_w_up,
                           ffn_w_down, final_norm_w)
    f = lambda a: np.ascontiguousarray(a, dtype=np.float32)
    in_maps = []
    for b in range(B):
        d = dict(shared)
        d["x"] = f(x[b])
        d["ctx"] = f(np.asarray(ctx)[b])
        d["cT"] = f(np.asarray(c)[b].reshape(8, 128).T)
        in_maps.append(d)
    res = run_bass_kernel_spmd(nc, in_maps, core_ids=list(range(B)))
    return np.stack([np.asarray(r["out"], dtype=np.float32) for r in res.results], axis=0)
```

```python
import numpy as np
import ml_dtypes
import concourse.bass as bass
import concourse.mybir as mybir
from concourse.bass_utils import run_bass_kernel_spmd

F32 = mybir.dt.float32
BF16 = mybir.dt.bfloat16
AF = mybir.ActivationFunctionType
ALU = mybir.AluOpType

SAME_ENG_SYNC = True
NOSYNC_ENGS = ()
EPOCH = 20000
EPS = 1e-6
DM = 1024
CTX = 256
TT = 256
NEG = -30000.0


class Buf:
    __slots__ = ("t", "lw", "rd", "sem", "cnt", "name", "dram", "_scope", "semq")

    def __init__(self, t, name, dram=False):
        self.t = t
        self.name = name
        self.lw = {}
        self.rd = {}
        self.sem = None
        self.cnt = 0
        self.dram = dram
        self._scope = 0
        self.semq = None

    def __getitem__(self, idx):
        return self.t[idx]


class EngState:
    def __init__(self, name):
        self.name = name
        self.ops = []
        self.count = 0
        self.waited = {}
        self.needed = set()


class KB:
    def __init__(self, nc):
        self.nc = nc
        self.eng = {n: EngState(n) for n in ("pe", "act", "dve", "pool", "sp")}
        self._ctx = []
        self.sems = {}
        self.sempool = []
        self.sembufs = []
        self._sem_ctx = []

    def mark(self):
        return len(self._ctx)

    def release(self, m):
        for b in self.sembufs:
            if b.sem is not None and (not b.dram) and b._scope >= m:
                self.sempool.append((b.sem, b.cnt, b.semq))
                b.sem = None
        self.sembufs = [b for b in self.sembufs if b.sem is not None]
        while len(self._ctx) > m:
            cm = self._ctx.pop()
            cm.__exit__(None, None, None)

    def sb(self, name, shape, dt=F32):
        cm = self.nc.sbuf_tensor("sb_" + name, list(shape), dt)
        t = cm.__enter__()
        b = Buf(t, name)
        b._scope = len(self._ctx)
        self._ctx.append(cm)
        return b

    def ps(self, name, shape, dt=F32):
        cm = self.nc.psum_tensor(name, list(shape), dt)
        t = cm.__enter__()
        self._ctx.append(cm)
        return t

    def dram(self, name, shape, dt=F32, kind="Internal"):
        t = self.nc.dram_tensor(name, list(shape), dt, kind=kind)
        return Buf(t, name, dram=True)

    def _get_sem(self, b, q):
        if b.sem is not None:
            assert b.semq == q, "buffer %s used by both DMA queue kinds" % b.name
        if b.sem is None:
            b.semq = q
            cand = [i for i, e in enumerate(self.sempool) if e[2] == q]
            if cand:
                b.sem, b.cnt, _ = self.sempool.pop(cand[-1])
            else:
                cm = self.nc.semaphore("s%d" % len(self.sems))
                b.sem = cm.__enter__()
                self._sem_ctx.append(cm)
                b.cnt = 0
                self.sems[id(b.sem)] = b.sem
            self.sembufs.append(b)

    def _waits(self, E, reads, writes, extra=None):
        deps = {}

        def add(d):
            for k, v in d.items():
                if deps.get(k, -1) < v:
                    deps[k] = v

        for b in reads:
            add(b.lw)
        for b in writes:
            add(b.lw)
            add(b.rd)
        if extra:
            add(extra)
        for k, v in deps.items():
            if k[0] == "E" and k[1] == E.name:
                if E.name in ("pe", "sp") or not SAME_ENG_SYNC or E.name in NOSYNC_ENGS:
                    continue
            if E.waited.get(k, -1) >= v:
                continue
            E.waited[k] = v
            if k[0] == "E":
                self.eng[k[1]].needed.add(v)
            E.ops.append(("w", k, v))

    def op(self, eng, fn, reads=(), writes=()):
        E = self.eng[eng]
        self._waits(E, reads, writes)
        idx = E.count
        E.count += 1
        E.ops.append(("o", fn, idx))
        key = ("E", eng)
        for b in writes:
            if b.dram:
                b.lw[key] = idx
            else:
                b.lw = {key: idx}
                b.rd = {}
        for b in reads:
            b.rd[key] = idx

    def dma(self, q, out, in_, reads, writes):
        E = self.eng[q]
        self._waits(E, reads, writes)
        cand = [b for b in list(writes) + list(reads) if not b.dram]
        sb = cand[0]
        self._get_sem(sb, q)
        sb.cnt += 16
        c = sb.cnt
        key = ("S", id(sb.sem))
        E.ops.append(("d", out, in_, sb.sem))
        for b in writes:
            if b.dram:
                b.lw[key] = c
            else:
                b.lw = {key: c}
                b.rd = {}
        for b in reads:
            b.rd[key] = c

    def barrier(self):
        ev = {}
        for n in ("pe", "act", "dve", "pool"):
            if self.eng[n].count > 0:
                ev[("E", n)] = self.eng[n].count - 1
        for b in self.sembufs:
            if b.sem is not None and b.cnt > 0:
                ev[("S", id(b.sem))] = b.cnt
        for n, E in self.eng.items():
            deps = dict(ev)
            for k, v in deps.items():
                if k[0] == "E" and k[1] == n:
                    continue
                if E.waited.get(k, -1) >= v:
                    continue
                E.waited[k] = v
                if k[0] == "E":
                    self.eng[k[1]].needed.add(v)
                E.ops.append(("w", k, v))

    def finalize(self):
        nc = self.nc
        engsems = {}
        rank = {}
        for name, E in self.eng.items():
            nd = sorted(E.needed)
            rank[name] = {idx: r for r, idx in enumerate(nd)}
            nsem = (len(nd) + EPOCH - 1) // EPOCH
            lst = []
            for i in range(nsem):
                cm = nc.semaphore("e_%s_%d" % (name, i))
                lst.append(cm.__enter__())
                self._sem_ctx.append(cm)
            engsems[name] = lst
        engobj = {"pe": nc.tensor, "act": nc.scalar, "dve": nc.vector, "pool": nc.gpsimd, "sp": nc.sync}

        def replay(E, e):
            for o in E.ops:
                if o[0] == "w":
                    k, v = o[1], o[2]
                    if k[0] == "E":
                        r = rank[k[1]][v]
                        e.wait_ge(engsems[k[1]][r // EPOCH], r % EPOCH + 1)
                    else:
                        e.wait_ge(self.sems[k[1]], v)
                elif o[0] == "o":
                    inst = o[1](e)
                    r = rank[E.name].get(o[2])
                    if r is not None:
                        inst.then_inc(engsems[E.name][r // EPOCH], 1)
                else:
                    e.dma_start(out=o[1], in_=o[2]).then_inc(o[3], 16)

        with nc.Block() as block:
            @block.tensor
            def _(e):
                replay(self.eng["pe"], e)

            @block.scalar
            def _(e):
                replay(self.eng["act"], e)

            @block.vector
            def _(e):
                replay(self.eng["dve"], e)

            @block.gpsimd
            def _(e):
                replay(self.eng["pool"], e)

            @block.sync
            def _(e):
                replay(self.eng["sp"], e)

    def close(self):
        while self._ctx:
            self._ctx.pop().__exit__(None, None, None)
        while self._sem_ctx:
            self._sem_ctx.pop().__exit__(None, None, None)


def AP(buf, offset, pairs):
    return bass.AP(buf.t if isinstance(buf, Buf) else buf, offset, [list(p) for p in pairs])


SPL = dict(gq=0, gk=512, gv=1024, gz=2048, lr=3072, dq=3104, dk=4128, dv=5152, dz=6176,
           ab=7200, g11=7232, g12=8256)
PIN = 9280
FFH = 2816


class Prog:
    def __init__(self, W, dbg=()):
        self.W = W
        self.L = 128 * W
        self.NT = CTX + self.L
        self.NCH = self.NT // 128
        self.dbg = set(dbg)
        nc = bass.Bass("TRN2", target_bir_lowering=False)
        self.nc = nc
        self.kb = KB(nc)
        kb = self.kb
        L, NT = self.L, self.NT
        ein = lambda n, s: kb.dram(n, s, F32, kind="ExternalInput")
        self.x = ein("x", [L, DM])
        self.ctx = ein("ctx", [CTX, DM])
        self.cT = ein("cT", [128, 8])
        self.cctxT = ein("cctxT", [128, 8])
        self.w_mod = ein("w_mod", [DM, 6 * DM])
        self.b_mod = ein("b_mod", [1, 6 * DM])
        self.n1bc = ein("n1bc", [128, DM])
        self.n2bc = ein("n2bc", [128, DM])
        self.fnbc = ein("fnbc", [128, DM])
        self.w_in = ein("w_in", [DM, PIN])
        self.lrw = ein("lrw", [2, 33, 512])
        self.glag = ein("glag", [128, DM])
        self.gdng = ein("gdng", [128, DM])
        self.cwT = ein("cwT", [128, 120])
        self.alog = ein("alog", [128, 16])
        self.dtb = ein("dtb", [128, 16])
        self.w_out = ein("w_out", [DM, DM])
        self.wg = ein("wg", [DM, FFH])
        self.wu = ein("wu", [DM, FFH])
        self.wd = ein("wd", [FFH, DM])
        self.identf_d = ein("identf", [128, 128])
        self.cumU = ein("cumU", [6, 128, 128])
        self.gmask = ein("gmask", [2, 128, 128])
        self.dmask = ein("dmask", [4, 128, 128])
        self.sel_d = ein("sel", [96, 32 * 128])
        self.lvlm = ein("lvlm", [14, 128, 128])
        self.out = kb.dram("out", [L, DM], F32, kind="ExternalOutput")

        def scr(n, s, dt):
            return kb.dram(n, s, dt, kind="ExternalOutput" if (n in self.dbg or n.startswith("dbg")) else "Internal")
        self.gq = [scr("gq%d" % d, [4, 128, NT], BF16) for d in range(2)]
        self.gk = [scr("gk%d" % d, [4, 128, NT], BF16) for d in range(2)]
        self.gkh = [scr("gkh%d" % d, [NT, 512], BF16) for d in range(2)]
        self.gv = scr("gv", [NT, DM], BF16)
        self.go = [scr("go%d" % d, [NT, DM], F32) for d in range(2)]
        self.dqT = scr("dqT", [8, 128, NT], BF16)
        self.dkt = scr("dkt", [NT, DM], BF16)
        self.du = [scr("du%d" % d, [NT, DM], F32) for d in range(2)]
        self.dw = [scr("dw%d" % d, [8, 128, NT], BF16) for d in range(2)]
        self.daq = [scr("daq%d" % d, [NT, DM], BF16) for d in range(2)]
        self.do = [scr("do%d" % d, [NT, DM], F32) for d in range(2)]
        self.dx1 = scr("dx1", [L, DM], F32)
        if "dbgX" in self.dbg:
            self.dbgX = scr("dbgX", [128, DM], BF16)
            self.dbgXT = scr("dbgXT", [128, DM], BF16)
            self.dbgMT = scr("dbgMT", [128, DM], BF16)
            self.dbg.update(["dbgXT", "dbgMT"])

        self.PP = [kb.ps("pp%d" % i, [128, 1024], F32) for i in range(4)]
        self.PPb = [p.bitcast(BF16) for p in self.PP]
        self.bank = [Buf(self.PP[i // 2], "bank%d" % i) for i in range(8)]

    def pf(self, b, c0, n, p=128):
        return AP(self.PP[b // 2], (b % 2) * 512 + c0, [[1024, p], [1, n]])

    def pb16(self, b, c0, n, p=128):
        return AP(self.PPb[b // 2], (b % 2) * 1024 + c0, [[2048, p], [1, n]])

    def mm(self, out, lhsT, rhs, start, stop, reads, writes):
        self.kb.op("pe", lambda e: e.matmul(out, lhsT=lhsT, rhs=rhs, start=start, stop=stop), reads, writes)

    def tr(self, out, in_, ident, reads, writes):
        self.kb.op("pe", lambda e: e.transpose(out=out, in_=in_, identity=ident), reads, writes)

    def act(self, out, in_, func, reads, writes, scale=None, bias=None, accum=None):
        kw = {}
        if scale is not None:
            kw["scale"] = scale
        if bias is not None:
            kw["bias"] = bias
        if accum is not None:
            kw["accum_out"] = accum
        self.kb.op("act", lambda e: e.activation(out=out, in_=in_, func=func, **kw), reads, writes)

    def tt(self, out, in0, in1, op, reads, writes):
        self.kb.op("dve", lambda e: e.tensor_tensor(out=out, in0=in0, in1=in1, op=op), reads, writes)

    def ptt(self, out, in0, in1, op, reads, writes):
        self.kb.op("pool", lambda e: e.tensor_tensor(out=out, in0=in0, in1=in1, op=op), reads, writes)

    def stt(self, out, in0, scalar, in1, op0, op1, reads, writes):
        self.kb.op("dve", lambda e: e.scalar_tensor_tensor(out=out, in0=in0, scalar=scalar, in1=in1,
                                                           op0=op0, op1=op1), reads, writes)

    def ts(self, out, in0, s1, s2, op0, op1, reads, writes):
        if s2 is None:
            self.kb.op("dve", lambda e: e.tensor_scalar(out=out, in0=in0, scalar1=s1, scalar2=None, op0=op0),
                       reads, writes)
        else:
            self.kb.op("dve", lambda e: e.tensor_scalar(out=out, in0=in0, scalar1=s1, scalar2=s2, op0=op0, op1=op1),
                       reads, writes)

    def cp(self, out, in_, reads, writes):
        self.kb.op("dve", lambda e: e.tensor_copy(out=out, in_=in_), reads, writes)

    def recip(self, out, in_, reads, writes):
        self.kb.op("dve", lambda e: e.reciprocal(out=out, in_=in_), reads, writes)

    def memset(self, ap, val, writes):
        self.kb.op("dve", lambda e: e.memset(ap, val), [], writes)

    def load(self, out, in_, src, dst, q="sp"):
        self.kb.dma(q, out, in_, [src], [dst])

    def store(self, out, in_, src, dst, q="pool"):
        self.kb.dma(q, out, in_, [src], [dst])

    def wload(self, dst, src, col0, ncols, nk=8, rowlen=None):
        rl = rowlen
        step = 4
        for k0 in range(0, nk, step):
            kn = min(step, nk - k0)
            self.kb.dma("pool", AP(dst, k0 * ncols, [[nk * ncols, 128], [ncols, kn], [1, ncols]]),
                        AP(src, k0 * 128 * rl + col0, [[rl, 128], [128 * rl, kn], [1, ncols]]), [src], [dst])

    def consts(self):
        kb = self.kb
        self.identf = kb.sb("identf", [128, 128], F32)
        self.identb = kb.sb("identb", [128, 128], BF16)
        self.load(self.identf[:, :], self.identf_d[:, :], self.identf_d, self.identf)
        self.kb.dma("pool", self.identb[:, :], self.identf_d[:, :], [self.identf_d], [self.identb])
        self.onesf = kb.sb("onesf", [128, 128], F32)
        self.onesb = kb.sb("onesb", [128, 128], BF16)
        self.memset(self.onesf[:, :], 1.0, [self.onesf])
        self.memset(self.onesb[:, :], 1.0, [self.onesb])
        self.junk = kb.sb("junk", [128, 256], BF16)
        self.ssq = kb.sb("ssq", [128, 16], F32)
        self.rsd = kb.sb("rsd", [128, 16], F32)
        self.ntmp = kb.sb("ntmp", [128, DM], F32)
        self.hnb = kb.sb("hnb", [128, DM], BF16)

    def phase0(self):
        kb = self.kb
        m = kb.mark()
        cT = kb.sb("cTs", [128, 8], F32)
        ccT = kb.sb("ccTs", [128, 8], F32)
        bm = kb.sb("bm", [1, 6 * DM], F32)
        n1 = kb.sb("n1", [128, DM], F32)
        n2 = kb.sb("n2", [128, DM], F32)
        SL = kb.sb("SL", [128, 8, 128], F32)
        SC = kb.sb("SC", [128, 8, 128], F32)
        wm = [kb.sb("wm%d" % i, [128, 8, 512], F32) for i in range(2)]
        self.load(cT[:, :], self.cT[:, :], self.cT, cT)
        self.load(ccT[:, :], self.cctxT[:, :], self.cctxT, ccT)
        self.load(bm[:, :], self.b_mod[:, :], self.b_mod, bm)
        self.load(n1[:, :], self.n1bc[:, :], self.n1bc, n1)
        self.load(n2[:, :], self.n2bc[:, :], self.n2bc, n2)
        for kc in range(8):
            self.act(SL[:, kc, :], self.onesf[:, :], AF.Silu, [self.onesf, cT], [SL], scale=cT[:, kc:kc + 1])
            self.act(SC[:, kc, :], self.onesf[:, :], AF.Silu, [self.onesf, ccT], [SC], scale=ccT[:, kc:kc + 1])
        for g in range(12):
            w = wm[g % 2]
            for k0 in (0, 4):
                self.load(AP(w, k0 * 512, [[8 * 512, 128], [512, 4], [1, 512]]),
                          AP(self.w_mod, k0 * 128 * 6 * DM + g * 512, [[6 * DM, 128], [128 * 6 * DM, 4], [1, 512]]),
                          self.w_mod, w)
            variants = [(SL, self.modL1 if g < 6 else self.modL2, 0, (g % 6) * 512)]
            if g < 4:
                variants.append((SC, self.modC, 1, g * 512))
            for (S_, dst, bi, dc0) in variants:
                bk = (g * 2 + bi) % 8
                for kc in range(8):
                    self.mm(self.pf(bk, 0, 512), S_[:, kc, :], w[:, kc, :], kc == 0, False, [S_, w], [self.bank[bk]])
                self.mm(self.pf(bk, 0, 512), self.onesf[0:1, :], bm[0:1, g * 512:(g + 1) * 512], False, True,
                        [self.onesf, bm], [self.bank[bk]])
                self.act(dst[:, dc0:dc0 + 512], self.pf(bk, 0, 512), AF.Copy, [self.bank[bk]], [dst])
        for (dst, nw, c0) in ((self.modL1, n1, DM), (self.modL2, n2, DM), (self.modC, n1, DM)):
            self.stt(dst[:, c0:c0 + DM], dst[:, c0:c0 + DM], 1.0, nw[:, :], ALU.add, ALU.mult, [dst, nw], [dst])
        kb.barrier()
        kb.release(m)
        ML, ML2, MC = self.modL1, self.modL2, self.modC
        self.modL = ML
        self.B1, self.A1, self.G1 = ML[:, 0:DM], ML[:, DM:2 * DM], ML[:, 2 * DM:3 * DM]
        self.B2, self.A2, self.G2 = ML2[:, 0:DM], ML2[:, DM:2 * DM], ML2[:, 2 * DM:3 * DM]
        self.B1c, self.A1c = MC[:, 0:DM], MC[:, DM:2 * DM]

    def norm_T(self, xt, xbuf, A, B, mbuf, hnT, Tn, sub, bk):
        ss, rs = self.ssq, self.rsd
        self.act(self.ntmp[:, :], xt, AF.Square, [xbuf], [self.ntmp, ss], accum=ss[:, 0:1])
        self.act(rs[:, 0:1], ss[:, 0:1], AF.Ln, [ss], [rs], scale=1.0 / DM, bias=self.epsb[:, 0:1])
        self.act(rs[:, 1:2], rs[:, 0:1], AF.Exp, [rs], [rs], scale=-0.5)
        self.stt(self.ntmp[:, :], xt, rs[:, 1:2], A, ALU.mult, ALU.mult, [xbuf, rs, mbuf], [self.ntmp])
        self.tt(self.hnb[:, :], self.ntmp[:, :], B, ALU.add, [self.ntmp, mbuf], [self.hnb])
        for kc in range(8):
            self.tr(self.pb16(bk, kc * 128, 128), self.hnb[:, kc * 128:(kc + 1) * 128], self.identb[:, :],
                    [self.hnb, self.identb], [self.bank[bk]])
        self.act(AP(hnT, sub * 128, [[8 * Tn, 128], [Tn, 8], [1, 128]]),
                 AP(self.PPb[bk // 2], (bk % 2) * 1024, [[2048, 128], [128, 8], [1, 128]]), AF.Copy,
                 [self.bank[bk]], [hnT])

    def tiles(self, colmajor):
        res = [(0, 0, True, None)]
        for i in range(self.L // TT):
            res.append((i + 1, CTX + i * TT, False, i))
        return res

    def x_src(self, is_ctx, li, colmajor):
        if is_ctx:
            return AP(self.ctx, 0, [[DM, 128], [128 * DM, 2], [1, DM]]), self.ctx
        if not colmajor:
            return AP(self.x, li * TT * DM, [[DM, 128], [128 * DM, 2], [1, DM]]), self.x
        c0 = li * 2
        return AP(self.x, c0 * DM, [[self.W * DM, 128], [DM, 2], [1, DM]]), self.x

    def phaseA(self):
        kb = self.kb
        m = kb.mark()
        NT = self.NT
        Wqk = kb.sb("Wqk", [128, 8, 1024], BF16)
        Wv = kb.sb("Wv", [128, 8, 1024], BF16)
        Wlr = kb.sb("Wlr", [128, 8, 32], BF16)
        self.wload(Wqk, self.w_in, SPL["gq"], 1024, rowlen=PIN)
        self.wload(Wv, self.w_in, SPL["gv"], 1024, rowlen=PIN)
        self.wload(Wlr, self.w_in, SPL["lr"], 32, rowlen=PIN)
        LRW = [kb.sb("LRW%d" % d, [33, 512], F32) for d in range(2)]
        U = [kb.sb("U%d" % d, [128, 128], F32) for d in range(2)]
        for d in range(2):
            self.load(LRW[d][:, :], AP(self.lrw, d * 33 * 512, [[512, 33], [1, 512]]), self.lrw, LRW[d])
            self.load(U[d][:, :], AP(self.cumU, d * 128 * 128, [[128, 128], [1, 128]]), self.cumU, U[d])
        xt = [kb.sb("xtA%d" % i, [128, 2, DM], F32) for i in range(2)]
        hnT = [kb.sb("hnTA%d" % i, [128, 8, TT], BF16) for i in range(2)]
        qkTs = [kb.sb("qkT%d" % i, [128, 8, TT], F32) for i in range(2)]
        vtok = [kb.sb("vtok%d" % i, [128, 2, DM], BF16) for i in range(2)]
        lraug = kb.sb("lraug", [33, TT], F32)
        self.memset(lraug[32:33, :], 1.0, [lraug])
        e1s = [kb.sb("e1_%d" % i, [128, 512], F32) for i in range(2)]
        sps = [kb.sb("sp_%d" % i, [128, 512], F32) for i in range(2)]
        eGs = [kb.sb("eG_%d" % i, [128, 512], F32) for i in range(2)]
        enGs = [kb.sb("enG_%d" % i, [128, 512], F32) for i in range(2)]
        dKs = [kb.sb("dK_%d" % i, [128, 512], F32) for i in range(2)]
        gends = [kb.sb("gend_%d" % i, [128, 4], F32) for i in range(2)]
        khTs = [kb.sb("khT_%d" % i, [128, 512], BF16) for i in range(2)]
        qs = [kb.sb("qs%d" % d, [128, 4, TT], BF16) for d in range(2)]
        ks = [kb.sb("ks%d" % d, [128, 4, TT], BF16) for d in range(2)]
        khs = [kb.sb("khs%d" % d, [128, 2, 512], BF16) for d in range(2)]
        print("phaseA sbuf remaining", self.nc.sbuf_bytes_remaining)
        tl = self.tiles(False)
        src, sbuf = self.x_src(tl[0][2], tl[0][3], False)
        self.load(xt[0][:, :, :], src, sbuf, xt[0])
        def do_norm(tj):
            is_c = tl[tj][2]
            A, B, mb = (self.A1c, self.B1c, self.modC) if is_c else (self.A1, self.B1, self.modL)
            for sub in range(2):
                self.norm_T(xt[tj % 2][:, sub, :], xt[tj % 2], A, B, mb, hnT[tj % 2], TT, sub, 0)

        if len(tl) > 1:
            src, sbuf = self.x_src(tl[1][2], tl[1][3], False)
            self.load(xt[1][:, :, :], src, sbuf, xt[1])
        do_norm(0)
        for ti, (idx, tok0, is_ctx, li) in enumerate(tl):
            X = xt[ti % 2]
            H = hnT[ti % 2]
            qkT = qkTs[ti % 2]
            for fc in range(8):
                bk = 1 + fc % 2
                for kc in range(8):
                    self.mm(self.pf(bk, 0, TT), Wqk[:, kc, fc * 128:(fc + 1) * 128], H[:, kc, :], kc == 0, kc == 7,
                            [Wqk, H], [self.bank[bk]])
                self.act(qkT[:, fc, :], self.pf(bk, 0, TT), AF.Copy, [self.bank[bk]], [qkT],
                         scale=(128.0 ** -0.5 if fc < 4 else 1.0))
            VT = vtok[ti % 2]
            for sub in range(2):
                for half in range(2):
                    bk = 3 + half
                    for kc in range(8):
                        self.mm(self.pf(bk, 0, 512), H[:, kc, sub * 128:(sub + 1) * 128],
                                Wv[:, kc, half * 512:(half + 1) * 512], kc == 0, kc == 7, [H, Wv], [self.bank[bk]])
                    self.cp(VT[:, sub, half * 512:(half + 1) * 512], self.pf(bk, 0, 512), [self.bank[bk]], [VT])
            self.store(AP(self.gv, tok0 * DM, [[DM, 128], [128 * DM, 2], [1, DM]]), VT[:, :, :], VT, self.gv)
            for kc in range(8):
                self.mm(self.pf(5, 0, TT, 32), Wlr[:, kc, :], H[:, kc, :], kc == 0, kc == 7, [Wlr, H], [self.bank[5]])
            self.act(lraug[0:32, :], self.pf(5, 0, TT, 32), AF.Copy, [self.bank[5]], [lraug])
            if ti + 1 < len(tl):
                do_norm(ti + 1)
                if ti + 2 < len(tl):
                    src, sbuf = self.x_src(tl[ti + 2][2], tl[ti + 2][3], False)
                    self.load(xt[ti % 2][:, :, :], src, sbuf, xt[ti % 2])
            def s1(sub, d):
                e1, sp = e1s[d], sps[d]
                bx = 6 if d == 0 else 1
                self.mm(self.pf(bx, 0, 512), lraug[0:33, sub * 128:(sub + 1) * 128], LRW[d][:, :], True, True,
                        [lraug, LRW[d]], [self.bank[bx]])
                self.act(e1[:, :], self.pf(bx, 0, 512), AF.Exp, [self.bank[bx]], [e1], scale=-1.0)
                self.act(sp[:, :], e1[:, :], AF.Ln, [e1], [sp], bias=self.oneb[:, 0:1])

            def s2(sub, d):
                ch = (tok0 // 128) + sub
                sp, eG, enG, dK, gend = sps[d], eGs[d], enGs[d], dKs[d], gends[d]
                bg = 7 if d == 0 else 2
                for h in range(4):
                    self.mm(self.pf(bg, h * 128, 128), sp[:, h * 128:(h + 1) * 128], U[d][:, :], True, True,
                            [sp, U[d]], [self.bank[bg]])
                G = self.pf(bg, 0, 512)
                ecol = 127 if d == 0 else 0
                self.act(eG[:, :], G, AF.Exp, [self.bank[bg]], [eG])
                self.act(gend[:, :], AP(self.PP[bg // 2], (bg % 2) * 512 + ecol, [[1024, 128], [128, 4]]), AF.Copy,
                         [self.bank[bg]], [gend])
                self.act(enG[:, :], G, AF.Exp, [self.bank[bg]], [enG], scale=-1.0)
                for h in range(4):
                    self.act(dK[:, h * 128:(h + 1) * 128], self.pf(bg, h * 128, 128), AF.Exp,
                             [self.bank[bg], gend], [dK], scale=-1.0, bias=gend[:, h:h + 1])
                self.cp(AP(self.Egla, (d * self.NCH + ch) * 4, [[2 * self.NCH * 4, 128], [1, 4]]),
                        AP(eG, ecol, [[512, 128], [128, 4]]), [eG], [self.Egla])

            def s3(sub, d):
                eG, enG, dK, khT = eGs[d], enGs[d], dKs[d], khTs[d]
                bt = 5 if d == 0 else 3
                qv = AP(qkT, sub * 128, [[8 * TT, 128], [TT, 4], [1, 128]])
                kv = AP(qkT, 4 * TT + sub * 128, [[8 * TT, 128], [TT, 4], [1, 128]])
                g3 = lambda t: AP(t, 0, [[512, 128], [128, 4], [1, 128]])
                self.tt(AP(qs[d], sub * 128, [[4 * TT, 128], [TT, 4], [1, 128]]), qv, g3(eG), ALU.mult,
                        [qkT, eG], [qs[d]])
                self.tt(AP(ks[d], sub * 128, [[4 * TT, 128], [TT, 4], [1, 128]]), kv, g3(enG), ALU.mult,
                        [qkT, enG], [ks[d]])
                self.tt(g3(khT), kv, g3(dK), ALU.mult, [qkT, dK], [khT])
                for h in range(4):
                    self.tr(self.pb16(bt, h * 128, 128), khT[:, h * 128:(h + 1) * 128], self.identb[:, :],
                            [khT, self.identb], [self.bank[bt]])
                self.cp(khs[d][:, sub, :], self.pb16(bt, 0, 512), [self.bank[bt]], [khs[d]])

            combos = [(sub, d) for sub in range(2) for d in range(2)]
            for k in range(len(combos) + 2):
                if k < len(combos):
                    s1(*combos[k])
                if 0 <= k - 1 < len(combos):
                    s2(*combos[k - 1])
                if 0 <= k - 2 < len(combos):
                    s3(*combos[k - 2])
            for d in range(2):
                self.store(AP(self.gq[d], tok0, [[NT, 128], [128 * NT, 4], [1, TT]]), qs[d][:, :, :], qs[d], self.gq[d])
                self.store(AP(self.gk[d], tok0, [[NT, 128], [128 * NT, 4], [1, TT]]), ks[d][:, :, :], ks[d], self.gk[d])
                self.store(AP(self.gkh[d], tok0 * 512, [[512, 128], [128 * 512, 2], [1, 512]]), khs[d][:, :, :],
                           khs[d], self.gkh[d])
        kb.barrier()
        kb.release(m)

    def gla_scan(self):
        kb = self.kb
        m = kb.mark()
        NT, NCH = self.NT, self.NCH
        S = [kb.sb("Sg%d" % d, [128, 4, 256], F32) for d in range(2)]
        Sb = [kb.sb("Sgb%d" % d, [128, 4, 256], BF16) for d in range(2)]
        msk = [kb.sb("gm%d" % d, [128, 128], F32) for d in range(2)]
        for d in range(2):
            self.memset(S[d][:, :, :], 0.0, [S[d]])
            self.memset(Sb[d][:, :, :], 0.0, [Sb[d]])
            self.load(msk[d][:, :], AP(self.gmask, d * 128 * 128, [[128, 128], [1, 128]]), self.gmask, msk[d])
        nb = 2
        qt = [[kb.sb("sq%d_%d" % (d, i), [128, 4, 128], BF16) for i in range(nb)] for d in range(2)]
        kt = [[kb.sb("sk%d_%d" % (d, i), [128, 4, 128], BF16) for i in range(nb)] for d in range(2)]
        kh = [[kb.sb("skh%d_%d" % (d, i), [128, 512], BF16) for i in range(nb)] for d in range(2)]
        vv = [[kb.sb("sv%d_%d" % (d, i), [128, DM], BF16) for i in range(nb)] for d in range(2)]
        AT = [kb.sb("AT%d" % d, [128, 4, 128], BF16) for d in range(2)]
        osb = [kb.sb("osb%d" % d, [128, DM], F32) for d in range(2)]
        order = [list(range(NCH)), [1, 0] + list(range(NCH - 1, 1, -1))]

        def issue_loads(s):
            for d in range(2):
                c = order[d][s]
                t0 = c * 128
                i = s % nb
                self.load(qt[d][i][:, :, :], AP(self.gq[d], t0, [[NT, 128], [128 * NT, 4], [1, 128]]), self.gq[d], qt[d][i])
                self.load(kt[d][i][:, :, :], AP(self.gk[d], t0, [[NT, 128], [128 * NT, 4], [1, 128]]), self.gk[d], kt[d][i])
                self.load(kh[d][i][:, :], AP(self.gkh[d], t0 * 512, [[512, 128], [1, 512]]), self.gkh[d], kh[d][i])
                self.load(vv[d][i][:, :], AP(self.gv, t0 * DM, [[DM, 128], [1, DM]]), self.gv, vv[d][i])

        issue_loads(0)
        for s in range(NCH):
            if s + 1 < NCH:
                issue_loads(s + 1)
            for d in range(2):
                c = order[d][s]
                i = s % nb
                Q, K_, KH, V = qt[d][i], kt[d][i], kh[d][i], vv[d][i]
                bA = d
                bO = 2 + 2 * d
                bS = 6
                for h in range(4):
                    self.mm(self.pf(bA, h * 128, 128), K_[:, h, :], Q[:, h, :], True, True, [K_, Q], [self.bank[bA]])
                self.tt(AT[d][:, :, :], AP(self.PP[bA // 2], (bA % 2) * 512, [[1024, 128], [128, 4], [1, 128]]),
                        AP(msk[d], 0, [[128, 128], [0, 4], [1, 128]]), ALU.mult, [self.bank[bA], msk[d]], [AT[d]])
                for h in range(4):
                    bk = bO + h // 2
                    o_ap = self.pf(bk, (h % 2) * 256, 256)
                    self.mm(o_ap, AT[d][:, h, :], V[:, h * 256:(h + 1) * 256], True, False, [AT[d], V], [self.bank[bk]])
                    self.mm(o_ap, Q[:, h, :], Sb[d][:, h, :], False, True, [Q, Sb[d]], [self.bank[bk]])
                self.act(osb[d][:, :], AP(self.PP[bO // 2], 0, [[1024, 128], [1, 1024]]), AF.Copy,
                         [self.bank[bO], self.bank[bO + 1]], [osb[d]])
                self.store(AP(self.go[d], c * 128 * DM, [[DM, 128], [1, DM]]), osb[d][:, :], osb[d], self.go[d])
                for h in range(4):
                    bk = bS + h // 2
                    self.mm(self.pf(bk, (h % 2) * 256, 256), KH[:, h * 128:(h + 1) * 128], V[:, h * 256:(h + 1) * 256],
                            True, True, [KH, V], [self.bank[bk]])
                for h in range(4):
                    bk = bS + h // 2
                    self.stt(S[d][:, h, :], S[d][:, h, :],
                             AP(self.Egla, (d * NCH + c) * 4 + h, [[2 * NCH * 4, 128], [1, 1]]),
                             self.pf(bk, (h % 2) * 256, 256), ALU.mult, ALU.add,
                             [S[d], self.Egla, self.bank[bk]], [S[d]])
                self.act(Sb[d][:, :, :], S[d][:, :, :], AF.Copy, [S[d]], [Sb[d]])
        kb.barrier()
        kb.release(m)

    def phaseB(self):
        kb = self.kb
        m = kb.mark()
        NT, NCH = self.NT, self.NCH
        Wgd = kb.sb("Wgd", [128, 8, 3072], BF16)
        Wab = kb.sb("Wab", [128, 8, 32], BF16)
        self.wload(Wgd, self.w_in, SPL["dq"], 3072, rowlen=PIN)
        self.wload(Wab, self.w_in, SPL["ab"], 32, rowlen=PIN)
        cw = kb.sb("cw", [128, 24, 5], F32)
        self.load(AP(cw, 0, [[120, 128], [1, 120]]), self.cwT[:, :], self.cwT, cw)
        negA = kb.sb("negA", [128, 16], F32)
        dtb = kb.sb("dtbs", [128, 16], F32)
        self.load(negA[:, :], self.alog[:, :], self.alog, negA)
        self.load(dtb[:, :], self.dtb[:, :], self.dtb, dtb)
        self.act(negA[:, :], negA[:, :], AF.Exp, [negA], [negA])
        self.kb.op("act", lambda e: e.mul(negA[:, :], negA[:, :], -1.0), [negA], [negA])
        CU = [kb.sb("CU%d" % i, [128, 128], F32) for i in range(4)]
        for i in range(4):
            self.load(CU[i][:, :], AP(self.cumU, (2 + i) * 128 * 128, [[128, 128], [1, 128]]), self.cumU, CU[i])
        UFi, UBi, UBs, UFs = CU
        MK = [kb.sb("MK%d" % i, [128, 128], BF16) for i in range(4)]
        for i in range(4):
            self.kb.dma("pool", MK[i][:, :], AP(self.dmask, i * 128 * 128, [[128, 128], [1, 128]]), [self.dmask], [MK[i]])
        Minc = [MK[0], MK[1]]
        Mlt, Mgt = MK[2], MK[3]
        sel = kb.sb("sel", [96, 32, 128], BF16)
        self.kb.dma("pool", AP(sel, 0, [[4096, 96], [1, 4096]]), self.sel_d[:, :], [self.sel_d], [sel])
        xt = [kb.sb("xtB%d" % i, [128, 2, DM], F32) for i in range(1)] * 2
        hnT = kb.sb("hnTB", [128, 8, TT], BF16)
        acc = [kb.sb("acc%d" % i, [128, TT], F32) for i in range(5)]
        sg = [kb.sb("sgb%d" % i, [128, TT], F32) for i in range(2)]
        sil = [kb.sb("sil%d" % i, [128, TT], F32) for i in range(4)]
        sq = [kb.sb("sqb%d" % i, [128, TT], BF16) for i in range(2)]
        rn = [kb.sb("rn%d" % i, [128, TT], F32) for i in range(2)]
        vTb = [kb.sb("vTb%d" % i, [128, TT], BF16) for i in range(2)]
        dqs = kb.sb("dqs", [128, 8, TT], BF16)
        dks = kb.sb("dks", [128, 8, TT], BF16)
        vtk = kb.sb("vtk", [128, 2, DM], BF16)
        ktk = kb.sb("ktk", [128, 2, DM], BF16)
        gbuf = []
        for gi in range(2):
            gbuf.append((kb.sb("g16_%d" % gi, [128, 16], F32), kb.sb("t16_%d" % gi, [128, 16], F32),
                         kb.sb("beta_%d" % gi, [128, 16], F32), kb.sb("lnb_%d" % gi, [128, 16], F32),
                         kb.sb("ba_%d" % gi, [128, 16], F32), kb.sb("R_%d" % gi, [128, 32], F32),
                         kb.sb("negG_%d" % gi, [128, 16], F32), kb.sb("R1_%d" % gi, [128, 32], F32),
                         kb.sb("Rs_%d" % gi, [128, 96], BF16), kb.sb("rows_%d" % gi, [96, 128], BF16),
                         kb.sb("nrows_%d" % gi, [96, 128], BF16)))
        Dm = kb.sb("Dm", [128, 8, 128], F32)
        E1 = kb.sb("E1m", [128, 8, 128], F32)
        E2 = Dm
        aqs = kb.sb("aqs", [128, 8, 128], BF16)
        X0 = kb.sb("X0n", [128, 8, 128], BF16)
        LT = [kb.sb("LTn%d" % d, [128, 8, 128], BF16) for d in range(2)]
        Dn = [[kb.sb("Dn%d_%d" % (d, g), [128, 4, 128], BF16) for g in range(2)] for d in range(2)]
        DTn = [[kb.sb("DTn%d_%d" % (d, g), [128, 4, 128], BF16) for g in range(2)] for d in range(2)]
        Wn = [[kb.sb("Wn%d_%d" % (d, g), [128, 4, 128], BF16) for g in range(2)] for d in range(2)]
        MT = [kb.sb("MTn%d" % d, [128, 8, 128], BF16) for d in range(2)]
        MaT = [kb.sb("MaTn%d" % d, [128, 8, 128], BF16) for d in range(2)]
        LM = kb.sb("LM", [128, 14, 128], BF16)
        self.kb.dma("pool", LM[:, :, :], AP(self.lvlm, 0, [[128, 128], [128 * 128, 14], [1, 128]]), [self.lvlm], [LM])
        lm = lambda i: AP(LM, i * 128, [[14 * 128, 128], [0, 8], [1, 128]])
        lm4 = lambda i: AP(LM, i * 128, [[14 * 128, 128], [0, 4], [1, 128]])
        nidb = kb.sb("nidb", [128, 128], BF16)
        self.kb.op("act", lambda e: e.mul(nidb[:, :], self.identb[:, :], -1.0), [self.identb], [nidb])
        us = kb.sb("us", [128, DM], F32)
        ws = kb.sb("wsn", [128, 8, 128], BF16)
        b3 = lambda t, c0: AP(t, c0, [[16, 128], [1, 8], [0, 128]])
        f3 = lambda t: AP(t, 0, [[1024, 128], [128, 8], [1, 128]])
        p3 = lambda k: AP(self.PP[k], 0, [[1024, 128], [128, 8], [1, 128]])
        pbk = lambda k: [self.bank[2 * k], self.bank[2 * k + 1]]
        print("phaseB sbuf remaining", self.nc.sbuf_bytes_remaining)
        tl = self.tiles(True)
        src, sbuf = self.x_src(tl[0][2], tl[0][3], True)
        self.load(xt[0][:, :, :], src, sbuf, xt[0])
        for ti, (idx, tok0, is_ctx, li) in enumerate(tl):
            Xt = xt[ti % 2]
            A, B, mb = (self.A1c, self.B1c, self.modC) if is_ctx else (self.A1, self.B1, self.modL)
            for sub in range(2):
                self.norm_T(Xt[:, sub, :], Xt, A, B, mb, hnT, TT, sub, 0)
            if ti + 1 < len(tl):
                src, sbuf = self.x_src(tl[ti + 1][2], tl[ti + 1][3], True)
                self.load(xt[(ti + 1) % 2][:, :, :], src, sbuf, xt[(ti + 1) % 2])
            def gates(sub):
                ch = tok0 // 128 + sub
                tsl = slice(sub * 128, (sub + 1) * 128)
                g16, t16, beta, lnb, ba, R, negG, R1, Rs, rows, nrows = gbuf[sub]
                for kc in range(8):
                    self.mm(self.pf(6, 0, 32), hnT[:, kc, tsl], Wab[:, kc, :], kc == 0, kc == 7, [hnT, Wab], [self.bank[6]])
                self.tt(t16[:, :], self.pf(6, 0, 16), dtb[:, :], ALU.add, [self.bank[6], dtb], [t16])
                self.act(t16[:, :], t16[:, :], AF.Exp, [t16], [t16])
                self.act(t16[:, :], t16[:, :], AF.Ln, [t16], [t16], bias=self.oneb[:, 0:1])
                self.tt(g16[:, :], t16[:, :], negA[:, :], ALU.mult, [t16, negA], [g16])
                self.act(lnb[:, :], self.pf(6, 16, 16), AF.Exp, [self.bank[6]], [lnb], scale=-1.0)
                self.act(lnb[:, :], lnb[:, :], AF.Ln, [lnb], [lnb], bias=self.oneb[:, 0:1])
                self.kb.op("act", lambda e: e.mul(lnb[:, :], lnb[:, :], -1.0), [lnb], [lnb])
                self.act(beta[:, :], lnb[:, :], AF.Exp, [lnb], [beta])
                self.mm(self.pf(7, 0, 8), UFi[:, :], g16[:, 0:8], True, True, [UFi, g16], [self.bank[7]])
                self.mm(self.pf(7, 8, 8), UBi[:, :], g16[:, 8:16], True, True, [UBi, g16], [self.bank[7]])
                self.mm(self.pf(7, 16, 8), UBs[:, :], g16[:, 0:8], True, True, [UBs, g16], [self.bank[7]])
                self.mm(self.pf(7, 24, 8), UFs[:, :], g16[:, 8:16], True, True, [UFs, g16], [self.bank[7]])
                self.mm(self.pf(7, 32, 16), self.onesf[:, :], g16[:, :], True, True, [self.onesf, g16], [self.bank[7]])
                self.act(R[:, 0:16], self.pf(7, 0, 16), AF.Copy, [self.bank[7]], [R])
                self.act(AP(self.GDa, ch * 16, [[NCH * 16, 128], [1, 16]]), self.pf(7, 0, 16), AF.Exp,
                         [self.bank[7]], [self.GDa])
                self.act(AP(self.GDd, ch * 16, [[NCH * 16, 128], [1, 16]]), self.pf(7, 16, 16), AF.Exp,
                         [self.bank[7]], [self.GDd])
                self.act(AP(self.GDe, ch * 16, [[NCH * 16, 128], [1, 16]]), self.pf(7, 32, 16), AF.Exp,
                         [self.bank[7]], [self.GDe])
                self.tt(ba[:, :], beta[:, :], AP(self.GDa, ch * 16, [[NCH * 16, 128], [1, 16]]), ALU.mult,
                        [beta, self.GDa], [ba])
                self.tt(R[:, 16:32], R[:, 0:16], lnb[:, :], ALU.add, [R, lnb], [R])
                self.kb.op("act", lambda e: e.mul(negG[:, :], R[:, 0:16], -1.0), [R], [negG])
                self.cp(Rs[:, 0:32], R[:, :], [R], [Rs])
                self.tt(R1[:, :], R[:, :], Rs[:, 0:32], ALU.subtract, [R, Rs], [R1])
                self.cp(Rs[:, 32:64], R1[:, :], [R1], [Rs])
                self.tt(R1[:, :], R1[:, :], Rs[:, 32:64], ALU.subtract, [R1, Rs], [R1])
                self.cp(Rs[:, 64:96], R1[:, :], [R1], [Rs])
                self.tr(self.pb16(6, 0, 128, 96), Rs[:, :], self.identb[:, :], [Rs, self.identb], [self.bank[6]])
                self.act(rows[:, :], self.pb16(6, 0, 128, 96), AF.Copy, [self.bank[6]], [rows])
                self.kb.op("act", lambda e: e.mul(nrows[:, :], rows[:, :], -1.0), [rows], [nrows])
            for sub in range(2):
                gates(sub)
            nseg, ls = (1, TT) if is_ctx else (2, 128)

            def stA(fc):
                bk = fc % 4
                for kc in range(8):
                    self.mm(self.pf(bk, 0, TT), Wgd[:, kc, fc * 128:(fc + 1) * 128], hnT[:, kc, :], kc == 0, kc == 7,
                            [Wgd, hnT], [self.bank[bk]])

            def conv_ops(fc):
                bk = fc % 4
                A_ = acc[fc % 5]
                ops = [lambda: self.ts(A_[:, :], self.pf(bk, 0, TT), cw[:, fc, 2:3], None, ALU.mult, None,
                                       [self.bank[bk], cw], [A_])]
                for tap in (0, 1, 3, 4):
                    sh = tap - 2
                    n = ls - abs(sh)
                    o0, i0 = (0, sh) if sh > 0 else (-sh, 0)
                    oap = AP(A_, o0, [[TT, 128], [ls, nseg], [1, n]])
                    iap = AP(self.PP[bk // 2], (bk % 2) * 512 + i0, [[1024, 128], [ls, nseg], [1, n]])
                    ops.append(lambda oap=oap, iap=iap, tap=tap: self.stt(oap, iap, cw[:, fc, tap:tap + 1], oap,
                                                                        ALU.mult, ALU.add, [self.bank[bk], cw, A_], [A_]))
                return ops

            def stB(fa, fb):
                oa = conv_ops(fa) if (fa is not None and 0 <= fa < 24) else None
                ob = conv_ops(fb) if (fb is not None and 0 <= fb < 24) else None
                seq = [(ob, 2), (oa, 0), (ob, 3), (oa, 1), (ob, 4)]
                for (o, i) in seq:
                    if o is not None:
                        o[i]()


            def stCE(fc_c, fc_e):
                okc = fc_c is not None and 0 <= fc_c < 24
                oke = fc_e is not None and 0 <= fc_e < 16
                if okc:
                    A_, G_ = acc[fc_c % 5], sg[fc_c % 2]
                    self.act(G_[:, :], A_[:, :], AF.Exp, [A_], [G_], scale=-1.0)
                if oke:
                    S_, Q_, R_ = sil[fc_e % 4], sq[fc_e % 2], rn[fc_e % 2]
                    bq = 6 + fc_e % 2
                    self.act(Q_[:, :], S_[:, :], AF.Square, [S_], [Q_])
                    self.mm(self.pf(bq, 0, TT), self.onesb[:, :], Q_[:, :], True, True, [self.onesb, Q_], [self.bank[bq]])
                if okc:
                    self.act(G_[:, :], G_[:, :], AF.Ln, [G_], [G_], bias=self.oneb[:, 0:1])
                if oke:
                    self.act(R_[:, :], self.pf(bq, 0, TT), AF.Ln, [self.bank[bq]], [R_], bias=self.epsb[:, 0:1])
                if okc:
                    self.act(G_[:, :], G_[:, :], AF.Exp, [G_], [G_], scale=-1.0)
                if oke:
                    self.act(R_[:, :], R_[:, :], AF.Exp, [R_], [R_], scale=-0.5,
                             bias=(self.lncb[:, 0:1] if fc_e < 8 else self.zerob[:, 0:1]))

            def stD(fc):
                A_, G_ = acc[fc % 5], sg[fc % 2]
                if fc < 16:
                    S_ = sil[fc % 4]
                    self.ptt(S_[:, :], A_[:, :], G_[:, :], ALU.mult, [A_, G_], [S_])
                else:
                    h = fc - 16
                    V_ = vTb[fc % 2]
                    self.ptt(V_[:, :], A_[:, :], G_[:, :], ALU.mult, [A_, G_], [V_])
                    for sub in range(2):
                        self.tr(self.pb16(4 + sub, h * 128, 128), V_[:, sub * 128:(sub + 1) * 128], self.identb[:, :],
                                [V_, self.identb], [self.bank[4 + sub]])

            def stF(fc):
                if fc < 16:
                    h = fc % 8
                    S_, R_ = sil[fc % 4], rn[fc % 2]
                    dst = dqs if fc < 8 else dks
                    self.ptt(dst[:, h, :], S_[:, :], R_[:, :], ALU.mult, [S_, R_], [dst])

            for it in range(24 + 6):
                if it < 24:
                    stA(it)
                stB(it - 1, it - 2)
                stCE(it - 3, it - 5)
                if 0 <= it - 4 < 24:
                    stD(it - 4)
                if 0 <= it - 6 < 24:
                    stF(it - 6)
            for sub in range(2):
                self.cp(vtk[:, sub, :], self.pb16(4 + sub, 0, 1024), [self.bank[4 + sub]], [vtk])
            for sub in range(2):
                for h in range(8):
                    self.tr(self.pb16(4 + sub, h * 128, 128), dks[:, h, sub * 128:(sub + 1) * 128], self.identb[:, :],
                            [dks, self.identb], [self.bank[4 + sub]])
                self.cp(ktk[:, sub, :], self.pb16(4 + sub, 0, 1024), [self.bank[4 + sub]], [ktk])
            self.store(AP(self.dqT, tok0, [[NT, 128], [128 * NT, 8], [1, TT]]), dqs[:, :, :], dqs, self.dqT)
            self.store(AP(self.dkt, tok0 * DM, [[DM, 128], [128 * DM, 2], [1, DM]]), ktk[:, :, :], ktk, self.dkt)
            for sub in range(2):
                ch = tok0 // 128 + sub
                tsl = slice(sub * 128, (sub + 1) * 128)
                g16, t16, beta, lnb, ba, R, negG, R1, Rs, rows, nrows = gbuf[sub]
                for h in range(8):
                    bk = h // 4
                    self.mm(self.pf(bk, (h % 4) * 128, 128), dks[:, h, tsl], dks[:, h, tsl], True, True,
                            [dks], [self.bank[bk]])
                for h in range(8):
                    bk = 2 + h // 4
                    self.mm(self.pf(bk, (h % 4) * 128, 128), dks[:, h, tsl], dqs[:, h, tsl], True, True,
                            [dks, dqs], [self.bank[bk]])
                for d in range(2):
                    mstrT = Mlt if d == 0 else Mgt
                    mstr = Mgt if d == 0 else Mlt
                    specs = [(Dm, 0, rows, nrows, 0, Minc[d]), (E1, 16, rows, nrows, 0, mstrT), (E2, 0, nrows, rows, 16, mstr)]
                    for si, (dst, so, rr, pr, po, mk) in enumerate(specs):
                        pk = 2 + (si % 2)
                        for hg in range(2):
                            bk = 2 * pk + hg
                            c0 = d * 8 + hg * 4
                            self.mm(self.pf(bk, 0, 512), self.identb[:, :], AP(mk, 0, [[128, 128], [0, 4], [1, 128]]),
                                    True, False, [self.identb, mk], [self.bank[bk]])
                            self.mm(self.pf(bk, 0, 512), pr[:, :], AP(sel, (po + c0) * 128, [[32 * 128, 96], [1, 512]]),
                                    False, False, [sel, pr], [self.bank[bk]])
                            for h4i in range(4):
                                self.mm(self.pf(bk, h4i * 128, 128), sel[:, so + c0 + h4i, :], rr[:, :], False, h4i == 3,
                                        [sel, rr], [self.bank[bk]])
                        for hg in range(2):
                            bk = 2 * pk + hg
                            self.act(AP(dst, hg * 512, [[1024, 128], [1, 512]]), self.pf(bk, 0, 512), AF.Exp,
                                     [self.bank[bk]], [dst])
                        if si == 0:
                            self.tt(f3(aqs), p3(1), f3(Dm), ALU.mult, pbk(1) + [Dm], [aqs])
                            self.store(AP(self.daq[d], ch * 128 * DM, [[DM, 128], [1, DM]]), aqs[:, :, :], aqs, self.daq[d])
                    self.tt(f3(LT[d]), p3(0), f3(E1), ALU.mult, pbk(0) + [E1], [LT[d]])
                    self.tt(f3(X0), p3(0), f3(E2), ALU.mult, pbk(0) + [E2], [X0])
                    if "dbgX" in self.dbg and ch == 2 and d == 0:
                        self.store(AP(self.dbgX, 0, [[DM, 128], [1, DM]]), X0[:, :, :], X0, self.dbgX)
                        self.store(AP(self.dbgXT, 0, [[DM, 128], [1, DM]]), LT[d][:, :, :], LT[d], self.dbgXT)
                    for hg in range(2):
                        h4 = lambda t: AP(t, 0, [[512, 128], [128, 4], [1, 128]])
                        idb = AP(self.identb, 0, [[128, 128], [0, 4], [1, 128]])
                        src4 = lambda t: AP(t, hg * 512, [[1024, 128], [128, 4], [1, 128]])
                        W_, D_, DT_ = Wn[d][hg], Dn[d][hg], DTn[d][hg]
                        self.tt(h4(W_), src4(X0), lm4(d * 7), ALU.mult, [X0, LM], [W_])
                        self.tt(h4(D_), h4(W_), idb, ALU.add, [W_, self.identb], [D_])
                        self.tt(h4(W_), src4(LT[d]), lm4((1 - d) * 7), ALU.mult, [LT[d], LM], [W_])
                        self.tt(h4(DT_), h4(W_), idb, ALU.add, [W_, self.identb], [DT_])
                h4 = lambda t: AP(t, 0, [[512, 128], [128, 4], [1, 128]])
                chains = [(d, hg) for d in range(2) for hg in range(2)]
                for lvl in range(1, 7):
                    for (d, hg) in chains:
                        k = d * 2 + hg
                        ba_ = 2 * k
                        W_, D_ = Wn[d][hg], Dn[d][hg]
                        self.mm(self.pf(ba_, 0, 512), self.identb[:, :], AP(nidb, 0, [[128, 128], [0, 4], [1, 128]]),
                                True, False, [self.identb, nidb], [self.bank[ba_]])
                        for h4i in range(4):
                            h = hg * 4 + h4i
                            self.mm(self.pf(ba_, h4i * 128, 128), LT[d][:, h, :], D_[:, h4i, :], False, h4i == 3,
                                    [LT[d], D_], [self.bank[ba_]])
                        self.tt(h4(W_), AP(self.PP[ba_ // 2], (ba_ % 2) * 512, [[1024, 128], [128, 4], [1, 128]]),
                                lm4(d * 7 + lvl), ALU.mult, [self.bank[ba_], LM], [W_])
                    for (d, hg) in chains:
                        k = d * 2 + hg
                        ba_, bb_ = 2 * k, 2 * k + 1
                        W_, D_, DT_ = Wn[d][hg], Dn[d][hg], DTn[d][hg]
                        if lvl < 6:
                            for h4i in range(4):
                                self.mm(self.pf(bb_, h4i * 128, 128), DT_[:, h4i, :], W_[:, h4i, :], True, True,
                                        [DT_, W_], [self.bank[bb_]])
                        for h4i in range(4):
                            self.mm(self.pf(ba_, h4i * 128, 128), W_[:, h4i, :], DT_[:, h4i, :], True, True,
                                    [W_, DT_], [self.bank[ba_]])
                        if lvl < 6:
                            if hg == 0:
                                self.act(AP(D_, 0, [[512, 128], [1, 512]]), self.pf(bb_, 0, 512), AF.Copy, [self.bank[bb_]], [D_])
                                self.cp(AP(DT_, 0, [[512, 128], [1, 512]]), self.pf(ba_, 0, 512), [self.bank[ba_]], [DT_])
                            else:
                                self.cp(AP(D_, 0, [[512, 128], [1, 512]]), self.pf(bb_, 0, 512), [self.bank[bb_]], [D_])
                                self.act(AP(DT_, 0, [[512, 128], [1, 512]]), self.pf(ba_, 0, 512), AF.Copy, [self.bank[ba_]], [DT_])
                        else:
                            pv = AP(self.PP[ba_ // 2], (ba_ % 2) * 512, [[1024, 128], [128, 4], [1, 128]])
                            mo = lambda t: AP(t, hg * 512, [[1024, 128], [128, 4], [1, 128]])
                            b4 = lambda t, c0: AP(t, c0, [[16, 128], [1, 4], [0, 128]])
                            self.tt(mo(MT[d]), pv, b4(beta, d * 8 + hg * 4), ALU.mult, [self.bank[ba_], beta], [MT[d]])
                            self.tt(mo(MaT[d]), pv, b4(ba, d * 8 + hg * 4), ALU.mult, [self.bank[ba_], ba], [MaT[d]])
                for d in range(2):
                    pa, pb_ = 2 * d, 2 * d + 1
                    if "dbgX" in self.dbg and ch == 2 and d == 0:
                        self.store(AP(self.dbgMT, 0, [[DM, 128], [1, DM]]), MT[d][:, :, :], MT[d], self.dbgMT)
                    for h in range(8):
                        bk = 2 * pb_ + h // 4
                        self.mm(self.pf(bk, (h % 4) * 128, 128), MT[d][:, h, :], vtk[:, sub, h * 128:(h + 1) * 128], True, True,
                                [MT[d], vtk], [self.bank[bk]])
                    self.act(us[:, :], AP(self.PP[pb_], 0, [[1024, 128], [1, 1024]]), AF.Copy, pbk(pb_), [us])
                    self.store(AP(self.du[d], ch * 128 * DM, [[DM, 128], [1, DM]]), us[:, :], us, self.du[d])
                    for h in range(8):
                        bk = 2 * pa + h // 4
                        self.mm(self.pf(bk, (h % 4) * 128, 128), ktk[:, sub, h * 128:(h + 1) * 128], MaT[d][:, h, :], True, True,
                                [ktk, MaT[d]], [self.bank[bk]])
                    self.cp(f3(ws), p3(pa), pbk(pa), [ws])
                    self.store(AP(self.dw[d], ch * 128, [[NT, 128], [128 * NT, 8], [1, 128]]), ws[:, :, :], ws, self.dw[d])
        kb.barrier()
        kb.release(m)

    def gdn_scan(self):
        kb = self.kb
        m = kb.mark()
        NT, NCH = self.NT, self.NCH
        S = [kb.sb("Sd%d" % d, [128, 8, 128], F32) for d in range(2)]
        Sb = [kb.sb("Sdb%d" % d, [128, 8, 128], BF16) for d in range(2)]
        for d in range(2):
            self.memset(S[d][:, :, :], 0.0, [S[d]])
            self.memset(Sb[d][:, :, :], 0.0, [Sb[d]])
        nb = 2
        mk = lambda n, sh, dt: [[kb.sb("%s%d_%d" % (n, d, i), sh, dt) for i in range(nb)] for d in range(2)]
        wT = mk("lw", [128, 8, 128], BF16)
        qT = mk("lq", [128, 8, 128], BF16)
        uu = mk("lu", [128, DM], F32)
        aq = mk("la", [128, DM], BF16)
        kt = mk("lk", [128, DM], BF16)
        vn32 = kb.sb("vn32", [128, DM], F32)
        vnb = kb.sb("vnb", [128, DM], BF16)
        vnd = kb.sb("vnd", [128, DM], BF16)
        tq = kb.sb("tq", [128, DM], F32)
        osb = [kb.sb("odb%d" % d, [128, DM], F32) for d in range(2)]
        order = [list(range(NCH)), [1, 0] + list(range(NCH - 1, 1, -1))]
        b3 = lambda t, c0: AP(t, c0, [[NCH * 16, 128], [1, 8], [0, 128]])
        f3 = lambda t: AP(t, 0, [[1024, 128], [128, 8], [1, 128]])
        p3 = lambda k: AP(self.PP[k], 0, [[1024, 128], [128, 8], [1, 128]])
        pbk = lambda k: [self.bank[2 * k], self.bank[2 * k + 1]]

        def issue_loads(s):
            for d in range(2):
                c = order[d][s]
                t0 = c * 128
                i = s % nb
                self.load(wT[d][i][:, :, :], AP(self.dw[d], t0, [[NT, 128], [128 * NT, 8], [1, 128]]), self.dw[d], wT[d][i])
                self.load(qT[d][i][:, :, :], AP(self.dqT, t0, [[NT, 128], [128 * NT, 8], [1, 128]]), self.dqT, qT[d][i])
                self.load(uu[d][i][:, :], AP(self.du[d], t0 * DM, [[DM, 128], [1, DM]]), self.du[d], uu[d][i])
                self.load(aq[d][i][:, :], AP(self.daq[d], t0 * DM, [[DM, 128], [1, DM]]), self.daq[d], aq[d][i])
                self.load(kt[d][i][:, :], AP(self.dkt, t0 * DM, [[DM, 128], [1, DM]]), self.dkt, kt[d][i])

        issue_loads(0)
        for s in range(NCH):
            if s + 1 < NCH:
                issue_loads(s + 1)
            for d in range(2):
                c = order[d][s]
                i = s % nb
                Wt, Qt, Uu, Aq, Kt = wT[d][i], qT[d][i], uu[d][i], aq[d][i], kt[d][i]
                for h in range(8):
                    bk = h // 4
                    self.mm(self.pf(bk, (h % 4) * 128, 128), Wt[:, h, :], Sb[d][:, h, :], True, True, [Wt, Sb[d]], [self.bank[bk]])
                for h in range(8):
                    bk = 2 + h // 4
                    self.mm(self.pf(bk, (h % 4) * 128, 128), Qt[:, h, :], Sb[d][:, h, :], True, True, [Qt, Sb[d]], [self.bank[bk]])
                self.tt(vn32[:, :], Uu[:, :], AP(self.PP[0], 0, [[1024, 128], [1, 1024]]), ALU.subtract, [Uu] + pbk(0), [vn32])
                self.act(vnb[:, :], vn32[:, :], AF.Copy, [vn32], [vnb])
                self.tt(f3(vnd), f3(vn32), b3(self.GDd, c * 16 + d * 8), ALU.mult, [vn32, self.GDd], [vnd])
                for h in range(8):
                    bk = 4 + h // 4
                    hs = slice(h * 128, (h + 1) * 128)
                    self.mm(self.pf(bk, (h % 4) * 128, 128), Aq[:, hs], vnb[:, hs], True, True, [Aq, vnb], [self.bank[bk]])
                for h in range(8):
                    bk = 6 + h // 4
                    hs = slice(h * 128, (h + 1) * 128)
                    self.mm(self.pf(bk, (h % 4) * 128, 128), Kt[:, hs], vnd[:, hs], True, True, [Kt, vnd], [self.bank[bk]])
                self.tt(f3(tq), p3(1), b3(self.GDa, c * 16 + d * 8), ALU.mult, pbk(1) + [self.GDa], [tq])
                self.tt(osb[d][:, :], tq[:, :], AP(self.PP[2], 0, [[1024, 128], [1, 1024]]), ALU.add, [tq] + pbk(2), [osb[d]])
                self.store(AP(self.do[d], c * 128 * DM, [[DM, 128], [1, DM]]), osb[d][:, :], osb[d], self.do[d])
                self.tt(f3(S[d]), f3(S[d]), b3(self.GDe, c * 16 + d * 8), ALU.mult, [S[d], self.GDe], [S[d]])
                self.tt(f3(S[d]), f3(S[d]), p3(3), ALU.add, [S[d]] + pbk(3), [S[d]])
                self.act(Sb[d][:, :, :], S[d][:, :, :], AF.Copy, [S[d]], [Sb[d]])
        kb.barrier()
        kb.release(m)

    def phase3a(self):
        kb = self.kb
        m = kb.mark()
        W, NT = self.W, self.NT
        Wz = kb.sb("Wz", [128, 8, 4096], BF16)
        for gi, key in enumerate(("gz", "dz", "g11", "g12")):
            for k0 in (0, 4):
                self.kb.dma("pool", AP(Wz, k0 * 4096 + gi * 1024, [[8 * 4096, 128], [4096, 4], [1, 1024]]),
                            AP(self.w_in, k0 * 128 * PIN + SPL[key], [[PIN, 128], [128 * PIN, 4], [1, 1024]]),
                            [self.w_in], [Wz])
        Wo = kb.sb("Wo", [128, 8, DM], BF16)
        self.wload(Wo, self.w_out, 0, DM, rowlen=DM)
        gg = kb.sb("ggl", [128, DM], F32)
        gd = kb.sb("ggd", [128, DM], F32)
        self.load(gg[:, :], self.glag[:, :], self.glag, gg)
        self.load(gd[:, :], self.gdng[:, :], self.gdng, gd)
        xt = [kb.sb("xt3%d" % i, [128, DM], F32) for i in range(3)]
        ol = [[kb.sb("ol%d_%d" % (j, i), [128, DM], F32) for i in range(2)] for j in range(4)]
        hnT = [kb.sb("hnT3_%d" % i, [128, 8, 128], BF16) for i in range(2)]
        sz = [kb.sb("sz%d" % i, [128, 2048], BF16) for i in range(2)]
        sg = [kb.sb("sg%d" % i, [128, 2048], BF16) for i in range(2)]
        ssl = [kb.sb("ss3_%d" % i, [128, 12], F32) for i in range(2)]
        rsl = [kb.sb("rs3_%d" % i, [128, 12], F32) for i in range(2)]
        mb16 = [kb.sb("mb16_%d" % i, [128, DM], BF16) for i in range(2)]
        mT = [kb.sb("mT%d" % i, [128, 8, 128], BF16) for i in range(2)]
        x1 = [kb.sb("x1s%d" % i, [128, DM], F32) for i in range(2)]
        print("phase3a sbuf remaining", self.nc.sbuf_bytes_remaining)
        ntile = self.L // 128

        def issue_loads(t):
            i = t % 2
            self.load(xt[t % 3][:, :], AP(self.x, t * 128 * DM, [[DM, 128], [1, DM]]), self.x, xt[t % 3])
            g0 = (CTX + t * 128) * DM
            self.load(ol[0][i][:, :], AP(self.go[0], g0, [[DM, 128], [1, DM]]), self.go[0], ol[0][i])
            self.load(ol[1][i][:, :], AP(self.go[1], g0, [[DM, 128], [1, DM]]), self.go[1], ol[1][i])
            nr = 128 // W if W <= 128 else 1
            for j in (0, 1):
                for rr in range(128 // W):
                    r = t * (128 // W) + rr
                    self.load(AP(ol[2 + j][i], rr * W * DM, [[DM, W], [1, DM]]),
                              AP(self.do[j], (CTX + r) * DM, [[128 * DM, W], [1, DM]]), self.do[j], ol[2 + j][i])

        def stage_a0(t):
            self.norm_T(xt[t % 3][:, :], xt[t % 3], self.A1, self.B1, self.modL, hnT[t % 2], 128, 0, 0)

        def stage_a(t):
            i = t % 2
            H, SZ, SG = hnT[i], sz[i], sg[i]
            for g in range(8):
                bk = 1 + g % 4
                for kc in range(8):
                    self.mm(self.pf(bk, 0, 512), H[:, kc, :], Wz[:, kc, g * 512:(g + 1) * 512], kc == 0, kc == 7,
                            [H, Wz], [self.bank[bk]])
                if g < 4:
                    self.act(SZ[:, g * 512:(g + 1) * 512], self.pf(bk, 0, 512), AF.Silu, [self.bank[bk]], [SZ])
                else:
                    self.act(SG[:, (g - 4) * 512:(g - 3) * 512], self.pf(bk, 0, 512), AF.Sigmoid, [self.bank[bk]], [SG])

        def stage_b(t):
            i = t % 2
            og, od = ol[0][i], ol[2][i]
            ss, rs = ssl[i], rsl[i]
            self.tt(og[:, :], ol[0][i][:, :], ol[1][i][:, :], ALU.add, [ol[0][i], ol[1][i]], [og])
            self.tt(od[:, :], ol[2][i][:, :], ol[3][i][:, :], ALU.add, [ol[2][i], ol[3][i]], [od])
            for h in range(4):
                self.act(self.junk[:, 0:256], og[:, h * 256:(h + 1) * 256], AF.Square, [og], [self.junk, ss],
                         accum=ss[:, h:h + 1])
            for h in range(8):
                self.act(self.junk[:, 0:128], od[:, h * 128:(h + 1) * 128], AF.Square, [od], [self.junk, ss],
                         accum=ss[:, 4 + h:5 + h])
            self.act(rs[:, 0:4], ss[:, 0:4], AF.Ln, [ss], [rs], scale=1.0 / 256, bias=self.epsb[:, 0:1])
            self.act(rs[:, 4:12], ss[:, 4:12], AF.Ln, [ss], [rs], scale=1.0 / 128, bias=self.epsb[:, 0:1])
            self.act(rs[:, :], rs[:, :], AF.Exp, [rs], [rs], scale=-0.5)

        def stage_b1(t):
            i = t % 2
            SZ, SG, MB, MT_ = sz[i], sg[i], mb16[i], mT[i]
            og, od = ol[0][i], ol[2][i]
            ss, rs = ssl[i], rsl[i]
            self.tt(AP(og, 0, [[DM, 128], [256, 4], [1, 256]]), AP(og, 0, [[DM, 128], [256, 4], [1, 256]]),
                    AP(rs, 0, [[12, 128], [1, 4], [0, 256]]), ALU.mult, [og, rs], [og])
            self.tt(AP(od, 0, [[DM, 128], [128, 8], [1, 128]]), AP(od, 0, [[DM, 128], [128, 8], [1, 128]]),
                    AP(rs, 4, [[12, 128], [1, 8], [0, 128]]), ALU.mult, [od, rs], [od])
            self.tt(og[:, :], og[:, :], gg[:, :], ALU.mult, [og, gg], [og])
            self.tt(od[:, :], od[:, :], gd[:, :], ALU.mult, [od, gd], [od])
            self.tt(og[:, :], og[:, :], SZ[:, 0:1024], ALU.mult, [og, SZ], [og])
            self.tt(od[:, :], od[:, :], SZ[:, 1024:2048], ALU.mult, [od, SZ], [od])
            self.tt(og[:, :], og[:, :], SG[:, 0:1024], ALU.mult, [og, SG], [og])
            self.tt(od[:, :], od[:, :], SG[:, 1024:2048], ALU.mult, [od, SG], [od])
            self.tt(MB[:, :], og[:, :], od[:, :], ALU.add, [og, od], [MB])
            for kc in range(8):
                self.tr(self.pb16(5, kc * 128, 128), MB[:, kc * 128:(kc + 1) * 128], self.identb[:, :],
                        [MB, self.identb], [self.bank[5]])
            self.act(AP(MT_, 0, [[1024, 128], [1, 1024]]), self.pb16(5, 0, 1024), AF.Copy, [self.bank[5]], [MT_])

        def stage_b2(t):
            i = t % 2
            MT_ = mT[i]
            for half in range(2):
                bk = 6 + half
                for kc in range(8):
                    self.mm(self.pf(bk, 0, 512), MT_[:, kc, :], Wo[:, kc, half * 512:(half + 1) * 512], kc == 0, kc == 7,
                            [MT_, Wo], [self.bank[bk]])

        def stage_c(t):
            i = t % 2
            X = xt[t % 3]
            self.tt(x1[i][:, :], AP(self.PP[3], 0, [[1024, 128], [1, 1024]]), self.G1, ALU.mult,
                    [self.bank[6], self.bank[7], self.modL], [x1[i]])
            self.tt(x1[i][:, :], x1[i][:, :], X[:, :], ALU.add, [x1[i], X], [x1[i]])
            self.store(AP(self.dx1, t * 128 * DM, [[DM, 128], [1, DM]]), x1[i][:, :], x1[i], self.dx1)

        issue_loads(0)
        if ntile > 1:
            issue_loads(1)
        stage_a0(0)
        stage_a(0)
        for t in range(ntile):
            if t + 1 < ntile:
                stage_a0(t + 1)
            stage_b(t)
            if t + 1 < ntile:
                stage_a(t + 1)
            stage_b1(t)
            if t >= 1:
                stage_c(t - 1)
            stage_b2(t)
            if t + 2 < ntile:
                issue_loads(t + 2)
        stage_c(ntile - 1)
        kb.barrier()
        kb.release(m)

    def phase3b(self):
        kb = self.kb
        m = kb.mark()
        Wg = kb.sb("Wg", [128, 8, FFH], BF16)
        Wu = kb.sb("Wu", [128, 8, FFH], BF16)
        Wd = kb.sb("Wd", [128, 22, DM], BF16)
        for (dst, src) in ((Wg, self.wg), (Wu, self.wu)):
            for k0 in range(0, 8, 2):
                self.kb.dma("pool", AP(dst, k0 * FFH, [[8 * FFH, 128], [FFH, 2], [1, FFH]]),
                            AP(src, k0 * 128 * FFH, [[FFH, 128], [128 * FFH, 2], [1, FFH]]), [src], [dst])
        for k0 in range(0, 22, 2):
            self.kb.dma("pool", AP(Wd, k0 * DM, [[22 * DM, 128], [DM, 2], [1, DM]]),
                        AP(self.wd, k0 * 128 * DM, [[DM, 128], [128 * DM, 2], [1, DM]]), [self.wd], [Wd])
        fn = kb.sb("fnw", [128, DM], F32)
        self.load(fn[:, :], self.fnbc[:, :], self.fnbc, fn)
        xt = [kb.sb("x1t%d" % i, [128, 2, DM], F32) for i in range(2)]
        h2Ts = [kb.sb("h2T%d" % i, [128, 8, TT], BF16) for i in range(2)]
        sgl = kb.sb("sgl", [128, TT], F32)
        actT = kb.sb("actT", [128, 22, TT], BF16)
        ty = kb.sb("ty2", [128, 512], F32)
        x2s = [kb.sb("x2_%d" % i, [128, DM], F32) for i in range(2)]
        print("phase3b sbuf remaining", self.nc.sbuf_bytes_remaining)
        ss, rs = self.ssq, self.rsd
        ntile = self.L // TT

        def issue_load(t):
            self.load(xt[t % 2][:, :, :], AP(self.dx1, t * TT * DM, [[DM, 128], [128 * DM, 2], [1, DM]]), self.dx1, xt[t % 2])

        def do_norm(t):
            for sub in range(2):
                self.norm_T(xt[t % 2][:, sub, :], xt[t % 2], self.A2, self.B2, self.modL2, h2Ts[t % 2], TT, sub, 0)

        issue_load(0)
        if ntile > 1:
            issue_load(1)
        do_norm(0)
        oc = 0
        for t in range(ntile):
            X = xt[t % 2]
            h2T = h2Ts[t % 2]
            for hc in range(22):
                bg = 1 + (hc % 2) * 2
                bu = bg + 1
                for kc in range(8):
                    self.mm(self.pf(bg, 0, TT), Wg[:, kc, hc * 128:(hc + 1) * 128], h2T[:, kc, :], kc == 0, kc == 7,
                            [Wg, h2T], [self.bank[bg]])
                for kc in range(8):
                    self.mm(self.pf(bu, 0, TT), Wu[:, kc, hc * 128:(hc + 1) * 128], h2T[:, kc, :], kc == 0, kc == 7,
                            [Wu, h2T], [self.bank[bu]])
                self.act(sgl[:, :], self.pf(bg, 0, TT), AF.Silu, [self.bank[bg]], [sgl])
                self.tt(actT[:, hc, :], sgl[:, :], self.pf(bu, 0, TT), ALU.mult, [sgl, self.bank[bu]], [actT])
            if t + 1 < ntile:
                do_norm(t + 1)
            for sub in range(2):
                x2 = x2s[oc % 2]
                for half in range(2):
                    bk = 5 + half
                    for hc in range(22):
                        self.mm(self.pf(bk, 0, 512), actT[:, hc, sub * 128:(sub + 1) * 128],
                                Wd[:, hc, half * 512:(half + 1) * 512], hc == 0, hc == 21, [actT, Wd], [self.bank[bk]])
                    hs = slice(half * 512, (half + 1) * 512)
                    self.tt(ty[:, :], self.pf(bk, 0, 512), AP(self.modL2, 2 * DM + half * 512, [[3 * DM, 128], [1, 512]]),
                            ALU.mult, [self.bank[bk], self.modL2], [ty])
                    self.tt(x2[:, hs], ty[:, :], X[:, sub, hs], ALU.add, [ty, X], [x2])
                self.act(self.ntmp[:, :], x2[:, :], AF.Square, [x2], [self.ntmp, ss], accum=ss[:, 2:3])
                self.act(rs[:, 2:3], ss[:, 2:3], AF.Sqrt, [ss], [rs], scale=1.0 / DM, bias=self.epsb[:, 0:1])
                self.recip(rs[:, 3:4], rs[:, 2:3], [rs], [rs])
                O = x2
                oc += 1
                self.stt(O[:, :], x2[:, :], rs[:, 3:4], fn[:, :], ALU.mult, ALU.mult, [x2, rs, fn], [O])
                self.store(AP(self.out, (t * TT + sub * 128) * DM, [[DM, 128], [1, DM]]), O[:, :], O, self.out)
            if t + 2 < ntile:
                issue_load(t + 2)
        kb.barrier()
        kb.release(m)

    def build(self, phases="0ASBG3F"):
        kb = self.kb
        self.consts()
        self.epsb = kb.sb("epsb", [128, 1], F32)
        self.oneb = kb.sb("oneb", [128, 1], F32)
        self.lncb = kb.sb("lncb", [128, 1], F32)
        self.zerob = kb.sb("zerob", [128, 1], F32)
        self.memset(self.lncb[:, :], -0.5 * float(np.log(128.0)), [self.lncb])
        self.memset(self.zerob[:, :], 0.0, [self.zerob])
        self.memset(self.epsb[:, :], EPS, [self.epsb])
        self.memset(self.oneb[:, :], 1.0, [self.oneb])
        NCH = self.NCH
        self.modL2 = kb.sb("modL2", [128, 3 * DM], F32)
        ma = kb.mark()
        self.modL1 = kb.sb("modL1", [128, 3 * DM], F32)
        mb_ = kb.mark()
        self.modC = kb.sb("modC", [128, 2 * DM], F32)
        self.Egla = kb.sb("Egla", [128, 2 * NCH * 4], F32)
        self.GDa = kb.sb("GDa", [128, NCH * 16], F32)
        self.GDd = kb.sb("GDd", [128, NCH * 16], F32)
        self.GDe = kb.sb("GDe", [128, NCH * 16], F32)
        self.phase0()
        if "A" in phases:
            self.phaseA()
        if "S" in phases:
            self.gla_scan()
        if "B" in phases:
            self.phaseB()
        if "G" in phases:
            self.gdn_scan()
        kb.release(mb_)
        if "3" in phases:
            self.phase3a()
        kb.release(ma)
        if "F" in phases:
            self.phase3b()
        kb.barrier()
        kb.finalize()
        kb.close()
        return self.nc


def host_consts():
    p = np.arange(128)[:, None]
    f = np.arange(128)[None, :]
    le = (p <= f).astype(np.float32)
    ge = (p >= f).astype(np.float32)
    lt = (p < f).astype(np.float32)
    gt = (p > f).astype(np.float32)
    cumU = np.stack([le * (-1.0 / 16.0), ge * (-1.0 / 16.0), le, ge, gt, lt]).astype(np.float32)
    gmask = np.stack([le, ge]).astype(np.float32)
    dmask = np.stack([(1 - le) * NEG, (1 - ge) * NEG, (1 - lt) * NEG, (1 - gt) * NEG]).astype(np.float32)
    sel = np.zeros((96, 32, 128), np.float32)
    for c in range(32):
        for part in range(3):
            sel[part * 32 + c, c, :] = 1.0
    lv = np.zeros((14, 128, 128), np.float32)
    pi = np.arange(128)[:, None]
    fj = np.arange(128)[None, :]
    for s_ in range(7):
        b = 1 << s_
        mlow = ((pi // (2 * b)) == (fj // (2 * b))) & ((pi % (2 * b)) >= b) & ((fj % (2 * b)) < b)
        dg = np.eye(128, dtype=np.float32) if s_ >= 1 else 0.0
        lv[s_] = -mlow.astype(np.float32) - dg
        lv[7 + s_] = -mlow.T.astype(np.float32) - dg
    return dict(identf=np.eye(128, dtype=np.float32), cumU=cumU, gmask=gmask, dmask=dmask,
                sel=sel.reshape(96, 32 * 128), lvlm=lv)


def host_inputs(b, x, c, ctx, c_ctx, w_mod, b_mod, norm1_w, norm2_w, w_in, gla_lr_w, gla_lr_b, gla_norm_w,
                gdn_conv_w, gdn_a_log, gdn_dt_bias, gdn_norm_w, w_out, ffn_w_gate, ffn_w_up, ffn_w_down,
                final_norm_w, shared):
    f = lambda a: np.ascontiguousarray(a, dtype=np.float32)
    d = dict(shared)
    d["x"] = f(x[b])
    d["ctx"] = f(ctx[b])
    d["cT"] = f(np.asarray(c[b]).reshape(8, 128).T)
    return d


def shared_inputs(c_ctx, w_mod, b_mod, norm1_w, norm2_w, w_in, gla_lr_w, gla_lr_b, gla_norm_w,
                  gdn_conv_w, gdn_a_log, gdn_dt_bias, gdn_norm_w, w_out, ffn_w_gate, ffn_w_up, ffn_w_down,
                  final_norm_w):
    f = lambda a: np.ascontiguousarray(a, dtype=np.float32)
    bc = lambda v: f(np.broadcast_to(np.asarray(v).reshape(1, -1), (128, np.asarray(v).size)))
    d = host_consts()
    d["cctxT"] = f(np.asarray(c_ctx).reshape(8, 128).T)
    d["w_mod"] = f(w_mod[0])
    d["b_mod"] = f(np.asarray(b_mod[0]).reshape(1, -1))
    d["n1bc"] = bc(norm1_w[0])
    d["n2bc"] = bc(norm2_w[0])
    d["fnbc"] = bc(final_norm_w)
    d["w_in"] = f(w_in[0])
    lrw = np.zeros((2, 33, 512), np.float32)
    lrw[0, 0:16] = np.asarray(gla_lr_w[0, 0])
    lrw[1, 16:32] = np.asarray(gla_lr_w[0, 1])
    lrw[0, 32] = np.asarray(gla_lr_b[0, 0])
    lrw[1, 32] = np.asarray(gla_lr_b[0, 1])
    d["lrw"] = lrw
    d["glag"] = bc(np.asarray(gla_norm_w[0]).reshape(-1))
    d["gdng"] = bc(np.asarray(gdn_norm_w[0]).reshape(-1))
    cw = np.asarray(gdn_conv_w[0])
    d["cwT"] = f(cw.reshape(5, 24, 128).transpose(2, 1, 0).reshape(128, 120))
    d["alog"] = bc(np.asarray(gdn_a_log[0]).reshape(-1))
    d["dtb"] = bc(np.asarray(gdn_dt_bias[0]).reshape(-1))
    d["w_out"] = f(w_out[0])
    d["wg"] = f(ffn_w_gate[0])
    d["wu"] = f(ffn_w_up[0])
    d["wd"] = f(ffn_w_down[0])
    return d


_CACHE = {}


def kernel(x, c, ctx, c_ctx, w_mod, b_mod, norm1_w, norm2_w, w_in, gla_lr_w, gla_lr_b, gla_norm_w,
           gdn_conv_w, gdn_a_log, gdn_dt_bias, gdn_norm_w, w_out, ffn_w_gate, ffn_w_up, ffn_w_down,
           final_norm_w):
    x = np.asarray(x)
    B, L, _ = x.shape
    W = L // 128
    nc = Prog(W).build()
    shared = shared_inputs(c_ctx, w_mod, b_mod, norm1_w, norm2_w, w_in, gla_lr_w, gla_lr_b, gla_norm_w,
                           gdn_conv_w, gdn_a_log, gdn_dt_bias, gdn_norm_w, w_out, ffn_w_gate, ffn_w_up,
                           ffn_w_down, final_norm_w)
    f = lambda a: np.ascontiguousarray(a, dtype=np.float32)
    in_maps = []
    for b in range(B):
        d = dict(shared)
        d["x"] = f(x[b])
        d["ctx"] = f(np.asarray(ctx)[b])
        d["cT"] = f(np.asarray(c)[b].reshape(8, 128).T)
        in_maps.append(d)
    res = run_bass_kernel_spmd(nc, in_maps, core_ids=list(range(B)))
    return np.stack([np.asarray(r["out"], dtype=np.float32) for r in res.results], axis=0)
```
